# Optimizing a Trainium2 kernel written in Bass

```python
import math
import jax, jax.numpy as jnp
from jax import lax
import numpy as np

D_MODEL = 1024
BATCH = 2
SEQ = 8192
DEPTH = 1

N_META = 16
MIX_WIDTH = D_MODEL
POOL_WIDTH = MIX_WIDTH // 2
POOL_WINDOWS = (2, 4, 8, 16)
POOL_GROUPS = len(POOL_WINDOWS)
POOL_GROUP_DIM = POOL_WIDTH // POOL_GROUPS
ATTN_WIDTH = MIX_WIDTH - POOL_WIDTH
N_HEADS = 4
V_DIM = ATTN_WIDTH // N_HEADS
QK_DIM = V_DIM // 2
QK_COLS = N_HEADS * 2 * QK_DIM
IN_COLS = POOL_WIDTH + 2 * QK_COLS + ATTN_WIDTH
D_FF = 2816
CONV_WIDTH = 3
Q_BLOCK = 128
EPS = 1e-6

kernel_name = "hymba_pool_diffattn_convglu"


def rmsnorm(x, g):
    xf = x.astype(jnp.float32)
    y = xf * lax.rsqrt(jnp.mean(xf * xf, axis=-1, keepdims=True) + EPS)
    return (y * g.astype(jnp.float32)).astype(x.dtype)


def causal_multiscale_pool(u):
    B, L, _ = u.shape
    ug = u.reshape(B, L, POOL_GROUPS, POOL_GROUP_DIM).astype(jnp.float32)
    cs = jnp.cumsum(ug, axis=1)
    t = jnp.arange(L)
    outs = []
    for g, w in enumerate(POOL_WINDOWS):
        c = cs[:, :, g]
        prev = jnp.pad(c, ((0, 0), (w, 0), (0, 0)))[:, :L]
        cnt = jnp.minimum(t + 1, w).astype(jnp.float32)[None, :, None]
        outs.append((c - prev) / cnt - ug[:, :, g])
    return jnp.stack(outs, axis=2)


def diff_attention(q, k, v, lam, lam_init, subln_g):
    B, L = q.shape[:2]
    Lp = -(-L // Q_BLOCK) * Q_BLOCK
    pad = Lp - L
    q = jnp.pad(q, ((0, 0), (0, pad), (0, 0), (0, 0), (0, 0)))
    k = jnp.pad(k, ((0, 0), (0, pad), (0, 0), (0, 0), (0, 0)))
    v = jnp.pad(v, ((0, 0), (0, pad), (0, 0), (0, 0)))
    kt = k.transpose(0, 2, 3, 1, 4)
    vt = v.transpose(0, 2, 1, 3)
    nblk = Lp // Q_BLOCK
    qb = q.reshape(B, nblk, Q_BLOCK, N_HEADS, 2, QK_DIM).transpose(1, 0, 3, 4, 2, 5)
    key_pos = jnp.arange(Lp)
    scale = QK_DIM ** -0.5

    def one_block(args):
        qblk, i = args
        s = jnp.einsum('bhcqd,bhckd->bhcqk', qblk, kt).astype(jnp.float32) * scale
        q_pos = i * Q_BLOCK + jnp.arange(Q_BLOCK)
        mask = key_pos[None, :] <= q_pos[:, None]
        s = jnp.where(mask, s, -jnp.inf)
        p = jax.nn.softmax(s, axis=-1)
        a = p[:, :, 0] - lam * p[:, :, 1]
        return jnp.einsum('bhqk,bhkd->bqhd', a.astype(vt.dtype), vt)

    o = lax.map(one_block, (qb, jnp.arange(nblk)))
    o = o.transpose(1, 0, 2, 3, 4).reshape(B, Lp, N_HEADS, V_DIM)[:, :L]
    o = rmsnorm(o, subln_g) * (1.0 - lam_init)
    return o.reshape(B, L, ATTN_WIDTH)


def conv_glu_ffn(h, w_up, conv_w, conv_b, w_down):
    u = h @ w_up
    L = u.shape[1]
    up = jnp.pad(u, ((0, 0), (CONV_WIDTH - 1, 0), (0, 0)))
    c = up[:, 0:L] * conv_w[0] + up[:, 1:L + 1] * conv_w[1] + up[:, 2:L + 2] * conv_w[2] + conv_b
    gate, val = jnp.split(c, 2, axis=-1)
    return (jax.nn.silu(gate) * val) @ w_down


def setup_inputs(seed: int = 0) -> dict:
    key = jax.random.key(seed)
    ks = jax.random.split(key, 20)
    f32 = jnp.float32
    nrm = lambda k, shape, s: (jax.random.normal(k, shape, f32) * s)
    Dp = DEPTH
    return {
        "x": nrm(ks[0], (BATCH, SEQ, D_MODEL), 1.0),
        "meta_tokens": nrm(ks[1], (N_META, D_MODEL), 1.0),
        "norm_mix_g": 1.0 + nrm(ks[2], (Dp, D_MODEL), 0.02),
        "w_in": nrm(ks[3], (Dp, D_MODEL, IN_COLS), D_MODEL ** -0.5),
        "w_pool": nrm(ks[4], (Dp, POOL_GROUPS, POOL_GROUP_DIM, POOL_GROUP_DIM), POOL_GROUP_DIM ** -0.5),
        "b_pool": nrm(ks[5], (Dp, POOL_GROUPS, POOL_GROUP_DIM), 0.02),
        "pool_scale": 1.0 + nrm(ks[6], (Dp, POOL_WIDTH), 0.05),
        "q_norm_g": 1.0 + nrm(ks[7], (Dp, QK_DIM), 0.02),
        "k_norm_g": 1.0 + nrm(ks[8], (Dp, QK_DIM), 0.02),
        "lambda_q1": nrm(ks[9], (Dp, QK_DIM), 0.1),
        "lambda_k1": nrm(ks[10], (Dp, QK_DIM), 0.1),
        "lambda_q2": nrm(ks[11], (Dp, QK_DIM), 0.1),
        "lambda_k2": nrm(ks[12], (Dp, QK_DIM), 0.1),
        "subln_g": 1.0 + nrm(ks[13], (Dp, V_DIM), 0.02),
        "w_out": nrm(ks[14], (Dp, MIX_WIDTH, D_MODEL), MIX_WIDTH ** -0.5),
        "norm_ffn_g": 1.0 + nrm(ks[15], (Dp, D_MODEL), 0.02),
        "w_up": nrm(ks[16], (Dp, D_MODEL, 2 * D_FF), D_MODEL ** -0.5),
        "conv_w": nrm(ks[17], (Dp, CONV_WIDTH, 2 * D_FF), 0.3) + jnp.array([0.0, 0.0, 1.0], f32)[None, :, None],
        "conv_b": nrm(ks[18], (Dp, 2 * D_FF), 0.02),
        "w_down": nrm(ks[19], (Dp, D_FF, D_MODEL), D_FF ** -0.5),
    }


def reference(x, meta_tokens, norm_mix_g, w_in, w_pool, b_pool, pool_scale, q_norm_g, k_norm_g,
              lambda_q1, lambda_k1, lambda_q2, lambda_k2, subln_g, w_out, norm_ffn_g,
              w_up, conv_w, conv_b, w_down):
    B = x.shape[0]
    meta = jnp.broadcast_to(meta_tokens[None].astype(x.dtype), (B, N_META, D_MODEL))
    h = jnp.concatenate([meta, x], axis=1)
    L = h.shape[1]
    for i in range(DEPTH):
        lam_init = 0.8 - 0.6 * math.exp(-0.3 * i)
        n = rmsnorm(h, norm_mix_g[i])
        proj = n @ w_in[i]
        u, q, k, v = jnp.split(proj, [POOL_WIDTH, POOL_WIDTH + QK_COLS, POOL_WIDTH + 2 * QK_COLS], axis=-1)
        pooled = causal_multiscale_pool(u)
        ya = jnp.einsum('blgc,gcd->blgd', pooled, w_pool[i].astype(jnp.float32)) + b_pool[i].astype(jnp.float32)
        ya = (ya.reshape(B, L, POOL_WIDTH) * pool_scale[i].astype(jnp.float32)).astype(h.dtype)
        q = rmsnorm(q.reshape(B, L, N_HEADS, 2, QK_DIM), q_norm_g[i])
        k = rmsnorm(k.reshape(B, L, N_HEADS, 2, QK_DIM), k_norm_g[i])
        v = v.reshape(B, L, N_HEADS, V_DIM)
        lam = (jnp.exp(jnp.sum(lambda_q1[i].astype(jnp.float32) * lambda_k1[i].astype(jnp.float32)))
               - jnp.exp(jnp.sum(lambda_q2[i].astype(jnp.float32) * lambda_k2[i].astype(jnp.float32)))
               + lam_init)
        yb = diff_attention(q, k, v, lam, lam_init, subln_g[i])
        mix = jnp.concatenate([ya, yb], axis=-1)
        h = h + mix @ w_out[i]
        h = h + conv_glu_ffn(rmsnorm(h, norm_ffn_g[i]), w_up[i], conv_w[i], conv_b[i], w_down[i])
    return h[:, N_META:]
```

```python
import contextlib
import numpy as np
import concourse.bass as bass
import concourse.mybir as mybir
from concourse.bass_utils import run_bass_kernel_spmd

F32 = mybir.dt.float32
BF16 = mybir.dt.bfloat16
AF = mybir.ActivationFunctionType
ALU = mybir.AluOpType
AX = mybir.AxisListType

ENGS = ("pe", "act", "dve", "pool", "sp")


class Res:
    __slots__ = ("name", "last_w", "readers", "dma_sem", "dma_cnt", "last_dma")

    def __init__(self, name):
        self.name = name
        self.last_dma = None
        self.last_w = None
        self.readers = []
        self.dma_sem = None
        self.dma_cnt = 0


class Ins:
    __slots__ = ("eng", "fn", "deps", "inc_val", "is_dma", "dma_res", "dma_val", "needed")

    def __init__(self, eng, fn):
        self.eng = eng
        self.fn = fn
        self.deps = []
        self.inc_val = None
        self.is_dma = False
        self.dma_res = None
        self.dma_val = 0
        self.needed = False


class Prog:
    def __init__(self, nc):
        self.nc = nc
        self.streams = {e: [] for e in ENGS}
        self.all_res = []
        self.barrier_deps = []
        self.barrier_seen = {e: True for e in ENGS}

    def res(self, name):
        r = Res(name)
        self.all_res.append(r)
        return r

    def barrier(self):
        deps = []
        for e in ENGS:
            if self.streams[e]:
                deps.append(self.streams[e][-1])
        for r in self.all_res:
            if r.last_dma is not None:
                deps.append(r.last_dma)
        self.barrier_deps = deps
        self.barrier_seen = {e: False for e in ENGS}

    def _add(self, eng, fn, reads, writes, is_dma=False, sem_res=None):
        ins = Ins(eng, fn)
        ins.is_dma = is_dma
        deps = []
        if not self.barrier_seen[eng]:
            self.barrier_seen[eng] = True
            deps.extend(self.barrier_deps)
        for r in reads:
            if r.last_w is not None:
                deps.append(r.last_w)
        for r in writes:
            if r.last_w is not None:
                deps.append(r.last_w)
            deps.extend(r.readers)
        seen = set()
        for d in deps:
            if d is ins or id(d) in seen:
                continue
            seen.add(id(d))
            if d.eng == eng and not d.is_dma and eng in ("pe", "sp"):
                continue
            ins.deps.append(d)
            d.needed = True
        for r in reads:
            r.readers.append(ins)
        for r in writes:
            r.last_w = ins
            r.readers = []
        if is_dma:
            tgt = sem_res if sem_res is not None else writes[0]
            ins.dma_res = tgt
            tgt.dma_cnt += 16
            ins.dma_val = tgt.dma_cnt
            tgt.last_dma = ins
        self.streams[eng].append(ins)
        return ins

    def pe(self, fn, reads=(), writes=()):
        return self._add("pe", fn, list(reads), list(writes))

    def act(self, fn, reads=(), writes=()):
        return self._add("act", fn, list(reads), list(writes))

    def dve(self, fn, reads=(), writes=()):
        return self._add("dve", fn, list(reads), list(writes))

    def pool(self, fn, reads=(), writes=()):
        return self._add("pool", fn, list(reads), list(writes))

    def dma(self, fn, reads=(), writes=(), q="sp", sem_res=None):
        return self._add(q, fn, list(reads), list(writes), is_dma=True, sem_res=sem_res)

    def emit(self, final_waits=()):
        nc = self.nc
        with contextlib.ExitStack() as st:
            sems = {}
            for e in ENGS:
                sems[e] = st.enter_context(nc.semaphore("s_" + e))
            for r in self.all_res:
                if r.dma_cnt > 0:
                    r.dma_sem = st.enter_context(nc.semaphore("d_" + r.name))
            for e in ENGS:
                c = 0
                for ins in self.streams[e]:
                    if ins.is_dma:
                        continue
                    if ins.needed:
                        c += 1
                        ins.inc_val = c
            block = st.enter_context(nc.Block())
            engmap = {"pe": "tensor", "act": "scalar", "dve": "vector", "pool": "gpsimd", "sp": "sync"}

            def make(e):
                def body(eng):
                    waited = {}
                    for ins in self.streams[e]:
                        for d in ins.deps:
                            if d.is_dma:
                                key = ("d", id(d.dma_res))
                                val = d.dma_val
                                sem = d.dma_res.dma_sem
                            else:
                                key = ("e", d.eng)
                                val = d.inc_val
                                sem = sems[d.eng]
                            if waited.get(key, 0) >= val:
                                continue
                            waited[key] = val
                            eng.wait_ge(sem, val)
                        r = ins.fn(eng)
                        if ins.is_dma:
                            r.then_inc(ins.dma_res.dma_sem, 16)
                        elif ins.needed:
                            r.then_inc(sems[e], 1)
                    if e == "sp":
                        for d in final_waits:
                            eng.wait_ge(d.dma_res.dma_sem, d.dma_val)
                return body

            for e in ENGS:
                getattr(block, engmap[e])(make(e))


D = 1024
NQ = 1056
QT_SIZES = [32] + [128] * 8
QT_OFFS = [0] + [32 + 128 * i for i in range(8)]
GROUPS = [(0, 1, 2), (3, 4, 5), (6, 7, 8)]
DFF = 2816
EPS = 1e-6
LAM_INIT = 0.2
NEG = -30000.0
FG = 256
NPASS = 2
FB_PER_PASS = 22 // NPASS


def build_program():
    nc = bass.Bass("TRN2", target_bir_lowering=False)
    dr = lambda name, shape, kind="ExternalInput": nc.dram_tensor(name, shape, F32, kind=kind).ap()
    xq = dr("xq", [2, NQ, D])
    xk = dr("xk", [7, 1024, D])
    xm = dr("xm", [16, D])
    kval = dr("kval", [7, 1024])
    qval = dr("qval", [2, NQ])
    mval = dr("mval", [2, 16])
    sel = dr("sel", [14])
    pc16 = dr("pc16", [2, NQ])
    w_in = dr("w_in", [D, 2048])
    w_pool = dr("w_pool", [4, 128, 128])
    b_pool = dr("b_pool", [4, 128])
    pool_scale = dr("pool_scale", [512])
    qg = dr("q_norm_g", [64])
    kg = dr("k_norm_g", [64])
    lq1 = dr("lambda_q1", [64])
    lk1 = dr("lambda_k1", [64])
    lq2 = dr("lambda_q2", [64])
    lk2 = dr("lambda_k2", [64])
    subg = dr("subln_g", [128])
    w_out = dr("w_out", [D, D])
    gmix_d = dr("norm_mix_g", [D])
    gffn_d = dr("norm_ffn_g", [D])
    w_up = dr("w_up", [D, 2 * DFF])
    conv_w = dr("conv_w", [3, 2 * DFF])
    conv_b = dr("conv_b", [2 * DFF])
    w_down = dr("w_down", [DFF, D])
    y = dr("y", [2, 1024, D], kind="ExternalOutput")
    hmid = dr("hmid", [2, NQ, D], kind="Internal")

    p = Prog(nc)
    R = p.res

    with contextlib.ExitStack() as outer:
        def SB(st, name, shape, dt):
            return st.enter_context(nc.sbuf_tensor(name, shape, dt))

        def PS(st, name, shape, dt):
            return st.enter_context(nc.psum_tensor(name, shape, dt))

        ident = SB(outer, "ident", [128, 128], BF16)
        identf = SB(outer, "identf", [128, 128], F32)
        maskneg = SB(outer, "maskneg", [128, 128], BF16)
        blk1 = SB(outer, "blk1", [128, 128], BF16)
        gmix_b = SB(outer, "gmix_b", [128, D], F32)
        qkg = SB(outer, "qkg", [128, 2], F32)
        selt = SB(outer, "selt", [128, 14], F32)
        lamt = SB(outer, "lamt", [128, 4, 64], F32)
        lamw = SB(outer, "lamw", [128, 8], F32)
        subgt = SB(outer, "subgt", [128, 1], F32)
        bpt = SB(outer, "bpt", [128, 4], F32)
        pst = SB(outer, "pst", [128, 4], F32)
        bps = SB(outer, "bps", [128, 4], F32)
        ones4 = SB(outer, "ones4", [128, 4], F32)
        zer = SB(outer, "zer", [128, 512], BF16)
        cw = SB(outer, "cw", [128, 3, 44], F32)
        cb = SB(outer, "cb", [128, 44], F32)
        mixYb = SB(outer, "mixYb", [128, 4, 2, NQ], BF16)
        r_const = R("const")
        r_mixT = [[R(f"mixT{X}_{t}") for t in range(9)] for X in range(2)]
        r_mixYa = [R(f"mixYa{X}") for X in range(2)]

        pb_h = [PS(outer, f"pb{i}", [128, 512], F32) for i in range(4)]
        sc_h = [PS(outer, f"sc{i}", [128, 1024], F32) for i in range(2)]
        pbank = list(pb_h) + [sc_h[i // 2][:, (i % 2) * 512:(i % 2 + 1) * 512] for i in range(4)]
        r_pb = [R(f"pb{i}") for i in range(8)]
        pbank_bf = [b.bitcast(BF16) for b in pb_h]

        p.pool(lambda e: e.memset(identf[:], 0.0), writes=[r_const])
        p.pool(lambda e: e.affine_select(out=identf[:], in_=identf[:], compare_op=ALU.not_equal, fill=1.0,
                                         base=0, pattern=[[-1, 128]], channel_multiplier=1),
               reads=[r_const], writes=[r_const])
        p.dve(lambda e: e.tensor_copy(out=ident[:], in_=identf[:]), reads=[r_const], writes=[r_const])
        p.pool(lambda e: e.memset(identf[:], 0.0), reads=[r_const], writes=[r_const])
        p.pool(lambda e: e.affine_select(out=identf[:], in_=identf[:], compare_op=ALU.is_ge, fill=NEG,
                                         base=0, pattern=[[1, 128]], channel_multiplier=-1),
               reads=[r_const], writes=[r_const])
        p.dve(lambda e: e.tensor_copy(out=maskneg[:], in_=identf[:]), reads=[r_const], writes=[r_const])
        p.dve(lambda e: e.memset(blk1[:], 0.0), reads=[r_const], writes=[r_const])
        p.dve(lambda e: e.memset(blk1[0:64, 0:64], 1.0), reads=[r_const], writes=[r_const])
        p.dve(lambda e: e.memset(blk1[64:128, 64:128], 1.0), reads=[r_const], writes=[r_const])
        p.dve(lambda e: e.memset(ones4[:], 1.0), reads=[r_const], writes=[r_const])
        p.dve(lambda e: e.memset(zer[:], 0.0), reads=[r_const], writes=[r_const])

        def small_dma(out_ap, in_ap):
            p.dma(lambda e: e.dma_start(out=out_ap, in_=in_ap, allow_slow_non_contiguous=True),
                  reads=[], writes=[r_const], q="sp")

        small_dma(gmix_b[:], gmix_d.partition_broadcast(128))
        small_dma(qkg[0:64, 0:1], qg.rearrange("(p o) -> p o", o=1))
        small_dma(qkg[64:128, 0:1], qg.rearrange("(p o) -> p o", o=1))
        small_dma(qkg[0:64, 1:2], kg.rearrange("(p o) -> p o", o=1))
        small_dma(qkg[64:128, 1:2], kg.rearrange("(p o) -> p o", o=1))
        small_dma(selt[:], sel.partition_broadcast(128))
        for i, l in enumerate((lq1, lk1, lq2, lk2)):
            small_dma(lamt[:, i, :], l.partition_broadcast(128))
        small_dma(subgt[:], subg.rearrange("(p o) -> p o", o=1))
        small_dma(bpt[:], b_pool.rearrange("g p -> p g"))
        small_dma(pst[:], pool_scale.rearrange("(g p) -> p g", p=128))
        small_dma(cw[:], conv_w.rearrange("k (f p) -> p k f", p=128))
        small_dma(cb[:], conv_b.rearrange("(f p) -> p f", p=128))
        p.dve(lambda e: e.tensor_scalar(out=qkg[:, 0:1], in0=qkg[:, 0:1], scalar1=0.125, scalar2=None, op0=ALU.mult),
              reads=[r_const], writes=[r_const])
        p.dve(lambda e: e.tensor_tensor(out=lamt[:, 0, :], in0=lamt[:, 0, :], in1=lamt[:, 1, :], op=ALU.mult),
              reads=[r_const], writes=[r_const])
        p.dve(lambda e: e.tensor_tensor(out=lamt[:, 2, :], in0=lamt[:, 2, :], in1=lamt[:, 3, :], op=ALU.mult),
              reads=[r_const], writes=[r_const])
        p.dve(lambda e: e.reduce_sum(out=lamw[:, 0:1], in_=lamt[:, 0, :], axis=AX.X), reads=[r_const], writes=[r_const])
        p.dve(lambda e: e.reduce_sum(out=lamw[:, 1:2], in_=lamt[:, 2, :], axis=AX.X), reads=[r_const], writes=[r_const])
        p.act(lambda e: e.activation(out=lamw[:, 2:4], in_=lamw[:, 0:2], func=AF.Exp), reads=[r_const], writes=[r_const])
        p.dve(lambda e: e.tensor_tensor(out=lamw[:, 4:5], in0=lamw[:, 3:4], in1=lamw[:, 2:3], op=ALU.subtract),
              reads=[r_const], writes=[r_const])
        p.dve(lambda e: e.tensor_scalar(out=lamw[:, 6:7], in0=lamw[:, 4:5], scalar1=-LAM_INIT, scalar2=None, op0=ALU.add),
              reads=[r_const], writes=[r_const])
        p.dve(lambda e: e.tensor_tensor(out=bps[:], in0=bpt[:], in1=pst[:], op=ALU.mult), reads=[r_const], writes=[r_const])
        p.dve(lambda e: e.tensor_scalar(out=subgt[:], in0=subgt[:], scalar1=1.0 - LAM_INIT, scalar2=None, op0=ALU.mult),
              reads=[r_const], writes=[r_const])

        NFE = 2
        xt = [SB(outer, f"xt{i}", [128, D], F32) for i in range(NFE)]
        r_xt = [R(f"xt{i}") for i in range(NFE)]
        xs = [SB(outer, f"xs{i}", [128, D], BF16) for i in range(NFE)]
        r_xs = [R(f"xs{i}") for i in range(NFE)]
        NNT = 4
        nT = [SB(outer, f"nT{i}", [128, 8, 128], BF16) for i in range(NNT)]
        r_nT = [R(f"nT{i}") for i in range(NNT)]
        st2 = [SB(outer, f"st2_{i}", [128, 4], F32) for i in range(NFE)]
        r_st2 = [R(f"st2_{i}") for i in range(NFE)]
        epst = SB(outer, "epst", [128, 1], F32)
        p.dve(lambda e: e.memset(epst[:], EPS), reads=[r_const], writes=[r_const])
        cnt = {"fe": 0, "nt": 0, "ub": 0}

        def rms_rows(src_t, r_src, T, dst_bf, r_dst, stt, r_stt, gb, r_gb=None):
            p.act(lambda e: e.activation(out=dst_bf[:T, :], in_=src_t[:T, :], func=AF.Square, accum_out=stt[:T, 0:1]),
                  reads=[r_src], writes=[r_dst, r_stt])
            p.act(lambda e: e.activation(out=stt[:T, 1:2], in_=stt[:T, 0:1], func=AF.Ln, scale=1.0 / D, bias=epst[:T, 0:1]),
                  reads=[r_stt, r_const], writes=[r_stt])
            p.act(lambda e: e.activation(out=stt[:T, 2:3], in_=stt[:T, 1:2], func=AF.Exp, scale=-0.5),
                  reads=[r_stt], writes=[r_stt])
            p.dve(lambda e: e.scalar_tensor_tensor(out=dst_bf[:T, :], in0=src_t[:T, :], scalar=stt[:T, 2:3], in1=gb[:T, :],
                                                   op0=ALU.mult, op1=ALU.mult),
                  reads=[r_src, r_stt, r_const] + ([r_gb] if r_gb else []), writes=[r_dst])

        def transpose_rows(src_bf, r_src, T, dstT, r_dstT, bank=0):
            pt = pbank_bf[bank]
            for c in range(8):
                p.pe(lambda e, c=c: e.transpose(out=pt[:, c * 128:c * 128 + T], in_=src_bf[:T, c * 128:(c + 1) * 128],
                                                identity=ident[:T, :T]),
                     reads=[r_src, r_const], writes=[r_pb[bank]])
            p.dve(lambda e: e.tensor_copy(out=dstT[:, :, :T], in_=pt[:, :].rearrange("p (c t) -> p c t", c=8)[:, :, :T]),
                  reads=[r_pb[bank]], writes=[r_dstT])

        def front_end(src_ap, T, tbank=0):
            i = cnt["fe"] % NFE
            cnt["fe"] += 1
            ni = cnt["nt"] % NNT
            cnt["nt"] += 1
            p.dma(lambda e: e.dma_start(out=xt[i][:T, :], in_=src_ap), writes=[r_xt[i]], q="sp")
            rms_rows(xt[i], r_xt[i], T, xs[i], r_xs[i], st2[i], r_st2[i], gmix_b)
            transpose_rows(xs[i], r_xs[i], T, nT[ni], r_nT[ni], bank=tbank)
            return ni

        with contextlib.ExitStack() as ab:
            wqkv = SB(ab, "wqkv", [128, 8, 1536], BF16)
            r_wqkv_l = [R(f"wqkv{c}") for c in range(8)]
            w3 = w_in.rearrange("(c p) n -> p c n", p=128)
            for c in range(8):
                p.dma(lambda e, c=c: e.dma_start(out=wqkv[:, c, :], in_=w3[:, c, 512:2048]), writes=[r_wqkv_l[c]], q="pool")

            NKV = 1
            KT = [SB(ab, f"KT{i}", [128, 4, NQ + 16], BF16) for i in range(NKV)]
            VX = [SB(ab, f"VX{i}", [128, 10, 4, 130], BF16) for i in range(NKV)]
            r_KT = [R(f"KT{i}") for i in range(NKV)]
            r_VX = [R(f"VX{i}") for i in range(NKV)]
            QT = [SB(ab, f"QT{X}", [128, 4, NQ], BF16) for X in range(2)]
            r_QT = [R(f"QT{X}") for X in range(2)]
            Qs = SB(ab, "Qs", [128, 4, NQ], BF16)
            r_Qs = R("Qs")
            OA = [SB(ab, f"O{X}", [128, 9, 4, 2, 129], F32) for X in range(2)]
            r_O = [[[R(f"O{X}_{t}_{h}") for h in range(4)] for t in range(9)] for X in range(2)]
            sqb = [SB(ab, f"sqb{i}", [128, 4, 128], BF16) for i in range(2)]
            r_sqb = [R(f"sqb{i}") for i in range(2)]
            lnb = [SB(ab, "lnb0", [128, 4, 128], F32)] * 2
            r_lnb = [R("lnb0")] * 2
            cnt["sq"] = 0
            cnt["kvt"] = 0
            valt = [SB(ab, f"valt{i}", [128, 1], F32) for i in range(4)]
            r_valt = [R(f"valt{i}") for i in range(4)]
            NPT = 2
            Pt = [SB(ab, f"Pt{i}", [128, 2, 384], BF16) for i in range(NPT)]
            r_Pt = [R(f"Pt{i}") for i in range(NPT)]
            cnt["val"] = 0
            cnt["pt"] = 0
            cnt["sc"] = 0

            rsq = [[SB(ab, f"rsq{a}_{i}", [128, 4, 128], F32) for i in range(2)] for a in range(2)]
            r_rsq = [[R(f"rsq{a}_{i}") for i in range(2)] for a in range(2)]
            KB = [[1, 2], [3, 4]]
            SBK, VBK, TBK = 5, 6, 0

            def kv_f1(c):
                i = cnt["fe"] % NFE
                cnt["fe"] += 1
                c["xi"] = i
                T = c["T"]
                p.dma(lambda e: e.dma_start(out=xt[i][:T, :], in_=c["src"]), writes=[r_xt[i]], q="sp")
                rms_rows(xt[i], r_xt[i], T, xs[i], r_xs[i], st2[i], r_st2[i], gmix_b)
                vi = cnt["val"] % 4
                cnt["val"] += 1
                c["vi"] = vi
                p.dma(lambda e: e.dma_start(out=valt[vi][:T, :], in_=c["val"]), writes=[r_valt[vi]], q="sp")

            def kv_f2(c):
                ni = cnt["nt"] % NNT
                cnt["nt"] += 1
                c["ni"] = ni
                i = c["xi"]
                transpose_rows(xs[i], r_xs[i], c["T"], nT[ni], r_nT[ni], bank=TBK)

            def kv_b1(c):
                T, ni, par = c["T"], c["ni"], c["par"]
                for a, col0 in ((0, 512), (1, 0)):
                    if a == 1 and c["own"] is None:
                        continue
                    bank = KB[a][par]
                    pk = pbank[bank]
                    for h in range(4):
                        for cc in range(8):
                            p.pe(lambda e, h=h, cc=cc, pk=pk, col0=col0: e.matmul(
                                pk[:, h * 128:h * 128 + T], lhsT=wqkv[:, cc, col0 + h * 128:col0 + (h + 1) * 128],
                                rhs=nT[ni][:, cc, :T], start=(cc == 0), stop=(cc == 7)),
                                reads=[r_wqkv_l[cc], r_nT[ni]], writes=[r_pb[bank]])
                    pk3 = pk[:, :].rearrange("p (h t) -> p h t", h=4)
                    j = cnt["sq"] % 2
                    cnt["sq"] += 1
                    p.act(lambda e, pk3=pk3, j=j: e.activation(out=sqb[j][:, :, :T], in_=pk3[:, :, :T], func=AF.Square),
                          reads=[r_pb[bank]], writes=[r_sqb[j]])
                    pss = pbank[SBK]
                    for h in range(4):
                        p.pe(lambda e, h=h, j=j, pss=pss: e.matmul(pss[:, h * 128:h * 128 + T], lhsT=blk1[:, :],
                                                                   rhs=sqb[j][:, h, :T], start=True, stop=True),
                             reads=[r_sqb[j], r_const], writes=[r_pb[SBK]])
                    pss3 = pss[:, :].rearrange("p (h t) -> p h t", h=4)
                    p.act(lambda e, pss3=pss3: e.activation(out=lnb[0][:, :, :T], in_=pss3[:, :, :T], func=AF.Ln,
                                                            scale=1.0 / 64, bias=epst[:, 0:1]),
                          reads=[r_pb[SBK], r_const], writes=[r_lnb[0]])
                    p.act(lambda e, a=a: e.activation(out=rsq[a][par][:, :, :T], in_=lnb[0][:, :, :T], func=AF.Exp, scale=-0.5),
                          reads=[r_lnb[0]], writes=[r_rsq[a][par]])

            def kv_b2(c):
                T, ni, par, kb, vidx, vi = c["T"], c["ni"], c["par"], c["kb"], c["vidx"], c["vi"]
                for a in range(2):
                    if a == 1 and c["own"] is None:
                        continue
                    bank = KB[a][par]
                    pk3 = pbank[bank][:, :].rearrange("p (h t) -> p h t", h=4)
                    if a == 0:
                        dst, r_dst, doff, gcol = KT[kb], r_KT[kb], c["koff"], 1
                    else:
                        X, doff = c["own"]
                        dst, r_dst, gcol = QT[X], r_QT[X], 0
                    p.dve(lambda e, pk3=pk3, dst=dst, doff=doff, gcol=gcol, a=a: e.scalar_tensor_tensor(
                        out=dst[:, :, doff:doff + T], in0=pk3[:, :, :T], scalar=qkg[:, gcol:gcol + 1],
                        in1=rsq[a][par][:, :, :T], op0=ALU.mult, op1=ALU.mult),
                        reads=[r_pb[bank], r_rsq[a][par], r_const], writes=[r_dst])
                pv = pbank[VBK]
                for cc in range(8):
                    p.pe(lambda e, cc=cc: e.matmul(pv[:T, :], lhsT=nT[ni][:, cc, :T], rhs=wqkv[:, cc, 1024:1536],
                                                   start=(cc == 0), stop=(cc == 7)),
                         reads=[r_wqkv_l[cc], r_nT[ni]], writes=[r_pb[VBK]])
                p.act(lambda e: e.activation(out=VX[kb][:T, vidx, :, 0:128],
                                             in_=pv[:T, :].rearrange("p (h d) -> p h d", h=4), func=AF.Copy,
                                             scale=valt[vi][:T, 0:1]),
                      reads=[r_pb[VBK], r_valt[vi]], writes=[r_VX[kb]])
                p.dve(lambda e: e.tensor_scalar(out=VX[kb][:T, vidx, :, 128:129], in0=ones4[:T, :].rearrange("p (h o) -> p h o", o=1),
                                                scalar1=valt[vi][:T, 0:1], scalar2=None, op0=ALU.mult),
                      reads=[r_valt[vi], r_const], writes=[r_VX[kb]])

            def run_kv(tiles):
                cs = []
                for (src, T, kb, koff, vidx, val, own) in tiles:
                    cs.append(dict(src=src, T=T, kb=kb, koff=koff, vidx=vidx, val=val, own=own, par=cnt["kvt"] % 2))
                    cnt["kvt"] += 1
                n = len(cs)
                for it in range(n + 3):
                    if it < n:
                        kv_f1(cs[it])
                    if 0 <= it - 1 < n:
                        kv_f2(cs[it - 1])
                    if 0 <= it - 2 < n:
                        kv_b1(cs[it - 2])
                    if 0 <= it - 3 < n:
                        kv_b2(cs[it - 3])

            def attention(kb, Qsrc, r_Qsrc, ktiles, diag, finish):
                steps = []
                for h in range(4):
                    for G in GROUPS:
                        g0 = QT_OFFS[G[0]]
                        g1 = QT_OFFS[G[-1]] + QT_SIZES[G[-1]]
                        kl = [k for k in ktiles if (not diag) or k[3] is None or k[3] <= G[-1]]
                        last_for = {}
                        for ki, (koff, nk, vidx, lt) in enumerate(kl):
                            for t in G:
                                if diag and lt is not None and lt > t:
                                    continue
                                last_for[t] = ki
                        for ki, k in enumerate(kl):
                            steps.append(dict(h=h, G=G, g0=g0, g1=g1, ki=ki, k=k, first=(ki == 0), last=(ki == len(kl) - 1),
                                              last_for=last_for))

                def emit_score(st):
                    koff, nk, vidx, lt = st["k"]
                    G, h = st["G"], st["h"]
                    qs = max(st["g0"], QT_OFFS[lt]) if (diag and lt is not None) else st["g0"]
                    n = st["g1"] - qs
                    si = cnt["sc"] % 2
                    cnt["sc"] += 1
                    st.update(qs=qs, n=n, si=si)
                    for c in range(2):
                        sb = 4 + 2 * si + c
                        has_mask = diag and lt is not None and lt >= G[0]
                        p.pe(lambda e, c=c, sb=sb: e.matmul(
                            pbank[sb][:nk, 0:n], lhsT=KT[kb][64 * c:64 * c + 64, h, koff:koff + nk],
                            rhs=Qsrc[64 * c:64 * c + 64, h, qs:qs + n], start=True, stop=True),
                            reads=[r_KT[kb], r_Qsrc], writes=[r_pb[sb]])
                        if has_mask:
                            p.pe(lambda e, c=c, sb=sb: e.matmul(
                                pbank[sb][:nk, 0:nk], lhsT=ident[:nk, :nk], rhs=maskneg[:nk, :nk],
                                start=False, stop=True, skip_group_check=True),
                                reads=[r_const], writes=[r_pb[sb]])

                def emit_rest(st):
                    koff, nk, vidx, lt = st["k"]
                    G, h, qs, n, si, ki = st["G"], st["h"], st["qs"], st["n"], st["si"], st["ki"]
                    if st["first"]:
                        for t in G:
                            ab_ = 1 + (t - G[0])
                            p.pe(lambda e, ab_=ab_, nt=QT_SIZES[t]: e.matmul(pbank[ab_][:nt, 0:258], lhsT=zer[0:1, 0:nt],
                                                                            rhs=zer[0:1, 0:258], start=True, stop=True),
                                 reads=[r_const], writes=[r_pb[ab_]])
                    pi = cnt["pt"] % NPT
                    cnt["pt"] += 1
                    scv = sc_h[si][:nk, :].rearrange("p (c m) -> p c m", c=2)
                    p.act(lambda e, scv=scv: e.activation(out=Pt[pi][:nk, :, 0:n], in_=scv[:, :, 0:n], func=AF.Exp),
                          reads=[r_pb[4 + 2 * si], r_pb[5 + 2 * si]], writes=[r_Pt[pi]])
                    for t in G:
                        if diag and lt is not None and lt > t:
                            continue
                        nt = QT_SIZES[t]
                        po = QT_OFFS[t] - qs
                        ab_ = 1 + (t - G[0])
                        acc = pbank[ab_][:, 0:258].rearrange("p (c d) -> p c d", c=2)
                        for c in range(2):
                            p.pe(lambda e, c=c, nt=nt, po=po, acc=acc, f=(st["last_for"][t] == ki): e.matmul(
                                acc[:nt, c, :], lhsT=Pt[pi][:nk, c, po:po + nt], rhs=VX[kb][:nk, vidx, h, 0:129],
                                start=False, stop=f, skip_group_check=True),
                                reads=[r_Pt[pi], r_VX[kb]], writes=[r_pb[ab_]])
                    if st["last"]:
                        for t in G:
                            ab_ = 1 + (t - G[0])
                            acc = pbank[ab_][:, 0:258].rearrange("p (c d) -> p c d", c=2)
                            finish(t, h, acc, r_pb[ab_], QT_SIZES[t])

                emit_score(steps[0])
                for si_, st in enumerate(steps):
                    if si_ + 1 < len(steps):
                        emit_score(steps[si_ + 1])
                    emit_rest(st)

            for X in range(2):
                kb = X % NKV
                tl = []
                for t in range(9):
                    T = QT_SIZES[t]
                    o = QT_OFFS[t]
                    tl.append((xq[X, o:o + T, :], T, kb, o, t, qval[X, o:o + T].rearrange("(p o) -> p o", o=1), (X, o)))
                tl.append((xm[:, :], 16, kb, NQ, 9, mval[X, :].rearrange("(p o) -> p o", o=1), None))
                run_kv(tl)
                ktiles = [(NQ, 16, 9, None)] + [(QT_OFFS[t], QT_SIZES[t], t, t) for t in range(9)]

                def fin_diag(t, h, acc, r_acc, nt, X=X):
                    p.dve(lambda e: e.tensor_copy(out=OA[X][:nt, t, h, :, :], in_=acc[:nt, :, :]),
                          reads=[r_acc], writes=[r_O[X][t][h]])
                attention(kb, QT[X], r_QT[X], ktiles, True, fin_diag)

            for i in range(7):
                kb = i % NKV
                run_kv([(xk[i, kt * 128:(kt + 1) * 128, :], 128, kb, kt * 128, kt,
                         kval[i, kt * 128:(kt + 1) * 128].rearrange("(p o) -> p o", o=1), None) for kt in range(8)])
                p.dve(lambda e, i=i: e.tensor_scalar(out=Qs[:, :, :], in0=QT[0][:, :, :], scalar1=selt[:, i:i + 1],
                                                     scalar2=None, op0=ALU.mult),
                      reads=[r_QT[0], r_const], writes=[r_Qs])
                p.dve(lambda e, i=i: e.scalar_tensor_tensor(out=Qs[:, :, :], in0=QT[1][:, :, :], scalar=selt[:, 7 + i:8 + i],
                                                            in1=Qs[:, :, :], op0=ALU.mult, op1=ALU.add),
                      reads=[r_QT[1], r_Qs, r_const], writes=[r_Qs])
                ktiles = [(kt * 128, 128, kt, None) for kt in range(8)]

                def fin_full(t, h, acc, r_acc, nt, i=i):
                    for X in range(2):
                        p.dve(lambda e, X=X: e.scalar_tensor_tensor(
                            out=OA[X][:nt, t, h, :, :], in0=acc[:nt, :, :], scalar=selt[:nt, 7 * X + i:7 * X + i + 1],
                            in1=OA[X][:nt, t, h, :, :], op0=ALU.mult, op1=ALU.add),
                            reads=[r_acc, r_O[X][t][h], r_const], writes=[r_O[X][t][h]])
                attention(kb, Qs, r_Qs, ktiles, False, fin_full)

            ob = [SB(ab, f"ob{i}", [128, 4, 128], F32) for i in range(2)]
            r_ob = [R(f"ob{i}") for i in range(2)]
            obf = [SB(ab, f"obf{i}", [128, 4, 128], BF16) for i in range(2)]
            r_obf = [R(f"obf{i}") for i in range(2)]
            rl = [SB(ab, f"rl{i}", [128, 4, 2], F32) for i in range(2)]
            r_rl = [R(f"rl{i}") for i in range(2)]
            s4 = [SB(ab, f"s4{i}", [128, 12], F32) for i in range(2)]
            r_s4 = [R(f"s4{i}") for i in range(2)]
            def norm_A(X, t, i):
                nt = QT_SIZES[t]
                o = QT_OFFS[t]
                rO = [r_O[X][t][h] for h in range(4)]
                p.dve(lambda e, X=X, t=t, nt=nt, i=i: e.tensor_scalar(out=rl[i][:nt, :, :], in0=OA[X][:nt, t, :, :, 128],
                                                                      scalar1=1e-30, scalar2=None, op0=ALU.max),
                      reads=rO, writes=[r_rl[i]])
                p.dve(lambda e, nt=nt, i=i: e.reciprocal(out=rl[i][:nt, :, :], in_=rl[i][:nt, :, :]),
                      reads=[r_rl[i]], writes=[r_rl[i]])
                p.dve(lambda e, nt=nt, i=i: e.tensor_scalar(out=rl[i][:nt, :, 1:2], in0=rl[i][:nt, :, 1:2],
                                                            scalar1=lamw[:nt, 6:7], scalar2=None, op0=ALU.mult),
                      reads=[r_rl[i], r_const], writes=[r_rl[i]])
                for h in range(4):
                    p.dve(lambda e, X=X, t=t, nt=nt, i=i, h=h: e.tensor_scalar(
                        out=ob[i][:nt, h, :], in0=OA[X][:nt, t, h, 0, 0:128], scalar1=rl[i][:nt, h, 0:1],
                        scalar2=None, op0=ALU.mult), reads=rO + [r_rl[i]], writes=[r_ob[i]])
                    p.dve(lambda e, X=X, t=t, nt=nt, i=i, h=h: e.scalar_tensor_tensor(
                        out=ob[i][:nt, h, :], in0=OA[X][:nt, t, h, 1, 0:128], scalar=rl[i][:nt, h, 1:2],
                        in1=ob[i][:nt, h, :], op0=ALU.mult, op1=ALU.add), reads=rO + [r_rl[i], r_ob[i]], writes=[r_ob[i]])
                    p.act(lambda e, nt=nt, i=i, h=h: e.activation(out=obf[i][:nt, h, :], in_=ob[i][:nt, h, :], func=AF.Square,
                                                                  accum_out=s4[i][:nt, h:h + 1]),
                          reads=[r_ob[i]], writes=[r_obf[i], r_s4[i]])

            def norm_B(X, t, i):
                nt = QT_SIZES[t]
                o = QT_OFFS[t]
                p.act(lambda e, nt=nt, i=i: e.activation(out=s4[i][:nt, 8:12], in_=s4[i][:nt, 0:4], func=AF.Ln,
                                                         scale=1.0 / 128, bias=epst[:nt, 0:1]),
                      reads=[r_s4[i], r_const], writes=[r_s4[i]])
                p.act(lambda e, nt=nt, i=i: e.activation(out=s4[i][:nt, 4:8], in_=s4[i][:nt, 8:12], func=AF.Exp, scale=-0.5),
                      reads=[r_s4[i]], writes=[r_s4[i]])
                for h in range(4):
                    p.dve(lambda e, nt=nt, i=i, h=h: e.tensor_scalar(out=obf[i][:nt, h, :], in0=ob[i][:nt, h, :],
                                                                     scalar1=s4[i][:nt, 4 + h:5 + h], scalar2=None,
                                                                     op0=ALU.mult),
                          reads=[r_ob[i], r_s4[i]], writes=[r_obf[i]])
                pt = pbank_bf[0]
                for h in range(4):
                    p.pe(lambda e, nt=nt, i=i, h=h: e.transpose(out=pt[:, h * 128:h * 128 + nt], in_=obf[i][:nt, h, :],
                                                                identity=ident[:nt, :nt]),
                         reads=[r_obf[i], r_const], writes=[r_pb[0]])
                p.dve(lambda e, X=X, nt=nt, o=o: e.tensor_scalar(
                    out=mixYb[:, :, X, o:o + nt], in0=pt[:, 0:512].rearrange("p (h t) -> p h t", h=4)[:, :, :nt],
                    scalar1=subgt[:, 0:1], scalar2=None, op0=ALU.mult),
                    reads=[r_pb[0], r_const], writes=[r_mixT[X][t]])

            ntl = [(X, t) for X in range(2) for t in range(9)]
            norm_A(ntl[0][0], ntl[0][1], 0)
            for k, (X, t) in enumerate(ntl):
                if k + 1 < len(ntl):
                    norm_A(ntl[k + 1][0], ntl[k + 1][1], (k + 1) % 2)
                norm_B(X, t, k % 2)

        p.barrier()
        n2T = SB(outer, "n2T", [128, 8, 2, NQ], BF16)
        r_n2T = [[R(f"n2T{X}_{t}") for t in range(9)] for X in range(2)]
        r_hmid = [[R(f"hmid{X}_{t}") for t in range(9)] for X in range(2)]
        with contextlib.ExitStack() as c1:
            wu = SB(c1, "wu", [128, 8, 512], BF16)
            r_wu_l = [R(f"wu{c}") for c in range(8)]
            wo = SB(c1, "wo", [128, 8, D], BF16)
            r_wo_l = [R(f"wo{c}") for c in range(8)]
            mixYa = SB(c1, "mixYa", [128, 4, 2, NQ], BF16)
            gffn_b = SB(c1, "gffn_b", [128, D], F32)
            r_gffn = R("gffn_b")
            p.dma(lambda e: e.dma_start(out=gffn_b[:], in_=gffn_d.partition_broadcast(128), allow_slow_non_contiguous=True),
                  writes=[r_gffn], q="sp")
            wpl = SB(c1, "wpl", [128, 4, 128], BF16)
            r_wpl = R("wpl")
            w3 = w_in.rearrange("(c p) n -> p c n", p=128)
            wo3 = w_out.rearrange("(c p) n -> p c n", p=128)
            for c in range(8):
                p.dma(lambda e, c=c: e.dma_start(out=wu[:, c, :], in_=w3[:, c, 0:512]), writes=[r_wu_l[c]], q="pool")
            for c in range(8):
                p.dma(lambda e, c=c: e.dma_start(out=wo[:, c, :], in_=wo3[:, c, :]), writes=[r_wo_l[c]], q="pool")
            p.dma(lambda e: e.dma_start(out=wpl[:, :, :], in_=w_pool.rearrange("g c d -> c g d")), writes=[r_wpl], q="pool")

            PADW = 16
            uT = SB(c1, "uT", [128, 4, PADW + NQ], F32)
            r_uT = R("uT")
            sA = SB(c1, "sA", [128, PADW + NQ], F32)
            sB = SB(c1, "sB", [128, PADW + NQ], F32)
            r_sA = R("sA")
            r_sB = R("sB")
            plb = SB(c1, "plb", [128, 4, NQ], BF16)
            r_plb = R("plb")
            hm = [SB(c1, f"hm{i}", [128, D], F32) for i in range(3)]
            r_hm = [R(f"hm{i}") for i in range(3)]
            cnt["hm"] = 0
            cnt["wb"] = 0
            cnt["x2"] = 0
            xs2 = [SB(c1, f"xs2_{i}", [128, D], BF16) for i in range(2)]
            r_xs2 = [R(f"xs2_{i}") for i in range(2)]
            st3 = [SB(c1, f"st3_{i}", [128, 4], F32) for i in range(2)]
            r_st3 = [R(f"st3_{i}") for i in range(2)]
            pcb = SB(c1, "pcb", [128, NQ], F32)
            r_pcb = R("pcb")
            p.dve(lambda e: e.memset(uT[:], 0.0), writes=[r_uT])
            p.dve(lambda e: e.memset(sA[:], 0.0), writes=[r_sA])
            p.dve(lambda e: e.memset(sB[:], 0.0), writes=[r_sB])
            def build_chunk(X):
                ucs = [dict(T=QT_SIZES[t], o=QT_OFFS[t]) for t in range(9)]

                def u_f1(c, X=X):
                    i = cnt["fe"] % NFE
                    cnt["fe"] += 1
                    c["xi"] = i
                    T, o = c["T"], c["o"]
                    p.dma(lambda e: e.dma_start(out=xt[i][:T, :], in_=xq[X, o:o + T, :]), writes=[r_xt[i]], q="sp")
                    rms_rows(xt[i], r_xt[i], T, xs[i], r_xs[i], st2[i], r_st2[i], gmix_b)

                def u_f2(c, X=X):
                    ni = cnt["nt"] % NNT
                    cnt["nt"] += 1
                    c["ni"] = ni
                    transpose_rows(xs[c["xi"]], r_xs[c["xi"]], c["T"], nT[ni], r_nT[ni], bank=0)

                def u_f3(c, X=X):
                    T, o, ni = c["T"], c["o"], c["ni"]
                    ub = 1 + (cnt["ub"] % 2)
                    cnt["ub"] += 1
                    pu = pbank[ub]
                    for g in range(4):
                        for cc in range(8):
                            p.pe(lambda e, g=g, cc=cc: e.matmul(pu[:, g * 128:g * 128 + T], lhsT=wu[:, cc, g * 128:(g + 1) * 128],
                                                                rhs=nT[ni][:, cc, :T], start=(cc == 0), stop=(cc == 7)),
                                 reads=[r_wu_l[cc], r_nT[ni]], writes=[r_pb[ub]])
                    p.act(lambda e: e.activation(out=uT[:, :, PADW + o:PADW + o + T],
                                                 in_=pu[:, :].rearrange("p (g t) -> p g t", g=4)[:, :, :T],
                                                 func=AF.Copy), reads=[r_pb[ub]], writes=[r_uT])

                def p1_it(it):
                    if it < 9:
                        u_f1(ucs[it])
                    if 0 <= it - 1 < 9:
                        u_f2(ucs[it - 1])
                    if 0 <= it - 2 < 9:
                        u_f3(ucs[it - 2])
                def mid():
                    for g, w in enumerate((2, 4, 8, 16)):
                        src = uT[:, g, :]
                        r_src = r_uT
                        bufs = [(sA, r_sA), (sB, r_sB)]
                        sh = 1
                        bi = 0
                        while sh < w:
                            dst, r_dst = bufs[bi]
                            p.dve(lambda e, src=src, dst=dst, sh=sh: e.tensor_tensor(
                                out=dst[:, PADW:PADW + NQ], in0=src[:, PADW:PADW + NQ], in1=src[:, PADW - sh:PADW + NQ - sh],
                                op=ALU.add), reads=[r_src], writes=[r_dst])
                            src, r_src = dst, r_dst
                            sh *= 2
                            bi ^= 1
                        if w < 16:
                            p.dve(lambda e, src=src, g=g, w=w: e.scalar_tensor_tensor(
                                out=plb[:, g, :], in0=src[:, PADW:PADW + NQ], scalar=1.0 / w, in1=uT[:, g, PADW:PADW + NQ],
                                op0=ALU.mult, op1=ALU.subtract), reads=[r_src, r_uT], writes=[r_plb])
                        else:
                            p.dma(lambda e, X=X: e.dma_start(out=pcb[:, :], in_=pc16[X, :].partition_broadcast(128),
                                                             allow_slow_non_contiguous=True), writes=[r_pcb], q="sp")
                            p.dve(lambda e, src=src: e.tensor_tensor(out=src[:, PADW:PADW + NQ], in0=src[:, PADW:PADW + NQ],
                                                                     in1=pcb[:, :], op=ALU.mult),
                                  reads=[r_src, r_pcb], writes=[r_src])
                            p.dve(lambda e, src=src, g=g: e.tensor_tensor(out=plb[:, g, :], in0=src[:, PADW:PADW + NQ],
                                                                          in1=uT[:, g, PADW:PADW + NQ], op=ALU.subtract),
                                  reads=[r_src, r_uT], writes=[r_plb])
                    for g in range(4):
                        for (c0, n) in ((0, 352), (352, 352), (704, 352)):
                            pq = pbank[3]
                            p.pe(lambda e, g=g, c0=c0, n=n: e.matmul(pq[:, 0:n], lhsT=wpl[:, g, :], rhs=plb[:, g, c0:c0 + n],
                                                                     start=True, stop=True),
                                 reads=[r_wpl, r_plb], writes=[r_pb[3]])
                            p.act(lambda e, g=g, c0=c0, n=n, X=X: e.activation(out=mixYa[:, g, X, c0:c0 + n], in_=pq[:, 0:n],
                                                                              func=AF.Identity, scale=pst[:, g:g + 1],
                                                                              bias=bps[:, g:g + 1]),
                                  reads=[r_pb[3], r_const], writes=[r_mixYa[X]])
                wcs = [dict(t=t, T=QT_SIZES[t], o=QT_OFFS[t]) for t in range(9)]

                def w_g1(c, X=X):
                    t, T, o = c["t"], c["T"], c["o"]
                    i = cnt["hm"] % 3
                    cnt["hm"] += 1
                    c["hi"] = i
                    par = cnt["wb"] % 2
                    cnt["wb"] += 1
                    p.dma(lambda e: e.dma_start(out=hm[i][:T, :], in_=xq[X, o:o + T, :]), writes=[r_hm[i]], q="sp")
                    for half in range(2):
                        bk = 4 + 2 * par + half
                        ph = pbank[bk]
                        for fc in range(8):
                            p.pe(lambda e, fc=fc, half=half, ph=ph: e.matmul(
                                ph[:T, :], lhsT=(mixYa[:, fc, X, o:o + T] if fc < 4 else mixYb[:, fc - 4, X, o:o + T]),
                                rhs=wo[:, fc, half * 512:(half + 1) * 512], start=(fc == 0), stop=(fc == 7)),
                                reads=[r_wo_l[fc], r_mixT[X][t], r_mixYa[X]], writes=[r_pb[bk]])
                        p.dve(lambda e, half=half, ph=ph: e.tensor_tensor(
                            out=hm[i][:T, half * 512:(half + 1) * 512], in0=ph[:T, :], in1=hm[i][:T, half * 512:(half + 1) * 512],
                            op=ALU.add), reads=[r_pb[bk], r_hm[i]], writes=[r_hm[i]])
                    p.dma(lambda e: e.dma_start(out=hmid[X, o:o + T, :], in_=hm[i][:T, :]),
                          reads=[r_hm[i]], writes=[r_hmid[X][t]], q="sp")

                def w_g2(c, X=X):
                    i = c["hi"]
                    j = cnt["x2"] % 2
                    cnt["x2"] += 1
                    c["xj"] = j
                    rms_rows(hm[i], r_hm[i], c["T"], xs2[j], r_xs2[j], st3[j], r_st3[j], gffn_b, r_gffn)

                def w_g3(c, X=X):
                    t, T, o, j = c["t"], c["T"], c["o"], c["xj"]
                    pt = pbank_bf[3]
                    for cc in range(8):
                        p.pe(lambda e, cc=cc: e.transpose(out=pt[:, cc * 128:cc * 128 + T],
                                                          in_=xs2[j][:T, cc * 128:(cc + 1) * 128], identity=ident[:T, :T]),
                             reads=[r_xs2[j], r_const], writes=[r_pb[3]])
                    p.dve(lambda e: e.tensor_copy(out=n2T[:, :, X, o:o + T],
                                                  in_=pt[:, :].rearrange("p (c t) -> p c t", c=8)[:, :, :T]),
                          reads=[r_pb[3]], writes=[r_n2T[X][t]])

                def p3_it(it):
                    if it < 9:
                        w_g1(wcs[it])
                    if 0 <= it - 1 < 9:
                        w_g2(wcs[it - 1])
                    if 0 <= it - 2 < 9:
                        w_g3(wcs[it - 2])
                return p1_it, mid, p3_it

            chA = build_chunk(0)
            chB = build_chunk(1)
            for it in range(11):
                chA[0](it)
            chA[1]()
            for it in range(11):
                chA[2](it)
                chB[0](it)
            chB[1]()
            for it in range(11):
                chB[2](it)

        p.barrier()
        outs = []
        with contextlib.ExitStack() as c2:
            NFB = FB_PER_PASS
            wup = SB(c2, "wup", [128, 8, 2, NFB * 128], BF16)
            wdn = SB(c2, "wdn", [128, NFB, D], BF16)
            FGRP = [(0, 4), (4, 8), (8, NFB)]
            r_wup_l = [[R(f"wup{s_}_{k}") for k in range(len(FGRP))] for s_ in range(2)]
            r_wdn_l = [R(f"wdn{f}") for f in range(NFB)]
            Gt = [SB(c2, f"Gt{i}", [128, NFB, FG], BF16) for i in range(2)]
            r_Gt = [R(f"Gt{i}") for i in range(2)]
            cbuf = [[SB(c2, f"cbuf{i}_{s}", [128, FG], F32) for s in range(2)] for i in range(2)]
            r_cbuf = [[R(f"cbuf{i}_{s}") for s in range(2)] for i in range(2)]
            sg = [SB(c2, f"sg{i}", [128, FG], F32) for i in range(2)]
            r_sg = [R(f"sg{i}") for i in range(2)]
            yt = [SB(c2, f"yt{i}", [128, D], F32) for i in range(2)]
            r_yt = [R(f"yt{i}") for i in range(2)]
            r_yst = [R(f"yst{i}") for i in range(2)]
            r_y = [[R(f"y{X}_{t}") for t in range(8)] for X in range(2)]
            wu3 = w_up.rearrange("(c p) n -> p c n", p=128)
            wd3 = w_down.rearrange("(f p) n -> p f n", p=128)
            wk = 0
            gk = 0
            for ps_ in range(NPASS):
                fb0 = ps_ * NFB
                for k, (fa, fz) in enumerate(FGRP):
                    for s in range(2):
                        col0 = s * DFF + (fb0 + fa) * 128
                        p.dma(lambda e, s=s, col0=col0, fa=fa, fz=fz: e.dma_start(
                            out=wup[:, :, s, fa * 128:fz * 128], in_=wu3[:, :, col0:col0 + (fz - fa) * 128]),
                            writes=[r_wup_l[s][k]], q="pool")
                for f in range(NFB):
                    p.dma(lambda e, f=f, fb0=fb0: e.dma_start(out=wdn[:, f, :], in_=wd3[:, fb0 + f, :]), writes=[r_wdn_l[f]], q="pool")
                def up_fb(g, f):
                    X, t0, gb, rn = g["X"], g["t0"], g["gb"], g["rn"]
                    fb = fb0 + f
                    ci = f % 2
                    for s_ in range(2):
                        pu = pbank[ci * 2 + s_]
                        for c in range(8):
                            p.pe(lambda e, c=c, s_=s_, pu=pu: e.matmul(
                                pu[:, 0:FG + 2], lhsT=wup[:, c, s_, f * 128:(f + 1) * 128],
                                rhs=n2T[:, c, X, t0 - 2:t0 + FG], start=(c == 0), stop=(c == 7)),
                                reads=[r_wup_l[s_][[k for k, (fa, fz) in enumerate(FGRP) if fa <= f < fz][0]]] + rn,
                                writes=[r_pb[ci * 2 + s_]])
                        col = s_ * 22 + fb
                        cbt = cbuf[ci][s_]
                        rcb = r_cbuf[ci][s_]
                        p.act(lambda e, pu=pu, cbt=cbt, col=col: e.activation(
                            out=cbt[:, :], in_=pu[:, 2:FG + 2], func=AF.Identity, scale=cw[:, 2, col:col + 1],
                            bias=cb[:, col:col + 1]), reads=[r_pb[ci * 2 + s_], r_const], writes=[rcb])
                        p.dve(lambda e, pu=pu, cbt=cbt, col=col: e.scalar_tensor_tensor(
                            out=cbt[:, :], in0=pu[:, 1:FG + 1], scalar=cw[:, 1, col:col + 1], in1=cbt[:, :],
                            op0=ALU.mult, op1=ALU.add), reads=[r_pb[ci * 2 + s_], rcb, r_const], writes=[rcb])
                        p.dve(lambda e, pu=pu, cbt=cbt, col=col: e.scalar_tensor_tensor(
                            out=cbt[:, :], in0=pu[:, 0:FG], scalar=cw[:, 0, col:col + 1], in1=cbt[:, :],
                            op0=ALU.mult, op1=ALU.add), reads=[r_pb[ci * 2 + s_], rcb, r_const], writes=[rcb])
                    p.act(lambda e: e.activation(out=sg[ci][:, :], in_=cbuf[ci][0][:, :], func=AF.Silu),
                          reads=[r_cbuf[ci][0]], writes=[r_sg[ci]])
                    p.dve(lambda e: e.tensor_tensor(out=Gt[gb][:, f, :], in0=sg[ci][:, :], in1=cbuf[ci][1][:, :], op=ALU.mult),
                          reads=[r_sg[ci], r_cbuf[ci][1]], writes=[r_Gt[gb]])

                def down_unit(g, q):
                    X, gb = g["X"], g["gb"]
                    tt = g["tiles"][q]
                    yrow = (tt - 1) * 128
                    yi = q % 2
                    if ps_ == 0:
                        p.dma(lambda e: e.dma_start(out=yt[yi][:, :], in_=hmid[X, QT_OFFS[tt]:QT_OFFS[tt] + 128, :]),
                              reads=[r_hmid[X][tt]], writes=[r_yt[yi]], q="sp")
                    else:
                        p.dma(lambda e: e.dma_start(out=yt[yi][:, :], in_=y[X, yrow:yrow + 128, :]),
                              reads=[r_y[X][tt - 1]], writes=[r_yt[yi]], q="sp")
                    for half in range(2):
                        bk = 4 + 2 * (q % 2) + half
                        pd = pbank[bk]
                        for f in range(NFB):
                            p.pe(lambda e, f=f, half=half, pd=pd: e.matmul(
                                pd[:, :], lhsT=Gt[gb][:, f, q * 128:(q + 1) * 128],
                                rhs=wdn[:, f, half * 512:(half + 1) * 512], start=(f == 0), stop=(f == NFB - 1)),
                                reads=[r_Gt[gb], r_wdn_l[f]], writes=[r_pb[bk]])
                        p.dve(lambda e, half=half, pd=pd: e.tensor_tensor(
                            out=yt[yi][:, half * 512:(half + 1) * 512], in0=pd[:, :],
                            in1=yt[yi][:, half * 512:(half + 1) * 512], op=ALU.add),
                            reads=[r_pb[bk], r_yt[yi]], writes=[r_yt[yi]])
                    od = p.dma(lambda e: e.dma_start(out=y[X, yrow:yrow + 128, :], in_=yt[yi][:, :]),
                               reads=[r_yt[yi]], writes=[r_y[X][tt - 1]], q="sp", sem_res=r_yst[yi])
                    if ps_ == NPASS - 1:
                        outs.append(od)

                groups = []
                for X in range(2):
                    for gi in range(1024 // FG):
                        tiles = [1 + (gi * FG) // 128 + q for q in range(FG // 128)]
                        groups.append(dict(X=X, t0=32 + gi * FG, gb=gk % 2, tiles=tiles,
                                           rn=[r_n2T[X][t] for t in tiles] + [r_n2T[X][tiles[0] - 1]]))
                        gk += 1
                slots = {3: 0, 8: 1}
                prev = None
                for g in groups:
                    for f in range(NFB):
                        up_fb(g, f)
                        if prev is not None and f in slots:
                            down_unit(prev, slots[f])
                    prev = g
                down_unit(prev, 0)
                down_unit(prev, 1)
            p.emit(final_waits=outs)
    return nc


_NC_CACHE = {}


def _host_layout(x, meta_tokens):
    per_core = []
    for core in range(8):
        b, j = core // 4, core % 4
        xb = x[b]
        chunks = (j, 7 - j)
        xq = np.zeros((2, NQ, D), np.float32)
        qval = np.ones((2, NQ), np.float32)
        mval = np.ones((2, 16), np.float32)
        for X, c in enumerate(chunks):
            s = 1024 * c
            xq[X, 32:] = xb[s:s + 1024]
            if c == 0:
                xq[X, 16:32] = meta_tokens
                qval[X, 0:16] = 0.0
                mval[X, :] = 0.0
            else:
                xq[X, 0:32] = xb[s - 32:s]
        xk = np.zeros((7, 1024, D), np.float32)
        kval = np.zeros((7, 1024), np.float32)
        sel = np.zeros((14,), np.float32)
        pc16 = np.full((2, NQ), 1.0 / 16.0, np.float32)
        if chunks[0] == 0:
            for pp in range(15):
                pc16[0, 16 + pp] = 1.0 / float(pp + 1)
        for i in range(7):
            if i < j:
                X, c, t = 0, chunks[0], i
            else:
                X, c, t = 1, chunks[1], i - j
            lim = 1024 * c - 32
            lo = 1024 * t
            hi = min(lo + 1024, lim)
            xk[i, :hi - lo] = xb[lo:hi]
            kval[i, :hi - lo] = 1.0
            sel[7 * X + i] = 1.0
        per_core.append(dict(xq=xq, xk=xk, xm=np.ascontiguousarray(meta_tokens, np.float32), kval=kval, qval=qval,
                             mval=mval, sel=sel, pc16=pc16))
    return per_core


def kernel(x, meta_tokens, norm_mix_g, w_in, w_pool, b_pool, pool_scale, q_norm_g, k_norm_g,
           lambda_q1, lambda_k1, lambda_q2, lambda_k2, subln_g, w_out, norm_ffn_g,
           w_up, conv_w, conv_b, w_down):
    f = lambda a: np.ascontiguousarray(np.asarray(a, np.float32))
    x = f(x)
    meta_tokens = f(meta_tokens)
    shared = {
        "w_in": f(w_in)[0], "w_pool": f(w_pool)[0], "b_pool": f(b_pool)[0], "pool_scale": f(pool_scale)[0],
        "q_norm_g": f(q_norm_g)[0], "k_norm_g": f(k_norm_g)[0], "lambda_q1": f(lambda_q1)[0],
        "lambda_k1": f(lambda_k1)[0], "lambda_q2": f(lambda_q2)[0], "lambda_k2": f(lambda_k2)[0],
        "subln_g": f(subln_g)[0], "w_out": f(w_out)[0], "norm_mix_g": f(norm_mix_g)[0],
        "norm_ffn_g": f(norm_ffn_g)[0], "w_up": f(w_up)[0], "conv_w": f(conv_w)[0], "conv_b": f(conv_b)[0],
        "w_down": f(w_down)[0],
    }
    if "nc" not in _NC_CACHE:
        _NC_CACHE["nc"] = build_program()
    nc = _NC_CACHE["nc"]
    per_core = _host_layout(x, meta_tokens)
    in_maps = [dict(shared, **pc) for pc in per_core]
    res = run_bass_kernel_spmd(nc, in_maps, core_ids=list(range(8)))
    out = np.empty((2, 8192, D), np.float32)
    for core in range(8):
        b, j = core // 4, core % 4
        yc = res.results[core]["y"]
        out[b, 1024 * j:1024 * (j + 1)] = yc[0]
        out[b, 1024 * (7 - j):1024 * (8 - j)] = yc[1]
    return out
```

```python
import contextlib
import numpy as np
import concourse.bass as bass
import concourse.mybir as mybir
from concourse.bass_utils import run_bass_kernel_spmd

F32 = mybir.dt.float32
BF16 = mybir.dt.bfloat16
AF = mybir.ActivationFunctionType
ALU = mybir.AluOpType
AX = mybir.AxisListType

ENGS = ("pe", "act", "dve", "pool", "sp")


class Res:
    __slots__ = ("name", "last_w", "readers", "dma_sem", "dma_cnt", "last_dma")

    def __init__(self, name):
        self.name = name
        self.last_dma = None
        self.last_w = None
        self.readers = []
        self.dma_sem = None
        self.dma_cnt = 0


class Ins:
    __slots__ = ("eng", "fn", "deps", "inc_val", "is_dma", "dma_res", "dma_val", "needed")

    def __init__(self, eng, fn):
        self.eng = eng
        self.fn = fn
        self.deps = []
        self.inc_val = None
        self.is_dma = False
        self.dma_res = None
        self.dma_val = 0
        self.needed = False


class Prog:
    def __init__(self, nc):
        self.nc = nc
        self.streams = {e: [] for e in ENGS}
        self.all_res = []
        self.barrier_deps = []
        self.barrier_seen = {e: True for e in ENGS}

    def res(self, name):
        r = Res(name)
        self.all_res.append(r)
        return r

    def barrier(self):
        deps = []
        for e in ENGS:
            if self.streams[e]:
                deps.append(self.streams[e][-1])
        for r in self.all_res:
            if r.last_dma is not None:
                deps.append(r.last_dma)
        self.barrier_deps = deps
        self.barrier_seen = {e: False for e in ENGS}

    def _add(self, eng, fn, reads, writes, is_dma=False, sem_res=None):
        ins = Ins(eng, fn)
        ins.is_dma = is_dma
        deps = []
        if not self.barrier_seen[eng]:
            self.barrier_seen[eng] = True
            deps.extend(self.barrier_deps)
        for r in reads:
            if r.last_w is not None:
                deps.append(r.last_w)
        for r in writes:
            if r.last_w is not None:
                deps.append(r.last_w)
            deps.extend(r.readers)
        seen = set()
        for d in deps:
            if d is ins or id(d) in seen:
                continue
            seen.add(id(d))
            if d.eng == eng and not d.is_dma and eng in ("pe", "sp"):
                continue
            ins.deps.append(d)
            d.needed = True
        for r in reads:
            r.readers.append(ins)
        for r in writes:
            r.last_w = ins
            r.readers = []
        if is_dma:
            tgt = sem_res if sem_res is not None else writes[0]
            ins.dma_res = tgt
            tgt.dma_cnt += 16
            ins.dma_val = tgt.dma_cnt
            tgt.last_dma = ins
        self.streams[eng].append(ins)
        return ins

    def pe(self, fn, reads=(), writes=()):
        return self._add("pe", fn, list(reads), list(writes))

    def act(self, fn, reads=(), writes=()):
        return self._add("act", fn, list(reads), list(writes))

    def dve(self, fn, reads=(), writes=()):
        return self._add("dve", fn, list(reads), list(writes))

    def pool(self, fn, reads=(), writes=()):
        return self._add("pool", fn, list(reads), list(writes))

    def dma(self, fn, reads=(), writes=(), q="sp", sem_res=None):
        return self._add(q, fn, list(reads), list(writes), is_dma=True, sem_res=sem_res)

    def emit(self, final_waits=()):
        nc = self.nc
        with contextlib.ExitStack() as st:
            sems = {}
            for e in ENGS:
                sems[e] = st.enter_context(nc.semaphore("s_" + e))
            for r in self.all_res:
                if r.dma_cnt > 0:
                    r.dma_sem = st.enter_context(nc.semaphore("d_" + r.name))
            for e in ENGS:
                c = 0
                for ins in self.streams[e]:
                    if ins.is_dma:
                        continue
                    if ins.needed:
                        c += 1
                        ins.inc_val = c
            block = st.enter_context(nc.Block())
            engmap = {"pe": "tensor", "act": "scalar", "dve": "vector", "pool": "gpsimd", "sp": "sync"}

            def make(e):
                def body(eng):
                    waited = {}
                    for ins in self.streams[e]:
                        for d in ins.deps:
                            if d.is_dma:
                                key = ("d", id(d.dma_res))
                                val = d.dma_val
                                sem = d.dma_res.dma_sem
                            else:
                                key = ("e", d.eng)
                                val = d.inc_val
                                sem = sems[d.eng]
                            if waited.get(key, 0) >= val:
                                continue
                            waited[key] = val
                            eng.wait_ge(sem, val)
                        r = ins.fn(eng)
                        if ins.is_dma:
                            r.then_inc(ins.dma_res.dma_sem, 16)
                        elif ins.needed:
                            r.then_inc(sems[e], 1)
                    if e == "sp":
                        for d in final_waits:
                            eng.wait_ge(d.dma_res.dma_sem, d.dma_val)
                return body

            for e in ENGS:
                getattr(block, engmap[e])(make(e))


D = 1024
NQ = 1056
QT_SIZES = [32] + [128] * 8
QT_OFFS = [0] + [32 + 128 * i for i in range(8)]
GROUPS = [(0, 1, 2), (3, 4, 5), (6, 7, 8)]
DFF = 2816
EPS = 1e-6
LAM_INIT = 0.2
NEG = -30000.0
FG = 384
NPASS = 2
FB_PER_PASS = 22 // NPASS


def build_program():
    nc = bass.Bass("TRN2", target_bir_lowering=False)
    dr = lambda name, shape, kind="ExternalInput": nc.dram_tensor(name, shape, F32, kind=kind).ap()
    xq = dr("xq", [2, NQ, D])
    xk = dr("xk", [7, 1024, D])
    xm = dr("xm", [16, D])
    kval = dr("kval", [7, 1024])
    qval = dr("qval", [2, NQ])
    mval = dr("mval", [2, 16])
    sel = dr("sel", [14])
    pc16 = dr("pc16", [2, NQ])
    w_in = dr("w_in", [D, 2048])
    w_pool = dr("w_pool", [4, 128, 128])
    b_pool = dr("b_pool", [4, 128])
    pool_scale = dr("pool_scale", [512])
    qg = dr("q_norm_g", [64])
    kg = dr("k_norm_g", [64])
    lq1 = dr("lambda_q1", [64])
    lk1 = dr("lambda_k1", [64])
    lq2 = dr("lambda_q2", [64])
    lk2 = dr("lambda_k2", [64])
    subg = dr("subln_g", [128])
    w_out = dr("w_out", [D, D])
    gmix_d = dr("norm_mix_g", [D])
    gffn_d = dr("norm_ffn_g", [D])
    w_up = dr("w_up", [D, 2 * DFF])
    conv_w = dr("conv_w", [3, 2 * DFF])
    conv_b = dr("conv_b", [2 * DFF])
    w_down = dr("w_down", [DFF, D])
    y = dr("y", [2, 1024, D], kind="ExternalOutput")
    hmid = dr("hmid", [2, NQ, D], kind="Internal")

    p = Prog(nc)
    R = p.res

    with contextlib.ExitStack() as outer:
        def SB(st, name, shape, dt):
            return st.enter_context(nc.sbuf_tensor(name, shape, dt))

        def PS(st, name, shape, dt):
            return st.enter_context(nc.psum_tensor(name, shape, dt))

        ident = SB(outer, "ident", [128, 128], BF16)
        identf = SB(outer, "identf", [128, 128], F32)
        maskneg = SB(outer, "maskneg", [128, 128], BF16)
        blk1 = SB(outer, "blk1", [128, 128], BF16)
        gmix_b = SB(outer, "gmix_b", [128, D], F32)
        qkg = SB(outer, "qkg", [128, 2], F32)
        selt = SB(outer, "selt", [128, 14], F32)
        lamt = SB(outer, "lamt", [128, 4, 64], F32)
        lamw = SB(outer, "lamw", [128, 8], F32)
        subgt = SB(outer, "subgt", [128, 1], F32)
        bpt = SB(outer, "bpt", [128, 4], F32)
        pst = SB(outer, "pst", [128, 4], F32)
        bps = SB(outer, "bps", [128, 4], F32)
        ones4 = SB(outer, "ones4", [128, 4], F32)
        zer = SB(outer, "zer", [128, 512], BF16)
        cw = SB(outer, "cw", [128, 3, 44], F32)
        cb = SB(outer, "cb", [128, 44], F32)
        mixYb = SB(outer, "mixYb", [128, 4, 2, NQ], BF16)
        r_const = R("const")
        r_mixT = [[R(f"mixT{X}_{t}") for t in range(9)] for X in range(2)]
        r_mixYa = [R(f"mixYa{X}") for X in range(2)]

        pb_h = [PS(outer, f"pb{i}", [128, 512], F32) for i in range(4)]
        sc_h = [PS(outer, f"sc{i}", [128, 1024], F32) for i in range(2)]
        pbank = list(pb_h) + [sc_h[i // 2][:, (i % 2) * 512:(i % 2 + 1) * 512] for i in range(4)]
        r_pb = [R(f"pb{i}") for i in range(8)]
        pbank_bf = [b.bitcast(BF16) for b in pb_h]

        p.pool(lambda e: e.memset(identf[:], 0.0), writes=[r_const])
        p.pool(lambda e: e.affine_select(out=identf[:], in_=identf[:], compare_op=ALU.not_equal, fill=1.0,
                                         base=0, pattern=[[-1, 128]], channel_multiplier=1),
               reads=[r_const], writes=[r_const])
        p.dve(lambda e: e.tensor_copy(out=ident[:], in_=identf[:]), reads=[r_const], writes=[r_const])
        p.pool(lambda e: e.memset(identf[:], 0.0), reads=[r_const], writes=[r_const])
        p.pool(lambda e: e.affine_select(out=identf[:], in_=identf[:], compare_op=ALU.is_ge, fill=NEG,
                                         base=0, pattern=[[1, 128]], channel_multiplier=-1),
               reads=[r_const], writes=[r_const])
        p.dve(lambda e: e.tensor_copy(out=maskneg[:], in_=identf[:]), reads=[r_const], writes=[r_const])
        p.dve(lambda e: e.memset(blk1[:], 0.0), reads=[r_const], writes=[r_const])
        p.dve(lambda e: e.memset(blk1[0:64, 0:64], 1.0), reads=[r_const], writes=[r_const])
        p.dve(lambda e: e.memset(blk1[64:128, 64:128], 1.0), reads=[r_const], writes=[r_const])
        p.dve(lambda e: e.memset(ones4[:], 1.0), reads=[r_const], writes=[r_const])
        p.dve(lambda e: e.memset(zer[:], 0.0), reads=[r_const], writes=[r_const])

        def small_dma(out_ap, in_ap):
            p.dma(lambda e: e.dma_start(out=out_ap, in_=in_ap, allow_slow_non_contiguous=True),
                  reads=[], writes=[r_const], q="sp")

        small_dma(gmix_b[:], gmix_d.partition_broadcast(128))
        small_dma(qkg[0:64, 0:1], qg.rearrange("(p o) -> p o", o=1))
        small_dma(qkg[64:128, 0:1], qg.rearrange("(p o) -> p o", o=1))
        small_dma(qkg[0:64, 1:2], kg.rearrange("(p o) -> p o", o=1))
        small_dma(qkg[64:128, 1:2], kg.rearrange("(p o) -> p o", o=1))
        small_dma(selt[:], sel.partition_broadcast(128))
        for i, l in enumerate((lq1, lk1, lq2, lk2)):
            small_dma(lamt[:, i, :], l.partition_broadcast(128))
        small_dma(subgt[:], subg.rearrange("(p o) -> p o", o=1))
        small_dma(bpt[:], b_pool.rearrange("g p -> p g"))
        small_dma(pst[:], pool_scale.rearrange("(g p) -> p g", p=128))
        small_dma(cw[:], conv_w.rearrange("k (f p) -> p k f", p=128))
        small_dma(cb[:], conv_b.rearrange("(f p) -> p f", p=128))
        p.dve(lambda e: e.tensor_scalar(out=qkg[:, 0:1], in0=qkg[:, 0:1], scalar1=0.125, scalar2=None, op0=ALU.mult),
              reads=[r_const], writes=[r_const])
        p.dve(lambda e: e.tensor_tensor(out=lamt[:, 0, :], in0=lamt[:, 0, :], in1=lamt[:, 1, :], op=ALU.mult),
              reads=[r_const], writes=[r_const])
        p.dve(lambda e: e.tensor_tensor(out=lamt[:, 2, :], in0=lamt[:, 2, :], in1=lamt[:, 3, :], op=ALU.mult),
              reads=[r_const], writes=[r_const])
        p.dve(lambda e: e.reduce_sum(out=lamw[:, 0:1], in_=lamt[:, 0, :], axis=AX.X), reads=[r_const], writes=[r_const])
        p.dve(lambda e: e.reduce_sum(out=lamw[:, 1:2], in_=lamt[:, 2, :], axis=AX.X), reads=[r_const], writes=[r_const])
        p.act(lambda e: e.activation(out=lamw[:, 2:4], in_=lamw[:, 0:2], func=AF.Exp), reads=[r_const], writes=[r_const])
        p.dve(lambda e: e.tensor_tensor(out=lamw[:, 4:5], in0=lamw[:, 3:4], in1=lamw[:, 2:3], op=ALU.subtract),
              reads=[r_const], writes=[r_const])
        p.dve(lambda e: e.tensor_scalar(out=lamw[:, 6:7], in0=lamw[:, 4:5], scalar1=-LAM_INIT, scalar2=None, op0=ALU.add),
              reads=[r_const], writes=[r_const])
        p.dve(lambda e: e.tensor_tensor(out=bps[:], in0=bpt[:], in1=pst[:], op=ALU.mult), reads=[r_const], writes=[r_const])
        p.dve(lambda e: e.tensor_scalar(out=subgt[:], in0=subgt[:], scalar1=1.0 - LAM_INIT, scalar2=None, op0=ALU.mult),
              reads=[r_const], writes=[r_const])

        NFE = 2
        xt = [SB(outer, f"xt{i}", [128, D], F32) for i in range(NFE)]
        r_xt = [R(f"xt{i}") for i in range(NFE)]
        xs = [SB(outer, f"xs{i}", [128, D], BF16) for i in range(NFE)]
        r_xs = [R(f"xs{i}") for i in range(NFE)]
        NNT = 4
        nT = [SB(outer, f"nT{i}", [128, 8, 128], BF16) for i in range(NNT)]
        r_nT = [R(f"nT{i}") for i in range(NNT)]
        st2 = [SB(outer, f"st2_{i}", [128, 4], F32) for i in range(NFE)]
        r_st2 = [R(f"st2_{i}") for i in range(NFE)]
        epst = SB(outer, "epst", [128, 1], F32)
        p.dve(lambda e: e.memset(epst[:], EPS), reads=[r_const], writes=[r_const])
        cnt = {"fe": 0, "nt": 0, "ub": 0}

        def rms_rows(src_t, r_src, T, dst_bf, r_dst, stt, r_stt, gb, r_gb=None):
            p.act(lambda e: e.activation(out=dst_bf[:T, :], in_=src_t[:T, :], func=AF.Square, accum_out=stt[:T, 0:1]),
                  reads=[r_src], writes=[r_dst, r_stt])
            p.act(lambda e: e.activation(out=stt[:T, 1:2], in_=stt[:T, 0:1], func=AF.Ln, scale=1.0 / D, bias=epst[:T, 0:1]),
                  reads=[r_stt, r_const], writes=[r_stt])
            p.act(lambda e: e.activation(out=stt[:T, 2:3], in_=stt[:T, 1:2], func=AF.Exp, scale=-0.5),
                  reads=[r_stt], writes=[r_stt])
            p.dve(lambda e: e.scalar_tensor_tensor(out=dst_bf[:T, :], in0=src_t[:T, :], scalar=stt[:T, 2:3], in1=gb[:T, :],
                                                   op0=ALU.mult, op1=ALU.mult),
                  reads=[r_src, r_stt, r_const] + ([r_gb] if r_gb else []), writes=[r_dst])

        def transpose_rows(src_bf, r_src, T, dstT, r_dstT, bank=0):
            pt = pbank_bf[bank]
            for c in range(8):
                p.pe(lambda e, c=c: e.transpose(out=pt[:, c * 128:c * 128 + T], in_=src_bf[:T, c * 128:(c + 1) * 128],
                                                identity=ident[:T, :T]),
                     reads=[r_src, r_const], writes=[r_pb[bank]])
            p.dve(lambda e: e.tensor_copy(out=dstT[:, :, :T], in_=pt[:, :].rearrange("p (c t) -> p c t", c=8)[:, :, :T]),
                  reads=[r_pb[bank]], writes=[r_dstT])

        def front_end(src_ap, T, tbank=0):
            i = cnt["fe"] % NFE
            cnt["fe"] += 1
            ni = cnt["nt"] % NNT
            cnt["nt"] += 1
            p.dma(lambda e: e.dma_start(out=xt[i][:T, :], in_=src_ap), writes=[r_xt[i]], q="sp")
            rms_rows(xt[i], r_xt[i], T, xs[i], r_xs[i], st2[i], r_st2[i], gmix_b)
            transpose_rows(xs[i], r_xs[i], T, nT[ni], r_nT[ni], bank=tbank)
            return ni

        with contextlib.ExitStack() as ab:
            wqkv = SB(ab, "wqkv", [128, 8, 1536], BF16)
            r_wqkv_l = [R(f"wqkv{c}") for c in range(8)]
            w3 = w_in.rearrange("(c p) n -> p c n", p=128)
            for c in range(8):
                p.dma(lambda e, c=c: e.dma_start(out=wqkv[:, c, :], in_=w3[:, c, 512:2048]), writes=[r_wqkv_l[c]], q="pool")

            NKV = 1
            KT = [SB(ab, f"KT{i}", [128, 4, NQ + 16], BF16) for i in range(NKV)]
            VX = [SB(ab, f"VX{i}", [128, 10, 4, 130], BF16) for i in range(NKV)]
            r_KT = [R(f"KT{i}") for i in range(NKV)]
            r_VX = [R(f"VX{i}") for i in range(NKV)]
            QT = [SB(ab, f"QT{X}", [128, 4, NQ], BF16) for X in range(2)]
            r_QT = [R(f"QT{X}") for X in range(2)]
            Qs = SB(ab, "Qs", [128, 4, NQ], BF16)
            r_Qs = R("Qs")
            OA = [SB(ab, f"O{X}", [128, 9, 4, 2, 129], F32) for X in range(2)]
            r_O = [[[R(f"O{X}_{t}_{h}") for h in range(4)] for t in range(9)] for X in range(2)]
            sqb = [SB(ab, f"sqb{i}", [128, 4, 128], BF16) for i in range(2)]
            r_sqb = [R(f"sqb{i}") for i in range(2)]
            lnb = [SB(ab, "lnb0", [128, 4, 128], F32)] * 2
            r_lnb = [R("lnb0")] * 2
            cnt["sq"] = 0
            cnt["kvt"] = 0
            valt = [SB(ab, f"valt{i}", [128, 1], F32) for i in range(4)]
            r_valt = [R(f"valt{i}") for i in range(4)]
            NPT = 2
            Pt = [SB(ab, f"Pt{i}", [128, 2, 384], BF16) for i in range(NPT)]
            r_Pt = [R(f"Pt{i}") for i in range(NPT)]
            cnt["val"] = 0
            cnt["pt"] = 0
            cnt["sc"] = 0

            rsq = [[SB(ab, f"rsq{a}_{i}", [128, 4, 128], F32) for i in range(2)] for a in range(2)]
            r_rsq = [[R(f"rsq{a}_{i}") for i in range(2)] for a in range(2)]
            KB = [[1, 2], [3, 4]]
            SBK, VBK, TBK = 5, 6, 0

            def kv_f1(c):
                i = cnt["fe"] % NFE
                cnt["fe"] += 1
                c["xi"] = i
                T = c["T"]
                p.dma(lambda e: e.dma_start(out=xt[i][:T, :], in_=c["src"]), writes=[r_xt[i]], q="sp")
                rms_rows(xt[i], r_xt[i], T, xs[i], r_xs[i], st2[i], r_st2[i], gmix_b)
                vi = cnt["val"] % 4
                cnt["val"] += 1
                c["vi"] = vi
                p.dma(lambda e: e.dma_start(out=valt[vi][:T, :], in_=c["val"]), writes=[r_valt[vi]], q="sp")

            def kv_f2(c):
                ni = cnt["nt"] % NNT
                cnt["nt"] += 1
                c["ni"] = ni
                i = c["xi"]
                transpose_rows(xs[i], r_xs[i], c["T"], nT[ni], r_nT[ni], bank=TBK)

            def kv_b1(c):
                T, ni, par = c["T"], c["ni"], c["par"]
                for a, col0 in ((0, 512), (1, 0)):
                    if a == 1 and c["own"] is None:
                        continue
                    bank = KB[a][par]
                    pk = pbank[bank]
                    for h in range(4):
                        for cc in range(8):
                            p.pe(lambda e, h=h, cc=cc, pk=pk, col0=col0: e.matmul(
                                pk[:, h * 128:h * 128 + T], lhsT=wqkv[:, cc, col0 + h * 128:col0 + (h + 1) * 128],
                                rhs=nT[ni][:, cc, :T], start=(cc == 0), stop=(cc == 7)),
                                reads=[r_wqkv_l[cc], r_nT[ni]], writes=[r_pb[bank]])
                    pk3 = pk[:, :].rearrange("p (h t) -> p h t", h=4)
                    j = cnt["sq"] % 2
                    cnt["sq"] += 1
                    p.act(lambda e, pk3=pk3, j=j: e.activation(out=sqb[j][:, :, :T], in_=pk3[:, :, :T], func=AF.Square),
                          reads=[r_pb[bank]], writes=[r_sqb[j]])
                    pss = pbank[SBK]
                    for h in range(4):
                        p.pe(lambda e, h=h, j=j, pss=pss: e.matmul(pss[:, h * 128:h * 128 + T], lhsT=blk1[:, :],
                                                                   rhs=sqb[j][:, h, :T], start=True, stop=True),
                             reads=[r_sqb[j], r_const], writes=[r_pb[SBK]])
                    pss3 = pss[:, :].rearrange("p (h t) -> p h t", h=4)
                    p.act(lambda e, pss3=pss3: e.activation(out=lnb[0][:, :, :T], in_=pss3[:, :, :T], func=AF.Ln,
                                                            scale=1.0 / 64, bias=epst[:, 0:1]),
                          reads=[r_pb[SBK], r_const], writes=[r_lnb[0]])
                    p.act(lambda e, a=a: e.activation(out=rsq[a][par][:, :, :T], in_=lnb[0][:, :, :T], func=AF.Exp, scale=-0.5),
                          reads=[r_lnb[0]], writes=[r_rsq[a][par]])

            def kv_b2(c):
                T, ni, par, kb, vidx, vi = c["T"], c["ni"], c["par"], c["kb"], c["vidx"], c["vi"]
                for a in range(2):
                    if a == 1 and c["own"] is None:
                        continue
                    bank = KB[a][par]
                    pk3 = pbank[bank][:, :].rearrange("p (h t) -> p h t", h=4)
                    if a == 0:
                        dst, r_dst, doff, gcol = KT[kb], r_KT[kb], c["koff"], 1
                    else:
                        X, doff = c["own"]
                        dst, r_dst, gcol = QT[X], r_QT[X], 0
                    p.dve(lambda e, pk3=pk3, dst=dst, doff=doff, gcol=gcol, a=a: e.scalar_tensor_tensor(
                        out=dst[:, :, doff:doff + T], in0=pk3[:, :, :T], scalar=qkg[:, gcol:gcol + 1],
                        in1=rsq[a][par][:, :, :T], op0=ALU.mult, op1=ALU.mult),
                        reads=[r_pb[bank], r_rsq[a][par], r_const], writes=[r_dst])
                pv = pbank[VBK]
                for cc in range(8):
                    p.pe(lambda e, cc=cc: e.matmul(pv[:T, :], lhsT=nT[ni][:, cc, :T], rhs=wqkv[:, cc, 1024:1536],
                                                   start=(cc == 0), stop=(cc == 7)),
                         reads=[r_wqkv_l[cc], r_nT[ni]], writes=[r_pb[VBK]])
                p.act(lambda e: e.activation(out=VX[kb][:T, vidx, :, 0:128],
                                             in_=pv[:T, :].rearrange("p (h d) -> p h d", h=4), func=AF.Copy,
                                             scale=valt[vi][:T, 0:1]),
                      reads=[r_pb[VBK], r_valt[vi]], writes=[r_VX[kb]])
                p.dve(lambda e: e.tensor_scalar(out=VX[kb][:T, vidx, :, 128:129], in0=ones4[:T, :].rearrange("p (h o) -> p h o", o=1),
                                                scalar1=valt[vi][:T, 0:1], scalar2=None, op0=ALU.mult),
                      reads=[r_valt[vi], r_const], writes=[r_VX[kb]])

            def run_kv(tiles):
                cs = []
                for (src, T, kb, koff, vidx, val, own) in tiles:
                    cs.append(dict(src=src, T=T, kb=kb, koff=koff, vidx=vidx, val=val, own=own, par=cnt["kvt"] % 2))
                    cnt["kvt"] += 1
                n = len(cs)
                for it in range(n + 3):
                    if it < n:
                        kv_f1(cs[it])
                    if 0 <= it - 1 < n:
                        kv_f2(cs[it - 1])
                    if 0 <= it - 2 < n:
                        kv_b1(cs[it - 2])
                    if 0 <= it - 3 < n:
                        kv_b2(cs[it - 3])

            def attention(kb, Qsrc, r_Qsrc, ktiles, diag, finish):
                steps = []
                for h in range(4):
                    for G in GROUPS:
                        g0 = QT_OFFS[G[0]]
                        g1 = QT_OFFS[G[-1]] + QT_SIZES[G[-1]]
                        kl = [k for k in ktiles if (not diag) or k[3] is None or k[3] <= G[-1]]
                        last_for = {}
                        for ki, (koff, nk, vidx, lt) in enumerate(kl):
                            for t in G:
                                if diag and lt is not None and lt > t:
                                    continue
                                last_for[t] = ki
                        for ki, k in enumerate(kl):
                            steps.append(dict(h=h, G=G, g0=g0, g1=g1, ki=ki, k=k, first=(ki == 0), last=(ki == len(kl) - 1),
                                              last_for=last_for))

                def emit_score(st):
                    koff, nk, vidx, lt = st["k"]
                    G, h = st["G"], st["h"]
                    qs = max(st["g0"], QT_OFFS[lt]) if (diag and lt is not None) else st["g0"]
                    n = st["g1"] - qs
                    si = cnt["sc"] % 2
                    cnt["sc"] += 1
                    st.update(qs=qs, n=n, si=si)
                    for c in range(2):
                        sb = 4 + 2 * si + c
                        has_mask = diag and lt is not None and lt >= G[0]
                        p.pe(lambda e, c=c, sb=sb: e.matmul(
                            pbank[sb][:nk, 0:n], lhsT=KT[kb][64 * c:64 * c + 64, h, koff:koff + nk],
                            rhs=Qsrc[64 * c:64 * c + 64, h, qs:qs + n], start=True, stop=True),
                            reads=[r_KT[kb], r_Qsrc], writes=[r_pb[sb]])
                        if has_mask:
                            p.pe(lambda e, c=c, sb=sb: e.matmul(
                                pbank[sb][:nk, 0:nk], lhsT=ident[:nk, :nk], rhs=maskneg[:nk, :nk],
                                start=False, stop=True, skip_group_check=True),
                                reads=[r_const], writes=[r_pb[sb]])

                def emit_rest(st):
                    koff, nk, vidx, lt = st["k"]
                    G, h, qs, n, si, ki = st["G"], st["h"], st["qs"], st["n"], st["si"], st["ki"]
                    if st["first"]:
                        for t in G:
                            ab_ = 1 + (t - G[0])
                            p.pe(lambda e, ab_=ab_, nt=QT_SIZES[t]: e.matmul(pbank[ab_][:nt, 0:258], lhsT=zer[0:1, 0:nt],
                                                                            rhs=zer[0:1, 0:258], start=True, stop=True),
                                 reads=[r_const], writes=[r_pb[ab_]])
                    pi = cnt["pt"] % NPT
                    cnt["pt"] += 1
                    scv = sc_h[si][:nk, :].rearrange("p (c m) -> p c m", c=2)
                    p.act(lambda e, scv=scv: e.activation(out=Pt[pi][:nk, :, 0:n], in_=scv[:, :, 0:n], func=AF.Exp),
                          reads=[r_pb[4 + 2 * si], r_pb[5 + 2 * si]], writes=[r_Pt[pi]])
                    for t in G:
                        if diag and lt is not None and lt > t:
                            continue
                        nt = QT_SIZES[t]
                        po = QT_OFFS[t] - qs
                        ab_ = 1 + (t - G[0])
                        acc = pbank[ab_][:, 0:258].rearrange("p (c d) -> p c d", c=2)
                        for c in range(2):
                            p.pe(lambda e, c=c, nt=nt, po=po, acc=acc, f=(st["last_for"][t] == ki): e.matmul(
                                acc[:nt, c, :], lhsT=Pt[pi][:nk, c, po:po + nt], rhs=VX[kb][:nk, vidx, h, 0:129],
                                start=False, stop=f, skip_group_check=True),
                                reads=[r_Pt[pi], r_VX[kb]], writes=[r_pb[ab_]])
                    if st["last"]:
                        for t in G:
                            ab_ = 1 + (t - G[0])
                            acc = pbank[ab_][:, 0:258].rearrange("p (c d) -> p c d", c=2)
                            finish(t, h, acc, r_pb[ab_], QT_SIZES[t])

                emit_score(steps[0])
                for si_, st in enumerate(steps):
                    if si_ + 1 < len(steps):
                        emit_score(steps[si_ + 1])
                    emit_rest(st)

            for X in range(2):
                kb = X % NKV
                tl = []
                for t in range(9):
                    T = QT_SIZES[t]
                    o = QT_OFFS[t]
                    tl.append((xq[X, o:o + T, :], T, kb, o, t, qval[X, o:o + T].rearrange("(p o) -> p o", o=1), (X, o)))
                tl.append((xm[:, :], 16, kb, NQ, 9, mval[X, :].rearrange("(p o) -> p o", o=1), None))
                run_kv(tl)
                ktiles = [(NQ, 16, 9, None)] + [(QT_OFFS[t], QT_SIZES[t], t, t) for t in range(9)]

                def fin_diag(t, h, acc, r_acc, nt, X=X):
                    p.dve(lambda e: e.tensor_copy(out=OA[X][:nt, t, h, :, :], in_=acc[:nt, :, :]),
                          reads=[r_acc], writes=[r_O[X][t][h]])
                attention(kb, QT[X], r_QT[X], ktiles, True, fin_diag)

            for i in range(7):
                kb = i % NKV
                run_kv([(xk[i, kt * 128:(kt + 1) * 128, :], 128, kb, kt * 128, kt,
                         kval[i, kt * 128:(kt + 1) * 128].rearrange("(p o) -> p o", o=1), None) for kt in range(8)])
                p.dve(lambda e, i=i: e.tensor_scalar(out=Qs[:, :, :], in0=QT[0][:, :, :], scalar1=selt[:, i:i + 1],
                                                     scalar2=None, op0=ALU.mult),
                      reads=[r_QT[0], r_const], writes=[r_Qs])
                p.dve(lambda e, i=i: e.scalar_tensor_tensor(out=Qs[:, :, :], in0=QT[1][:, :, :], scalar=selt[:, 7 + i:8 + i],
                                                            in1=Qs[:, :, :], op0=ALU.mult, op1=ALU.add),
                      reads=[r_QT[1], r_Qs, r_const], writes=[r_Qs])
                ktiles = [(kt * 128, 128, kt, None) for kt in range(8)]

                def fin_full(t, h, acc, r_acc, nt, i=i):
                    for X in range(2):
                        p.dve(lambda e, X=X: e.scalar_tensor_tensor(
                            out=OA[X][:nt, t, h, :, :], in0=acc[:nt, :, :], scalar=selt[:nt, 7 * X + i:7 * X + i + 1],
                            in1=OA[X][:nt, t, h, :, :], op0=ALU.mult, op1=ALU.add),
                            reads=[r_acc, r_O[X][t][h], r_const], writes=[r_O[X][t][h]])
                attention(kb, Qs, r_Qs, ktiles, False, fin_full)

            ob = [SB(ab, f"ob{i}", [128, 4, 128], F32) for i in range(2)]
            r_ob = [R(f"ob{i}") for i in range(2)]
            obf = [SB(ab, f"obf{i}", [128, 4, 128], BF16) for i in range(2)]
            r_obf = [R(f"obf{i}") for i in range(2)]
            rl = [SB(ab, f"rl{i}", [128, 4, 2], F32) for i in range(2)]
            r_rl = [R(f"rl{i}") for i in range(2)]
            s4 = [SB(ab, f"s4{i}", [128, 12], F32) for i in range(2)]
            r_s4 = [R(f"s4{i}") for i in range(2)]
            k = 0
            for X in range(2):
                for t in range(9):
                    nt = QT_SIZES[t]
                    o = QT_OFFS[t]
                    i = k % 2
                    k += 1
                    rO = [r_O[X][t][h] for h in range(4)]
                    p.dve(lambda e, X=X, t=t, nt=nt, i=i: e.tensor_scalar(out=rl[i][:nt, :, :], in0=OA[X][:nt, t, :, :, 128],
                                                                          scalar1=1e-30, scalar2=None, op0=ALU.max),
                          reads=rO, writes=[r_rl[i]])
                    p.dve(lambda e, nt=nt, i=i: e.reciprocal(out=rl[i][:nt, :, :], in_=rl[i][:nt, :, :]),
                          reads=[r_rl[i]], writes=[r_rl[i]])
                    p.dve(lambda e, nt=nt, i=i: e.tensor_scalar(out=rl[i][:nt, :, 1:2], in0=rl[i][:nt, :, 1:2],
                                                                scalar1=lamw[:nt, 6:7], scalar2=None, op0=ALU.mult),
                          reads=[r_rl[i], r_const], writes=[r_rl[i]])
                    for h in range(4):
                        p.dve(lambda e, X=X, t=t, nt=nt, i=i, h=h: e.tensor_scalar(
                            out=ob[i][:nt, h, :], in0=OA[X][:nt, t, h, 0, 0:128], scalar1=rl[i][:nt, h, 0:1],
                            scalar2=None, op0=ALU.mult), reads=rO + [r_rl[i]], writes=[r_ob[i]])
                        p.dve(lambda e, X=X, t=t, nt=nt, i=i, h=h: e.scalar_tensor_tensor(
                            out=ob[i][:nt, h, :], in0=OA[X][:nt, t, h, 1, 0:128], scalar=rl[i][:nt, h, 1:2],
                            in1=ob[i][:nt, h, :], op0=ALU.mult, op1=ALU.add), reads=rO + [r_rl[i], r_ob[i]], writes=[r_ob[i]])
                        p.act(lambda e, nt=nt, i=i, h=h: e.activation(out=obf[i][:nt, h, :], in_=ob[i][:nt, h, :], func=AF.Square,
                                                                      accum_out=s4[i][:nt, h:h + 1]),
                              reads=[r_ob[i]], writes=[r_obf[i], r_s4[i]])
                    p.act(lambda e, nt=nt, i=i: e.activation(out=s4[i][:nt, 8:12], in_=s4[i][:nt, 0:4], func=AF.Ln,
                                                             scale=1.0 / 128, bias=epst[:nt, 0:1]),
                          reads=[r_s4[i], r_const], writes=[r_s4[i]])
                    p.act(lambda e, nt=nt, i=i: e.activation(out=s4[i][:nt, 4:8], in_=s4[i][:nt, 8:12], func=AF.Exp, scale=-0.5),
                          reads=[r_s4[i]], writes=[r_s4[i]])
                    for h in range(4):
                        p.dve(lambda e, nt=nt, i=i, h=h: e.tensor_scalar(out=obf[i][:nt, h, :], in0=ob[i][:nt, h, :],
                                                                         scalar1=s4[i][:nt, 4 + h:5 + h], scalar2=None,
                                                                         op0=ALU.mult),
                              reads=[r_ob[i], r_s4[i]], writes=[r_obf[i]])
                    pt = pbank_bf[0]
                    for h in range(4):
                        p.pe(lambda e, nt=nt, i=i, h=h: e.transpose(out=pt[:, h * 128:h * 128 + nt], in_=obf[i][:nt, h, :],
                                                                    identity=ident[:nt, :nt]),
                             reads=[r_obf[i], r_const], writes=[r_pb[0]])
                    p.dve(lambda e, X=X, nt=nt, o=o: e.tensor_scalar(
                        out=mixYb[:, :, X, o:o + nt], in0=pt[:, 0:512].rearrange("p (h t) -> p h t", h=4)[:, :, :nt],
                        scalar1=subgt[:, 0:1], scalar2=None, op0=ALU.mult),
                        reads=[r_pb[0], r_const], writes=[r_mixT[X][t]])

        p.barrier()
        n2T = SB(outer, "n2T", [128, 8, 2, NQ], BF16)
        r_n2T = [[R(f"n2T{X}_{t}") for t in range(9)] for X in range(2)]
        r_hmid = [[R(f"hmid{X}_{t}") for t in range(9)] for X in range(2)]
        with contextlib.ExitStack() as c1:
            wu = SB(c1, "wu", [128, 8, 512], BF16)
            r_wu_l = [R(f"wu{c}") for c in range(8)]
            wo = SB(c1, "wo", [128, 8, D], BF16)
            r_wo_l = [R(f"wo{c}") for c in range(8)]
            mixYa = SB(c1, "mixYa", [128, 4, 2, NQ], BF16)
            gffn_b = SB(c1, "gffn_b", [128, D], F32)
            r_gffn = R("gffn_b")
            p.dma(lambda e: e.dma_start(out=gffn_b[:], in_=gffn_d.partition_broadcast(128), allow_slow_non_contiguous=True),
                  writes=[r_gffn], q="sp")
            wpl = SB(c1, "wpl", [128, 4, 128], BF16)
            r_wpl = R("wpl")
            w3 = w_in.rearrange("(c p) n -> p c n", p=128)
            wo3 = w_out.rearrange("(c p) n -> p c n", p=128)
            for c in range(8):
                p.dma(lambda e, c=c: e.dma_start(out=wu[:, c, :], in_=w3[:, c, 0:512]), writes=[r_wu_l[c]], q="pool")
            for c in range(8):
                p.dma(lambda e, c=c: e.dma_start(out=wo[:, c, :], in_=wo3[:, c, :]), writes=[r_wo_l[c]], q="pool")
            p.dma(lambda e: e.dma_start(out=wpl[:, :, :], in_=w_pool.rearrange("g c d -> c g d")), writes=[r_wpl], q="pool")

            PADW = 16
            uT = SB(c1, "uT", [128, 4, PADW + NQ], F32)
            r_uT = R("uT")
            sA = SB(c1, "sA", [128, PADW + NQ], F32)
            sB = SB(c1, "sB", [128, PADW + NQ], F32)
            r_sA = R("sA")
            r_sB = R("sB")
            plb = SB(c1, "plb", [128, 4, NQ], BF16)
            r_plb = R("plb")
            hm = [SB(c1, f"hm{i}", [128, D], F32) for i in range(3)]
            r_hm = [R(f"hm{i}") for i in range(3)]
            cnt["hm"] = 0
            cnt["wb"] = 0
            cnt["x2"] = 0
            xs2 = [SB(c1, f"xs2_{i}", [128, D], BF16) for i in range(2)]
            r_xs2 = [R(f"xs2_{i}") for i in range(2)]
            st3 = [SB(c1, f"st3_{i}", [128, 4], F32) for i in range(2)]
            r_st3 = [R(f"st3_{i}") for i in range(2)]
            pcb = SB(c1, "pcb", [128, NQ], F32)
            r_pcb = R("pcb")
            p.dve(lambda e: e.memset(uT[:], 0.0), writes=[r_uT])
            p.dve(lambda e: e.memset(sA[:], 0.0), writes=[r_sA])
            p.dve(lambda e: e.memset(sB[:], 0.0), writes=[r_sB])
            def build_chunk(X):
                ucs = [dict(T=QT_SIZES[t], o=QT_OFFS[t]) for t in range(9)]

                def u_f1(c, X=X):
                    i = cnt["fe"] % NFE
                    cnt["fe"] += 1
                    c["xi"] = i
                    T, o = c["T"], c["o"]
                    p.dma(lambda e: e.dma_start(out=xt[i][:T, :], in_=xq[X, o:o + T, :]), writes=[r_xt[i]], q="sp")
                    rms_rows(xt[i], r_xt[i], T, xs[i], r_xs[i], st2[i], r_st2[i], gmix_b)

                def u_f2(c, X=X):
                    ni = cnt["nt"] % NNT
                    cnt["nt"] += 1
                    c["ni"] = ni
                    transpose_rows(xs[c["xi"]], r_xs[c["xi"]], c["T"], nT[ni], r_nT[ni], bank=0)

                def u_f3(c, X=X):
                    T, o, ni = c["T"], c["o"], c["ni"]
                    ub = 1 + (cnt["ub"] % 2)
                    cnt["ub"] += 1
                    pu = pbank[ub]
                    for g in range(4):
                        for cc in range(8):
                            p.pe(lambda e, g=g, cc=cc: e.matmul(pu[:, g * 128:g * 128 + T], lhsT=wu[:, cc, g * 128:(g + 1) * 128],
                                                                rhs=nT[ni][:, cc, :T], start=(cc == 0), stop=(cc == 7)),
                                 reads=[r_wu_l[cc], r_nT[ni]], writes=[r_pb[ub]])
                    p.act(lambda e: e.activation(out=uT[:, :, PADW + o:PADW + o + T],
                                                 in_=pu[:, :].rearrange("p (g t) -> p g t", g=4)[:, :, :T],
                                                 func=AF.Copy), reads=[r_pb[ub]], writes=[r_uT])

                def p1_it(it):
                    if it < 9:
                        u_f1(ucs[it])
                    if 0 <= it - 1 < 9:
                        u_f2(ucs[it - 1])
                    if 0 <= it - 2 < 9:
                        u_f3(ucs[it - 2])
                def mid():
                    for g, w in enumerate((2, 4, 8, 16)):
                        src = uT[:, g, :]
                        r_src = r_uT
                        bufs = [(sA, r_sA), (sB, r_sB)]
                        sh = 1
                        bi = 0
                        while sh < w:
                            dst, r_dst = bufs[bi]
                            p.dve(lambda e, src=src, dst=dst, sh=sh: e.tensor_tensor(
                                out=dst[:, PADW:PADW + NQ], in0=src[:, PADW:PADW + NQ], in1=src[:, PADW - sh:PADW + NQ - sh],
                                op=ALU.add), reads=[r_src], writes=[r_dst])
                            src, r_src = dst, r_dst
                            sh *= 2
                            bi ^= 1
                        if w < 16:
                            p.dve(lambda e, src=src, g=g, w=w: e.scalar_tensor_tensor(
                                out=plb[:, g, :], in0=src[:, PADW:PADW + NQ], scalar=1.0 / w, in1=uT[:, g, PADW:PADW + NQ],
                                op0=ALU.mult, op1=ALU.subtract), reads=[r_src, r_uT], writes=[r_plb])
                        else:
                            p.dma(lambda e, X=X: e.dma_start(out=pcb[:, :], in_=pc16[X, :].partition_broadcast(128),
                                                             allow_slow_non_contiguous=True), writes=[r_pcb], q="sp")
                            p.dve(lambda e, src=src: e.tensor_tensor(out=src[:, PADW:PADW + NQ], in0=src[:, PADW:PADW + NQ],
                                                                     in1=pcb[:, :], op=ALU.mult),
                                  reads=[r_src, r_pcb], writes=[r_src])
                            p.dve(lambda e, src=src, g=g: e.tensor_tensor(out=plb[:, g, :], in0=src[:, PADW:PADW + NQ],
                                                                          in1=uT[:, g, PADW:PADW + NQ], op=ALU.subtract),
                                  reads=[r_src, r_uT], writes=[r_plb])
                    for g in range(4):
                        for (c0, n) in ((0, 352), (352, 352), (704, 352)):
                            pq = pbank[3]
                            p.pe(lambda e, g=g, c0=c0, n=n: e.matmul(pq[:, 0:n], lhsT=wpl[:, g, :], rhs=plb[:, g, c0:c0 + n],
                                                                     start=True, stop=True),
                                 reads=[r_wpl, r_plb], writes=[r_pb[3]])
                            p.act(lambda e, g=g, c0=c0, n=n, X=X: e.activation(out=mixYa[:, g, X, c0:c0 + n], in_=pq[:, 0:n],
                                                                              func=AF.Identity, scale=pst[:, g:g + 1],
                                                                              bias=bps[:, g:g + 1]),
                                  reads=[r_pb[3], r_const], writes=[r_mixYa[X]])
                wcs = [dict(t=t, T=QT_SIZES[t], o=QT_OFFS[t]) for t in range(9)]

                def w_g1(c, X=X):
                    t, T, o = c["t"], c["T"], c["o"]
                    i = cnt["hm"] % 3
                    cnt["hm"] += 1
                    c["hi"] = i
                    par = cnt["wb"] % 2
                    cnt["wb"] += 1
                    p.dma(lambda e: e.dma_start(out=hm[i][:T, :], in_=xq[X, o:o + T, :]), writes=[r_hm[i]], q="sp")
                    for half in range(2):
                        bk = 4 + 2 * par + half
                        ph = pbank[bk]
                        for fc in range(8):
                            p.pe(lambda e, fc=fc, half=half, ph=ph: e.matmul(
                                ph[:T, :], lhsT=(mixYa[:, fc, X, o:o + T] if fc < 4 else mixYb[:, fc - 4, X, o:o + T]),
                                rhs=wo[:, fc, half * 512:(half + 1) * 512], start=(fc == 0), stop=(fc == 7)),
                                reads=[r_wo_l[fc], r_mixT[X][t], r_mixYa[X]], writes=[r_pb[bk]])
                        p.dve(lambda e, half=half, ph=ph: e.tensor_tensor(
                            out=hm[i][:T, half * 512:(half + 1) * 512], in0=ph[:T, :], in1=hm[i][:T, half * 512:(half + 1) * 512],
                            op=ALU.add), reads=[r_pb[bk], r_hm[i]], writes=[r_hm[i]])
                    p.dma(lambda e: e.dma_start(out=hmid[X, o:o + T, :], in_=hm[i][:T, :]),
                          reads=[r_hm[i]], writes=[r_hmid[X][t]], q="sp")

                def w_g2(c, X=X):
                    i = c["hi"]
                    j = cnt["x2"] % 2
                    cnt["x2"] += 1
                    c["xj"] = j
                    rms_rows(hm[i], r_hm[i], c["T"], xs2[j], r_xs2[j], st3[j], r_st3[j], gffn_b, r_gffn)

                def w_g3(c, X=X):
                    t, T, o, j = c["t"], c["T"], c["o"], c["xj"]
                    pt = pbank_bf[3]
                    for cc in range(8):
                        p.pe(lambda e, cc=cc: e.transpose(out=pt[:, cc * 128:cc * 128 + T],
                                                          in_=xs2[j][:T, cc * 128:(cc + 1) * 128], identity=ident[:T, :T]),
                             reads=[r_xs2[j], r_const], writes=[r_pb[3]])
                    p.dve(lambda e: e.tensor_copy(out=n2T[:, :, X, o:o + T],
                                                  in_=pt[:, :].rearrange("p (c t) -> p c t", c=8)[:, :, :T]),
                          reads=[r_pb[3]], writes=[r_n2T[X][t]])

                def p3_it(it):
                    if it < 9:
                        w_g1(wcs[it])
                    if 0 <= it - 1 < 9:
                        w_g2(wcs[it - 1])
                    if 0 <= it - 2 < 9:
                        w_g3(wcs[it - 2])
                return p1_it, mid, p3_it

            chA = build_chunk(0)
            chB = build_chunk(1)
            for it in range(11):
                chA[0](it)
            chA[1]()
            for it in range(11):
                chA[2](it)
                chB[0](it)
            chB[1]()
            for it in range(11):
                chB[2](it)

        p.barrier()
        outs = []
        with contextlib.ExitStack() as c2:
            NFB = FB_PER_PASS
            wup = SB(c2, "wup", [128, 8, 2, NFB * 128], BF16)
            wdn = SB(c2, "wdn", [128, NFB, D], BF16)
            FGRP = [(0, 4), (4, 8), (8, NFB)]
            r_wup_l = [[R(f"wup{s_}_{k}") for k in range(len(FGRP))] for s_ in range(2)]
            r_wdn_l = [R(f"wdn{f}") for f in range(NFB)]
            Gt = [SB(c2, f"Gt{i}", [128, NFB, FG], BF16) for i in range(2)]
            r_Gt = [R(f"Gt{i}") for i in range(2)]
            cbuf = [[SB(c2, f"cbuf{i}_{s}", [128, FG], F32) for s in range(2)] for i in range(2)]
            r_cbuf = [[R(f"cbuf{i}_{s}") for s in range(2)] for i in range(2)]
            sg = [SB(c2, f"sg{i}", [128, FG], F32) for i in range(2)]
            r_sg = [R(f"sg{i}") for i in range(2)]
            yt = [SB(c2, f"yt{i}", [128, D], F32) for i in range(2)]
            r_yt = [R(f"yt{i}") for i in range(2)]
            r_yst = [R(f"yst{i}") for i in range(2)]
            r_y = [[R(f"y{X}_{t}") for t in range(8)] for X in range(2)]
            wu3 = w_up.rearrange("(c p) n -> p c n", p=128)
            wd3 = w_down.rearrange("(f p) n -> p f n", p=128)
            wk = 0
            gk = 0
            for ps_ in range(NPASS):
                fb0 = ps_ * NFB
                for k, (fa, fz) in enumerate(FGRP):
                    for s in range(2):
                        col0 = s * DFF + (fb0 + fa) * 128
                        p.dma(lambda e, s=s, col0=col0, fa=fa, fz=fz: e.dma_start(
                            out=wup[:, :, s, fa * 128:fz * 128], in_=wu3[:, :, col0:col0 + (fz - fa) * 128]),
                            writes=[r_wup_l[s][k]], q="pool")
                for f in range(NFB):
                    p.dma(lambda e, f=f, fb0=fb0: e.dma_start(out=wdn[:, f, :], in_=wd3[:, fb0 + f, :]), writes=[r_wdn_l[f]], q="pool")
                def up_fb(g, f):
                    X, t0, gb, rn, W = g["X"], g["t0"], g["gb"], g["rn"], g["W"]
                    fb = fb0 + f
                    ci = f % 2
                    for s_ in range(2):
                        pu = pbank[ci * 2 + s_]
                        for c in range(8):
                            p.pe(lambda e, c=c, s_=s_, pu=pu: e.matmul(
                                pu[:, 0:W + 2], lhsT=wup[:, c, s_, f * 128:(f + 1) * 128],
                                rhs=n2T[:, c, X, t0 - 2:t0 + W], start=(c == 0), stop=(c == 7)),
                                reads=[r_wup_l[s_][[k for k, (fa, fz) in enumerate(FGRP) if fa <= f < fz][0]]] + rn,
                                writes=[r_pb[ci * 2 + s_]])
                        col = s_ * 22 + fb
                        cbt = cbuf[ci][s_]
                        rcb = r_cbuf[ci][s_]
                        p.act(lambda e, pu=pu, cbt=cbt, col=col: e.activation(
                            out=cbt[:, 0:W], in_=pu[:, 2:W + 2], func=AF.Identity, scale=cw[:, 2, col:col + 1],
                            bias=cb[:, col:col + 1]), reads=[r_pb[ci * 2 + s_], r_const], writes=[rcb])
                        p.dve(lambda e, pu=pu, cbt=cbt, col=col: e.scalar_tensor_tensor(
                            out=cbt[:, 0:W], in0=pu[:, 1:W + 1], scalar=cw[:, 1, col:col + 1], in1=cbt[:, 0:W],
                            op0=ALU.mult, op1=ALU.add), reads=[r_pb[ci * 2 + s_], rcb, r_const], writes=[rcb])
                        p.dve(lambda e, pu=pu, cbt=cbt, col=col: e.scalar_tensor_tensor(
                            out=cbt[:, 0:W], in0=pu[:, 0:W], scalar=cw[:, 0, col:col + 1], in1=cbt[:, 0:W],
                            op0=ALU.mult, op1=ALU.add), reads=[r_pb[ci * 2 + s_], rcb, r_const], writes=[rcb])
                    p.act(lambda e: e.activation(out=sg[ci][:, 0:W], in_=cbuf[ci][0][:, 0:W], func=AF.Silu),
                          reads=[r_cbuf[ci][0]], writes=[r_sg[ci]])
                    p.dve(lambda e: e.tensor_tensor(out=Gt[gb][:, f, 0:W], in0=sg[ci][:, 0:W], in1=cbuf[ci][1][:, 0:W], op=ALU.mult),
                          reads=[r_sg[ci], r_cbuf[ci][1]], writes=[r_Gt[gb]])

                def down_unit(g, q):
                    X, gb = g["X"], g["gb"]
                    tt = g["tiles"][q]
                    yrow = (tt - 1) * 128
                    yi = q % 2
                    if ps_ == 0:
                        p.dma(lambda e: e.dma_start(out=yt[yi][:, :], in_=hmid[X, QT_OFFS[tt]:QT_OFFS[tt] + 128, :]),
                              reads=[r_hmid[X][tt]], writes=[r_yt[yi]], q="sp")
                    else:
                        p.dma(lambda e: e.dma_start(out=yt[yi][:, :], in_=y[X, yrow:yrow + 128, :]),
                              reads=[r_y[X][tt - 1]], writes=[r_yt[yi]], q="sp")
                    for half in range(2):
                        bk = 4 + 2 * (q % 2) + half
                        pd = pbank[bk]
                        for f in range(NFB):
                            p.pe(lambda e, f=f, half=half, pd=pd: e.matmul(
                                pd[:, :], lhsT=Gt[gb][:, f, q * 128:(q + 1) * 128],
                                rhs=wdn[:, f, half * 512:(half + 1) * 512], start=(f == 0), stop=(f == NFB - 1)),
                                reads=[r_Gt[gb], r_wdn_l[f]], writes=[r_pb[bk]])
                        p.dve(lambda e, half=half, pd=pd: e.tensor_tensor(
                            out=yt[yi][:, half * 512:(half + 1) * 512], in0=pd[:, :],
                            in1=yt[yi][:, half * 512:(half + 1) * 512], op=ALU.add),
                            reads=[r_pb[bk], r_yt[yi]], writes=[r_yt[yi]])
                    od = p.dma(lambda e: e.dma_start(out=y[X, yrow:yrow + 128, :], in_=yt[yi][:, :]),
                               reads=[r_yt[yi]], writes=[r_y[X][tt - 1]], q="sp", sem_res=r_yst[yi])
                    if ps_ == NPASS - 1:
                        outs.append(od)

                groups = []
                for X in range(2):
                    for tiles in ([1, 2, 3], [4, 5, 6], [7, 8]):
                        groups.append(dict(X=X, t0=QT_OFFS[tiles[0]], gb=gk % 2, tiles=tiles, W=128 * len(tiles),
                                           rn=[r_n2T[X][t] for t in tiles] + [r_n2T[X][tiles[0] - 1]]))
                        gk += 1
                slots = {2: 0, 5: 1, 8: 2}
                prev = None
                for g in groups:
                    for f in range(NFB):
                        up_fb(g, f)
                        if prev is not None and f in slots and slots[f] < len(prev["tiles"]):
                            down_unit(prev, slots[f])
                    prev = g
                for q in range(len(prev["tiles"])):
                    down_unit(prev, q)
            p.emit(final_waits=outs)
    return nc


_NC_CACHE = {}


def _host_layout(x, meta_tokens):
    per_core = []
    for core in range(8):
        b, j = core // 4, core % 4
        xb = x[b]
        chunks = (j, 7 - j)
        xq = np.zeros((2, NQ, D), np.float32)
        qval = np.ones((2, NQ), np.float32)
        mval = np.ones((2, 16), np.float32)
        for X, c in enumerate(chunks):
            s = 1024 * c
            xq[X, 32:] = xb[s:s + 1024]
            if c == 0:
                xq[X, 16:32] = meta_tokens
                qval[X, 0:16] = 0.0
                mval[X, :] = 0.0
            else:
                xq[X, 0:32] = xb[s - 32:s]
        xk = np.zeros((7, 1024, D), np.float32)
        kval = np.zeros((7, 1024), np.float32)
        sel = np.zeros((14,), np.float32)
        pc16 = np.full((2, NQ), 1.0 / 16.0, np.float32)
        if chunks[0] == 0:
            for pp in range(15):
                pc16[0, 16 + pp] = 1.0 / float(pp + 1)
        for i in range(7):
            if i < j:
                X, c, t = 0, chunks[0], i
            else:
                X, c, t = 1, chunks[1], i - j
            lim = 1024 * c - 32
            lo = 1024 * t
            hi = min(lo + 1024, lim)
            xk[i, :hi - lo] = xb[lo:hi]
            kval[i, :hi - lo] = 1.0
            sel[7 * X + i] = 1.0
        per_core.append(dict(xq=xq, xk=xk, xm=np.ascontiguousarray(meta_tokens, np.float32), kval=kval, qval=qval,
                             mval=mval, sel=sel, pc16=pc16))
    return per_core


def kernel(x, meta_tokens, norm_mix_g, w_in, w_pool, b_pool, pool_scale, q_norm_g, k_norm_g,
           lambda_q1, lambda_k1, lambda_q2, lambda_k2, subln_g, w_out, norm_ffn_g,
           w_up, conv_w, conv_b, w_down):
    f = lambda a: np.ascontiguousarray(np.asarray(a, np.float32))
    x = f(x)
    meta_tokens = f(meta_tokens)
    shared = {
        "w_in": f(w_in)[0], "w_pool": f(w_pool)[0], "b_pool": f(b_pool)[0], "pool_scale": f(pool_scale)[0],
        "q_norm_g": f(q_norm_g)[0], "k_norm_g": f(k_norm_g)[0], "lambda_q1": f(lambda_q1)[0],
        "lambda_k1": f(lambda_k1)[0], "lambda_q2": f(lambda_q2)[0], "lambda_k2": f(lambda_k2)[0],
        "subln_g": f(subln_g)[0], "w_out": f(w_out)[0], "norm_mix_g": f(norm_mix_g)[0],
        "norm_ffn_g": f(norm_ffn_g)[0], "w_up": f(w_up)[0], "conv_w": f(conv_w)[0], "conv_b": f(conv_b)[0],
        "w_down": f(w_down)[0],
    }
    if "nc" not in _NC_CACHE:
        _NC_CACHE["nc"] = build_program()
    nc = _NC_CACHE["nc"]
    per_core = _host_layout(x, meta_tokens)
    in_maps = [dict(shared, **pc) for pc in per_core]
    res = run_bass_kernel_spmd(nc, in_maps, core_ids=list(range(8)))
    out = np.empty((2, 8192, D), np.float32)
    for core in range(8):
        b, j = core // 4, core % 4
        yc = res.results[core]["y"]
        out[b, 1024 * j:1024 * (j + 1)] = yc[0]
        out[b, 1024 * (7 - j):1024 * (8 - j)] = yc[1]
    return out
```

```python
import contextlib
import numpy as np
import concourse.bass as bass
import concourse.mybir as mybir
from concourse.bass_utils import run_bass_kernel_spmd

F32 = mybir.dt.float32
BF16 = mybir.dt.bfloat16
AF = mybir.ActivationFunctionType
ALU = mybir.AluOpType
AX = mybir.AxisListType

ENGS = ("pe", "act", "dve", "pool", "sp")


class Res:
    __slots__ = ("name", "last_w", "readers", "dma_sem", "dma_cnt", "last_dma")

    def __init__(self, name):
        self.name = name
        self.last_dma = None
        self.last_w = None
        self.readers = []
        self.dma_sem = None
        self.dma_cnt = 0


class Ins:
    __slots__ = ("eng", "fn", "deps", "inc_val", "is_dma", "dma_res", "dma_val", "needed")

    def __init__(self, eng, fn):
        self.eng = eng
        self.fn = fn
        self.deps = []
        self.inc_val = None
        self.is_dma = False
        self.dma_res = None
        self.dma_val = 0
        self.needed = False


class Prog:
    def __init__(self, nc):
        self.nc = nc
        self.streams = {e: [] for e in ENGS}
        self.all_res = []
        self.barrier_deps = []
        self.barrier_seen = {e: True for e in ENGS}

    def res(self, name):
        r = Res(name)
        self.all_res.append(r)
        return r

    def barrier(self):
        deps = []
        for e in ENGS:
            if self.streams[e]:
                deps.append(self.streams[e][-1])
        for r in self.all_res:
            if r.last_dma is not None:
                deps.append(r.last_dma)
        self.barrier_deps = deps
        self.barrier_seen = {e: False for e in ENGS}

    def _add(self, eng, fn, reads, writes, is_dma=False, sem_res=None):
        ins = Ins(eng, fn)
        ins.is_dma = is_dma
        deps = []
        if not self.barrier_seen[eng]:
            self.barrier_seen[eng] = True
            deps.extend(self.barrier_deps)
        for r in reads:
            if r.last_w is not None:
                deps.append(r.last_w)
        for r in writes:
            if r.last_w is not None:
                deps.append(r.last_w)
            deps.extend(r.readers)
        seen = set()
        for d in deps:
            if d is ins or id(d) in seen:
                continue
            seen.add(id(d))
            if d.eng == eng and not d.is_dma and eng in ("pe", "sp"):
                continue
            ins.deps.append(d)
            d.needed = True
        for r in reads:
            r.readers.append(ins)
        for r in writes:
            r.last_w = ins
            r.readers = []
        if is_dma:
            tgt = sem_res if sem_res is not None else writes[0]
            ins.dma_res = tgt
            tgt.dma_cnt += 16
            ins.dma_val = tgt.dma_cnt
            tgt.last_dma = ins
        self.streams[eng].append(ins)
        return ins

    def pe(self, fn, reads=(), writes=()):
        return self._add("pe", fn, list(reads), list(writes))

    def act(self, fn, reads=(), writes=()):
        return self._add("act", fn, list(reads), list(writes))

    def dve(self, fn, reads=(), writes=()):
        return self._add("dve", fn, list(reads), list(writes))

    def pool(self, fn, reads=(), writes=()):
        return self._add("pool", fn, list(reads), list(writes))

    def dma(self, fn, reads=(), writes=(), q="sp", sem_res=None):
        return self._add(q, fn, list(reads), list(writes), is_dma=True, sem_res=sem_res)

    def emit(self, final_waits=()):
        nc = self.nc
        with contextlib.ExitStack() as st:
            sems = {}
            for e in ENGS:
                sems[e] = st.enter_context(nc.semaphore("s_" + e))
            for r in self.all_res:
                if r.dma_cnt > 0:
                    r.dma_sem = st.enter_context(nc.semaphore("d_" + r.name))
            for e in ENGS:
                c = 0
                for ins in self.streams[e]:
                    if ins.is_dma:
                        continue
                    if ins.needed:
                        c += 1
                        ins.inc_val = c
            block = st.enter_context(nc.Block())
            engmap = {"pe": "tensor", "act": "scalar", "dve": "vector", "pool": "gpsimd", "sp": "sync"}

            def make(e):
                def body(eng):
                    waited = {}
                    for ins in self.streams[e]:
                        for d in ins.deps:
                            if d.is_dma:
                                key = ("d", id(d.dma_res))
                                val = d.dma_val
                                sem = d.dma_res.dma_sem
                            else:
                                key = ("e", d.eng)
                                val = d.inc_val
                                sem = sems[d.eng]
                            if waited.get(key, 0) >= val:
                                continue
                            waited[key] = val
                            eng.wait_ge(sem, val)
                        r = ins.fn(eng)
                        if ins.is_dma:
                            r.then_inc(ins.dma_res.dma_sem, 16)
                        elif ins.needed:
                            r.then_inc(sems[e], 1)
                    if e == "sp":
                        for d in final_waits:
                            eng.wait_ge(d.dma_res.dma_sem, d.dma_val)
                return body

            for e in ENGS:
                getattr(block, engmap[e])(make(e))


D = 1024
NQ = 1056
QT_SIZES = [32] + [128] * 8
QT_OFFS = [0] + [32 + 128 * i for i in range(8)]
GROUPS = [(0, 1, 2), (3, 4, 5), (6, 7, 8)]
DFF = 2816
EPS = 1e-6
LAM_INIT = 0.2
NEG = -30000.0
FG = 384
NPASS = 2
FB_PER_PASS = 22 // NPASS


def build_program():
    nc = bass.Bass("TRN2", target_bir_lowering=False)
    dr = lambda name, shape, kind="ExternalInput": nc.dram_tensor(name, shape, F32, kind=kind).ap()
    xq = dr("xq", [2, NQ, D])
    xk = dr("xk", [7, 1024, D])
    xm = dr("xm", [16, D])
    kval = dr("kval", [7, 1024])
    qval = dr("qval", [2, NQ])
    mval = dr("mval", [2, 16])
    sel = dr("sel", [14])
    pc16 = dr("pc16", [2, NQ])
    w_in = dr("w_in", [D, 2048])
    w_pool = dr("w_pool", [4, 128, 128])
    b_pool = dr("b_pool", [4, 128])
    pool_scale = dr("pool_scale", [512])
    qg = dr("q_norm_g", [64])
    kg = dr("k_norm_g", [64])
    lq1 = dr("lambda_q1", [64])
    lk1 = dr("lambda_k1", [64])
    lq2 = dr("lambda_q2", [64])
    lk2 = dr("lambda_k2", [64])
    subg = dr("subln_g", [128])
    w_out = dr("w_out", [D, D])
    gmix_d = dr("norm_mix_g", [D])
    gffn_d = dr("norm_ffn_g", [D])
    w_up = dr("w_up", [D, 2 * DFF])
    conv_w = dr("conv_w", [3, 2 * DFF])
    conv_b = dr("conv_b", [2 * DFF])
    w_down = dr("w_down", [DFF, D])
    y = dr("y", [2, 1024, D], kind="ExternalOutput")
    hmid = dr("hmid", [2, NQ, D], kind="Internal")

    p = Prog(nc)
    R = p.res

    with contextlib.ExitStack() as outer:
        def SB(st, name, shape, dt):
            return st.enter_context(nc.sbuf_tensor(name, shape, dt))

        def PS(st, name, shape, dt):
            return st.enter_context(nc.psum_tensor(name, shape, dt))

        ident = SB(outer, "ident", [128, 128], BF16)
        identf = SB(outer, "identf", [128, 128], F32)
        maskneg = SB(outer, "maskneg", [128, 128], BF16)
        blk1 = SB(outer, "blk1", [128, 128], BF16)
        gmix_b = SB(outer, "gmix_b", [128, D], F32)
        qkg = SB(outer, "qkg", [128, 2], F32)
        selt = SB(outer, "selt", [128, 14], F32)
        lamt = SB(outer, "lamt", [128, 4, 64], F32)
        lamw = SB(outer, "lamw", [128, 8], F32)
        subgt = SB(outer, "subgt", [128, 1], F32)
        bpt = SB(outer, "bpt", [128, 4], F32)
        pst = SB(outer, "pst", [128, 4], F32)
        bps = SB(outer, "bps", [128, 4], F32)
        ones4 = SB(outer, "ones4", [128, 4], F32)
        zer = SB(outer, "zer", [128, 512], BF16)
        cw = SB(outer, "cw", [128, 3, 44], F32)
        cb = SB(outer, "cb", [128, 44], F32)
        mixYb = SB(outer, "mixYb", [128, 4, 2, NQ], BF16)
        r_const = R("const")
        r_mixT = [[R(f"mixT{X}_{t}") for t in range(9)] for X in range(2)]
        r_mixYa = [R(f"mixYa{X}") for X in range(2)]

        pb_h = [PS(outer, f"pb{i}", [128, 512], F32) for i in range(4)]
        sc_h = [PS(outer, f"sc{i}", [128, 1024], F32) for i in range(2)]
        pbank = list(pb_h) + [sc_h[i // 2][:, (i % 2) * 512:(i % 2 + 1) * 512] for i in range(4)]
        r_pb = [R(f"pb{i}") for i in range(8)]
        pbank_bf = [b.bitcast(BF16) for b in pb_h]

        p.pool(lambda e: e.memset(identf[:], 0.0), writes=[r_const])
        p.pool(lambda e: e.affine_select(out=identf[:], in_=identf[:], compare_op=ALU.not_equal, fill=1.0,
                                         base=0, pattern=[[-1, 128]], channel_multiplier=1),
               reads=[r_const], writes=[r_const])
        p.dve(lambda e: e.tensor_copy(out=ident[:], in_=identf[:]), reads=[r_const], writes=[r_const])
        p.pool(lambda e: e.memset(identf[:], 0.0), reads=[r_const], writes=[r_const])
        p.pool(lambda e: e.affine_select(out=identf[:], in_=identf[:], compare_op=ALU.is_ge, fill=NEG,
                                         base=0, pattern=[[1, 128]], channel_multiplier=-1),
               reads=[r_const], writes=[r_const])
        p.dve(lambda e: e.tensor_copy(out=maskneg[:], in_=identf[:]), reads=[r_const], writes=[r_const])
        p.dve(lambda e: e.memset(blk1[:], 0.0), reads=[r_const], writes=[r_const])
        p.dve(lambda e: e.memset(blk1[0:64, 0:64], 1.0), reads=[r_const], writes=[r_const])
        p.dve(lambda e: e.memset(blk1[64:128, 64:128], 1.0), reads=[r_const], writes=[r_const])
        p.dve(lambda e: e.memset(ones4[:], 1.0), reads=[r_const], writes=[r_const])
        p.dve(lambda e: e.memset(zer[:], 0.0), reads=[r_const], writes=[r_const])

        def small_dma(out_ap, in_ap):
            p.dma(lambda e: e.dma_start(out=out_ap, in_=in_ap, allow_slow_non_contiguous=True),
                  reads=[], writes=[r_const], q="sp")

        small_dma(gmix_b[:], gmix_d.partition_broadcast(128))
        small_dma(qkg[0:64, 0:1], qg.rearrange("(p o) -> p o", o=1))
        small_dma(qkg[64:128, 0:1], qg.rearrange("(p o) -> p o", o=1))
        small_dma(qkg[0:64, 1:2], kg.rearrange("(p o) -> p o", o=1))
        small_dma(qkg[64:128, 1:2], kg.rearrange("(p o) -> p o", o=1))
        small_dma(selt[:], sel.partition_broadcast(128))
        for i, l in enumerate((lq1, lk1, lq2, lk2)):
            small_dma(lamt[:, i, :], l.partition_broadcast(128))
        small_dma(subgt[:], subg.rearrange("(p o) -> p o", o=1))
        small_dma(bpt[:], b_pool.rearrange("g p -> p g"))
        small_dma(pst[:], pool_scale.rearrange("(g p) -> p g", p=128))
        small_dma(cw[:], conv_w.rearrange("k (f p) -> p k f", p=128))
        small_dma(cb[:], conv_b.rearrange("(f p) -> p f", p=128))
        p.dve(lambda e: e.tensor_scalar(out=qkg[:, 0:1], in0=qkg[:, 0:1], scalar1=0.125, scalar2=None, op0=ALU.mult),
              reads=[r_const], writes=[r_const])
        p.dve(lambda e: e.tensor_tensor(out=lamt[:, 0, :], in0=lamt[:, 0, :], in1=lamt[:, 1, :], op=ALU.mult),
              reads=[r_const], writes=[r_const])
        p.dve(lambda e: e.tensor_tensor(out=lamt[:, 2, :], in0=lamt[:, 2, :], in1=lamt[:, 3, :], op=ALU.mult),
              reads=[r_const], writes=[r_const])
        p.dve(lambda e: e.reduce_sum(out=lamw[:, 0:1], in_=lamt[:, 0, :], axis=AX.X), reads=[r_const], writes=[r_const])
        p.dve(lambda e: e.reduce_sum(out=lamw[:, 1:2], in_=lamt[:, 2, :], axis=AX.X), reads=[r_const], writes=[r_const])
        p.act(lambda e: e.activation(out=lamw[:, 2:4], in_=lamw[:, 0:2], func=AF.Exp), reads=[r_const], writes=[r_const])
        p.dve(lambda e: e.tensor_tensor(out=lamw[:, 4:5], in0=lamw[:, 3:4], in1=lamw[:, 2:3], op=ALU.subtract),
              reads=[r_const], writes=[r_const])
        p.dve(lambda e: e.tensor_scalar(out=lamw[:, 6:7], in0=lamw[:, 4:5], scalar1=-LAM_INIT, scalar2=None, op0=ALU.add),
              reads=[r_const], writes=[r_const])
        p.dve(lambda e: e.tensor_tensor(out=bps[:], in0=bpt[:], in1=pst[:], op=ALU.mult), reads=[r_const], writes=[r_const])
        p.dve(lambda e: e.tensor_scalar(out=subgt[:], in0=subgt[:], scalar1=1.0 - LAM_INIT, scalar2=None, op0=ALU.mult),
              reads=[r_const], writes=[r_const])

        NFE = 2
        xt = [SB(outer, f"xt{i}", [128, D], F32) for i in range(NFE)]
        r_xt = [R(f"xt{i}") for i in range(NFE)]
        xs = [SB(outer, f"xs{i}", [128, D], BF16) for i in range(NFE)]
        r_xs = [R(f"xs{i}") for i in range(NFE)]
        NNT = 4
        nT = [SB(outer, f"nT{i}", [128, 8, 128], BF16) for i in range(NNT)]
        r_nT = [R(f"nT{i}") for i in range(NNT)]
        st2 = [SB(outer, f"st2_{i}", [128, 4], F32) for i in range(NFE)]
        r_st2 = [R(f"st2_{i}") for i in range(NFE)]
        epst = SB(outer, "epst", [128, 1], F32)
        p.dve(lambda e: e.memset(epst[:], EPS), reads=[r_const], writes=[r_const])
        cnt = {"fe": 0, "nt": 0, "ub": 0}

        def rms_rows(src_t, r_src, T, dst_bf, r_dst, stt, r_stt, gb, r_gb=None):
            p.act(lambda e: e.activation(out=dst_bf[:T, :], in_=src_t[:T, :], func=AF.Square, accum_out=stt[:T, 0:1]),
                  reads=[r_src], writes=[r_dst, r_stt])
            p.act(lambda e: e.activation(out=stt[:T, 1:2], in_=stt[:T, 0:1], func=AF.Ln, scale=1.0 / D, bias=epst[:T, 0:1]),
                  reads=[r_stt, r_const], writes=[r_stt])
            p.act(lambda e: e.activation(out=stt[:T, 2:3], in_=stt[:T, 1:2], func=AF.Exp, scale=-0.5),
                  reads=[r_stt], writes=[r_stt])
            p.dve(lambda e: e.scalar_tensor_tensor(out=dst_bf[:T, :], in0=src_t[:T, :], scalar=stt[:T, 2:3], in1=gb[:T, :],
                                                   op0=ALU.mult, op1=ALU.mult),
                  reads=[r_src, r_stt, r_const] + ([r_gb] if r_gb else []), writes=[r_dst])

        def transpose_rows(src_bf, r_src, T, dstT, r_dstT, bank=0):
            pt = pbank_bf[bank]
            for c in range(8):
                p.pe(lambda e, c=c: e.transpose(out=pt[:, c * 128:c * 128 + T], in_=src_bf[:T, c * 128:(c + 1) * 128],
                                                identity=ident[:T, :T]),
                     reads=[r_src, r_const], writes=[r_pb[bank]])
            p.dve(lambda e: e.tensor_copy(out=dstT[:, :, :T], in_=pt[:, :].rearrange("p (c t) -> p c t", c=8)[:, :, :T]),
                  reads=[r_pb[bank]], writes=[r_dstT])

        def front_end(src_ap, T, tbank=0):
            i = cnt["fe"] % NFE
            cnt["fe"] += 1
            ni = cnt["nt"] % NNT
            cnt["nt"] += 1
            p.dma(lambda e: e.dma_start(out=xt[i][:T, :], in_=src_ap), writes=[r_xt[i]], q="sp")
            rms_rows(xt[i], r_xt[i], T, xs[i], r_xs[i], st2[i], r_st2[i], gmix_b)
            transpose_rows(xs[i], r_xs[i], T, nT[ni], r_nT[ni], bank=tbank)
            return ni

        with contextlib.ExitStack() as ab:
            wqkv = SB(ab, "wqkv", [128, 8, 1536], BF16)
            r_wqkv_l = [R(f"wqkv{c}") for c in range(8)]
            w3 = w_in.rearrange("(c p) n -> p c n", p=128)
            for c in range(8):
                p.dma(lambda e, c=c: e.dma_start(out=wqkv[:, c, :], in_=w3[:, c, 512:2048]), writes=[r_wqkv_l[c]], q="pool")

            NKV = 1
            KT = [SB(ab, f"KT{i}", [128, 4, NQ + 16], BF16) for i in range(NKV)]
            VX = [SB(ab, f"VX{i}", [128, 10, 4, 130], BF16) for i in range(NKV)]
            r_KT = [R(f"KT{i}") for i in range(NKV)]
            r_VX = [R(f"VX{i}") for i in range(NKV)]
            QT = [SB(ab, f"QT{X}", [128, 4, NQ], BF16) for X in range(2)]
            r_QT = [R(f"QT{X}") for X in range(2)]
            Qs = SB(ab, "Qs", [128, 4, NQ], BF16)
            r_Qs = R("Qs")
            OA = [SB(ab, f"O{X}", [128, 9, 4, 2, 129], F32) for X in range(2)]
            r_O = [[[R(f"O{X}_{t}_{h}") for h in range(4)] for t in range(9)] for X in range(2)]
            sqb = [SB(ab, f"sqb{i}", [128, 4, 128], BF16) for i in range(2)]
            r_sqb = [R(f"sqb{i}") for i in range(2)]
            lnb = [SB(ab, "lnb0", [128, 4, 128], F32)] * 2
            r_lnb = [R("lnb0")] * 2
            cnt["sq"] = 0
            cnt["kvt"] = 0
            valt = [SB(ab, f"valt{i}", [128, 1], F32) for i in range(4)]
            r_valt = [R(f"valt{i}") for i in range(4)]
            NPT = 3
            Pt = [SB(ab, f"Pt{i}", [128, 2, 384], BF16) for i in range(NPT)]
            r_Pt = [R(f"Pt{i}") for i in range(NPT)]
            cnt["val"] = 0
            cnt["pt"] = 0
            cnt["sc"] = 0

            rsq = [[SB(ab, f"rsq{a}_{i}", [128, 4, 128], F32) for i in range(2)] for a in range(2)]
            r_rsq = [[R(f"rsq{a}_{i}") for i in range(2)] for a in range(2)]
            KB = [[1, 2], [3, 4]]
            SBK, VBK, TBK = 5, 6, 0

            def kv_f1(c):
                i = cnt["fe"] % NFE
                cnt["fe"] += 1
                c["xi"] = i
                T = c["T"]
                p.dma(lambda e: e.dma_start(out=xt[i][:T, :], in_=c["src"]), writes=[r_xt[i]], q="sp")
                rms_rows(xt[i], r_xt[i], T, xs[i], r_xs[i], st2[i], r_st2[i], gmix_b)
                vi = cnt["val"] % 4
                cnt["val"] += 1
                c["vi"] = vi
                p.dma(lambda e: e.dma_start(out=valt[vi][:T, :], in_=c["val"]), writes=[r_valt[vi]], q="sp")

            def kv_f2(c):
                ni = cnt["nt"] % NNT
                cnt["nt"] += 1
                c["ni"] = ni
                i = c["xi"]
                transpose_rows(xs[i], r_xs[i], c["T"], nT[ni], r_nT[ni], bank=TBK)

            def kv_b1(c):
                T, ni, par = c["T"], c["ni"], c["par"]
                for a, col0 in ((0, 512), (1, 0)):
                    if a == 1 and c["own"] is None:
                        continue
                    bank = KB[a][par]
                    pk = pbank[bank]
                    for h in range(4):
                        for cc in range(8):
                            p.pe(lambda e, h=h, cc=cc, pk=pk, col0=col0: e.matmul(
                                pk[:, h * 128:h * 128 + T], lhsT=wqkv[:, cc, col0 + h * 128:col0 + (h + 1) * 128],
                                rhs=nT[ni][:, cc, :T], start=(cc == 0), stop=(cc == 7)),
                                reads=[r_wqkv_l[cc], r_nT[ni]], writes=[r_pb[bank]])
                    pk3 = pk[:, :].rearrange("p (h t) -> p h t", h=4)
                    j = cnt["sq"] % 2
                    cnt["sq"] += 1
                    p.act(lambda e, pk3=pk3, j=j: e.activation(out=sqb[j][:, :, :T], in_=pk3[:, :, :T], func=AF.Square),
                          reads=[r_pb[bank]], writes=[r_sqb[j]])
                    pss = pbank[SBK]
                    for h in range(4):
                        p.pe(lambda e, h=h, j=j, pss=pss: e.matmul(pss[:, h * 128:h * 128 + T], lhsT=blk1[:, :],
                                                                   rhs=sqb[j][:, h, :T], start=True, stop=True),
                             reads=[r_sqb[j], r_const], writes=[r_pb[SBK]])
                    pss3 = pss[:, :].rearrange("p (h t) -> p h t", h=4)
                    p.act(lambda e, pss3=pss3: e.activation(out=lnb[0][:, :, :T], in_=pss3[:, :, :T], func=AF.Ln,
                                                            scale=1.0 / 64, bias=epst[:, 0:1]),
                          reads=[r_pb[SBK], r_const], writes=[r_lnb[0]])
                    p.act(lambda e, a=a: e.activation(out=rsq[a][par][:, :, :T], in_=lnb[0][:, :, :T], func=AF.Exp, scale=-0.5),
                          reads=[r_lnb[0]], writes=[r_rsq[a][par]])

            def kv_b2(c):
                T, ni, par, kb, vidx, vi = c["T"], c["ni"], c["par"], c["kb"], c["vidx"], c["vi"]
                for a in range(2):
                    if a == 1 and c["own"] is None:
                        continue
                    bank = KB[a][par]
                    pk3 = pbank[bank][:, :].rearrange("p (h t) -> p h t", h=4)
                    if a == 0:
                        dst, r_dst, doff, gcol = KT[kb], r_KT[kb], c["koff"], 1
                    else:
                        X, doff = c["own"]
                        dst, r_dst, gcol = QT[X], r_QT[X], 0
                    p.dve(lambda e, pk3=pk3, dst=dst, doff=doff, gcol=gcol, a=a: e.scalar_tensor_tensor(
                        out=dst[:, :, doff:doff + T], in0=pk3[:, :, :T], scalar=qkg[:, gcol:gcol + 1],
                        in1=rsq[a][par][:, :, :T], op0=ALU.mult, op1=ALU.mult),
                        reads=[r_pb[bank], r_rsq[a][par], r_const], writes=[r_dst])
                pv = pbank[VBK]
                for cc in range(8):
                    p.pe(lambda e, cc=cc: e.matmul(pv[:T, :], lhsT=nT[ni][:, cc, :T], rhs=wqkv[:, cc, 1024:1536],
                                                   start=(cc == 0), stop=(cc == 7)),
                         reads=[r_wqkv_l[cc], r_nT[ni]], writes=[r_pb[VBK]])
                p.act(lambda e: e.activation(out=VX[kb][:T, vidx, :, 0:128],
                                             in_=pv[:T, :].rearrange("p (h d) -> p h d", h=4), func=AF.Copy,
                                             scale=valt[vi][:T, 0:1]),
                      reads=[r_pb[VBK], r_valt[vi]], writes=[r_VX[kb]])
                p.dve(lambda e: e.tensor_scalar(out=VX[kb][:T, vidx, :, 128:129], in0=ones4[:T, :].rearrange("p (h o) -> p h o", o=1),
                                                scalar1=valt[vi][:T, 0:1], scalar2=None, op0=ALU.mult),
                      reads=[r_valt[vi], r_const], writes=[r_VX[kb]])

            def run_kv(tiles):
                cs = []
                for (src, T, kb, koff, vidx, val, own) in tiles:
                    cs.append(dict(src=src, T=T, kb=kb, koff=koff, vidx=vidx, val=val, own=own, par=cnt["kvt"] % 2))
                    cnt["kvt"] += 1
                n = len(cs)
                for it in range(n + 3):
                    if it < n:
                        kv_f1(cs[it])
                    if 0 <= it - 1 < n:
                        kv_f2(cs[it - 1])
                    if 0 <= it - 2 < n:
                        kv_b1(cs[it - 2])
                    if 0 <= it - 3 < n:
                        kv_b2(cs[it - 3])

            def attention(kb, Qsrc, r_Qsrc, ktiles, diag, finish):
                steps = []
                for h in range(4):
                    for G in GROUPS:
                        g0 = QT_OFFS[G[0]]
                        g1 = QT_OFFS[G[-1]] + QT_SIZES[G[-1]]
                        kl = [k for k in ktiles if (not diag) or k[3] is None or k[3] <= G[-1]]
                        last_for = {}
                        for ki, (koff, nk, vidx, lt) in enumerate(kl):
                            for t in G:
                                if diag and lt is not None and lt > t:
                                    continue
                                last_for[t] = ki
                        for ki, k in enumerate(kl):
                            steps.append(dict(h=h, G=G, g0=g0, g1=g1, ki=ki, k=k, first=(ki == 0), last=(ki == len(kl) - 1),
                                              last_for=last_for))

                def emit_score(st):
                    koff, nk, vidx, lt = st["k"]
                    G, h = st["G"], st["h"]
                    qs = max(st["g0"], QT_OFFS[lt]) if (diag and lt is not None) else st["g0"]
                    n = st["g1"] - qs
                    si = cnt["sc"] % 2
                    cnt["sc"] += 1
                    st.update(qs=qs, n=n, si=si)
                    for c in range(2):
                        sb = 4 + 2 * si + c
                        has_mask = diag and lt is not None and lt >= G[0]
                        p.pe(lambda e, c=c, sb=sb: e.matmul(
                            pbank[sb][:nk, 0:n], lhsT=KT[kb][64 * c:64 * c + 64, h, koff:koff + nk],
                            rhs=Qsrc[64 * c:64 * c + 64, h, qs:qs + n], start=True, stop=True),
                            reads=[r_KT[kb], r_Qsrc], writes=[r_pb[sb]])
                        if has_mask:
                            p.pe(lambda e, c=c, sb=sb: e.matmul(
                                pbank[sb][:nk, 0:nk], lhsT=ident[:nk, :nk], rhs=maskneg[:nk, :nk],
                                start=False, stop=True, skip_group_check=True),
                                reads=[r_const], writes=[r_pb[sb]])

                def emit_rest(st):
                    koff, nk, vidx, lt = st["k"]
                    G, h, qs, n, si, ki = st["G"], st["h"], st["qs"], st["n"], st["si"], st["ki"]
                    if st["first"]:
                        for t in G:
                            ab_ = 1 + (t - G[0])
                            p.pe(lambda e, ab_=ab_, nt=QT_SIZES[t]: e.matmul(pbank[ab_][:nt, 0:258], lhsT=zer[0:1, 0:nt],
                                                                            rhs=zer[0:1, 0:258], start=True, stop=True),
                                 reads=[r_const], writes=[r_pb[ab_]])
                    pi = cnt["pt"] % NPT
                    cnt["pt"] += 1
                    scv = sc_h[si][:nk, :].rearrange("p (c m) -> p c m", c=2)
                    p.act(lambda e, scv=scv: e.activation(out=Pt[pi][:nk, :, 0:n], in_=scv[:, :, 0:n], func=AF.Exp),
                          reads=[r_pb[4 + 2 * si], r_pb[5 + 2 * si]], writes=[r_Pt[pi]])
                    for t in G:
                        if diag and lt is not None and lt > t:
                            continue
                        nt = QT_SIZES[t]
                        po = QT_OFFS[t] - qs
                        ab_ = 1 + (t - G[0])
                        acc = pbank[ab_][:, 0:258].rearrange("p (c d) -> p c d", c=2)
                        for c in range(2):
                            p.pe(lambda e, c=c, nt=nt, po=po, acc=acc, f=(st["last_for"][t] == ki): e.matmul(
                                acc[:nt, c, :], lhsT=Pt[pi][:nk, c, po:po + nt], rhs=VX[kb][:nk, vidx, h, 0:129],
                                start=False, stop=f, skip_group_check=True),
                                reads=[r_Pt[pi], r_VX[kb]], writes=[r_pb[ab_]])
                    if st["last"]:
                        for t in G:
                            ab_ = 1 + (t - G[0])
                            acc = pbank[ab_][:, 0:258].rearrange("p (c d) -> p c d", c=2)
                            finish(t, h, acc, r_pb[ab_], QT_SIZES[t])

                emit_score(steps[0])
                for si_, st in enumerate(steps):
                    if si_ + 1 < len(steps):
                        emit_score(steps[si_ + 1])
                    emit_rest(st)

            for X in range(2):
                kb = X % NKV
                tl = []
                for t in range(9):
                    T = QT_SIZES[t]
                    o = QT_OFFS[t]
                    tl.append((xq[X, o:o + T, :], T, kb, o, t, qval[X, o:o + T].rearrange("(p o) -> p o", o=1), (X, o)))
                tl.append((xm[:, :], 16, kb, NQ, 9, mval[X, :].rearrange("(p o) -> p o", o=1), None))
                run_kv(tl)
                ktiles = [(NQ, 16, 9, None)] + [(QT_OFFS[t], QT_SIZES[t], t, t) for t in range(9)]

                def fin_diag(t, h, acc, r_acc, nt, X=X):
                    p.dve(lambda e: e.tensor_copy(out=OA[X][:nt, t, h, :, :], in_=acc[:nt, :, :]),
                          reads=[r_acc], writes=[r_O[X][t][h]])
                attention(kb, QT[X], r_QT[X], ktiles, True, fin_diag)

            for i in range(7):
                kb = i % NKV
                run_kv([(xk[i, kt * 128:(kt + 1) * 128, :], 128, kb, kt * 128, kt,
                         kval[i, kt * 128:(kt + 1) * 128].rearrange("(p o) -> p o", o=1), None) for kt in range(8)])
                p.dve(lambda e, i=i: e.tensor_scalar(out=Qs[:, :, :], in0=QT[0][:, :, :], scalar1=selt[:, i:i + 1],
                                                     scalar2=None, op0=ALU.mult),
                      reads=[r_QT[0], r_const], writes=[r_Qs])
                p.dve(lambda e, i=i: e.scalar_tensor_tensor(out=Qs[:, :, :], in0=QT[1][:, :, :], scalar=selt[:, 7 + i:8 + i],
                                                            in1=Qs[:, :, :], op0=ALU.mult, op1=ALU.add),
                      reads=[r_QT[1], r_Qs, r_const], writes=[r_Qs])
                ktiles = [(kt * 128, 128, kt, None) for kt in range(8)]

                def fin_full(t, h, acc, r_acc, nt, i=i):
                    for X in range(2):
                        p.dve(lambda e, X=X: e.scalar_tensor_tensor(
                            out=OA[X][:nt, t, h, :, :], in0=acc[:nt, :, :], scalar=selt[:nt, 7 * X + i:7 * X + i + 1],
                            in1=OA[X][:nt, t, h, :, :], op0=ALU.mult, op1=ALU.add),
                            reads=[r_acc, r_O[X][t][h], r_const], writes=[r_O[X][t][h]])
                attention(kb, Qs, r_Qs, ktiles, False, fin_full)

            ob = [SB(ab, f"ob{i}", [128, 4, 128], F32) for i in range(2)]
            r_ob = [R(f"ob{i}") for i in range(2)]
            obf = [SB(ab, f"obf{i}", [128, 4, 128], BF16) for i in range(2)]
            r_obf = [R(f"obf{i}") for i in range(2)]
            rl = [SB(ab, f"rl{i}", [128, 4, 2], F32) for i in range(2)]
            r_rl = [R(f"rl{i}") for i in range(2)]
            s4 = [SB(ab, f"s4{i}", [128, 12], F32) for i in range(2)]
            r_s4 = [R(f"s4{i}") for i in range(2)]
            k = 0
            for X in range(2):
                for t in range(9):
                    nt = QT_SIZES[t]
                    o = QT_OFFS[t]
                    i = k % 2
                    k += 1
                    rO = [r_O[X][t][h] for h in range(4)]
                    p.dve(lambda e, X=X, t=t, nt=nt, i=i: e.tensor_scalar(out=rl[i][:nt, :, :], in0=OA[X][:nt, t, :, :, 128],
                                                                          scalar1=1e-30, scalar2=None, op0=ALU.max),
                          reads=rO, writes=[r_rl[i]])
                    p.dve(lambda e, nt=nt, i=i: e.reciprocal(out=rl[i][:nt, :, :], in_=rl[i][:nt, :, :]),
                          reads=[r_rl[i]], writes=[r_rl[i]])
                    p.dve(lambda e, nt=nt, i=i: e.tensor_scalar(out=rl[i][:nt, :, 1:2], in0=rl[i][:nt, :, 1:2],
                                                                scalar1=lamw[:nt, 6:7], scalar2=None, op0=ALU.mult),
                          reads=[r_rl[i], r_const], writes=[r_rl[i]])
                    for h in range(4):
                        p.dve(lambda e, X=X, t=t, nt=nt, i=i, h=h: e.tensor_scalar(
                            out=ob[i][:nt, h, :], in0=OA[X][:nt, t, h, 0, 0:128], scalar1=rl[i][:nt, h, 0:1],
                            scalar2=None, op0=ALU.mult), reads=rO + [r_rl[i]], writes=[r_ob[i]])
                        p.dve(lambda e, X=X, t=t, nt=nt, i=i, h=h: e.scalar_tensor_tensor(
                            out=ob[i][:nt, h, :], in0=OA[X][:nt, t, h, 1, 0:128], scalar=rl[i][:nt, h, 1:2],
                            in1=ob[i][:nt, h, :], op0=ALU.mult, op1=ALU.add), reads=rO + [r_rl[i], r_ob[i]], writes=[r_ob[i]])
                        p.act(lambda e, nt=nt, i=i, h=h: e.activation(out=obf[i][:nt, h, :], in_=ob[i][:nt, h, :], func=AF.Square,
                                                                      accum_out=s4[i][:nt, h:h + 1]),
                              reads=[r_ob[i]], writes=[r_obf[i], r_s4[i]])
                    p.act(lambda e, nt=nt, i=i: e.activation(out=s4[i][:nt, 8:12], in_=s4[i][:nt, 0:4], func=AF.Ln,
                                                             scale=1.0 / 128, bias=epst[:nt, 0:1]),
                          reads=[r_s4[i], r_const], writes=[r_s4[i]])
                    p.act(lambda e, nt=nt, i=i: e.activation(out=s4[i][:nt, 4:8], in_=s4[i][:nt, 8:12], func=AF.Exp, scale=-0.5),
                          reads=[r_s4[i]], writes=[r_s4[i]])
                    for h in range(4):
                        p.dve(lambda e, nt=nt, i=i, h=h: e.tensor_scalar(out=obf[i][:nt, h, :], in0=ob[i][:nt, h, :],
                                                                         scalar1=s4[i][:nt, 4 + h:5 + h], scalar2=None,
                                                                         op0=ALU.mult),
                              reads=[r_ob[i], r_s4[i]], writes=[r_obf[i]])
                    pt = pbank_bf[0]
                    for h in range(4):
                        p.pe(lambda e, nt=nt, i=i, h=h: e.transpose(out=pt[:, h * 128:h * 128 + nt], in_=obf[i][:nt, h, :],
                                                                    identity=ident[:nt, :nt]),
                             reads=[r_obf[i], r_const], writes=[r_pb[0]])
                    p.dve(lambda e, X=X, nt=nt, o=o: e.tensor_scalar(
                        out=mixYb[:, :, X, o:o + nt], in0=pt[:, 0:512].rearrange("p (h t) -> p h t", h=4)[:, :, :nt],
                        scalar1=subgt[:, 0:1], scalar2=None, op0=ALU.mult),
                        reads=[r_pb[0], r_const], writes=[r_mixT[X][t]])

        p.barrier()
        n2T = SB(outer, "n2T", [128, 8, 2, NQ], BF16)
        r_n2T = [[R(f"n2T{X}_{t}") for t in range(9)] for X in range(2)]
        r_hmid = [[R(f"hmid{X}_{t}") for t in range(9)] for X in range(2)]
        with contextlib.ExitStack() as c1:
            wu = SB(c1, "wu", [128, 8, 512], BF16)
            r_wu_l = [R(f"wu{c}") for c in range(8)]
            wo = SB(c1, "wo", [128, 8, D], BF16)
            r_wo_l = [R(f"wo{c}") for c in range(8)]
            mixYa = SB(c1, "mixYa", [128, 4, 2, NQ], BF16)
            gffn_b = SB(c1, "gffn_b", [128, D], F32)
            r_gffn = R("gffn_b")
            p.dma(lambda e: e.dma_start(out=gffn_b[:], in_=gffn_d.partition_broadcast(128), allow_slow_non_contiguous=True),
                  writes=[r_gffn], q="sp")
            wpl = SB(c1, "wpl", [128, 4, 128], BF16)
            r_wpl = R("wpl")
            w3 = w_in.rearrange("(c p) n -> p c n", p=128)
            wo3 = w_out.rearrange("(c p) n -> p c n", p=128)
            for c in range(8):
                p.dma(lambda e, c=c: e.dma_start(out=wu[:, c, :], in_=w3[:, c, 0:512]), writes=[r_wu_l[c]], q="pool")
            for c in range(8):
                p.dma(lambda e, c=c: e.dma_start(out=wo[:, c, :], in_=wo3[:, c, :]), writes=[r_wo_l[c]], q="pool")
            p.dma(lambda e: e.dma_start(out=wpl[:, :, :], in_=w_pool.rearrange("g c d -> c g d")), writes=[r_wpl], q="pool")

            PADW = 16
            uT = SB(c1, "uT", [128, 4, PADW + NQ], F32)
            r_uT = R("uT")
            sA = SB(c1, "sA", [128, PADW + NQ], F32)
            sB = SB(c1, "sB", [128, PADW + NQ], F32)
            r_sA = R("sA")
            r_sB = R("sB")
            plb = SB(c1, "plb", [128, 4, NQ], BF16)
            r_plb = R("plb")
            hm = [SB(c1, f"hm{i}", [128, D], F32) for i in range(3)]
            r_hm = [R(f"hm{i}") for i in range(3)]
            cnt["hm"] = 0
            cnt["wb"] = 0
            cnt["x2"] = 0
            xs2 = [SB(c1, f"xs2_{i}", [128, D], BF16) for i in range(2)]
            r_xs2 = [R(f"xs2_{i}") for i in range(2)]
            st3 = [SB(c1, f"st3_{i}", [128, 4], F32) for i in range(2)]
            r_st3 = [R(f"st3_{i}") for i in range(2)]
            pcb = SB(c1, "pcb", [128, NQ], F32)
            r_pcb = R("pcb")
            p.dve(lambda e: e.memset(uT[:], 0.0), writes=[r_uT])
            p.dve(lambda e: e.memset(sA[:], 0.0), writes=[r_sA])
            p.dve(lambda e: e.memset(sB[:], 0.0), writes=[r_sB])
            def build_chunk(X):
                ucs = [dict(T=QT_SIZES[t], o=QT_OFFS[t]) for t in range(9)]

                def u_f1(c, X=X):
                    i = cnt["fe"] % NFE
                    cnt["fe"] += 1
                    c["xi"] = i
                    T, o = c["T"], c["o"]
                    p.dma(lambda e: e.dma_start(out=xt[i][:T, :], in_=xq[X, o:o + T, :]), writes=[r_xt[i]], q="sp")
                    rms_rows(xt[i], r_xt[i], T, xs[i], r_xs[i], st2[i], r_st2[i], gmix_b)

                def u_f2(c, X=X):
                    ni = cnt["nt"] % NNT
                    cnt["nt"] += 1
                    c["ni"] = ni
                    transpose_rows(xs[c["xi"]], r_xs[c["xi"]], c["T"], nT[ni], r_nT[ni], bank=0)

                def u_f3(c, X=X):
                    T, o, ni = c["T"], c["o"], c["ni"]
                    ub = 1 + (cnt["ub"] % 2)
                    cnt["ub"] += 1
                    pu = pbank[ub]
                    for g in range(4):
                        for cc in range(8):
                            p.pe(lambda e, g=g, cc=cc: e.matmul(pu[:, g * 128:g * 128 + T], lhsT=wu[:, cc, g * 128:(g + 1) * 128],
                                                                rhs=nT[ni][:, cc, :T], start=(cc == 0), stop=(cc == 7)),
                                 reads=[r_wu_l[cc], r_nT[ni]], writes=[r_pb[ub]])
                    p.act(lambda e: e.activation(out=uT[:, :, PADW + o:PADW + o + T],
                                                 in_=pu[:, :].rearrange("p (g t) -> p g t", g=4)[:, :, :T],
                                                 func=AF.Copy), reads=[r_pb[ub]], writes=[r_uT])

                def p1_it(it):
                    if it < 9:
                        u_f1(ucs[it])
                    if 0 <= it - 1 < 9:
                        u_f2(ucs[it - 1])
                    if 0 <= it - 2 < 9:
                        u_f3(ucs[it - 2])
                def mid():
                    for g, w in enumerate((2, 4, 8, 16)):
                        src = uT[:, g, :]
                        r_src = r_uT
                        bufs = [(sA, r_sA), (sB, r_sB)]
                        sh = 1
                        bi = 0
                        while sh < w:
                            dst, r_dst = bufs[bi]
                            p.dve(lambda e, src=src, dst=dst, sh=sh: e.tensor_tensor(
                                out=dst[:, PADW:PADW + NQ], in0=src[:, PADW:PADW + NQ], in1=src[:, PADW - sh:PADW + NQ - sh],
                                op=ALU.add), reads=[r_src], writes=[r_dst])
                            src, r_src = dst, r_dst
                            sh *= 2
                            bi ^= 1
                        if w < 16:
                            p.dve(lambda e, src=src, g=g, w=w: e.scalar_tensor_tensor(
                                out=plb[:, g, :], in0=src[:, PADW:PADW + NQ], scalar=1.0 / w, in1=uT[:, g, PADW:PADW + NQ],
                                op0=ALU.mult, op1=ALU.subtract), reads=[r_src, r_uT], writes=[r_plb])
                        else:
                            p.dma(lambda e, X=X: e.dma_start(out=pcb[:, :], in_=pc16[X, :].partition_broadcast(128),
                                                             allow_slow_non_contiguous=True), writes=[r_pcb], q="sp")
                            p.dve(lambda e, src=src: e.tensor_tensor(out=src[:, PADW:PADW + NQ], in0=src[:, PADW:PADW + NQ],
                                                                     in1=pcb[:, :], op=ALU.mult),
                                  reads=[r_src, r_pcb], writes=[r_src])
                            p.dve(lambda e, src=src, g=g: e.tensor_tensor(out=plb[:, g, :], in0=src[:, PADW:PADW + NQ],
                                                                          in1=uT[:, g, PADW:PADW + NQ], op=ALU.subtract),
                                  reads=[r_src, r_uT], writes=[r_plb])
                    for g in range(4):
                        for (c0, n) in ((0, 352), (352, 352), (704, 352)):
                            pq = pbank[3]
                            p.pe(lambda e, g=g, c0=c0, n=n: e.matmul(pq[:, 0:n], lhsT=wpl[:, g, :], rhs=plb[:, g, c0:c0 + n],
                                                                     start=True, stop=True),
                                 reads=[r_wpl, r_plb], writes=[r_pb[3]])
                            p.act(lambda e, g=g, c0=c0, n=n, X=X: e.activation(out=mixYa[:, g, X, c0:c0 + n], in_=pq[:, 0:n],
                                                                              func=AF.Identity, scale=pst[:, g:g + 1],
                                                                              bias=bps[:, g:g + 1]),
                                  reads=[r_pb[3], r_const], writes=[r_mixYa[X]])
                wcs = [dict(t=t, T=QT_SIZES[t], o=QT_OFFS[t]) for t in range(9)]

                def w_g1(c, X=X):
                    t, T, o = c["t"], c["T"], c["o"]
                    i = cnt["hm"] % 3
                    cnt["hm"] += 1
                    c["hi"] = i
                    par = cnt["wb"] % 2
                    cnt["wb"] += 1
                    p.dma(lambda e: e.dma_start(out=hm[i][:T, :], in_=xq[X, o:o + T, :]), writes=[r_hm[i]], q="sp")
                    for half in range(2):
                        bk = 4 + 2 * par + half
                        ph = pbank[bk]
                        for fc in range(8):
                            p.pe(lambda e, fc=fc, half=half, ph=ph: e.matmul(
                                ph[:T, :], lhsT=(mixYa[:, fc, X, o:o + T] if fc < 4 else mixYb[:, fc - 4, X, o:o + T]),
                                rhs=wo[:, fc, half * 512:(half + 1) * 512], start=(fc == 0), stop=(fc == 7)),
                                reads=[r_wo_l[fc], r_mixT[X][t], r_mixYa[X]], writes=[r_pb[bk]])
                        p.dve(lambda e, half=half, ph=ph: e.tensor_tensor(
                            out=hm[i][:T, half * 512:(half + 1) * 512], in0=ph[:T, :], in1=hm[i][:T, half * 512:(half + 1) * 512],
                            op=ALU.add), reads=[r_pb[bk], r_hm[i]], writes=[r_hm[i]])
                    p.dma(lambda e: e.dma_start(out=hmid[X, o:o + T, :], in_=hm[i][:T, :]),
                          reads=[r_hm[i]], writes=[r_hmid[X][t]], q="pool")

                def w_g2(c, X=X):
                    i = c["hi"]
                    j = cnt["x2"] % 2
                    cnt["x2"] += 1
                    c["xj"] = j
                    rms_rows(hm[i], r_hm[i], c["T"], xs2[j], r_xs2[j], st3[j], r_st3[j], gffn_b, r_gffn)

                def w_g3(c, X=X):
                    t, T, o, j = c["t"], c["T"], c["o"], c["xj"]
                    pt = pbank_bf[3]
                    for cc in range(8):
                        p.pe(lambda e, cc=cc: e.transpose(out=pt[:, cc * 128:cc * 128 + T],
                                                          in_=xs2[j][:T, cc * 128:(cc + 1) * 128], identity=ident[:T, :T]),
                             reads=[r_xs2[j], r_const], writes=[r_pb[3]])
                    p.dve(lambda e: e.tensor_copy(out=n2T[:, :, X, o:o + T],
                                                  in_=pt[:, :].rearrange("p (c t) -> p c t", c=8)[:, :, :T]),
                          reads=[r_pb[3]], writes=[r_n2T[X][t]])

                def p3_it(it):
                    if it < 9:
                        w_g1(wcs[it])
                    if 0 <= it - 1 < 9:
                        w_g2(wcs[it - 1])
                    if 0 <= it - 2 < 9:
                        w_g3(wcs[it - 2])
                return p1_it, mid, p3_it

            chA = build_chunk(0)
            chB = build_chunk(1)
            for it in range(11):
                chA[0](it)
            chA[1]()
            for it in range(11):
                chA[2](it)
                chB[0](it)
            chB[1]()
            for it in range(11):
                chB[2](it)

        p.barrier()
        outs = []
        with contextlib.ExitStack() as c2:
            NFB = FB_PER_PASS
            wup = SB(c2, "wup", [128, 8, 2, NFB * 128], BF16)
            wdn = SB(c2, "wdn", [128, NFB, D], BF16)
            FGRP = [(0, 4), (4, 8), (8, NFB)]
            r_wup_l = [[R(f"wup{s_}_{k}") for k in range(len(FGRP))] for s_ in range(2)]
            r_wdn_l = [R(f"wdn{f}") for f in range(NFB)]
            Gt = [SB(c2, f"Gt{i}", [128, NFB, FG], BF16) for i in range(2)]
            r_Gt = [R(f"Gt{i}") for i in range(2)]
            cbuf = [[SB(c2, f"cbuf{i}_{s}", [128, FG], F32) for s in range(2)] for i in range(2)]
            r_cbuf = [[R(f"cbuf{i}_{s}") for s in range(2)] for i in range(2)]
            sg = [SB(c2, f"sg{i}", [128, FG], F32) for i in range(2)]
            r_sg = [R(f"sg{i}") for i in range(2)]
            yt = [SB(c2, f"yt{i}", [128, D], F32) for i in range(2)]
            r_yt = [R(f"yt{i}") for i in range(2)]
            r_yst = [R(f"yst{i}") for i in range(2)]
            r_y = [[R(f"y{X}_{t}") for t in range(8)] for X in range(2)]
            wu3 = w_up.rearrange("(c p) n -> p c n", p=128)
            wd3 = w_down.rearrange("(f p) n -> p f n", p=128)
            wk = 0
            gk = 0
            for ps_ in range(NPASS):
                fb0 = ps_ * NFB
                for k, (fa, fz) in enumerate(FGRP):
                    for s in range(2):
                        col0 = s * DFF + (fb0 + fa) * 128
                        p.dma(lambda e, s=s, col0=col0, fa=fa, fz=fz: e.dma_start(
                            out=wup[:, :, s, fa * 128:fz * 128], in_=wu3[:, :, col0:col0 + (fz - fa) * 128]),
                            writes=[r_wup_l[s][k]], q="pool")
                for f in range(NFB):
                    p.dma(lambda e, f=f, fb0=fb0: e.dma_start(out=wdn[:, f, :], in_=wd3[:, fb0 + f, :]), writes=[r_wdn_l[f]], q="pool")
                def up_fb(g, f):
                    X, t0, gb, rn, W = g["X"], g["t0"], g["gb"], g["rn"], g["W"]
                    fb = fb0 + f
                    ci = f % 2
                    for s_ in range(2):
                        pu = pbank[ci * 2 + s_]
                        for c in range(8):
                            p.pe(lambda e, c=c, s_=s_, pu=pu: e.matmul(
                                pu[:, 0:W + 2], lhsT=wup[:, c, s_, f * 128:(f + 1) * 128],
                                rhs=n2T[:, c, X, t0 - 2:t0 + W], start=(c == 0), stop=(c == 7)),
                                reads=[r_wup_l[s_][[k for k, (fa, fz) in enumerate(FGRP) if fa <= f < fz][0]]] + rn,
                                writes=[r_pb[ci * 2 + s_]])
                        col = s_ * 22 + fb
                        cbt = cbuf[ci][s_]
                        rcb = r_cbuf[ci][s_]
                        p.act(lambda e, pu=pu, cbt=cbt, col=col: e.activation(
                            out=cbt[:, 0:W], in_=pu[:, 2:W + 2], func=AF.Identity, scale=cw[:, 2, col:col + 1],
                            bias=cb[:, col:col + 1]), reads=[r_pb[ci * 2 + s_], r_const], writes=[rcb])
                        p.dve(lambda e, pu=pu, cbt=cbt, col=col: e.scalar_tensor_tensor(
                            out=cbt[:, 0:W], in0=pu[:, 1:W + 1], scalar=cw[:, 1, col:col + 1], in1=cbt[:, 0:W],
                            op0=ALU.mult, op1=ALU.add), reads=[r_pb[ci * 2 + s_], rcb, r_const], writes=[rcb])
                        p.dve(lambda e, pu=pu, cbt=cbt, col=col: e.scalar_tensor_tensor(
                            out=cbt[:, 0:W], in0=pu[:, 0:W], scalar=cw[:, 0, col:col + 1], in1=cbt[:, 0:W],
                            op0=ALU.mult, op1=ALU.add), reads=[r_pb[ci * 2 + s_], rcb, r_const], writes=[rcb])
                    p.act(lambda e: e.activation(out=sg[ci][:, 0:W], in_=cbuf[ci][0][:, 0:W], func=AF.Silu),
                          reads=[r_cbuf[ci][0]], writes=[r_sg[ci]])
                    p.dve(lambda e: e.tensor_tensor(out=Gt[gb][:, f, 0:W], in0=sg[ci][:, 0:W], in1=cbuf[ci][1][:, 0:W], op=ALU.mult),
                          reads=[r_sg[ci], r_cbuf[ci][1]], writes=[r_Gt[gb]])

                def down_unit(g, q):
                    X, gb = g["X"], g["gb"]
                    tt = g["tiles"][q]
                    yrow = (tt - 1) * 128
                    yi = q % 2
                    if ps_ == 0:
                        p.dma(lambda e: e.dma_start(out=yt[yi][:, :], in_=hmid[X, QT_OFFS[tt]:QT_OFFS[tt] + 128, :]),
                              reads=[r_hmid[X][tt]], writes=[r_yt[yi]], q="sp")
                    else:
                        p.dma(lambda e: e.dma_start(out=yt[yi][:, :], in_=y[X, yrow:yrow + 128, :]),
                              reads=[r_y[X][tt - 1]], writes=[r_yt[yi]], q="sp")
                    for half in range(2):
                        bk = 4 + 2 * (q % 2) + half
                        pd = pbank[bk]
                        for f in range(NFB):
                            p.pe(lambda e, f=f, half=half, pd=pd: e.matmul(
                                pd[:, :], lhsT=Gt[gb][:, f, q * 128:(q + 1) * 128],
                                rhs=wdn[:, f, half * 512:(half + 1) * 512], start=(f == 0), stop=(f == NFB - 1)),
                                reads=[r_Gt[gb], r_wdn_l[f]], writes=[r_pb[bk]])
                        p.dve(lambda e, half=half, pd=pd: e.tensor_tensor(
                            out=yt[yi][:, half * 512:(half + 1) * 512], in0=pd[:, :],
                            in1=yt[yi][:, half * 512:(half + 1) * 512], op=ALU.add),
                            reads=[r_pb[bk], r_yt[yi]], writes=[r_yt[yi]])
                    od = p.dma(lambda e: e.dma_start(out=y[X, yrow:yrow + 128, :], in_=yt[yi][:, :]),
                               reads=[r_yt[yi]], writes=[r_y[X][tt - 1]], q="pool", sem_res=r_yst[yi])
                    if ps_ == NPASS - 1:
                        outs.append(od)

                groups = []
                for X in range(2):
                    for tiles in ([1, 2, 3], [4, 5, 6], [7, 8]):
                        groups.append(dict(X=X, t0=QT_OFFS[tiles[0]], gb=gk % 2, tiles=tiles, W=128 * len(tiles),
                                           rn=[r_n2T[X][t] for t in tiles] + [r_n2T[X][tiles[0] - 1]]))
                        gk += 1
                slots = {2: 0, 5: 1, 8: 2}
                prev = None
                for g in groups:
                    for f in range(NFB):
                        up_fb(g, f)
                        if prev is not None and f in slots and slots[f] < len(prev["tiles"]):
                            down_unit(prev, slots[f])
                    prev = g
                for q in range(len(prev["tiles"])):
                    down_unit(prev, q)
            p.emit(final_waits=outs)
    return nc


_NC_CACHE = {}


def _host_layout(x, meta_tokens):
    per_core = []
    for core in range(8):
        b, j = core // 4, core % 4
        xb = x[b]
        chunks = (j, 7 - j)
        xq = np.zeros((2, NQ, D), np.float32)
        qval = np.ones((2, NQ), np.float32)
        mval = np.ones((2, 16), np.float32)
        for X, c in enumerate(chunks):
            s = 1024 * c
            xq[X, 32:] = xb[s:s + 1024]
            if c == 0:
                xq[X, 16:32] = meta_tokens
                qval[X, 0:16] = 0.0
                mval[X, :] = 0.0
            else:
                xq[X, 0:32] = xb[s - 32:s]
        xk = np.zeros((7, 1024, D), np.float32)
        kval = np.zeros((7, 1024), np.float32)
        sel = np.zeros((14,), np.float32)
        pc16 = np.full((2, NQ), 1.0 / 16.0, np.float32)
        if chunks[0] == 0:
            for pp in range(15):
                pc16[0, 16 + pp] = 1.0 / float(pp + 1)
        for i in range(7):
            if i < j:
                X, c, t = 0, chunks[0], i
            else:
                X, c, t = 1, chunks[1], i - j
            lim = 1024 * c - 32
            lo = 1024 * t
            hi = min(lo + 1024, lim)
            xk[i, :hi - lo] = xb[lo:hi]
            kval[i, :hi - lo] = 1.0
            sel[7 * X + i] = 1.0
        per_core.append(dict(xq=xq, xk=xk, xm=np.ascontiguousarray(meta_tokens, np.float32), kval=kval, qval=qval,
                             mval=mval, sel=sel, pc16=pc16))
    return per_core


def kernel(x, meta_tokens, norm_mix_g, w_in, w_pool, b_pool, pool_scale, q_norm_g, k_norm_g,
           lambda_q1, lambda_k1, lambda_q2, lambda_k2, subln_g, w_out, norm_ffn_g,
           w_up, conv_w, conv_b, w_down):
    f = lambda a: np.ascontiguousarray(np.asarray(a, np.float32))
    x = f(x)
    meta_tokens = f(meta_tokens)
    shared = {
        "w_in": f(w_in)[0], "w_pool": f(w_pool)[0], "b_pool": f(b_pool)[0], "pool_scale": f(pool_scale)[0],
        "q_norm_g": f(q_norm_g)[0], "k_norm_g": f(k_norm_g)[0], "lambda_q1": f(lambda_q1)[0],
        "lambda_k1": f(lambda_k1)[0], "lambda_q2": f(lambda_q2)[0], "lambda_k2": f(lambda_k2)[0],
        "subln_g": f(subln_g)[0], "w_out": f(w_out)[0], "norm_mix_g": f(norm_mix_g)[0],
        "norm_ffn_g": f(norm_ffn_g)[0], "w_up": f(w_up)[0], "conv_w": f(conv_w)[0], "conv_b": f(conv_b)[0],
        "w_down": f(w_down)[0],
    }
    if "nc" not in _NC_CACHE:
        _NC_CACHE["nc"] = build_program()
    nc = _NC_CACHE["nc"]
    per_core = _host_layout(x, meta_tokens)
    in_maps = [dict(shared, **pc) for pc in per_core]
    res = run_bass_kernel_spmd(nc, in_maps, core_ids=list(range(8)))
    out = np.empty((2, 8192, D), np.float32)
    for core in range(8):
        b, j = core // 4, core % 4
        yc = res.results[core]["y"]
        out[b, 1024 * j:1024 * (j + 1)] = yc[0]
        out[b, 1024 * (7 - j):1024 * (8 - j)] = yc[1]
    return out
```

```python
import contextlib
import numpy as np
import concourse.bass as bass
import concourse.mybir as mybir
from concourse.bass_utils import run_bass_kernel_spmd

F32 = mybir.dt.float32
BF16 = mybir.dt.bfloat16
AF = mybir.ActivationFunctionType
ALU = mybir.AluOpType
AX = mybir.AxisListType

ENGS = ("pe", "act", "dve", "pool", "sp")


class Res:
    __slots__ = ("name", "last_w", "readers", "dma_sem", "dma_cnt", "last_dma")

    def __init__(self, name):
        self.name = name
        self.last_dma = None
        self.last_w = None
        self.readers = []
        self.dma_sem = None
        self.dma_cnt = 0


class Ins:
    __slots__ = ("eng", "fn", "deps", "inc_val", "is_dma", "dma_res", "dma_val", "needed")

    def __init__(self, eng, fn):
        self.eng = eng
        self.fn = fn
        self.deps = []
        self.inc_val = None
        self.is_dma = False
        self.dma_res = None
        self.dma_val = 0
        self.needed = False


class Prog:
    def __init__(self, nc):
        self.nc = nc
        self.streams = {e: [] for e in ENGS}
        self.all_res = []
        self.barrier_deps = []
        self.barrier_seen = {e: True for e in ENGS}

    def res(self, name):
        r = Res(name)
        self.all_res.append(r)
        return r

    def barrier(self):
        deps = []
        for e in ENGS:
            if self.streams[e]:
                deps.append(self.streams[e][-1])
        for r in self.all_res:
            if r.last_dma is not None:
                deps.append(r.last_dma)
        self.barrier_deps = deps
        self.barrier_seen = {e: False for e in ENGS}

    def _add(self, eng, fn, reads, writes, is_dma=False, sem_res=None):
        ins = Ins(eng, fn)
        ins.is_dma = is_dma
        deps = []
        if not self.barrier_seen[eng]:
            self.barrier_seen[eng] = True
            deps.extend(self.barrier_deps)
        for r in reads:
            if r.last_w is not None:
                deps.append(r.last_w)
        for r in writes:
            if r.last_w is not None:
                deps.append(r.last_w)
            deps.extend(r.readers)
        seen = set()
        for d in deps:
            if d is ins or id(d) in seen:
                continue
            seen.add(id(d))
            if d.eng == eng and not d.is_dma and eng in ("pe", "sp"):
                continue
            ins.deps.append(d)
            d.needed = True
        for r in reads:
            r.readers.append(ins)
        for r in writes:
            r.last_w = ins
            r.readers = []
        if is_dma:
            tgt = sem_res if sem_res is not None else writes[0]
            ins.dma_res = tgt
            tgt.dma_cnt += 16
            ins.dma_val = tgt.dma_cnt
            tgt.last_dma = ins
        self.streams[eng].append(ins)
        return ins

    def pe(self, fn, reads=(), writes=()):
        return self._add("pe", fn, list(reads), list(writes))

    def act(self, fn, reads=(), writes=()):
        return self._add("act", fn, list(reads), list(writes))

    def dve(self, fn, reads=(), writes=()):
        return self._add("dve", fn, list(reads), list(writes))

    def pool(self, fn, reads=(), writes=()):
        return self._add("pool", fn, list(reads), list(writes))

    def dma(self, fn, reads=(), writes=(), q="sp", sem_res=None):
        return self._add(q, fn, list(reads), list(writes), is_dma=True, sem_res=sem_res)

    def emit(self, final_waits=()):
        nc = self.nc
        with contextlib.ExitStack() as st:
            sems = {}
            for e in ENGS:
                sems[e] = st.enter_context(nc.semaphore("s_" + e))
            for r in self.all_res:
                if r.dma_cnt > 0:
                    r.dma_sem = st.enter_context(nc.semaphore("d_" + r.name))
            for e in ENGS:
                c = 0
                for ins in self.streams[e]:
                    if ins.is_dma:
                        continue
                    if ins.needed:
                        c += 1
                        ins.inc_val = c
            block = st.enter_context(nc.Block())
            engmap = {"pe": "tensor", "act": "scalar", "dve": "vector", "pool": "gpsimd", "sp": "sync"}

            def make(e):
                def body(eng):
                    waited = {}
                    for ins in self.streams[e]:
                        for d in ins.deps:
                            if d.is_dma:
                                key = ("d", id(d.dma_res))
                                val = d.dma_val
                                sem = d.dma_res.dma_sem
                            else:
                                key = ("e", d.eng)
                                val = d.inc_val
                                sem = sems[d.eng]
                            if waited.get(key, 0) >= val:
                                continue
                            waited[key] = val
                            eng.wait_ge(sem, val)
                        r = ins.fn(eng)
                        if ins.is_dma:
                            r.then_inc(ins.dma_res.dma_sem, 16)
                        elif ins.needed:
                            r.then_inc(sems[e], 1)
                    if e == "sp":
                        for d in final_waits:
                            eng.wait_ge(d.dma_res.dma_sem, d.dma_val)
                return body

            for e in ENGS:
                getattr(block, engmap[e])(make(e))


D = 1024
NQ = 1056
QT_SIZES = [32] + [128] * 8
QT_OFFS = [0] + [32 + 128 * i for i in range(8)]
GROUPS = [(0, 1, 2), (3, 4, 5), (6, 7, 8)]
DFF = 2816
EPS = 1e-6
LAM_INIT = 0.2
NEG = -30000.0
FG = 384
NPASS = 2
FB_PER_PASS = 22 // NPASS


def build_program():
    nc = bass.Bass("TRN2", target_bir_lowering=False)
    dr = lambda name, shape, kind="ExternalInput": nc.dram_tensor(name, shape, F32, kind=kind).ap()
    xq = dr("xq", [2, NQ, D])
    xk = dr("xk", [7, 1024, D])
    xm = dr("xm", [16, D])
    kval = dr("kval", [7, 1024])
    qval = dr("qval", [2, NQ])
    mval = dr("mval", [2, 16])
    sel = dr("sel", [14])
    pc16 = dr("pc16", [2, NQ])
    w_in = dr("w_in", [D, 2048])
    w_pool = dr("w_pool", [4, 128, 128])
    b_pool = dr("b_pool", [4, 128])
    pool_scale = dr("pool_scale", [512])
    qg = dr("q_norm_g", [64])
    kg = dr("k_norm_g", [64])
    lq1 = dr("lambda_q1", [64])
    lk1 = dr("lambda_k1", [64])
    lq2 = dr("lambda_q2", [64])
    lk2 = dr("lambda_k2", [64])
    subg = dr("subln_g", [128])
    w_out = dr("w_out", [D, D])
    gmix_d = dr("norm_mix_g", [D])
    gffn_d = dr("norm_ffn_g", [D])
    w_up = dr("w_up", [D, 2 * DFF])
    conv_w = dr("conv_w", [3, 2 * DFF])
    conv_b = dr("conv_b", [2 * DFF])
    w_down = dr("w_down", [DFF, D])
    y = dr("y", [2, 1024, D], kind="ExternalOutput")
    hmid = dr("hmid", [2, NQ, D], kind="Internal")

    p = Prog(nc)
    R = p.res

    with contextlib.ExitStack() as outer:
        def SB(st, name, shape, dt):
            return st.enter_context(nc.sbuf_tensor(name, shape, dt))

        def PS(st, name, shape, dt):
            return st.enter_context(nc.psum_tensor(name, shape, dt))

        ident = SB(outer, "ident", [128, 128], BF16)
        identf = SB(outer, "identf", [128, 128], F32)
        maskneg = SB(outer, "maskneg", [128, 128], BF16)
        blk1 = SB(outer, "blk1", [128, 128], BF16)
        gmix_b = SB(outer, "gmix_b", [128, D], F32)
        qkg = SB(outer, "qkg", [128, 2], F32)
        selt = SB(outer, "selt", [128, 14], F32)
        lamt = SB(outer, "lamt", [128, 4, 64], F32)
        lamw = SB(outer, "lamw", [128, 8], F32)
        subgt = SB(outer, "subgt", [128, 1], F32)
        bpt = SB(outer, "bpt", [128, 4], F32)
        pst = SB(outer, "pst", [128, 4], F32)
        bps = SB(outer, "bps", [128, 4], F32)
        ones4 = SB(outer, "ones4", [128, 4], F32)
        zer = SB(outer, "zer", [128, 512], BF16)
        cw = SB(outer, "cw", [128, 3, 44], F32)
        cb = SB(outer, "cb", [128, 44], F32)
        mixYb = SB(outer, "mixYb", [128, 4, 2, NQ], BF16)
        r_const = R("const")
        r_mixT = [[R(f"mixT{X}_{t}") for t in range(9)] for X in range(2)]
        r_mixYa = [R(f"mixYa{X}") for X in range(2)]

        pb_h = [PS(outer, f"pb{i}", [128, 512], F32) for i in range(4)]
        sc_h = [PS(outer, f"sc{i}", [128, 1024], F32) for i in range(2)]
        pbank = list(pb_h) + [sc_h[i // 2][:, (i % 2) * 512:(i % 2 + 1) * 512] for i in range(4)]
        r_pb = [R(f"pb{i}") for i in range(8)]
        pbank_bf = [b.bitcast(BF16) for b in pb_h]

        p.pool(lambda e: e.memset(identf[:], 0.0), writes=[r_const])
        p.pool(lambda e: e.affine_select(out=identf[:], in_=identf[:], compare_op=ALU.not_equal, fill=1.0,
                                         base=0, pattern=[[-1, 128]], channel_multiplier=1),
               reads=[r_const], writes=[r_const])
        p.dve(lambda e: e.tensor_copy(out=ident[:], in_=identf[:]), reads=[r_const], writes=[r_const])
        p.pool(lambda e: e.memset(identf[:], 0.0), reads=[r_const], writes=[r_const])
        p.pool(lambda e: e.affine_select(out=identf[:], in_=identf[:], compare_op=ALU.is_ge, fill=NEG,
                                         base=0, pattern=[[1, 128]], channel_multiplier=-1),
               reads=[r_const], writes=[r_const])
        p.dve(lambda e: e.tensor_copy(out=maskneg[:], in_=identf[:]), reads=[r_const], writes=[r_const])
        p.dve(lambda e: e.memset(blk1[:], 0.0), reads=[r_const], writes=[r_const])
        p.dve(lambda e: e.memset(blk1[0:64, 0:64], 1.0), reads=[r_const], writes=[r_const])
        p.dve(lambda e: e.memset(blk1[64:128, 64:128], 1.0), reads=[r_const], writes=[r_const])
        p.dve(lambda e: e.memset(ones4[:], 1.0), reads=[r_const], writes=[r_const])
        p.dve(lambda e: e.memset(zer[:], 0.0), reads=[r_const], writes=[r_const])

        def small_dma(out_ap, in_ap):
            p.dma(lambda e: e.dma_start(out=out_ap, in_=in_ap, allow_slow_non_contiguous=True),
                  reads=[], writes=[r_const], q="sp")

        small_dma(gmix_b[:], gmix_d.partition_broadcast(128))
        small_dma(qkg[0:64, 0:1], qg.rearrange("(p o) -> p o", o=1))
        small_dma(qkg[64:128, 0:1], qg.rearrange("(p o) -> p o", o=1))
        small_dma(qkg[0:64, 1:2], kg.rearrange("(p o) -> p o", o=1))
        small_dma(qkg[64:128, 1:2], kg.rearrange("(p o) -> p o", o=1))
        small_dma(selt[:], sel.partition_broadcast(128))
        for i, l in enumerate((lq1, lk1, lq2, lk2)):
            small_dma(lamt[:, i, :], l.partition_broadcast(128))
        small_dma(subgt[:], subg.rearrange("(p o) -> p o", o=1))
        small_dma(bpt[:], b_pool.rearrange("g p -> p g"))
        small_dma(pst[:], pool_scale.rearrange("(g p) -> p g", p=128))
        small_dma(cw[:], conv_w.rearrange("k (f p) -> p k f", p=128))
        small_dma(cb[:], conv_b.rearrange("(f p) -> p f", p=128))
        p.dve(lambda e: e.tensor_scalar(out=qkg[:, 0:1], in0=qkg[:, 0:1], scalar1=0.125, scalar2=None, op0=ALU.mult),
              reads=[r_const], writes=[r_const])
        p.dve(lambda e: e.tensor_tensor(out=lamt[:, 0, :], in0=lamt[:, 0, :], in1=lamt[:, 1, :], op=ALU.mult),
              reads=[r_const], writes=[r_const])
        p.dve(lambda e: e.tensor_tensor(out=lamt[:, 2, :], in0=lamt[:, 2, :], in1=lamt[:, 3, :], op=ALU.mult),
              reads=[r_const], writes=[r_const])
        p.dve(lambda e: e.reduce_sum(out=lamw[:, 0:1], in_=lamt[:, 0, :], axis=AX.X), reads=[r_const], writes=[r_const])
        p.dve(lambda e: e.reduce_sum(out=lamw[:, 1:2], in_=lamt[:, 2, :], axis=AX.X), reads=[r_const], writes=[r_const])
        p.act(lambda e: e.activation(out=lamw[:, 2:4], in_=lamw[:, 0:2], func=AF.Exp), reads=[r_const], writes=[r_const])
        p.dve(lambda e: e.tensor_tensor(out=lamw[:, 4:5], in0=lamw[:, 3:4], in1=lamw[:, 2:3], op=ALU.subtract),
              reads=[r_const], writes=[r_const])
        p.dve(lambda e: e.tensor_scalar(out=lamw[:, 6:7], in0=lamw[:, 4:5], scalar1=-LAM_INIT, scalar2=None, op0=ALU.add),
              reads=[r_const], writes=[r_const])
        p.dve(lambda e: e.tensor_tensor(out=bps[:], in0=bpt[:], in1=pst[:], op=ALU.mult), reads=[r_const], writes=[r_const])
        p.dve(lambda e: e.tensor_scalar(out=subgt[:], in0=subgt[:], scalar1=1.0 - LAM_INIT, scalar2=None, op0=ALU.mult),
              reads=[r_const], writes=[r_const])

        NFE = 2
        xt = [SB(outer, f"xt{i}", [128, D], F32) for i in range(NFE)]
        r_xt = [R(f"xt{i}") for i in range(NFE)]
        xs = [SB(outer, f"xs{i}", [128, D], BF16) for i in range(NFE)]
        r_xs = [R(f"xs{i}") for i in range(NFE)]
        NNT = 4
        nT = [SB(outer, f"nT{i}", [128, 8, 128], BF16) for i in range(NNT)]
        r_nT = [R(f"nT{i}") for i in range(NNT)]
        st2 = [SB(outer, f"st2_{i}", [128, 4], F32) for i in range(NFE)]
        r_st2 = [R(f"st2_{i}") for i in range(NFE)]
        epst = SB(outer, "epst", [128, 1], F32)
        p.dve(lambda e: e.memset(epst[:], EPS), reads=[r_const], writes=[r_const])
        cnt = {"fe": 0, "nt": 0, "ub": 0}

        def rms_rows(src_t, r_src, T, dst_bf, r_dst, stt, r_stt, gb, r_gb=None):
            p.act(lambda e: e.activation(out=dst_bf[:T, :], in_=src_t[:T, :], func=AF.Square, accum_out=stt[:T, 0:1]),
                  reads=[r_src], writes=[r_dst, r_stt])
            p.act(lambda e: e.activation(out=stt[:T, 1:2], in_=stt[:T, 0:1], func=AF.Ln, scale=1.0 / D, bias=epst[:T, 0:1]),
                  reads=[r_stt, r_const], writes=[r_stt])
            p.act(lambda e: e.activation(out=stt[:T, 2:3], in_=stt[:T, 1:2], func=AF.Exp, scale=-0.5),
                  reads=[r_stt], writes=[r_stt])
            p.dve(lambda e: e.scalar_tensor_tensor(out=dst_bf[:T, :], in0=src_t[:T, :], scalar=stt[:T, 2:3], in1=gb[:T, :],
                                                   op0=ALU.mult, op1=ALU.mult),
                  reads=[r_src, r_stt, r_const] + ([r_gb] if r_gb else []), writes=[r_dst])

        def transpose_rows(src_bf, r_src, T, dstT, r_dstT, bank=0):
            pt = pbank_bf[bank]
            for c in range(8):
                p.pe(lambda e, c=c: e.transpose(out=pt[:, c * 128:c * 128 + T], in_=src_bf[:T, c * 128:(c + 1) * 128],
                                                identity=ident[:T, :T]),
                     reads=[r_src, r_const], writes=[r_pb[bank]])
            p.dve(lambda e: e.tensor_copy(out=dstT[:, :, :T], in_=pt[:, :].rearrange("p (c t) -> p c t", c=8)[:, :, :T]),
                  reads=[r_pb[bank]], writes=[r_dstT])

        def front_end(src_ap, T, tbank=0):
            i = cnt["fe"] % NFE
            cnt["fe"] += 1
            ni = cnt["nt"] % NNT
            cnt["nt"] += 1
            p.dma(lambda e: e.dma_start(out=xt[i][:T, :], in_=src_ap), writes=[r_xt[i]], q="sp")
            rms_rows(xt[i], r_xt[i], T, xs[i], r_xs[i], st2[i], r_st2[i], gmix_b)
            transpose_rows(xs[i], r_xs[i], T, nT[ni], r_nT[ni], bank=tbank)
            return ni

        with contextlib.ExitStack() as ab:
            wqkv = SB(ab, "wqkv", [128, 8, 1536], BF16)
            r_wqkv_l = [R(f"wqkv{c}") for c in range(8)]
            w3 = w_in.rearrange("(c p) n -> p c n", p=128)
            for c in range(8):
                p.dma(lambda e, c=c: e.dma_start(out=wqkv[:, c, :], in_=w3[:, c, 512:2048]), writes=[r_wqkv_l[c]], q="pool")

            NKV = 1
            KT = [SB(ab, f"KT{i}", [128, 4, NQ + 16], BF16) for i in range(NKV)]
            VX = [SB(ab, f"VX{i}", [128, 10, 4, 130], BF16) for i in range(NKV)]
            r_KT = [R(f"KT{i}") for i in range(NKV)]
            r_VX = [R(f"VX{i}") for i in range(NKV)]
            QT = [SB(ab, f"QT{X}", [128, 4, NQ], BF16) for X in range(2)]
            r_QT = [R(f"QT{X}") for X in range(2)]
            Qs = SB(ab, "Qs", [128, 4, NQ], BF16)
            r_Qs = R("Qs")
            OA = [SB(ab, f"O{X}", [128, 9, 4, 2, 129], F32) for X in range(2)]
            r_O = [[[R(f"O{X}_{t}_{h}") for h in range(4)] for t in range(9)] for X in range(2)]
            sqb = [SB(ab, f"sqb{i}", [128, 4, 128], BF16) for i in range(2)]
            r_sqb = [R(f"sqb{i}") for i in range(2)]
            lnb = [SB(ab, "lnb0", [128, 4, 128], F32)] * 2
            r_lnb = [R("lnb0")] * 2
            cnt["sq"] = 0
            cnt["kvt"] = 0
            valt = [SB(ab, f"valt{i}", [128, 1], F32) for i in range(4)]
            r_valt = [R(f"valt{i}") for i in range(4)]
            NPT = 3
            Pt = [SB(ab, f"Pt{i}", [128, 2, 384], BF16) for i in range(NPT)]
            r_Pt = [R(f"Pt{i}") for i in range(NPT)]
            cnt["val"] = 0
            cnt["pt"] = 0
            cnt["grp"] = 0
            cnt["sc"] = 0

            rsq = [[SB(ab, f"rsq{a}_{i}", [128, 4, 128], F32) for i in range(2)] for a in range(2)]
            r_rsq = [[R(f"rsq{a}_{i}") for i in range(2)] for a in range(2)]
            KB = [[1, 2], [3, 4]]
            SBK, VBK, TBK = 5, 6, 0

            def kv_f1(c):
                i = cnt["fe"] % NFE
                cnt["fe"] += 1
                c["xi"] = i
                T = c["T"]
                p.dma(lambda e: e.dma_start(out=xt[i][:T, :], in_=c["src"]), writes=[r_xt[i]], q="sp")
                rms_rows(xt[i], r_xt[i], T, xs[i], r_xs[i], st2[i], r_st2[i], gmix_b)
                vi = cnt["val"] % 4
                cnt["val"] += 1
                c["vi"] = vi
                p.dma(lambda e: e.dma_start(out=valt[vi][:T, :], in_=c["val"]), writes=[r_valt[vi]], q="sp")

            def kv_f2(c):
                ni = cnt["nt"] % NNT
                cnt["nt"] += 1
                c["ni"] = ni
                i = c["xi"]
                transpose_rows(xs[i], r_xs[i], c["T"], nT[ni], r_nT[ni], bank=TBK)

            def kv_b1(c):
                T, ni, par = c["T"], c["ni"], c["par"]
                for a, col0 in ((0, 512), (1, 0)):
                    if a == 1 and c["own"] is None:
                        continue
                    bank = KB[a][par]
                    pk = pbank[bank]
                    for h in range(4):
                        for cc in range(8):
                            p.pe(lambda e, h=h, cc=cc, pk=pk, col0=col0: e.matmul(
                                pk[:, h * 128:h * 128 + T], lhsT=wqkv[:, cc, col0 + h * 128:col0 + (h + 1) * 128],
                                rhs=nT[ni][:, cc, :T], start=(cc == 0), stop=(cc == 7)),
                                reads=[r_wqkv_l[cc], r_nT[ni]], writes=[r_pb[bank]])
                    pk3 = pk[:, :].rearrange("p (h t) -> p h t", h=4)
                    j = cnt["sq"] % 2
                    cnt["sq"] += 1
                    p.act(lambda e, pk3=pk3, j=j: e.activation(out=sqb[j][:, :, :T], in_=pk3[:, :, :T], func=AF.Square),
                          reads=[r_pb[bank]], writes=[r_sqb[j]])
                    pss = pbank[SBK]
                    for h in range(4):
                        p.pe(lambda e, h=h, j=j, pss=pss: e.matmul(pss[:, h * 128:h * 128 + T], lhsT=blk1[:, :],
                                                                   rhs=sqb[j][:, h, :T], start=True, stop=True),
                             reads=[r_sqb[j], r_const], writes=[r_pb[SBK]])
                    pss3 = pss[:, :].rearrange("p (h t) -> p h t", h=4)
                    p.act(lambda e, pss3=pss3: e.activation(out=lnb[0][:, :, :T], in_=pss3[:, :, :T], func=AF.Ln,
                                                            scale=1.0 / 64, bias=epst[:, 0:1]),
                          reads=[r_pb[SBK], r_const], writes=[r_lnb[0]])
                    p.act(lambda e, a=a: e.activation(out=rsq[a][par][:, :, :T], in_=lnb[0][:, :, :T], func=AF.Exp, scale=-0.5),
                          reads=[r_lnb[0]], writes=[r_rsq[a][par]])

            def kv_b2(c):
                T, ni, par, kb, vidx, vi = c["T"], c["ni"], c["par"], c["kb"], c["vidx"], c["vi"]
                for a in range(2):
                    if a == 1 and c["own"] is None:
                        continue
                    bank = KB[a][par]
                    pk3 = pbank[bank][:, :].rearrange("p (h t) -> p h t", h=4)
                    if a == 0:
                        dst, r_dst, doff, gcol = KT[kb], r_KT[kb], c["koff"], 1
                    else:
                        X, doff = c["own"]
                        dst, r_dst, gcol = QT[X], r_QT[X], 0
                    p.dve(lambda e, pk3=pk3, dst=dst, doff=doff, gcol=gcol, a=a: e.scalar_tensor_tensor(
                        out=dst[:, :, doff:doff + T], in0=pk3[:, :, :T], scalar=qkg[:, gcol:gcol + 1],
                        in1=rsq[a][par][:, :, :T], op0=ALU.mult, op1=ALU.mult),
                        reads=[r_pb[bank], r_rsq[a][par], r_const], writes=[r_dst])
                pv = pbank[VBK]
                for cc in range(8):
                    p.pe(lambda e, cc=cc: e.matmul(pv[:T, :], lhsT=nT[ni][:, cc, :T], rhs=wqkv[:, cc, 1024:1536],
                                                   start=(cc == 0), stop=(cc == 7)),
                         reads=[r_wqkv_l[cc], r_nT[ni]], writes=[r_pb[VBK]])
                p.act(lambda e: e.activation(out=VX[kb][:T, vidx, :, 0:128],
                                             in_=pv[:T, :].rearrange("p (h d) -> p h d", h=4), func=AF.Copy,
                                             scale=valt[vi][:T, 0:1]),
                      reads=[r_pb[VBK], r_valt[vi]], writes=[r_VX[kb]])
                p.dve(lambda e: e.tensor_scalar(out=VX[kb][:T, vidx, :, 128:129], in0=ones4[:T, :].rearrange("p (h o) -> p h o", o=1),
                                                scalar1=valt[vi][:T, 0:1], scalar2=None, op0=ALU.mult),
                      reads=[r_valt[vi], r_const], writes=[r_VX[kb]])

            def run_kv(tiles):
                cs = []
                for (src, T, kb, koff, vidx, val, own) in tiles:
                    cs.append(dict(src=src, T=T, kb=kb, koff=koff, vidx=vidx, val=val, own=own, par=cnt["kvt"] % 2))
                    cnt["kvt"] += 1
                n = len(cs)
                for it in range(n + 3):
                    if it < n:
                        kv_f1(cs[it])
                    if 0 <= it - 1 < n:
                        kv_f2(cs[it - 1])
                    if 0 <= it - 2 < n:
                        kv_b1(cs[it - 2])
                    if 0 <= it - 3 < n:
                        kv_b2(cs[it - 3])

            def attention(kb, Qsrc, r_Qsrc, ktiles, diag, finish):
                steps = []
                for h in range(4):
                    for G in GROUPS:
                        g0 = QT_OFFS[G[0]]
                        g1 = QT_OFFS[G[-1]] + QT_SIZES[G[-1]]
                        kl = [k for k in ktiles if (not diag) or k[3] is None or k[3] <= G[-1]]
                        last_for = {}
                        for ki, (koff, nk, vidx, lt) in enumerate(kl):
                            for t in G:
                                if diag and lt is not None and lt > t:
                                    continue
                                last_for[t] = ki
                        for ki, k in enumerate(kl):
                            steps.append(dict(h=h, G=G, g0=g0, g1=g1, ki=ki, k=k, first=(ki == 0), last=(ki == len(kl) - 1),
                                              last_for=last_for, gidx=cnt["grp"]))
                        cnt["grp"] += 1

                def emit_score(st):
                    koff, nk, vidx, lt = st["k"]
                    G, h = st["G"], st["h"]
                    qs = max(st["g0"], QT_OFFS[lt]) if (diag and lt is not None) else st["g0"]
                    n = st["g1"] - qs
                    si = cnt["sc"] % 2
                    cnt["sc"] += 1
                    st.update(qs=qs, n=n, si=si)
                    for c in range(2):
                        sb = 4 + 2 * si + c
                        has_mask = diag and lt is not None and lt >= G[0]
                        p.pe(lambda e, c=c, sb=sb: e.matmul(
                            pbank[sb][:nk, 0:n], lhsT=KT[kb][64 * c:64 * c + 64, h, koff:koff + nk],
                            rhs=Qsrc[64 * c:64 * c + 64, h, qs:qs + n], start=True, stop=True),
                            reads=[r_KT[kb], r_Qsrc], writes=[r_pb[sb]])
                        if has_mask:
                            p.pe(lambda e, c=c, sb=sb: e.matmul(
                                pbank[sb][:nk, 0:nk], lhsT=ident[:nk, :nk], rhs=maskneg[:nk, :nk],
                                start=False, stop=True, skip_group_check=True),
                                reads=[r_const], writes=[r_pb[sb]])

                def emit_rest(st):
                    koff, nk, vidx, lt = st["k"]
                    G, h, qs, n, si, ki = st["G"], st["h"], st["qs"], st["n"], st["si"], st["ki"]
                    if st["first"]:
                        for t in G:
                            ab_ = (3 * st["gidx"] + (t - G[0])) % 4
                            p.pe(lambda e, ab_=ab_, nt=QT_SIZES[t]: e.matmul(pbank[ab_][:nt, 0:258], lhsT=zer[0:1, 0:nt],
                                                                            rhs=zer[0:1, 0:258], start=True, stop=True),
                                 reads=[r_const], writes=[r_pb[ab_]])
                    pi = cnt["pt"] % NPT
                    cnt["pt"] += 1
                    scv = sc_h[si][:nk, :].rearrange("p (c m) -> p c m", c=2)
                    p.act(lambda e, scv=scv: e.activation(out=Pt[pi][:nk, :, 0:n], in_=scv[:, :, 0:n], func=AF.Exp),
                          reads=[r_pb[4 + 2 * si], r_pb[5 + 2 * si]], writes=[r_Pt[pi]])
                    for t in G:
                        if diag and lt is not None and lt > t:
                            continue
                        nt = QT_SIZES[t]
                        po = QT_OFFS[t] - qs
                        ab_ = (3 * st["gidx"] + (t - G[0])) % 4
                        acc = pbank[ab_][:, 0:258].rearrange("p (c d) -> p c d", c=2)
                        for c in range(2):
                            p.pe(lambda e, c=c, nt=nt, po=po, acc=acc, f=(st["last_for"][t] == ki): e.matmul(
                                acc[:nt, c, :], lhsT=Pt[pi][:nk, c, po:po + nt], rhs=VX[kb][:nk, vidx, h, 0:129],
                                start=False, stop=f, skip_group_check=True),
                                reads=[r_Pt[pi], r_VX[kb]], writes=[r_pb[ab_]])
                    if st["last"]:
                        for t in G:
                            ab_ = (3 * st["gidx"] + (t - G[0])) % 4
                            acc = pbank[ab_][:, 0:258].rearrange("p (c d) -> p c d", c=2)
                            finish(t, h, acc, r_pb[ab_], QT_SIZES[t])

                emit_score(steps[0])
                for si_, st in enumerate(steps):
                    if si_ + 1 < len(steps):
                        emit_score(steps[si_ + 1])
                    emit_rest(st)

            for X in range(2):
                kb = X % NKV
                tl = []
                for t in range(9):
                    T = QT_SIZES[t]
                    o = QT_OFFS[t]
                    tl.append((xq[X, o:o + T, :], T, kb, o, t, qval[X, o:o + T].rearrange("(p o) -> p o", o=1), (X, o)))
                tl.append((xm[:, :], 16, kb, NQ, 9, mval[X, :].rearrange("(p o) -> p o", o=1), None))
                run_kv(tl)
                ktiles = [(NQ, 16, 9, None)] + [(QT_OFFS[t], QT_SIZES[t], t, t) for t in range(9)]

                def fin_diag(t, h, acc, r_acc, nt, X=X):
                    p.dve(lambda e: e.tensor_copy(out=OA[X][:nt, t, h, :, :], in_=acc[:nt, :, :]),
                          reads=[r_acc], writes=[r_O[X][t][h]])
                attention(kb, QT[X], r_QT[X], ktiles, True, fin_diag)

            for i in range(7):
                kb = i % NKV
                run_kv([(xk[i, kt * 128:(kt + 1) * 128, :], 128, kb, kt * 128, kt,
                         kval[i, kt * 128:(kt + 1) * 128].rearrange("(p o) -> p o", o=1), None) for kt in range(8)])
                p.dve(lambda e, i=i: e.tensor_scalar(out=Qs[:, :, :], in0=QT[0][:, :, :], scalar1=selt[:, i:i + 1],
                                                     scalar2=None, op0=ALU.mult),
                      reads=[r_QT[0], r_const], writes=[r_Qs])
                p.dve(lambda e, i=i: e.scalar_tensor_tensor(out=Qs[:, :, :], in0=QT[1][:, :, :], scalar=selt[:, 7 + i:8 + i],
                                                            in1=Qs[:, :, :], op0=ALU.mult, op1=ALU.add),
                      reads=[r_QT[1], r_Qs, r_const], writes=[r_Qs])
                ktiles = [(kt * 128, 128, kt, None) for kt in range(8)]

                def fin_full(t, h, acc, r_acc, nt, i=i):
                    for X in range(2):
                        p.dve(lambda e, X=X: e.scalar_tensor_tensor(
                            out=OA[X][:nt, t, h, :, :], in0=acc[:nt, :, :], scalar=selt[:nt, 7 * X + i:7 * X + i + 1],
                            in1=OA[X][:nt, t, h, :, :], op0=ALU.mult, op1=ALU.add),
                            reads=[r_acc, r_O[X][t][h], r_const], writes=[r_O[X][t][h]])
                attention(kb, Qs, r_Qs, ktiles, False, fin_full)

            ob = [SB(ab, f"ob{i}", [128, 4, 128], F32) for i in range(2)]
            r_ob = [R(f"ob{i}") for i in range(2)]
            obf = [SB(ab, f"obf{i}", [128, 4, 128], BF16) for i in range(2)]
            r_obf = [R(f"obf{i}") for i in range(2)]
            rl = [SB(ab, f"rl{i}", [128, 4, 2], F32) for i in range(2)]
            r_rl = [R(f"rl{i}") for i in range(2)]
            s4 = [SB(ab, f"s4{i}", [128, 12], F32) for i in range(2)]
            r_s4 = [R(f"s4{i}") for i in range(2)]
            k = 0
            for X in range(2):
                for t in range(9):
                    nt = QT_SIZES[t]
                    o = QT_OFFS[t]
                    i = k % 2
                    k += 1
                    rO = [r_O[X][t][h] for h in range(4)]
                    p.dve(lambda e, X=X, t=t, nt=nt, i=i: e.tensor_scalar(out=rl[i][:nt, :, :], in0=OA[X][:nt, t, :, :, 128],
                                                                          scalar1=1e-30, scalar2=None, op0=ALU.max),
                          reads=rO, writes=[r_rl[i]])
                    p.dve(lambda e, nt=nt, i=i: e.reciprocal(out=rl[i][:nt, :, :], in_=rl[i][:nt, :, :]),
                          reads=[r_rl[i]], writes=[r_rl[i]])
                    p.dve(lambda e, nt=nt, i=i: e.tensor_scalar(out=rl[i][:nt, :, 1:2], in0=rl[i][:nt, :, 1:2],
                                                                scalar1=lamw[:nt, 6:7], scalar2=None, op0=ALU.mult),
                          reads=[r_rl[i], r_const], writes=[r_rl[i]])
                    for h in range(4):
                        p.dve(lambda e, X=X, t=t, nt=nt, i=i, h=h: e.tensor_scalar(
                            out=ob[i][:nt, h, :], in0=OA[X][:nt, t, h, 0, 0:128], scalar1=rl[i][:nt, h, 0:1],
                            scalar2=None, op0=ALU.mult), reads=rO + [r_rl[i]], writes=[r_ob[i]])
                        p.dve(lambda e, X=X, t=t, nt=nt, i=i, h=h: e.scalar_tensor_tensor(
                            out=ob[i][:nt, h, :], in0=OA[X][:nt, t, h, 1, 0:128], scalar=rl[i][:nt, h, 1:2],
                            in1=ob[i][:nt, h, :], op0=ALU.mult, op1=ALU.add), reads=rO + [r_rl[i], r_ob[i]], writes=[r_ob[i]])
                        p.act(lambda e, nt=nt, i=i, h=h: e.activation(out=obf[i][:nt, h, :], in_=ob[i][:nt, h, :], func=AF.Square,
                                                                      accum_out=s4[i][:nt, h:h + 1]),
                              reads=[r_ob[i]], writes=[r_obf[i], r_s4[i]])
                    p.act(lambda e, nt=nt, i=i: e.activation(out=s4[i][:nt, 8:12], in_=s4[i][:nt, 0:4], func=AF.Ln,
                                                             scale=1.0 / 128, bias=epst[:nt, 0:1]),
                          reads=[r_s4[i], r_const], writes=[r_s4[i]])
                    p.act(lambda e, nt=nt, i=i: e.activation(out=s4[i][:nt, 4:8], in_=s4[i][:nt, 8:12], func=AF.Exp, scale=-0.5),
                          reads=[r_s4[i]], writes=[r_s4[i]])
                    for h in range(4):
                        p.dve(lambda e, nt=nt, i=i, h=h: e.tensor_scalar(out=obf[i][:nt, h, :], in0=ob[i][:nt, h, :],
                                                                         scalar1=s4[i][:nt, 4 + h:5 + h], scalar2=None,
                                                                         op0=ALU.mult),
                              reads=[r_ob[i], r_s4[i]], writes=[r_obf[i]])
                    pt = pbank_bf[0]
                    for h in range(4):
                        p.pe(lambda e, nt=nt, i=i, h=h: e.transpose(out=pt[:, h * 128:h * 128 + nt], in_=obf[i][:nt, h, :],
                                                                    identity=ident[:nt, :nt]),
                             reads=[r_obf[i], r_const], writes=[r_pb[0]])
                    p.dve(lambda e, X=X, nt=nt, o=o: e.tensor_scalar(
                        out=mixYb[:, :, X, o:o + nt], in0=pt[:, 0:512].rearrange("p (h t) -> p h t", h=4)[:, :, :nt],
                        scalar1=subgt[:, 0:1], scalar2=None, op0=ALU.mult),
                        reads=[r_pb[0], r_const], writes=[r_mixT[X][t]])

        p.barrier()
        n2T = SB(outer, "n2T", [128, 8, 2, NQ], BF16)
        r_n2T = [[R(f"n2T{X}_{t}") for t in range(9)] for X in range(2)]
        r_hmid = [[R(f"hmid{X}_{t}") for t in range(9)] for X in range(2)]
        with contextlib.ExitStack() as c1:
            wu = SB(c1, "wu", [128, 8, 512], BF16)
            r_wu_l = [R(f"wu{c}") for c in range(8)]
            wo = SB(c1, "wo", [128, 8, D], BF16)
            r_wo_l = [R(f"wo{c}") for c in range(8)]
            mixYa = SB(c1, "mixYa", [128, 4, 2, NQ], BF16)
            gffn_b = SB(c1, "gffn_b", [128, D], F32)
            r_gffn = R("gffn_b")
            p.dma(lambda e: e.dma_start(out=gffn_b[:], in_=gffn_d.partition_broadcast(128), allow_slow_non_contiguous=True),
                  writes=[r_gffn], q="sp")
            wpl = SB(c1, "wpl", [128, 4, 128], BF16)
            r_wpl = R("wpl")
            w3 = w_in.rearrange("(c p) n -> p c n", p=128)
            wo3 = w_out.rearrange("(c p) n -> p c n", p=128)
            for c in range(8):
                p.dma(lambda e, c=c: e.dma_start(out=wu[:, c, :], in_=w3[:, c, 0:512]), writes=[r_wu_l[c]], q="pool")
            for c in range(8):
                p.dma(lambda e, c=c: e.dma_start(out=wo[:, c, :], in_=wo3[:, c, :]), writes=[r_wo_l[c]], q="pool")
            p.dma(lambda e: e.dma_start(out=wpl[:, :, :], in_=w_pool.rearrange("g c d -> c g d")), writes=[r_wpl], q="pool")

            PADW = 16
            uT = SB(c1, "uT", [128, 4, PADW + NQ], F32)
            r_uT = R("uT")
            sA = SB(c1, "sA", [128, PADW + NQ], F32)
            sB = SB(c1, "sB", [128, PADW + NQ], F32)
            r_sA = R("sA")
            r_sB = R("sB")
            plb = SB(c1, "plb", [128, 4, NQ], BF16)
            r_plb = R("plb")
            hm = [SB(c1, f"hm{i}", [128, D], F32) for i in range(3)]
            r_hm = [R(f"hm{i}") for i in range(3)]
            cnt["hm"] = 0
            cnt["wb"] = 0
            cnt["x2"] = 0
            xs2 = [SB(c1, f"xs2_{i}", [128, D], BF16) for i in range(2)]
            r_xs2 = [R(f"xs2_{i}") for i in range(2)]
            st3 = [SB(c1, f"st3_{i}", [128, 4], F32) for i in range(2)]
            r_st3 = [R(f"st3_{i}") for i in range(2)]
            pcb = SB(c1, "pcb", [128, NQ], F32)
            r_pcb = R("pcb")
            p.dve(lambda e: e.memset(uT[:], 0.0), writes=[r_uT])
            p.dve(lambda e: e.memset(sA[:], 0.0), writes=[r_sA])
            p.dve(lambda e: e.memset(sB[:], 0.0), writes=[r_sB])
            def build_chunk(X):
                ucs = [dict(T=QT_SIZES[t], o=QT_OFFS[t]) for t in range(9)]

                def u_f1(c, X=X):
                    i = cnt["fe"] % NFE
                    cnt["fe"] += 1
                    c["xi"] = i
                    T, o = c["T"], c["o"]
                    p.dma(lambda e: e.dma_start(out=xt[i][:T, :], in_=xq[X, o:o + T, :]), writes=[r_xt[i]], q="sp")
                    rms_rows(xt[i], r_xt[i], T, xs[i], r_xs[i], st2[i], r_st2[i], gmix_b)

                def u_f2(c, X=X):
                    ni = cnt["nt"] % NNT
                    cnt["nt"] += 1
                    c["ni"] = ni
                    transpose_rows(xs[c["xi"]], r_xs[c["xi"]], c["T"], nT[ni], r_nT[ni], bank=0)

                def u_f3(c, X=X):
                    T, o, ni = c["T"], c["o"], c["ni"]
                    ub = 1 + (cnt["ub"] % 2)
                    cnt["ub"] += 1
                    pu = pbank[ub]
                    for g in range(4):
                        for cc in range(8):
                            p.pe(lambda e, g=g, cc=cc: e.matmul(pu[:, g * 128:g * 128 + T], lhsT=wu[:, cc, g * 128:(g + 1) * 128],
                                                                rhs=nT[ni][:, cc, :T], start=(cc == 0), stop=(cc == 7)),
                                 reads=[r_wu_l[cc], r_nT[ni]], writes=[r_pb[ub]])
                    p.act(lambda e: e.activation(out=uT[:, :, PADW + o:PADW + o + T],
                                                 in_=pu[:, :].rearrange("p (g t) -> p g t", g=4)[:, :, :T],
                                                 func=AF.Copy), reads=[r_pb[ub]], writes=[r_uT])

                def p1_it(it):
                    if it < 9:
                        u_f1(ucs[it])
                    if 0 <= it - 1 < 9:
                        u_f2(ucs[it - 1])
                    if 0 <= it - 2 < 9:
                        u_f3(ucs[it - 2])
                def mid():
                    for g, w in enumerate((2, 4, 8, 16)):
                        src = uT[:, g, :]
                        r_src = r_uT
                        bufs = [(sA, r_sA), (sB, r_sB)]
                        sh = 1
                        bi = 0
                        while sh < w:
                            dst, r_dst = bufs[bi]
                            p.dve(lambda e, src=src, dst=dst, sh=sh: e.tensor_tensor(
                                out=dst[:, PADW:PADW + NQ], in0=src[:, PADW:PADW + NQ], in1=src[:, PADW - sh:PADW + NQ - sh],
                                op=ALU.add), reads=[r_src], writes=[r_dst])
                            src, r_src = dst, r_dst
                            sh *= 2
                            bi ^= 1
                        if w < 16:
                            p.dve(lambda e, src=src, g=g, w=w: e.scalar_tensor_tensor(
                                out=plb[:, g, :], in0=src[:, PADW:PADW + NQ], scalar=1.0 / w, in1=uT[:, g, PADW:PADW + NQ],
                                op0=ALU.mult, op1=ALU.subtract), reads=[r_src, r_uT], writes=[r_plb])
                        else:
                            p.dma(lambda e, X=X: e.dma_start(out=pcb[:, :], in_=pc16[X, :].partition_broadcast(128),
                                                             allow_slow_non_contiguous=True), writes=[r_pcb], q="sp")
                            p.dve(lambda e, src=src: e.tensor_tensor(out=src[:, PADW:PADW + NQ], in0=src[:, PADW:PADW + NQ],
                                                                     in1=pcb[:, :], op=ALU.mult),
                                  reads=[r_src, r_pcb], writes=[r_src])
                            p.dve(lambda e, src=src, g=g: e.tensor_tensor(out=plb[:, g, :], in0=src[:, PADW:PADW + NQ],
                                                                          in1=uT[:, g, PADW:PADW + NQ], op=ALU.subtract),
                                  reads=[r_src, r_uT], writes=[r_plb])
                    for g in range(4):
                        for (c0, n) in ((0, 352), (352, 352), (704, 352)):
                            pq = pbank[3]
                            p.pe(lambda e, g=g, c0=c0, n=n: e.matmul(pq[:, 0:n], lhsT=wpl[:, g, :], rhs=plb[:, g, c0:c0 + n],
                                                                     start=True, stop=True),
                                 reads=[r_wpl, r_plb], writes=[r_pb[3]])
                            p.act(lambda e, g=g, c0=c0, n=n, X=X: e.activation(out=mixYa[:, g, X, c0:c0 + n], in_=pq[:, 0:n],
                                                                              func=AF.Identity, scale=pst[:, g:g + 1],
                                                                              bias=bps[:, g:g + 1]),
                                  reads=[r_pb[3], r_const], writes=[r_mixYa[X]])
                wcs = [dict(t=t, T=QT_SIZES[t], o=QT_OFFS[t]) for t in range(9)]

                def w_g1(c, X=X):
                    t, T, o = c["t"], c["T"], c["o"]
                    i = cnt["hm"] % 3
                    cnt["hm"] += 1
                    c["hi"] = i
                    par = cnt["wb"] % 2
                    cnt["wb"] += 1
                    p.dma(lambda e: e.dma_start(out=hm[i][:T, :], in_=xq[X, o:o + T, :]), writes=[r_hm[i]], q="sp")
                    for half in range(2):
                        bk = 4 + 2 * par + half
                        ph = pbank[bk]
                        for fc in range(8):
                            p.pe(lambda e, fc=fc, half=half, ph=ph: e.matmul(
                                ph[:T, :], lhsT=(mixYa[:, fc, X, o:o + T] if fc < 4 else mixYb[:, fc - 4, X, o:o + T]),
                                rhs=wo[:, fc, half * 512:(half + 1) * 512], start=(fc == 0), stop=(fc == 7)),
                                reads=[r_wo_l[fc], r_mixT[X][t], r_mixYa[X]], writes=[r_pb[bk]])
                        p.dve(lambda e, half=half, ph=ph: e.tensor_tensor(
                            out=hm[i][:T, half * 512:(half + 1) * 512], in0=ph[:T, :], in1=hm[i][:T, half * 512:(half + 1) * 512],
                            op=ALU.add), reads=[r_pb[bk], r_hm[i]], writes=[r_hm[i]])
                    p.dma(lambda e: e.dma_start(out=hmid[X, o:o + T, :], in_=hm[i][:T, :]),
                          reads=[r_hm[i]], writes=[r_hmid[X][t]], q="pool")

                def w_g2(c, X=X):
                    i = c["hi"]
                    j = cnt["x2"] % 2
                    cnt["x2"] += 1
                    c["xj"] = j
                    rms_rows(hm[i], r_hm[i], c["T"], xs2[j], r_xs2[j], st3[j], r_st3[j], gffn_b, r_gffn)

                def w_g3(c, X=X):
                    t, T, o, j = c["t"], c["T"], c["o"], c["xj"]
                    pt = pbank_bf[3]
                    for cc in range(8):
                        p.pe(lambda e, cc=cc: e.transpose(out=pt[:, cc * 128:cc * 128 + T],
                                                          in_=xs2[j][:T, cc * 128:(cc + 1) * 128], identity=ident[:T, :T]),
                             reads=[r_xs2[j], r_const], writes=[r_pb[3]])
                    p.dve(lambda e: e.tensor_copy(out=n2T[:, :, X, o:o + T],
                                                  in_=pt[:, :].rearrange("p (c t) -> p c t", c=8)[:, :, :T]),
                          reads=[r_pb[3]], writes=[r_n2T[X][t]])

                def p3_it(it):
                    if it < 9:
                        w_g1(wcs[it])
                    if 0 <= it - 1 < 9:
                        w_g2(wcs[it - 1])
                    if 0 <= it - 2 < 9:
                        w_g3(wcs[it - 2])
                return p1_it, mid, p3_it

            chA = build_chunk(0)
            chB = build_chunk(1)
            for it in range(11):
                chA[0](it)
            chA[1]()
            for it in range(11):
                chA[2](it)
                chB[0](it)
            chB[1]()
            for it in range(11):
                chB[2](it)

        p.barrier()
        outs = []
        with contextlib.ExitStack() as c2:
            NFB = FB_PER_PASS
            wup = SB(c2, "wup", [128, 8, 2, NFB * 128], BF16)
            wdn = SB(c2, "wdn", [128, NFB, D], BF16)
            FGRP = [(0, 4), (4, 8), (8, NFB)]
            r_wup_l = [[R(f"wup{s_}_{k}") for k in range(len(FGRP))] for s_ in range(2)]
            r_wdn_l = [R(f"wdn{f}") for f in range(NFB)]
            Gt = [SB(c2, f"Gt{i}", [128, NFB, FG], BF16) for i in range(2)]
            r_Gt = [R(f"Gt{i}") for i in range(2)]
            cbuf = [[SB(c2, f"cbuf{i}_{s}", [128, FG], F32) for s in range(2)] for i in range(2)]
            r_cbuf = [[R(f"cbuf{i}_{s}") for s in range(2)] for i in range(2)]
            sg = [SB(c2, f"sg{i}", [128, FG], F32) for i in range(2)]
            r_sg = [R(f"sg{i}") for i in range(2)]
            yt = [SB(c2, f"yt{i}", [128, D], F32) for i in range(2)]
            r_yt = [R(f"yt{i}") for i in range(2)]
            r_yst = [R(f"yst{i}") for i in range(2)]
            r_y = [[R(f"y{X}_{t}") for t in range(8)] for X in range(2)]
            wu3 = w_up.rearrange("(c p) n -> p c n", p=128)
            wd3 = w_down.rearrange("(f p) n -> p f n", p=128)
            wk = 0
            gk = 0
            for ps_ in range(NPASS):
                fb0 = ps_ * NFB
                for k, (fa, fz) in enumerate(FGRP):
                    for s in range(2):
                        col0 = s * DFF + (fb0 + fa) * 128
                        p.dma(lambda e, s=s, col0=col0, fa=fa, fz=fz: e.dma_start(
                            out=wup[:, :, s, fa * 128:fz * 128], in_=wu3[:, :, col0:col0 + (fz - fa) * 128]),
                            writes=[r_wup_l[s][k]], q="pool")
                for f in range(NFB):
                    p.dma(lambda e, f=f, fb0=fb0: e.dma_start(out=wdn[:, f, :], in_=wd3[:, fb0 + f, :]), writes=[r_wdn_l[f]], q="pool")
                def up_fb(g, f):
                    X, t0, gb, rn, W = g["X"], g["t0"], g["gb"], g["rn"], g["W"]
                    fb = fb0 + f
                    ci = f % 2
                    for s_ in range(2):
                        pu = pbank[ci * 2 + s_]
                        for c in range(8):
                            p.pe(lambda e, c=c, s_=s_, pu=pu: e.matmul(
                                pu[:, 0:W + 2], lhsT=wup[:, c, s_, f * 128:(f + 1) * 128],
                                rhs=n2T[:, c, X, t0 - 2:t0 + W], start=(c == 0), stop=(c == 7)),
                                reads=[r_wup_l[s_][[k for k, (fa, fz) in enumerate(FGRP) if fa <= f < fz][0]]] + rn,
                                writes=[r_pb[ci * 2 + s_]])
                        col = s_ * 22 + fb
                        cbt = cbuf[ci][s_]
                        rcb = r_cbuf[ci][s_]
                        p.act(lambda e, pu=pu, cbt=cbt, col=col: e.activation(
                            out=cbt[:, 0:W], in_=pu[:, 2:W + 2], func=AF.Identity, scale=cw[:, 2, col:col + 1],
                            bias=cb[:, col:col + 1]), reads=[r_pb[ci * 2 + s_], r_const], writes=[rcb])
                        p.dve(lambda e, pu=pu, cbt=cbt, col=col: e.scalar_tensor_tensor(
                            out=cbt[:, 0:W], in0=pu[:, 1:W + 1], scalar=cw[:, 1, col:col + 1], in1=cbt[:, 0:W],
                            op0=ALU.mult, op1=ALU.add), reads=[r_pb[ci * 2 + s_], rcb, r_const], writes=[rcb])
                        p.dve(lambda e, pu=pu, cbt=cbt, col=col: e.scalar_tensor_tensor(
                            out=cbt[:, 0:W], in0=pu[:, 0:W], scalar=cw[:, 0, col:col + 1], in1=cbt[:, 0:W],
                            op0=ALU.mult, op1=ALU.add), reads=[r_pb[ci * 2 + s_], rcb, r_const], writes=[rcb])
                    p.act(lambda e: e.activation(out=sg[ci][:, 0:W], in_=cbuf[ci][0][:, 0:W], func=AF.Silu),
                          reads=[r_cbuf[ci][0]], writes=[r_sg[ci]])
                    p.dve(lambda e: e.tensor_tensor(out=Gt[gb][:, f, 0:W], in0=sg[ci][:, 0:W], in1=cbuf[ci][1][:, 0:W], op=ALU.mult),
                          reads=[r_sg[ci], r_cbuf[ci][1]], writes=[r_Gt[gb]])

                def down_unit(g, q):
                    X, gb = g["X"], g["gb"]
                    tt = g["tiles"][q]
                    yrow = (tt - 1) * 128
                    yi = q % 2
                    if ps_ == 0:
                        p.dma(lambda e: e.dma_start(out=yt[yi][:, :], in_=hmid[X, QT_OFFS[tt]:QT_OFFS[tt] + 128, :]),
                              reads=[r_hmid[X][tt]], writes=[r_yt[yi]], q="sp")
                    else:
                        p.dma(lambda e: e.dma_start(out=yt[yi][:, :], in_=y[X, yrow:yrow + 128, :]),
                              reads=[r_y[X][tt - 1]], writes=[r_yt[yi]], q="sp")
                    for half in range(2):
                        bk = 4 + 2 * (q % 2) + half
                        pd = pbank[bk]
                        for f in range(NFB):
                            p.pe(lambda e, f=f, half=half, pd=pd: e.matmul(
                                pd[:, :], lhsT=Gt[gb][:, f, q * 128:(q + 1) * 128],
                                rhs=wdn[:, f, half * 512:(half + 1) * 512], start=(f == 0), stop=(f == NFB - 1)),
                                reads=[r_Gt[gb], r_wdn_l[f]], writes=[r_pb[bk]])
                        p.dve(lambda e, half=half, pd=pd: e.tensor_tensor(
                            out=yt[yi][:, half * 512:(half + 1) * 512], in0=pd[:, :],
                            in1=yt[yi][:, half * 512:(half + 1) * 512], op=ALU.add),
                            reads=[r_pb[bk], r_yt[yi]], writes=[r_yt[yi]])
                    od = p.dma(lambda e: e.dma_start(out=y[X, yrow:yrow + 128, :], in_=yt[yi][:, :]),
                               reads=[r_yt[yi]], writes=[r_y[X][tt - 1]], q="pool", sem_res=r_yst[yi])
                    if ps_ == NPASS - 1:
                        outs.append(od)

                groups = []
                for X in range(2):
                    for tiles in ([1, 2, 3], [4, 5, 6], [7, 8]):
                        groups.append(dict(X=X, t0=QT_OFFS[tiles[0]], gb=gk % 2, tiles=tiles, W=128 * len(tiles),
                                           rn=[r_n2T[X][t] for t in tiles] + [r_n2T[X][tiles[0] - 1]]))
                        gk += 1
                slots = {2: 0, 5: 1, 8: 2}
                prev = None
                for g in groups:
                    for f in range(NFB):
                        up_fb(g, f)
                        if prev is not None and f in slots and slots[f] < len(prev["tiles"]):
                            down_unit(prev, slots[f])
                    prev = g
                for q in range(len(prev["tiles"])):
                    down_unit(prev, q)
            p.emit(final_waits=outs)
    return nc


_NC_CACHE = {}


def _host_layout(x, meta_tokens):
    per_core = []
    for core in range(8):
        b, j = core // 4, core % 4
        xb = x[b]
        chunks = (j, 7 - j)
        xq = np.zeros((2, NQ, D), np.float32)
        qval = np.ones((2, NQ), np.float32)
        mval = np.ones((2, 16), np.float32)
        for X, c in enumerate(chunks):
            s = 1024 * c
            xq[X, 32:] = xb[s:s + 1024]
            if c == 0:
                xq[X, 16:32] = meta_tokens
                qval[X, 0:16] = 0.0
                mval[X, :] = 0.0
            else:
                xq[X, 0:32] = xb[s - 32:s]
        xk = np.zeros((7, 1024, D), np.float32)
        kval = np.zeros((7, 1024), np.float32)
        sel = np.zeros((14,), np.float32)
        pc16 = np.full((2, NQ), 1.0 / 16.0, np.float32)
        if chunks[0] == 0:
            for pp in range(15):
                pc16[0, 16 + pp] = 1.0 / float(pp + 1)
        for i in range(7):
            if i < j:
                X, c, t = 0, chunks[0], i
            else:
                X, c, t = 1, chunks[1], i - j
            lim = 1024 * c - 32
            lo = 1024 * t
            hi = min(lo + 1024, lim)
            xk[i, :hi - lo] = xb[lo:hi]
            kval[i, :hi - lo] = 1.0
            sel[7 * X + i] = 1.0
        per_core.append(dict(xq=xq, xk=xk, xm=np.ascontiguousarray(meta_tokens, np.float32), kval=kval, qval=qval,
                             mval=mval, sel=sel, pc16=pc16))
    return per_core


def kernel(x, meta_tokens, norm_mix_g, w_in, w_pool, b_pool, pool_scale, q_norm_g, k_norm_g,
           lambda_q1, lambda_k1, lambda_q2, lambda_k2, subln_g, w_out, norm_ffn_g,
           w_up, conv_w, conv_b, w_down):
    f = lambda a: np.ascontiguousarray(np.asarray(a, np.float32))
    x = f(x)
    meta_tokens = f(meta_tokens)
    shared = {
        "w_in": f(w_in)[0], "w_pool": f(w_pool)[0], "b_pool": f(b_pool)[0], "pool_scale": f(pool_scale)[0],
        "q_norm_g": f(q_norm_g)[0], "k_norm_g": f(k_norm_g)[0], "lambda_q1": f(lambda_q1)[0],
        "lambda_k1": f(lambda_k1)[0], "lambda_q2": f(lambda_q2)[0], "lambda_k2": f(lambda_k2)[0],
        "subln_g": f(subln_g)[0], "w_out": f(w_out)[0], "norm_mix_g": f(norm_mix_g)[0],
        "norm_ffn_g": f(norm_ffn_g)[0], "w_up": f(w_up)[0], "conv_w": f(conv_w)[0], "conv_b": f(conv_b)[0],
        "w_down": f(w_down)[0],
    }
    if "nc" not in _NC_CACHE:
        _NC_CACHE["nc"] = build_program()
    nc = _NC_CACHE["nc"]
    per_core = _host_layout(x, meta_tokens)
    in_maps = [dict(shared, **pc) for pc in per_core]
    res = run_bass_kernel_spmd(nc, in_maps, core_ids=list(range(8)))
    out = np.empty((2, 8192, D), np.float32)
    for core in range(8):
        b, j = core // 4, core % 4
        yc = res.results[core]["y"]
        out[b, 1024 * j:1024 * (j + 1)] = yc[0]
        out[b, 1024 * (7 - j):1024 * (8 - j)] = yc[1]
    return out
```

```python
import contextlib
import numpy as np
import concourse.bass as bass
import concourse.mybir as mybir
from concourse.bass_utils import run_bass_kernel_spmd

F32 = mybir.dt.float32
BF16 = mybir.dt.bfloat16
AF = mybir.ActivationFunctionType
ALU = mybir.AluOpType
AX = mybir.AxisListType

ENGS = ("pe", "act", "dve", "pool", "sp")


class Res:
    __slots__ = ("name", "last_w", "readers", "dma_sem", "dma_cnt", "last_dma")

    def __init__(self, name):
        self.name = name
        self.last_dma = None
        self.last_w = None
        self.readers = []
        self.dma_sem = None
        self.dma_cnt = 0


class Ins:
    __slots__ = ("eng", "fn", "deps", "inc_val", "is_dma", "dma_res", "dma_val", "needed")

    def __init__(self, eng, fn):
        self.eng = eng
        self.fn = fn
        self.deps = []
        self.inc_val = None
        self.is_dma = False
        self.dma_res = None
        self.dma_val = 0
        self.needed = False


class Prog:
    def __init__(self, nc):
        self.nc = nc
        self.streams = {e: [] for e in ENGS}
        self.all_res = []
        self.barrier_deps = []
        self.barrier_seen = {e: True for e in ENGS}

    def res(self, name):
        r = Res(name)
        self.all_res.append(r)
        return r

    def barrier(self):
        deps = []
        for e in ENGS:
            if self.streams[e]:
                deps.append(self.streams[e][-1])
        for r in self.all_res:
            if r.last_dma is not None:
                deps.append(r.last_dma)
        self.barrier_deps = deps
        self.barrier_seen = {e: False for e in ENGS}

    def _add(self, eng, fn, reads, writes, is_dma=False, sem_res=None):
        ins = Ins(eng, fn)
        ins.is_dma = is_dma
        deps = []
        if not self.barrier_seen[eng]:
            self.barrier_seen[eng] = True
            deps.extend(self.barrier_deps)
        for r in reads:
            if r.last_w is not None:
                deps.append(r.last_w)
        for r in writes:
            if r.last_w is not None:
                deps.append(r.last_w)
            deps.extend(r.readers)
        seen = set()
        for d in deps:
            if d is ins or id(d) in seen:
                continue
            seen.add(id(d))
            if d.eng == eng and not d.is_dma and eng in ("pe", "sp"):
                continue
            ins.deps.append(d)
            d.needed = True
        for r in reads:
            r.readers.append(ins)
        for r in writes:
            r.last_w = ins
            r.readers = []
        if is_dma:
            tgt = sem_res if sem_res is not None else writes[0]
            ins.dma_res = tgt
            tgt.dma_cnt += 16
            ins.dma_val = tgt.dma_cnt
            tgt.last_dma = ins
        self.streams[eng].append(ins)
        return ins

    def pe(self, fn, reads=(), writes=()):
        return self._add("pe", fn, list(reads), list(writes))

    def act(self, fn, reads=(), writes=()):
        return self._add("act", fn, list(reads), list(writes))

    def dve(self, fn, reads=(), writes=()):
        return self._add("dve", fn, list(reads), list(writes))

    def pool(self, fn, reads=(), writes=()):
        return self._add("pool", fn, list(reads), list(writes))

    def dma(self, fn, reads=(), writes=(), q="sp", sem_res=None):
        return self._add(q, fn, list(reads), list(writes), is_dma=True, sem_res=sem_res)

    def emit(self, final_waits=()):
        nc = self.nc
        with contextlib.ExitStack() as st:
            sems = {}
            for e in ENGS:
                sems[e] = st.enter_context(nc.semaphore("s_" + e))
            for r in self.all_res:
                if r.dma_cnt > 0:
                    r.dma_sem = st.enter_context(nc.semaphore("d_" + r.name))
            for e in ENGS:
                c = 0
                for ins in self.streams[e]:
                    if ins.is_dma:
                        continue
                    if ins.needed:
                        c += 1
                        ins.inc_val = c
            block = st.enter_context(nc.Block())
            engmap = {"pe": "tensor", "act": "scalar", "dve": "vector", "pool": "gpsimd", "sp": "sync"}

            def make(e):
                def body(eng):
                    waited = {}
                    for ins in self.streams[e]:
                        for d in ins.deps:
                            if d.is_dma:
                                key = ("d", id(d.dma_res))
                                val = d.dma_val
                                sem = d.dma_res.dma_sem
                            else:
                                key = ("e", d.eng)
                                val = d.inc_val
                                sem = sems[d.eng]
                            if waited.get(key, 0) >= val:
                                continue
                            waited[key] = val
                            eng.wait_ge(sem, val)
                        r = ins.fn(eng)
                        if ins.is_dma:
                            r.then_inc(ins.dma_res.dma_sem, 16)
                        elif ins.needed:
                            r.then_inc(sems[e], 1)
                    if e == "sp":
                        for d in final_waits:
                            eng.wait_ge(d.dma_res.dma_sem, d.dma_val)
                return body

            for e in ENGS:
                getattr(block, engmap[e])(make(e))


D = 1024
NQ = 1056
QT_SIZES = [32] + [128] * 8
QT_OFFS = [0] + [32 + 128 * i for i in range(8)]
GROUPS = [(0, 1, 2), (3, 4, 5), (6, 7, 8)]
DFF = 2816
EPS = 1e-6
LAM_INIT = 0.2
NEG = -30000.0
FG = 384
NPASS = 2
FB_PER_PASS = 22 // NPASS


def build_program():
    nc = bass.Bass("TRN2", target_bir_lowering=False)
    dr = lambda name, shape, kind="ExternalInput": nc.dram_tensor(name, shape, F32, kind=kind).ap()
    xq = dr("xq", [2, NQ, D])
    xk = dr("xk", [7, 1024, D])
    xm = dr("xm", [16, D])
    kval = dr("kval", [7, 1024])
    qval = dr("qval", [2, NQ])
    mval = dr("mval", [2, 16])
    sel = dr("sel", [14])
    pc16 = dr("pc16", [2, NQ])
    w_in = dr("w_in", [D, 2048])
    w_pool = dr("w_pool", [4, 128, 128])
    b_pool = dr("b_pool", [4, 128])
    pool_scale = dr("pool_scale", [512])
    qg = dr("q_norm_g", [64])
    kg = dr("k_norm_g", [64])
    lq1 = dr("lambda_q1", [64])
    lk1 = dr("lambda_k1", [64])
    lq2 = dr("lambda_q2", [64])
    lk2 = dr("lambda_k2", [64])
    subg = dr("subln_g", [128])
    w_out = dr("w_out", [D, D])
    gmix_d = dr("norm_mix_g", [D])
    gffn_d = dr("norm_ffn_g", [D])
    w_up = dr("w_up", [D, 2 * DFF])
    conv_w = dr("conv_w", [3, 2 * DFF])
    conv_b = dr("conv_b", [2 * DFF])
    w_down = dr("w_down", [DFF, D])
    y = dr("y", [2, 1024, D], kind="ExternalOutput")
    hmid = dr("hmid", [2, NQ, D], kind="Internal")

    p = Prog(nc)
    R = p.res

    with contextlib.ExitStack() as outer:
        def SB(st, name, shape, dt):
            return st.enter_context(nc.sbuf_tensor(name, shape, dt))

        def PS(st, name, shape, dt):
            return st.enter_context(nc.psum_tensor(name, shape, dt))

        ident = SB(outer, "ident", [128, 128], BF16)
        identf = SB(outer, "identf", [128, 128], F32)
        maskneg = SB(outer, "maskneg", [128, 128], BF16)
        blk1 = SB(outer, "blk1", [128, 128], BF16)
        gmix_b = SB(outer, "gmix_b", [128, D], F32)
        qkg = SB(outer, "qkg", [128, 2], F32)
        selt = SB(outer, "selt", [128, 14], F32)
        lamt = SB(outer, "lamt", [128, 4, 64], F32)
        lamw = SB(outer, "lamw", [128, 8], F32)
        subgt = SB(outer, "subgt", [128, 1], F32)
        bpt = SB(outer, "bpt", [128, 4], F32)
        pst = SB(outer, "pst", [128, 4], F32)
        bps = SB(outer, "bps", [128, 4], F32)
        ones4 = SB(outer, "ones4", [128, 4], F32)
        zer = SB(outer, "zer", [128, 512], BF16)
        cw = SB(outer, "cw", [128, 3, 44], F32)
        cb = SB(outer, "cb", [128, 44], F32)
        mixYb = SB(outer, "mixYb", [128, 4, 2, NQ], BF16)
        r_const = R("const")
        r_mixT = [[R(f"mixT{X}_{t}") for t in range(9)] for X in range(2)]
        r_mixYa = [R(f"mixYa{X}") for X in range(2)]

        pb_h = [PS(outer, f"pb{i}", [128, 512], F32) for i in range(4)]
        sc_h = [PS(outer, f"sc{i}", [128, 1024], F32) for i in range(2)]
        pbank = list(pb_h) + [sc_h[i // 2][:, (i % 2) * 512:(i % 2 + 1) * 512] for i in range(4)]
        r_pb = [R(f"pb{i}") for i in range(8)]
        pbank_bf = [b.bitcast(BF16) for b in pb_h]

        p.pool(lambda e: e.memset(identf[:], 0.0), writes=[r_const])
        p.pool(lambda e: e.affine_select(out=identf[:], in_=identf[:], compare_op=ALU.not_equal, fill=1.0,
                                         base=0, pattern=[[-1, 128]], channel_multiplier=1),
               reads=[r_const], writes=[r_const])
        p.dve(lambda e: e.tensor_copy(out=ident[:], in_=identf[:]), reads=[r_const], writes=[r_const])
        p.pool(lambda e: e.memset(identf[:], 0.0), reads=[r_const], writes=[r_const])
        p.pool(lambda e: e.affine_select(out=identf[:], in_=identf[:], compare_op=ALU.is_ge, fill=NEG,
                                         base=0, pattern=[[1, 128]], channel_multiplier=-1),
               reads=[r_const], writes=[r_const])
        p.dve(lambda e: e.tensor_copy(out=maskneg[:], in_=identf[:]), reads=[r_const], writes=[r_const])
        p.dve(lambda e: e.memset(blk1[:], 0.0), reads=[r_const], writes=[r_const])
        p.dve(lambda e: e.memset(blk1[0:64, 0:64], 1.0), reads=[r_const], writes=[r_const])
        p.dve(lambda e: e.memset(blk1[64:128, 64:128], 1.0), reads=[r_const], writes=[r_const])
        p.dve(lambda e: e.memset(ones4[:], 1.0), reads=[r_const], writes=[r_const])
        p.dve(lambda e: e.memset(zer[:], 0.0), reads=[r_const], writes=[r_const])

        def small_dma(out_ap, in_ap):
            p.dma(lambda e: e.dma_start(out=out_ap, in_=in_ap, allow_slow_non_contiguous=True),
                  reads=[], writes=[r_const], q="sp")

        small_dma(gmix_b[:], gmix_d.partition_broadcast(128))
        small_dma(qkg[0:64, 0:1], qg.rearrange("(p o) -> p o", o=1))
        small_dma(qkg[64:128, 0:1], qg.rearrange("(p o) -> p o", o=1))
        small_dma(qkg[0:64, 1:2], kg.rearrange("(p o) -> p o", o=1))
        small_dma(qkg[64:128, 1:2], kg.rearrange("(p o) -> p o", o=1))
        small_dma(selt[:], sel.partition_broadcast(128))
        for i, l in enumerate((lq1, lk1, lq2, lk2)):
            small_dma(lamt[:, i, :], l.partition_broadcast(128))
        small_dma(subgt[:], subg.rearrange("(p o) -> p o", o=1))
        small_dma(bpt[:], b_pool.rearrange("g p -> p g"))
        small_dma(pst[:], pool_scale.rearrange("(g p) -> p g", p=128))
        small_dma(cw[:], conv_w.rearrange("k (f p) -> p k f", p=128))
        small_dma(cb[:], conv_b.rearrange("(f p) -> p f", p=128))
        p.dve(lambda e: e.tensor_scalar(out=qkg[:, 0:1], in0=qkg[:, 0:1], scalar1=0.125, scalar2=None, op0=ALU.mult),
              reads=[r_const], writes=[r_const])
        p.dve(lambda e: e.tensor_tensor(out=lamt[:, 0, :], in0=lamt[:, 0, :], in1=lamt[:, 1, :], op=ALU.mult),
              reads=[r_const], writes=[r_const])
        p.dve(lambda e: e.tensor_tensor(out=lamt[:, 2, :], in0=lamt[:, 2, :], in1=lamt[:, 3, :], op=ALU.mult),
              reads=[r_const], writes=[r_const])
        p.dve(lambda e: e.reduce_sum(out=lamw[:, 0:1], in_=lamt[:, 0, :], axis=AX.X), reads=[r_const], writes=[r_const])
        p.dve(lambda e: e.reduce_sum(out=lamw[:, 1:2], in_=lamt[:, 2, :], axis=AX.X), reads=[r_const], writes=[r_const])
        p.act(lambda e: e.activation(out=lamw[:, 2:4], in_=lamw[:, 0:2], func=AF.Exp), reads=[r_const], writes=[r_const])
        p.dve(lambda e: e.tensor_tensor(out=lamw[:, 4:5], in0=lamw[:, 3:4], in1=lamw[:, 2:3], op=ALU.subtract),
              reads=[r_const], writes=[r_const])
        p.dve(lambda e: e.tensor_scalar(out=lamw[:, 6:7], in0=lamw[:, 4:5], scalar1=-LAM_INIT, scalar2=None, op0=ALU.add),
              reads=[r_const], writes=[r_const])
        p.dve(lambda e: e.tensor_tensor(out=bps[:], in0=bpt[:], in1=pst[:], op=ALU.mult), reads=[r_const], writes=[r_const])
        p.dve(lambda e: e.tensor_scalar(out=subgt[:], in0=subgt[:], scalar1=1.0 - LAM_INIT, scalar2=None, op0=ALU.mult),
              reads=[r_const], writes=[r_const])

        NFE = 2
        xt = [SB(outer, f"xt{i}", [128, D], F32) for i in range(NFE)]
        r_xt = [R(f"xt{i}") for i in range(NFE)]
        xs = [SB(outer, f"xs{i}", [128, D], BF16) for i in range(NFE)]
        r_xs = [R(f"xs{i}") for i in range(NFE)]
        NNT = 4
        nT = [SB(outer, f"nT{i}", [128, 8, 128], BF16) for i in range(NNT)]
        r_nT = [R(f"nT{i}") for i in range(NNT)]
        st2 = [SB(outer, f"st2_{i}", [128, 4], F32) for i in range(NFE)]
        r_st2 = [R(f"st2_{i}") for i in range(NFE)]
        epst = SB(outer, "epst", [128, 1], F32)
        p.dve(lambda e: e.memset(epst[:], EPS), reads=[r_const], writes=[r_const])
        cnt = {"fe": 0, "nt": 0, "ub": 0}

        def rms_rows(src_t, r_src, T, dst_bf, r_dst, stt, r_stt, gb, r_gb=None):
            p.act(lambda e: e.activation(out=dst_bf[:T, :], in_=src_t[:T, :], func=AF.Square, accum_out=stt[:T, 0:1]),
                  reads=[r_src], writes=[r_dst, r_stt])
            p.act(lambda e: e.activation(out=stt[:T, 1:2], in_=stt[:T, 0:1], func=AF.Ln, scale=1.0 / D, bias=epst[:T, 0:1]),
                  reads=[r_stt, r_const], writes=[r_stt])
            p.act(lambda e: e.activation(out=stt[:T, 2:3], in_=stt[:T, 1:2], func=AF.Exp, scale=-0.5),
                  reads=[r_stt], writes=[r_stt])
            p.dve(lambda e: e.scalar_tensor_tensor(out=dst_bf[:T, :], in0=src_t[:T, :], scalar=stt[:T, 2:3], in1=gb[:T, :],
                                                   op0=ALU.mult, op1=ALU.mult),
                  reads=[r_src, r_stt, r_const] + ([r_gb] if r_gb else []), writes=[r_dst])

        def transpose_rows(src_bf, r_src, T, dstT, r_dstT, bank=0):
            pt = pbank_bf[bank]
            for c in range(8):
                p.pe(lambda e, c=c: e.transpose(out=pt[:, c * 128:c * 128 + T], in_=src_bf[:T, c * 128:(c + 1) * 128],
                                                identity=ident[:T, :T]),
                     reads=[r_src, r_const], writes=[r_pb[bank]])
            p.dve(lambda e: e.tensor_copy(out=dstT[:, :, :T], in_=pt[:, :].rearrange("p (c t) -> p c t", c=8)[:, :, :T]),
                  reads=[r_pb[bank]], writes=[r_dstT])

        def front_end(src_ap, T, tbank=0):
            i = cnt["fe"] % NFE
            cnt["fe"] += 1
            ni = cnt["nt"] % NNT
            cnt["nt"] += 1
            p.dma(lambda e: e.dma_start(out=xt[i][:T, :], in_=src_ap), writes=[r_xt[i]], q="sp")
            rms_rows(xt[i], r_xt[i], T, xs[i], r_xs[i], st2[i], r_st2[i], gmix_b)
            transpose_rows(xs[i], r_xs[i], T, nT[ni], r_nT[ni], bank=tbank)
            return ni

        with contextlib.ExitStack() as ab:
            wqkv = SB(ab, "wqkv", [128, 8, 1536], BF16)
            r_wqkv_l = [R(f"wqkv{c}") for c in range(8)]
            w3 = w_in.rearrange("(c p) n -> p c n", p=128)
            for c in range(8):
                p.dma(lambda e, c=c: e.dma_start(out=wqkv[:, c, :], in_=w3[:, c, 512:2048]), writes=[r_wqkv_l[c]], q="pool")

            NKV = 1
            KT = [SB(ab, f"KT{i}", [128, 4, NQ + 16], BF16) for i in range(NKV)]
            VX = [SB(ab, f"VX{i}", [128, 10, 4, 130], BF16) for i in range(NKV)]
            r_KT = [R(f"KT{i}") for i in range(NKV)]
            r_VX = [R(f"VX{i}") for i in range(NKV)]
            QT = [SB(ab, f"QT{X}", [128, 4, NQ], BF16) for X in range(2)]
            r_QT = [R(f"QT{X}") for X in range(2)]
            Qs = SB(ab, "Qs", [128, 4, NQ], BF16)
            r_Qs = R("Qs")
            OA = [SB(ab, f"O{X}", [128, 9, 4, 2, 129], F32) for X in range(2)]
            r_O = [[[R(f"O{X}_{t}_{h}") for h in range(4)] for t in range(9)] for X in range(2)]
            sqb = [SB(ab, f"sqb{i}", [128, 4, 128], BF16) for i in range(2)]
            r_sqb = [R(f"sqb{i}") for i in range(2)]
            lnb = [SB(ab, "lnb0", [128, 4, 128], F32)] * 2
            r_lnb = [R("lnb0")] * 2
            cnt["sq"] = 0
            cnt["kvt"] = 0
            valt = [SB(ab, f"valt{i}", [128, 1], F32) for i in range(4)]
            r_valt = [R(f"valt{i}") for i in range(4)]
            NPT = 3
            Pt = [SB(ab, f"Pt{i}", [128, 2, 384], BF16) for i in range(NPT)]
            r_Pt = [R(f"Pt{i}") for i in range(NPT)]
            cnt["val"] = 0
            cnt["pt"] = 0
            cnt["grp"] = 0
            cnt["sc"] = 0

            rsq = [[SB(ab, f"rsq{a}_{i}", [128, 4, 128], F32) for i in range(2)] for a in range(2)]
            r_rsq = [[R(f"rsq{a}_{i}") for i in range(2)] for a in range(2)]
            KB = [[1, 2], [4, 7]]
            TB2 = [0, 3]
            SBK, VBK, TBK = 5, 6, 0

            def kv_f1(c):
                i = cnt["fe"] % NFE
                cnt["fe"] += 1
                c["xi"] = i
                T = c["T"]
                p.dma(lambda e: e.dma_start(out=xt[i][:T, :], in_=c["src"]), writes=[r_xt[i]], q="sp")
                rms_rows(xt[i], r_xt[i], T, xs[i], r_xs[i], st2[i], r_st2[i], gmix_b)
                vi = cnt["val"] % 4
                cnt["val"] += 1
                c["vi"] = vi
                p.dma(lambda e: e.dma_start(out=valt[vi][:T, :], in_=c["val"]), writes=[r_valt[vi]], q="sp")

            def kv_f2(c):
                ni = cnt["nt"] % NNT
                cnt["nt"] += 1
                c["ni"] = ni
                i = c["xi"]
                transpose_rows(xs[i], r_xs[i], c["T"], nT[ni], r_nT[ni], bank=TB2[c["par"]])

            def kv_b1(c):
                T, ni, par = c["T"], c["ni"], c["par"]
                for a, col0 in ((0, 512), (1, 0)):
                    if a == 1 and c["own"] is None:
                        continue
                    bank = KB[a][par]
                    pk = pbank[bank]
                    for h in range(4):
                        for cc in range(8):
                            p.pe(lambda e, h=h, cc=cc, pk=pk, col0=col0: e.matmul(
                                pk[:, h * 128:h * 128 + T], lhsT=wqkv[:, cc, col0 + h * 128:col0 + (h + 1) * 128],
                                rhs=nT[ni][:, cc, :T], start=(cc == 0), stop=(cc == 7)),
                                reads=[r_wqkv_l[cc], r_nT[ni]], writes=[r_pb[bank]])
                    pk3 = pk[:, :].rearrange("p (h t) -> p h t", h=4)
                    j = cnt["sq"] % 2
                    cnt["sq"] += 1
                    p.act(lambda e, pk3=pk3, j=j: e.activation(out=sqb[j][:, :, :T], in_=pk3[:, :, :T], func=AF.Square),
                          reads=[r_pb[bank]], writes=[r_sqb[j]])
                    pss = pbank[SBK]
                    for h in range(4):
                        p.pe(lambda e, h=h, j=j, pss=pss: e.matmul(pss[:, h * 128:h * 128 + T], lhsT=blk1[:, :],
                                                                   rhs=sqb[j][:, h, :T], start=True, stop=True),
                             reads=[r_sqb[j], r_const], writes=[r_pb[SBK]])
                    pss3 = pss[:, :].rearrange("p (h t) -> p h t", h=4)
                    p.act(lambda e, pss3=pss3: e.activation(out=lnb[0][:, :, :T], in_=pss3[:, :, :T], func=AF.Ln,
                                                            scale=1.0 / 64, bias=epst[:, 0:1]),
                          reads=[r_pb[SBK], r_const], writes=[r_lnb[0]])
                    p.act(lambda e, a=a: e.activation(out=rsq[a][par][:, :, :T], in_=lnb[0][:, :, :T], func=AF.Exp, scale=-0.5),
                          reads=[r_lnb[0]], writes=[r_rsq[a][par]])

            def kv_b2(c):
                T, ni, par, kb, vidx, vi = c["T"], c["ni"], c["par"], c["kb"], c["vidx"], c["vi"]
                for a in range(2):
                    if a == 1 and c["own"] is None:
                        continue
                    bank = KB[a][par]
                    pk3 = pbank[bank][:, :].rearrange("p (h t) -> p h t", h=4)
                    if a == 0:
                        dst, r_dst, doff, gcol = KT[kb], r_KT[kb], c["koff"], 1
                    else:
                        X, doff = c["own"]
                        dst, r_dst, gcol = QT[X], r_QT[X], 0
                    p.dve(lambda e, pk3=pk3, dst=dst, doff=doff, gcol=gcol, a=a: e.scalar_tensor_tensor(
                        out=dst[:, :, doff:doff + T], in0=pk3[:, :, :T], scalar=qkg[:, gcol:gcol + 1],
                        in1=rsq[a][par][:, :, :T], op0=ALU.mult, op1=ALU.mult),
                        reads=[r_pb[bank], r_rsq[a][par], r_const], writes=[r_dst])
                pv = pbank[VBK]
                for cc in range(8):
                    p.pe(lambda e, cc=cc: e.matmul(pv[:T, :], lhsT=nT[ni][:, cc, :T], rhs=wqkv[:, cc, 1024:1536],
                                                   start=(cc == 0), stop=(cc == 7)),
                         reads=[r_wqkv_l[cc], r_nT[ni]], writes=[r_pb[VBK]])
                p.act(lambda e: e.activation(out=VX[kb][:T, vidx, :, 0:128],
                                             in_=pv[:T, :].rearrange("p (h d) -> p h d", h=4), func=AF.Copy,
                                             scale=valt[vi][:T, 0:1]),
                      reads=[r_pb[VBK], r_valt[vi]], writes=[r_VX[kb]])
                p.dve(lambda e: e.tensor_scalar(out=VX[kb][:T, vidx, :, 128:129], in0=ones4[:T, :].rearrange("p (h o) -> p h o", o=1),
                                                scalar1=valt[vi][:T, 0:1], scalar2=None, op0=ALU.mult),
                      reads=[r_valt[vi], r_const], writes=[r_VX[kb]])

            def run_kv(tiles):
                cs = []
                for (src, T, kb, koff, vidx, val, own) in tiles:
                    cs.append(dict(src=src, T=T, kb=kb, koff=koff, vidx=vidx, val=val, own=own, par=cnt["kvt"] % 2))
                    cnt["kvt"] += 1
                n = len(cs)
                for it in range(n + 3):
                    if it < n:
                        kv_f1(cs[it])
                    if 0 <= it - 1 < n:
                        kv_f2(cs[it - 1])
                    if 0 <= it - 2 < n:
                        kv_b1(cs[it - 2])
                    if 0 <= it - 3 < n:
                        kv_b2(cs[it - 3])

            def attention(kb, Qsrc, r_Qsrc, ktiles, diag, finish):
                steps = []
                for h in range(4):
                    for G in GROUPS:
                        g0 = QT_OFFS[G[0]]
                        g1 = QT_OFFS[G[-1]] + QT_SIZES[G[-1]]
                        kl = [k for k in ktiles if (not diag) or k[3] is None or k[3] <= G[-1]]
                        last_for = {}
                        for ki, (koff, nk, vidx, lt) in enumerate(kl):
                            for t in G:
                                if diag and lt is not None and lt > t:
                                    continue
                                last_for[t] = ki
                        for ki, k in enumerate(kl):
                            steps.append(dict(h=h, G=G, g0=g0, g1=g1, ki=ki, k=k, first=(ki == 0), last=(ki == len(kl) - 1),
                                              last_for=last_for, gidx=cnt["grp"]))
                        cnt["grp"] += 1

                def emit_score(st):
                    koff, nk, vidx, lt = st["k"]
                    G, h = st["G"], st["h"]
                    qs = max(st["g0"], QT_OFFS[lt]) if (diag and lt is not None) else st["g0"]
                    n = st["g1"] - qs
                    si = cnt["sc"] % 2
                    cnt["sc"] += 1
                    st.update(qs=qs, n=n, si=si)
                    for c in range(2):
                        sb = 4 + 2 * si + c
                        has_mask = diag and lt is not None and lt >= G[0]
                        p.pe(lambda e, c=c, sb=sb: e.matmul(
                            pbank[sb][:nk, 0:n], lhsT=KT[kb][64 * c:64 * c + 64, h, koff:koff + nk],
                            rhs=Qsrc[64 * c:64 * c + 64, h, qs:qs + n], start=True, stop=True),
                            reads=[r_KT[kb], r_Qsrc], writes=[r_pb[sb]])
                        if has_mask:
                            p.pe(lambda e, c=c, sb=sb: e.matmul(
                                pbank[sb][:nk, 0:nk], lhsT=ident[:nk, :nk], rhs=maskneg[:nk, :nk],
                                start=False, stop=True, skip_group_check=True),
                                reads=[r_const], writes=[r_pb[sb]])

                def emit_rest(st):
                    koff, nk, vidx, lt = st["k"]
                    G, h, qs, n, si, ki = st["G"], st["h"], st["qs"], st["n"], st["si"], st["ki"]
                    if st["first"]:
                        for t in G:
                            ab_ = (3 * st["gidx"] + (t - G[0])) % 4
                            p.pe(lambda e, ab_=ab_, nt=QT_SIZES[t]: e.matmul(pbank[ab_][:nt, 0:258], lhsT=zer[0:1, 0:nt],
                                                                            rhs=zer[0:1, 0:258], start=True, stop=True),
                                 reads=[r_const], writes=[r_pb[ab_]])
                    pi = cnt["pt"] % NPT
                    cnt["pt"] += 1
                    scv = sc_h[si][:nk, :].rearrange("p (c m) -> p c m", c=2)
                    p.act(lambda e, scv=scv: e.activation(out=Pt[pi][:nk, :, 0:n], in_=scv[:, :, 0:n], func=AF.Exp),
                          reads=[r_pb[4 + 2 * si], r_pb[5 + 2 * si]], writes=[r_Pt[pi]])
                    for t in G:
                        if diag and lt is not None and lt > t:
                            continue
                        nt = QT_SIZES[t]
                        po = QT_OFFS[t] - qs
                        ab_ = (3 * st["gidx"] + (t - G[0])) % 4
                        acc = pbank[ab_][:, 0:258].rearrange("p (c d) -> p c d", c=2)
                        for c in range(2):
                            p.pe(lambda e, c=c, nt=nt, po=po, acc=acc, f=(st["last_for"][t] == ki): e.matmul(
                                acc[:nt, c, :], lhsT=Pt[pi][:nk, c, po:po + nt], rhs=VX[kb][:nk, vidx, h, 0:129],
                                start=False, stop=f, skip_group_check=True),
                                reads=[r_Pt[pi], r_VX[kb]], writes=[r_pb[ab_]])
                    if st["last"]:
                        for t in G:
                            ab_ = (3 * st["gidx"] + (t - G[0])) % 4
                            acc = pbank[ab_][:, 0:258].rearrange("p (c d) -> p c d", c=2)
                            finish(t, h, acc, r_pb[ab_], QT_SIZES[t])

                emit_score(steps[0])
                for si_, st in enumerate(steps):
                    if si_ + 1 < len(steps):
                        emit_score(steps[si_ + 1])
                    emit_rest(st)

            for X in range(2):
                kb = X % NKV
                tl = []
                for t in range(9):
                    T = QT_SIZES[t]
                    o = QT_OFFS[t]
                    tl.append((xq[X, o:o + T, :], T, kb, o, t, qval[X, o:o + T].rearrange("(p o) -> p o", o=1), (X, o)))
                tl.append((xm[:, :], 16, kb, NQ, 9, mval[X, :].rearrange("(p o) -> p o", o=1), None))
                run_kv(tl)
                ktiles = [(NQ, 16, 9, None)] + [(QT_OFFS[t], QT_SIZES[t], t, t) for t in range(9)]

                def fin_diag(t, h, acc, r_acc, nt, X=X):
                    p.dve(lambda e: e.tensor_copy(out=OA[X][:nt, t, h, :, :], in_=acc[:nt, :, :]),
                          reads=[r_acc], writes=[r_O[X][t][h]])
                attention(kb, QT[X], r_QT[X], ktiles, True, fin_diag)

            for i in range(7):
                kb = i % NKV
                run_kv([(xk[i, kt * 128:(kt + 1) * 128, :], 128, kb, kt * 128, kt,
                         kval[i, kt * 128:(kt + 1) * 128].rearrange("(p o) -> p o", o=1), None) for kt in range(8)])
                p.dve(lambda e, i=i: e.tensor_scalar(out=Qs[:, :, :], in0=QT[0][:, :, :], scalar1=selt[:, i:i + 1],
                                                     scalar2=None, op0=ALU.mult),
                      reads=[r_QT[0], r_const], writes=[r_Qs])
                p.dve(lambda e, i=i: e.scalar_tensor_tensor(out=Qs[:, :, :], in0=QT[1][:, :, :], scalar=selt[:, 7 + i:8 + i],
                                                            in1=Qs[:, :, :], op0=ALU.mult, op1=ALU.add),
                      reads=[r_QT[1], r_Qs, r_const], writes=[r_Qs])
                ktiles = [(kt * 128, 128, kt, None) for kt in range(8)]

                def fin_full(t, h, acc, r_acc, nt, i=i):
                    for X in range(2):
                        p.dve(lambda e, X=X: e.scalar_tensor_tensor(
                            out=OA[X][:nt, t, h, :, :], in0=acc[:nt, :, :], scalar=selt[:nt, 7 * X + i:7 * X + i + 1],
                            in1=OA[X][:nt, t, h, :, :], op0=ALU.mult, op1=ALU.add),
                            reads=[r_acc, r_O[X][t][h], r_const], writes=[r_O[X][t][h]])
                attention(kb, Qs, r_Qs, ktiles, False, fin_full)

            ob = [SB(ab, f"ob{i}", [128, 4, 128], F32) for i in range(2)]
            r_ob = [R(f"ob{i}") for i in range(2)]
            obf = [SB(ab, f"obf{i}", [128, 4, 128], BF16) for i in range(2)]
            r_obf = [R(f"obf{i}") for i in range(2)]
            rl = [SB(ab, f"rl{i}", [128, 4, 2], F32) for i in range(2)]
            r_rl = [R(f"rl{i}") for i in range(2)]
            s4 = [SB(ab, f"s4{i}", [128, 12], F32) for i in range(2)]
            r_s4 = [R(f"s4{i}") for i in range(2)]
            k = 0
            for X in range(2):
                for t in range(9):
                    nt = QT_SIZES[t]
                    o = QT_OFFS[t]
                    i = k % 2
                    k += 1
                    rO = [r_O[X][t][h] for h in range(4)]
                    p.dve(lambda e, X=X, t=t, nt=nt, i=i: e.tensor_scalar(out=rl[i][:nt, :, :], in0=OA[X][:nt, t, :, :, 128],
                                                                          scalar1=1e-30, scalar2=None, op0=ALU.max),
                          reads=rO, writes=[r_rl[i]])
                    p.dve(lambda e, nt=nt, i=i: e.reciprocal(out=rl[i][:nt, :, :], in_=rl[i][:nt, :, :]),
                          reads=[r_rl[i]], writes=[r_rl[i]])
                    p.dve(lambda e, nt=nt, i=i: e.tensor_scalar(out=rl[i][:nt, :, 1:2], in0=rl[i][:nt, :, 1:2],
                                                                scalar1=lamw[:nt, 6:7], scalar2=None, op0=ALU.mult),
                          reads=[r_rl[i], r_const], writes=[r_rl[i]])
                    for h in range(4):
                        p.dve(lambda e, X=X, t=t, nt=nt, i=i, h=h: e.tensor_scalar(
                            out=ob[i][:nt, h, :], in0=OA[X][:nt, t, h, 0, 0:128], scalar1=rl[i][:nt, h, 0:1],
                            scalar2=None, op0=ALU.mult), reads=rO + [r_rl[i]], writes=[r_ob[i]])
                        p.dve(lambda e, X=X, t=t, nt=nt, i=i, h=h: e.scalar_tensor_tensor(
                            out=ob[i][:nt, h, :], in0=OA[X][:nt, t, h, 1, 0:128], scalar=rl[i][:nt, h, 1:2],
                            in1=ob[i][:nt, h, :], op0=ALU.mult, op1=ALU.add), reads=rO + [r_rl[i], r_ob[i]], writes=[r_ob[i]])
                        p.act(lambda e, nt=nt, i=i, h=h: e.activation(out=obf[i][:nt, h, :], in_=ob[i][:nt, h, :], func=AF.Square,
                                                                      accum_out=s4[i][:nt, h:h + 1]),
                              reads=[r_ob[i]], writes=[r_obf[i], r_s4[i]])
                    p.act(lambda e, nt=nt, i=i: e.activation(out=s4[i][:nt, 8:12], in_=s4[i][:nt, 0:4], func=AF.Ln,
                                                             scale=1.0 / 128, bias=epst[:nt, 0:1]),
                          reads=[r_s4[i], r_const], writes=[r_s4[i]])
                    p.act(lambda e, nt=nt, i=i: e.activation(out=s4[i][:nt, 4:8], in_=s4[i][:nt, 8:12], func=AF.Exp, scale=-0.5),
                          reads=[r_s4[i]], writes=[r_s4[i]])
                    for h in range(4):
                        p.dve(lambda e, nt=nt, i=i, h=h: e.tensor_scalar(out=obf[i][:nt, h, :], in0=ob[i][:nt, h, :],
                                                                         scalar1=s4[i][:nt, 4 + h:5 + h], scalar2=None,
                                                                         op0=ALU.mult),
                              reads=[r_ob[i], r_s4[i]], writes=[r_obf[i]])
                    pt = pbank_bf[0]
                    for h in range(4):
                        p.pe(lambda e, nt=nt, i=i, h=h: e.transpose(out=pt[:, h * 128:h * 128 + nt], in_=obf[i][:nt, h, :],
                                                                    identity=ident[:nt, :nt]),
                             reads=[r_obf[i], r_const], writes=[r_pb[0]])
                    p.dve(lambda e, X=X, nt=nt, o=o: e.tensor_scalar(
                        out=mixYb[:, :, X, o:o + nt], in0=pt[:, 0:512].rearrange("p (h t) -> p h t", h=4)[:, :, :nt],
                        scalar1=subgt[:, 0:1], scalar2=None, op0=ALU.mult),
                        reads=[r_pb[0], r_const], writes=[r_mixT[X][t]])

        p.barrier()
        n2T = SB(outer, "n2T", [128, 8, 2, NQ], BF16)
        r_n2T = [[R(f"n2T{X}_{t}") for t in range(9)] for X in range(2)]
        r_hmid = [[R(f"hmid{X}_{t}") for t in range(9)] for X in range(2)]
        with contextlib.ExitStack() as c1:
            wu = SB(c1, "wu", [128, 8, 512], BF16)
            r_wu_l = [R(f"wu{c}") for c in range(8)]
            wo = SB(c1, "wo", [128, 8, D], BF16)
            r_wo_l = [R(f"wo{c}") for c in range(8)]
            mixYa = SB(c1, "mixYa", [128, 4, 2, NQ], BF16)
            gffn_b = SB(c1, "gffn_b", [128, D], F32)
            r_gffn = R("gffn_b")
            p.dma(lambda e: e.dma_start(out=gffn_b[:], in_=gffn_d.partition_broadcast(128), allow_slow_non_contiguous=True),
                  writes=[r_gffn], q="sp")
            wpl = SB(c1, "wpl", [128, 4, 128], BF16)
            r_wpl = R("wpl")
            w3 = w_in.rearrange("(c p) n -> p c n", p=128)
            wo3 = w_out.rearrange("(c p) n -> p c n", p=128)
            for c in range(8):
                p.dma(lambda e, c=c: e.dma_start(out=wu[:, c, :], in_=w3[:, c, 0:512]), writes=[r_wu_l[c]], q="pool")
            for c in range(8):
                p.dma(lambda e, c=c: e.dma_start(out=wo[:, c, :], in_=wo3[:, c, :]), writes=[r_wo_l[c]], q="pool")
            p.dma(lambda e: e.dma_start(out=wpl[:, :, :], in_=w_pool.rearrange("g c d -> c g d")), writes=[r_wpl], q="pool")

            PADW = 16
            uT = SB(c1, "uT", [128, 4, PADW + NQ], F32)
            r_uT = R("uT")
            sA = SB(c1, "sA", [128, PADW + NQ], F32)
            sB = SB(c1, "sB", [128, PADW + NQ], F32)
            r_sA = R("sA")
            r_sB = R("sB")
            plb = SB(c1, "plb", [128, 4, NQ], BF16)
            r_plb = R("plb")
            hm = [SB(c1, f"hm{i}", [128, D], F32) for i in range(3)]
            r_hm = [R(f"hm{i}") for i in range(3)]
            cnt["hm"] = 0
            cnt["wb"] = 0
            cnt["x2"] = 0
            xs2 = [SB(c1, f"xs2_{i}", [128, D], BF16) for i in range(2)]
            r_xs2 = [R(f"xs2_{i}") for i in range(2)]
            st3 = [SB(c1, f"st3_{i}", [128, 4], F32) for i in range(2)]
            r_st3 = [R(f"st3_{i}") for i in range(2)]
            pcb = SB(c1, "pcb", [128, NQ], F32)
            r_pcb = R("pcb")
            p.dve(lambda e: e.memset(uT[:], 0.0), writes=[r_uT])
            p.dve(lambda e: e.memset(sA[:], 0.0), writes=[r_sA])
            p.dve(lambda e: e.memset(sB[:], 0.0), writes=[r_sB])
            def build_chunk(X):
                ucs = [dict(T=QT_SIZES[t], o=QT_OFFS[t]) for t in range(9)]

                def u_f1(c, X=X):
                    i = cnt["fe"] % NFE
                    cnt["fe"] += 1
                    c["xi"] = i
                    T, o = c["T"], c["o"]
                    p.dma(lambda e: e.dma_start(out=xt[i][:T, :], in_=xq[X, o:o + T, :]), writes=[r_xt[i]], q="sp")
                    rms_rows(xt[i], r_xt[i], T, xs[i], r_xs[i], st2[i], r_st2[i], gmix_b)

                def u_f2(c, X=X):
                    ni = cnt["nt"] % NNT
                    cnt["nt"] += 1
                    c["ni"] = ni
                    transpose_rows(xs[c["xi"]], r_xs[c["xi"]], c["T"], nT[ni], r_nT[ni], bank=0)

                def u_f3(c, X=X):
                    T, o, ni = c["T"], c["o"], c["ni"]
                    ub = 1 + (cnt["ub"] % 2)
                    cnt["ub"] += 1
                    pu = pbank[ub]
                    for g in range(4):
                        for cc in range(8):
                            p.pe(lambda e, g=g, cc=cc: e.matmul(pu[:, g * 128:g * 128 + T], lhsT=wu[:, cc, g * 128:(g + 1) * 128],
                                                                rhs=nT[ni][:, cc, :T], start=(cc == 0), stop=(cc == 7)),
                                 reads=[r_wu_l[cc], r_nT[ni]], writes=[r_pb[ub]])
                    p.act(lambda e: e.activation(out=uT[:, :, PADW + o:PADW + o + T],
                                                 in_=pu[:, :].rearrange("p (g t) -> p g t", g=4)[:, :, :T],
                                                 func=AF.Copy), reads=[r_pb[ub]], writes=[r_uT])

                def p1_it(it):
                    if it < 9:
                        u_f1(ucs[it])
                    if 0 <= it - 1 < 9:
                        u_f2(ucs[it - 1])
                    if 0 <= it - 2 < 9:
                        u_f3(ucs[it - 2])
                def mid():
                    for g, w in enumerate((2, 4, 8, 16)):
                        src = uT[:, g, :]
                        r_src = r_uT
                        bufs = [(sA, r_sA), (sB, r_sB)]
                        sh = 1
                        bi = 0
                        while sh < w:
                            dst, r_dst = bufs[bi]
                            p.dve(lambda e, src=src, dst=dst, sh=sh: e.tensor_tensor(
                                out=dst[:, PADW:PADW + NQ], in0=src[:, PADW:PADW + NQ], in1=src[:, PADW - sh:PADW + NQ - sh],
                                op=ALU.add), reads=[r_src], writes=[r_dst])
                            src, r_src = dst, r_dst
                            sh *= 2
                            bi ^= 1
                        if w < 16:
                            p.dve(lambda e, src=src, g=g, w=w: e.scalar_tensor_tensor(
                                out=plb[:, g, :], in0=src[:, PADW:PADW + NQ], scalar=1.0 / w, in1=uT[:, g, PADW:PADW + NQ],
                                op0=ALU.mult, op1=ALU.subtract), reads=[r_src, r_uT], writes=[r_plb])
                        else:
                            p.dma(lambda e, X=X: e.dma_start(out=pcb[:, :], in_=pc16[X, :].partition_broadcast(128),
                                                             allow_slow_non_contiguous=True), writes=[r_pcb], q="sp")
                            p.dve(lambda e, src=src: e.tensor_tensor(out=src[:, PADW:PADW + NQ], in0=src[:, PADW:PADW + NQ],
                                                                     in1=pcb[:, :], op=ALU.mult),
                                  reads=[r_src, r_pcb], writes=[r_src])
                            p.dve(lambda e, src=src, g=g: e.tensor_tensor(out=plb[:, g, :], in0=src[:, PADW:PADW + NQ],
                                                                          in1=uT[:, g, PADW:PADW + NQ], op=ALU.subtract),
                                  reads=[r_src, r_uT], writes=[r_plb])
                    for g in range(4):
                        for (c0, n) in ((0, 352), (352, 352), (704, 352)):
                            pq = pbank[3]
                            p.pe(lambda e, g=g, c0=c0, n=n: e.matmul(pq[:, 0:n], lhsT=wpl[:, g, :], rhs=plb[:, g, c0:c0 + n],
                                                                     start=True, stop=True),
                                 reads=[r_wpl, r_plb], writes=[r_pb[3]])
                            p.act(lambda e, g=g, c0=c0, n=n, X=X: e.activation(out=mixYa[:, g, X, c0:c0 + n], in_=pq[:, 0:n],
                                                                              func=AF.Identity, scale=pst[:, g:g + 1],
                                                                              bias=bps[:, g:g + 1]),
                                  reads=[r_pb[3], r_const], writes=[r_mixYa[X]])
                wcs = [dict(t=t, T=QT_SIZES[t], o=QT_OFFS[t]) for t in range(9)]

                def w_g1(c, X=X):
                    t, T, o = c["t"], c["T"], c["o"]
                    i = cnt["hm"] % 3
                    cnt["hm"] += 1
                    c["hi"] = i
                    par = cnt["wb"] % 2
                    cnt["wb"] += 1
                    p.dma(lambda e: e.dma_start(out=hm[i][:T, :], in_=xq[X, o:o + T, :]), writes=[r_hm[i]], q="sp")
                    for half in range(2):
                        bk = 4 + 2 * par + half
                        ph = pbank[bk]
                        for fc in range(8):
                            p.pe(lambda e, fc=fc, half=half, ph=ph: e.matmul(
                                ph[:T, :], lhsT=(mixYa[:, fc, X, o:o + T] if fc < 4 else mixYb[:, fc - 4, X, o:o + T]),
                                rhs=wo[:, fc, half * 512:(half + 1) * 512], start=(fc == 0), stop=(fc == 7)),
                                reads=[r_wo_l[fc], r_mixT[X][t], r_mixYa[X]], writes=[r_pb[bk]])
                        p.dve(lambda e, half=half, ph=ph: e.tensor_tensor(
                            out=hm[i][:T, half * 512:(half + 1) * 512], in0=ph[:T, :], in1=hm[i][:T, half * 512:(half + 1) * 512],
                            op=ALU.add), reads=[r_pb[bk], r_hm[i]], writes=[r_hm[i]])
                    p.dma(lambda e: e.dma_start(out=hmid[X, o:o + T, :], in_=hm[i][:T, :]),
                          reads=[r_hm[i]], writes=[r_hmid[X][t]], q="pool")

                def w_g2(c, X=X):
                    i = c["hi"]
                    j = cnt["x2"] % 2
                    cnt["x2"] += 1
                    c["xj"] = j
                    rms_rows(hm[i], r_hm[i], c["T"], xs2[j], r_xs2[j], st3[j], r_st3[j], gffn_b, r_gffn)

                def w_g3(c, X=X):
                    t, T, o, j = c["t"], c["T"], c["o"], c["xj"]
                    pt = pbank_bf[3]
                    for cc in range(8):
                        p.pe(lambda e, cc=cc: e.transpose(out=pt[:, cc * 128:cc * 128 + T],
                                                          in_=xs2[j][:T, cc * 128:(cc + 1) * 128], identity=ident[:T, :T]),
                             reads=[r_xs2[j], r_const], writes=[r_pb[3]])
                    p.dve(lambda e: e.tensor_copy(out=n2T[:, :, X, o:o + T],
                                                  in_=pt[:, :].rearrange("p (c t) -> p c t", c=8)[:, :, :T]),
                          reads=[r_pb[3]], writes=[r_n2T[X][t]])

                def p3_it(it):
                    if it < 9:
                        w_g1(wcs[it])
                    if 0 <= it - 1 < 9:
                        w_g2(wcs[it - 1])
                    if 0 <= it - 2 < 9:
                        w_g3(wcs[it - 2])
                return p1_it, mid, p3_it

            chA = build_chunk(0)
            chB = build_chunk(1)
            for it in range(11):
                chA[0](it)
            chA[1]()
            for it in range(11):
                chA[2](it)
                chB[0](it)
            chB[1]()
            for it in range(11):
                chB[2](it)

        p.barrier()
        outs = []
        with contextlib.ExitStack() as c2:
            NFB = FB_PER_PASS
            wup = SB(c2, "wup", [128, 8, 2, NFB * 128], BF16)
            wdn = SB(c2, "wdn", [128, NFB, D], BF16)
            FGRP = [(0, 4), (4, 8), (8, NFB)]
            r_wup_l = [[R(f"wup{s_}_{k}") for k in range(len(FGRP))] for s_ in range(2)]
            r_wdn_l = [R(f"wdn{f}") for f in range(NFB)]
            Gt = [SB(c2, f"Gt{i}", [128, NFB, FG], BF16) for i in range(2)]
            r_Gt = [R(f"Gt{i}") for i in range(2)]
            cbuf = [[SB(c2, f"cbuf{i}_{s}", [128, FG], F32) for s in range(2)] for i in range(2)]
            r_cbuf = [[R(f"cbuf{i}_{s}") for s in range(2)] for i in range(2)]
            sg = [SB(c2, f"sg{i}", [128, FG], F32) for i in range(2)]
            r_sg = [R(f"sg{i}") for i in range(2)]
            yt = [SB(c2, f"yt{i}", [128, D], F32) for i in range(2)]
            r_yt = [R(f"yt{i}") for i in range(2)]
            r_yst = [R(f"yst{i}") for i in range(2)]
            r_y = [[R(f"y{X}_{t}") for t in range(8)] for X in range(2)]
            wu3 = w_up.rearrange("(c p) n -> p c n", p=128)
            wd3 = w_down.rearrange("(f p) n -> p f n", p=128)
            wk = 0
            gk = 0
            for ps_ in range(NPASS):
                fb0 = ps_ * NFB
                for k, (fa, fz) in enumerate(FGRP):
                    for s in range(2):
                        col0 = s * DFF + (fb0 + fa) * 128
                        p.dma(lambda e, s=s, col0=col0, fa=fa, fz=fz: e.dma_start(
                            out=wup[:, :, s, fa * 128:fz * 128], in_=wu3[:, :, col0:col0 + (fz - fa) * 128]),
                            writes=[r_wup_l[s][k]], q="pool")
                for f in range(NFB):
                    p.dma(lambda e, f=f, fb0=fb0: e.dma_start(out=wdn[:, f, :], in_=wd3[:, fb0 + f, :]), writes=[r_wdn_l[f]], q="pool")
                def up_fb(g, f):
                    X, t0, gb, rn, W = g["X"], g["t0"], g["gb"], g["rn"], g["W"]
                    fb = fb0 + f
                    ci = f % 2
                    for s_ in range(2):
                        pu = pbank[ci * 2 + s_]
                        for c in range(8):
                            p.pe(lambda e, c=c, s_=s_, pu=pu: e.matmul(
                                pu[:, 0:W + 2], lhsT=wup[:, c, s_, f * 128:(f + 1) * 128],
                                rhs=n2T[:, c, X, t0 - 2:t0 + W], start=(c == 0), stop=(c == 7)),
                                reads=[r_wup_l[s_][[k for k, (fa, fz) in enumerate(FGRP) if fa <= f < fz][0]]] + rn,
                                writes=[r_pb[ci * 2 + s_]])
                        col = s_ * 22 + fb
                        cbt = cbuf[ci][s_]
                        rcb = r_cbuf[ci][s_]
                        p.act(lambda e, pu=pu, cbt=cbt, col=col: e.activation(
                            out=cbt[:, 0:W], in_=pu[:, 2:W + 2], func=AF.Identity, scale=cw[:, 2, col:col + 1],
                            bias=cb[:, col:col + 1]), reads=[r_pb[ci * 2 + s_], r_const], writes=[rcb])
                        p.dve(lambda e, pu=pu, cbt=cbt, col=col: e.scalar_tensor_tensor(
                            out=cbt[:, 0:W], in0=pu[:, 1:W + 1], scalar=cw[:, 1, col:col + 1], in1=cbt[:, 0:W],
                            op0=ALU.mult, op1=ALU.add), reads=[r_pb[ci * 2 + s_], rcb, r_const], writes=[rcb])
                        p.dve(lambda e, pu=pu, cbt=cbt, col=col: e.scalar_tensor_tensor(
                            out=cbt[:, 0:W], in0=pu[:, 0:W], scalar=cw[:, 0, col:col + 1], in1=cbt[:, 0:W],
                            op0=ALU.mult, op1=ALU.add), reads=[r_pb[ci * 2 + s_], rcb, r_const], writes=[rcb])
                    p.act(lambda e: e.activation(out=sg[ci][:, 0:W], in_=cbuf[ci][0][:, 0:W], func=AF.Silu),
                          reads=[r_cbuf[ci][0]], writes=[r_sg[ci]])
                    p.dve(lambda e: e.tensor_tensor(out=Gt[gb][:, f, 0:W], in0=sg[ci][:, 0:W], in1=cbuf[ci][1][:, 0:W], op=ALU.mult),
                          reads=[r_sg[ci], r_cbuf[ci][1]], writes=[r_Gt[gb]])

                def down_unit(g, q):
                    X, gb = g["X"], g["gb"]
                    tt = g["tiles"][q]
                    yrow = (tt - 1) * 128
                    yi = q % 2
                    if ps_ == 0:
                        p.dma(lambda e: e.dma_start(out=yt[yi][:, :], in_=hmid[X, QT_OFFS[tt]:QT_OFFS[tt] + 128, :]),
                              reads=[r_hmid[X][tt]], writes=[r_yt[yi]], q="sp")
                    else:
                        p.dma(lambda e: e.dma_start(out=yt[yi][:, :], in_=y[X, yrow:yrow + 128, :]),
                              reads=[r_y[X][tt - 1]], writes=[r_yt[yi]], q="sp")
                    for half in range(2):
                        bk = 4 + 2 * (q % 2) + half
                        pd = pbank[bk]
                        for f in range(NFB):
                            p.pe(lambda e, f=f, half=half, pd=pd: e.matmul(
                                pd[:, :], lhsT=Gt[gb][:, f, q * 128:(q + 1) * 128],
                                rhs=wdn[:, f, half * 512:(half + 1) * 512], start=(f == 0), stop=(f == NFB - 1)),
                                reads=[r_Gt[gb], r_wdn_l[f]], writes=[r_pb[bk]])
                        p.dve(lambda e, half=half, pd=pd: e.tensor_tensor(
                            out=yt[yi][:, half * 512:(half + 1) * 512], in0=pd[:, :],
                            in1=yt[yi][:, half * 512:(half + 1) * 512], op=ALU.add),
                            reads=[r_pb[bk], r_yt[yi]], writes=[r_yt[yi]])
                    od = p.dma(lambda e: e.dma_start(out=y[X, yrow:yrow + 128, :], in_=yt[yi][:, :]),
                               reads=[r_yt[yi]], writes=[r_y[X][tt - 1]], q="pool", sem_res=r_yst[yi])
                    if ps_ == NPASS - 1:
                        outs.append(od)

                groups = []
                for X in range(2):
                    for tiles in ([1, 2, 3], [4, 5, 6], [7, 8]):
                        groups.append(dict(X=X, t0=QT_OFFS[tiles[0]], gb=gk % 2, tiles=tiles, W=128 * len(tiles),
                                           rn=[r_n2T[X][t] for t in tiles] + [r_n2T[X][tiles[0] - 1]]))
                        gk += 1
                slots = {2: 0, 5: 1, 8: 2}
                prev = None
                for g in groups:
                    for f in range(NFB):
                        up_fb(g, f)
                        if prev is not None and f in slots and slots[f] < len(prev["tiles"]):
                            down_unit(prev, slots[f])
                    prev = g
                for q in range(len(prev["tiles"])):
                    down_unit(prev, q)
            p.emit(final_waits=outs)
    return nc


_NC_CACHE = {}


def _host_layout(x, meta_tokens):
    per_core = []
    for core in range(8):
        b, j = core // 4, core % 4
        xb = x[b]
        chunks = (j, 7 - j)
        xq = np.zeros((2, NQ, D), np.float32)
        qval = np.ones((2, NQ), np.float32)
        mval = np.ones((2, 16), np.float32)
        for X, c in enumerate(chunks):
            s = 1024 * c
            xq[X, 32:] = xb[s:s + 1024]
            if c == 0:
                xq[X, 16:32] = meta_tokens
                qval[X, 0:16] = 0.0
                mval[X, :] = 0.0
            else:
                xq[X, 0:32] = xb[s - 32:s]
        xk = np.zeros((7, 1024, D), np.float32)
        kval = np.zeros((7, 1024), np.float32)
        sel = np.zeros((14,), np.float32)
        pc16 = np.full((2, NQ), 1.0 / 16.0, np.float32)
        if chunks[0] == 0:
            for pp in range(15):
                pc16[0, 16 + pp] = 1.0 / float(pp + 1)
        for i in range(7):
            if i < j:
                X, c, t = 0, chunks[0], i
            else:
                X, c, t = 1, chunks[1], i - j
            lim = 1024 * c - 32
            lo = 1024 * t
            hi = min(lo + 1024, lim)
            xk[i, :hi - lo] = xb[lo:hi]
            kval[i, :hi - lo] = 1.0
            sel[7 * X + i] = 1.0
        per_core.append(dict(xq=xq, xk=xk, xm=np.ascontiguousarray(meta_tokens, np.float32), kval=kval, qval=qval,
                             mval=mval, sel=sel, pc16=pc16))
    return per_core


def kernel(x, meta_tokens, norm_mix_g, w_in, w_pool, b_pool, pool_scale, q_norm_g, k_norm_g,
           lambda_q1, lambda_k1, lambda_q2, lambda_k2, subln_g, w_out, norm_ffn_g,
           w_up, conv_w, conv_b, w_down):
    f = lambda a: np.ascontiguousarray(np.asarray(a, np.float32))
    x = f(x)
    meta_tokens = f(meta_tokens)
    shared = {
        "w_in": f(w_in)[0], "w_pool": f(w_pool)[0], "b_pool": f(b_pool)[0], "pool_scale": f(pool_scale)[0],
        "q_norm_g": f(q_norm_g)[0], "k_norm_g": f(k_norm_g)[0], "lambda_q1": f(lambda_q1)[0],
        "lambda_k1": f(lambda_k1)[0], "lambda_q2": f(lambda_q2)[0], "lambda_k2": f(lambda_k2)[0],
        "subln_g": f(subln_g)[0], "w_out": f(w_out)[0], "norm_mix_g": f(norm_mix_g)[0],
        "norm_ffn_g": f(norm_ffn_g)[0], "w_up": f(w_up)[0], "conv_w": f(conv_w)[0], "conv_b": f(conv_b)[0],
        "w_down": f(w_down)[0],
    }
    if "nc" not in _NC_CACHE:
        _NC_CACHE["nc"] = build_program()
    nc = _NC_CACHE["nc"]
    per_core = _host_layout(x, meta_tokens)
    in_maps = [dict(shared, **pc) for pc in per_core]
    res = run_bass_kernel_spmd(nc, in_maps, core_ids=list(range(8)))
    out = np.empty((2, 8192, D), np.float32)
    for core in range(8):
        b, j = core // 4, core % 4
        yc = res.results[core]["y"]
        out[b, 1024 * j:1024 * (j + 1)] = yc[0]
        out[b, 1024 * (7 - j):1024 * (8 - j)] = yc[1]
    return out
```

```python
import contextlib
import numpy as np
import concourse.bass as bass
import concourse.mybir as mybir
from concourse.bass_utils import run_bass_kernel_spmd

F32 = mybir.dt.float32
BF16 = mybir.dt.bfloat16
AF = mybir.ActivationFunctionType
ALU = mybir.AluOpType
AX = mybir.AxisListType

ENGS = ("pe", "act", "dve", "pool", "sp")


class Res:
    __slots__ = ("name", "last_w", "readers", "dma_sem", "dma_cnt", "last_dma")

    def __init__(self, name):
        self.name = name
        self.last_dma = None
        self.last_w = None
        self.readers = []
        self.dma_sem = None
        self.dma_cnt = 0


class Ins:
    __slots__ = ("eng", "fn", "deps", "inc_val", "is_dma", "dma_res", "dma_val", "needed")

    def __init__(self, eng, fn):
        self.eng = eng
        self.fn = fn
        self.deps = []
        self.inc_val = None
        self.is_dma = False
        self.dma_res = None
        self.dma_val = 0
        self.needed = False


class Prog:
    def __init__(self, nc):
        self.nc = nc
        self.streams = {e: [] for e in ENGS}
        self.all_res = []
        self.barrier_deps = []
        self.barrier_seen = {e: True for e in ENGS}

    def res(self, name):
        r = Res(name)
        self.all_res.append(r)
        return r

    def barrier(self):
        deps = []
        for e in ENGS:
            if self.streams[e]:
                deps.append(self.streams[e][-1])
        for r in self.all_res:
            if r.last_dma is not None:
                deps.append(r.last_dma)
        self.barrier_deps = deps
        self.barrier_seen = {e: False for e in ENGS}

    def _add(self, eng, fn, reads, writes, is_dma=False, sem_res=None):
        ins = Ins(eng, fn)
        ins.is_dma = is_dma
        deps = []
        if not self.barrier_seen[eng]:
            self.barrier_seen[eng] = True
            deps.extend(self.barrier_deps)
        for r in reads:
            if r.last_w is not None:
                deps.append(r.last_w)
        for r in writes:
            if r.last_w is not None:
                deps.append(r.last_w)
            deps.extend(r.readers)
        seen = set()
        for d in deps:
            if d is ins or id(d) in seen:
                continue
            seen.add(id(d))
            if d.eng == eng and not d.is_dma and eng in ("pe", "sp"):
                continue
            ins.deps.append(d)
            d.needed = True
        for r in reads:
            r.readers.append(ins)
        for r in writes:
            r.last_w = ins
            r.readers = []
        if is_dma:
            tgt = sem_res if sem_res is not None else writes[0]
            ins.dma_res = tgt
            tgt.dma_cnt += 16
            ins.dma_val = tgt.dma_cnt
            tgt.last_dma = ins
        self.streams[eng].append(ins)
        return ins

    def pe(self, fn, reads=(), writes=()):
        return self._add("pe", fn, list(reads), list(writes))

    def act(self, fn, reads=(), writes=()):
        return self._add("act", fn, list(reads), list(writes))

    def dve(self, fn, reads=(), writes=()):
        return self._add("dve", fn, list(reads), list(writes))

    def pool(self, fn, reads=(), writes=()):
        return self._add("pool", fn, list(reads), list(writes))

    def dma(self, fn, reads=(), writes=(), q="sp", sem_res=None):
        return self._add(q, fn, list(reads), list(writes), is_dma=True, sem_res=sem_res)

    def emit(self, final_waits=()):
        nc = self.nc
        with contextlib.ExitStack() as st:
            sems = {}
            for e in ENGS:
                sems[e] = st.enter_context(nc.semaphore("s_" + e))
            for r in self.all_res:
                if r.dma_cnt > 0:
                    r.dma_sem = st.enter_context(nc.semaphore("d_" + r.name))
            for e in ENGS:
                c = 0
                for ins in self.streams[e]:
                    if ins.is_dma:
                        continue
                    if ins.needed:
                        c += 1
                        ins.inc_val = c
            block = st.enter_context(nc.Block())
            engmap = {"pe": "tensor", "act": "scalar", "dve": "vector", "pool": "gpsimd", "sp": "sync"}

            def make(e):
                def body(eng):
                    waited = {}
                    for ins in self.streams[e]:
                        for d in ins.deps:
                            if d.is_dma:
                                key = ("d", id(d.dma_res))
                                val = d.dma_val
                                sem = d.dma_res.dma_sem
                            else:
                                key = ("e", d.eng)
                                val = d.inc_val
                                sem = sems[d.eng]
                            if waited.get(key, 0) >= val:
                                continue
                            waited[key] = val
                            eng.wait_ge(sem, val)
                        r = ins.fn(eng)
                        if ins.is_dma:
                            r.then_inc(ins.dma_res.dma_sem, 16)
                        elif ins.needed:
                            r.then_inc(sems[e], 1)
                    if e == "sp":
                        for d in final_waits:
                            eng.wait_ge(d.dma_res.dma_sem, d.dma_val)
                return body

            for e in ENGS:
                getattr(block, engmap[e])(make(e))


D = 1024
NQ = 1056
QT_SIZES = [32] + [128] * 8
QT_OFFS = [0] + [32 + 128 * i for i in range(8)]
GROUPS = [(0, 1, 2), (3, 4, 5), (6, 7, 8)]
DFF = 2816
EPS = 1e-6
LAM_INIT = 0.2
NEG = -30000.0
FG = 384
NPASS = 2
FB_PER_PASS = 22 // NPASS


def build_program():
    nc = bass.Bass("TRN2", target_bir_lowering=False)
    dr = lambda name, shape, kind="ExternalInput": nc.dram_tensor(name, shape, F32, kind=kind).ap()
    xq = dr("xq", [2, NQ, D])
    xk = dr("xk", [7, 1024, D])
    xm = dr("xm", [16, D])
    kval = dr("kval", [7, 1024])
    qval = dr("qval", [2, NQ])
    mval = dr("mval", [2, 16])
    sel = dr("sel", [14])
    pc16 = dr("pc16", [2, NQ])
    w_in = dr("w_in", [D, 2048])
    w_pool = dr("w_pool", [4, 128, 128])
    b_pool = dr("b_pool", [4, 128])
    pool_scale = dr("pool_scale", [512])
    qg = dr("q_norm_g", [64])
    kg = dr("k_norm_g", [64])
    lq1 = dr("lambda_q1", [64])
    lk1 = dr("lambda_k1", [64])
    lq2 = dr("lambda_q2", [64])
    lk2 = dr("lambda_k2", [64])
    subg = dr("subln_g", [128])
    w_out = dr("w_out", [D, D])
    gmix_d = dr("norm_mix_g", [D])
    gffn_d = dr("norm_ffn_g", [D])
    w_up = dr("w_up", [D, 2 * DFF])
    conv_w = dr("conv_w", [3, 2 * DFF])
    conv_b = dr("conv_b", [2 * DFF])
    w_down = dr("w_down", [DFF, D])
    y = dr("y", [2, 1024, D], kind="ExternalOutput")
    hmid = dr("hmid", [2, NQ, D], kind="Internal")

    p = Prog(nc)
    R = p.res

    with contextlib.ExitStack() as outer:
        def SB(st, name, shape, dt):
            return st.enter_context(nc.sbuf_tensor(name, shape, dt))

        def PS(st, name, shape, dt):
            return st.enter_context(nc.psum_tensor(name, shape, dt))

        ident = SB(outer, "ident", [128, 128], BF16)
        identf = SB(outer, "identf", [128, 128], F32)
        maskneg = SB(outer, "maskneg", [128, 128], BF16)
        blk1 = SB(outer, "blk1", [128, 128], BF16)
        gmix_b = SB(outer, "gmix_b", [128, D], F32)
        qkg = SB(outer, "qkg", [128, 2], F32)
        selt = SB(outer, "selt", [128, 14], F32)
        lamt = SB(outer, "lamt", [128, 4, 64], F32)
        lamw = SB(outer, "lamw", [128, 8], F32)
        subgt = SB(outer, "subgt", [128, 1], F32)
        bpt = SB(outer, "bpt", [128, 4], F32)
        pst = SB(outer, "pst", [128, 4], F32)
        bps = SB(outer, "bps", [128, 4], F32)
        ones4 = SB(outer, "ones4", [128, 4], F32)
        zer = SB(outer, "zer", [128, 512], BF16)
        cw = SB(outer, "cw", [128, 3, 44], F32)
        cb = SB(outer, "cb", [128, 44], F32)
        mixYb = SB(outer, "mixYb", [128, 4, 2, NQ], BF16)
        r_const = R("const")
        r_mixT = [[R(f"mixT{X}_{t}") for t in range(9)] for X in range(2)]
        r_mixYa = [R(f"mixYa{X}") for X in range(2)]

        pb_h = [PS(outer, f"pb{i}", [128, 512], F32) for i in range(4)]
        sc_h = [PS(outer, f"sc{i}", [128, 1024], F32) for i in range(2)]
        pbank = list(pb_h) + [sc_h[i // 2][:, (i % 2) * 512:(i % 2 + 1) * 512] for i in range(4)]
        r_pb = [R(f"pb{i}") for i in range(8)]
        pbank_bf = [b.bitcast(BF16) for b in pb_h]

        p.pool(lambda e: e.memset(identf[:], 0.0), writes=[r_const])
        p.pool(lambda e: e.affine_select(out=identf[:], in_=identf[:], compare_op=ALU.not_equal, fill=1.0,
                                         base=0, pattern=[[-1, 128]], channel_multiplier=1),
               reads=[r_const], writes=[r_const])
        p.dve(lambda e: e.tensor_copy(out=ident[:], in_=identf[:]), reads=[r_const], writes=[r_const])
        p.pool(lambda e: e.memset(identf[:], 0.0), reads=[r_const], writes=[r_const])
        p.pool(lambda e: e.affine_select(out=identf[:], in_=identf[:], compare_op=ALU.is_ge, fill=NEG,
                                         base=0, pattern=[[1, 128]], channel_multiplier=-1),
               reads=[r_const], writes=[r_const])
        p.dve(lambda e: e.tensor_copy(out=maskneg[:], in_=identf[:]), reads=[r_const], writes=[r_const])
        p.dve(lambda e: e.memset(blk1[:], 0.0), reads=[r_const], writes=[r_const])
        p.dve(lambda e: e.memset(blk1[0:64, 0:64], 1.0), reads=[r_const], writes=[r_const])
        p.dve(lambda e: e.memset(blk1[64:128, 64:128], 1.0), reads=[r_const], writes=[r_const])
        p.dve(lambda e: e.memset(ones4[:], 1.0), reads=[r_const], writes=[r_const])
        p.dve(lambda e: e.memset(zer[:], 0.0), reads=[r_const], writes=[r_const])

        def small_dma(out_ap, in_ap):
            p.dma(lambda e: e.dma_start(out=out_ap, in_=in_ap, allow_slow_non_contiguous=True),
                  reads=[], writes=[r_const], q="sp")

        small_dma(gmix_b[:], gmix_d.partition_broadcast(128))
        small_dma(qkg[0:64, 0:1], qg.rearrange("(p o) -> p o", o=1))
        small_dma(qkg[64:128, 0:1], qg.rearrange("(p o) -> p o", o=1))
        small_dma(qkg[0:64, 1:2], kg.rearrange("(p o) -> p o", o=1))
        small_dma(qkg[64:128, 1:2], kg.rearrange("(p o) -> p o", o=1))
        small_dma(selt[:], sel.partition_broadcast(128))
        for i, l in enumerate((lq1, lk1, lq2, lk2)):
            small_dma(lamt[:, i, :], l.partition_broadcast(128))
        small_dma(subgt[:], subg.rearrange("(p o) -> p o", o=1))
        small_dma(bpt[:], b_pool.rearrange("g p -> p g"))
        small_dma(pst[:], pool_scale.rearrange("(g p) -> p g", p=128))
        small_dma(cw[:], conv_w.rearrange("k (f p) -> p k f", p=128))
        small_dma(cb[:], conv_b.rearrange("(f p) -> p f", p=128))
        p.dve(lambda e: e.tensor_scalar(out=qkg[:, 0:1], in0=qkg[:, 0:1], scalar1=0.125, scalar2=None, op0=ALU.mult),
              reads=[r_const], writes=[r_const])
        p.dve(lambda e: e.tensor_tensor(out=lamt[:, 0, :], in0=lamt[:, 0, :], in1=lamt[:, 1, :], op=ALU.mult),
              reads=[r_const], writes=[r_const])
        p.dve(lambda e: e.tensor_tensor(out=lamt[:, 2, :], in0=lamt[:, 2, :], in1=lamt[:, 3, :], op=ALU.mult),
              reads=[r_const], writes=[r_const])
        p.dve(lambda e: e.reduce_sum(out=lamw[:, 0:1], in_=lamt[:, 0, :], axis=AX.X), reads=[r_const], writes=[r_const])
        p.dve(lambda e: e.reduce_sum(out=lamw[:, 1:2], in_=lamt[:, 2, :], axis=AX.X), reads=[r_const], writes=[r_const])
        p.act(lambda e: e.activation(out=lamw[:, 2:4], in_=lamw[:, 0:2], func=AF.Exp), reads=[r_const], writes=[r_const])
        p.dve(lambda e: e.tensor_tensor(out=lamw[:, 4:5], in0=lamw[:, 3:4], in1=lamw[:, 2:3], op=ALU.subtract),
              reads=[r_const], writes=[r_const])
        p.dve(lambda e: e.tensor_scalar(out=lamw[:, 6:7], in0=lamw[:, 4:5], scalar1=-LAM_INIT, scalar2=None, op0=ALU.add),
              reads=[r_const], writes=[r_const])
        p.dve(lambda e: e.tensor_tensor(out=bps[:], in0=bpt[:], in1=pst[:], op=ALU.mult), reads=[r_const], writes=[r_const])
        p.dve(lambda e: e.tensor_scalar(out=subgt[:], in0=subgt[:], scalar1=1.0 - LAM_INIT, scalar2=None, op0=ALU.mult),
              reads=[r_const], writes=[r_const])

        NFE = 2
        xt = [SB(outer, f"xt{i}", [128, D], F32) for i in range(NFE)]
        r_xt = [R(f"xt{i}") for i in range(NFE)]
        xs = [SB(outer, f"xs{i}", [128, D], BF16) for i in range(NFE)]
        r_xs = [R(f"xs{i}") for i in range(NFE)]
        NNT = 4
        nT = [SB(outer, f"nT{i}", [128, 8, 128], BF16) for i in range(NNT)]
        r_nT = [R(f"nT{i}") for i in range(NNT)]
        st2 = [SB(outer, f"st2_{i}", [128, 4], F32) for i in range(NFE)]
        r_st2 = [R(f"st2_{i}") for i in range(NFE)]
        epst = SB(outer, "epst", [128, 1], F32)
        p.dve(lambda e: e.memset(epst[:], EPS), reads=[r_const], writes=[r_const])
        cnt = {"fe": 0, "nt": 0, "ub": 0}

        def rms_rows(src_t, r_src, T, dst_bf, r_dst, stt, r_stt, gb, r_gb=None):
            p.act(lambda e: e.activation(out=dst_bf[:T, :], in_=src_t[:T, :], func=AF.Square, accum_out=stt[:T, 0:1]),
                  reads=[r_src], writes=[r_dst, r_stt])
            p.act(lambda e: e.activation(out=stt[:T, 1:2], in_=stt[:T, 0:1], func=AF.Ln, scale=1.0 / D, bias=epst[:T, 0:1]),
                  reads=[r_stt, r_const], writes=[r_stt])
            p.act(lambda e: e.activation(out=stt[:T, 2:3], in_=stt[:T, 1:2], func=AF.Exp, scale=-0.5),
                  reads=[r_stt], writes=[r_stt])
            p.dve(lambda e: e.scalar_tensor_tensor(out=dst_bf[:T, :], in0=src_t[:T, :], scalar=stt[:T, 2:3], in1=gb[:T, :],
                                                   op0=ALU.mult, op1=ALU.mult),
                  reads=[r_src, r_stt, r_const] + ([r_gb] if r_gb else []), writes=[r_dst])

        def transpose_rows(src_bf, r_src, T, dstT, r_dstT, bank=0):
            pt = pbank_bf[bank]
            for c in range(8):
                p.pe(lambda e, c=c: e.transpose(out=pt[:, c * 128:c * 128 + T], in_=src_bf[:T, c * 128:(c + 1) * 128],
                                                identity=ident[:T, :T]),
                     reads=[r_src, r_const], writes=[r_pb[bank]])
            p.dve(lambda e: e.tensor_copy(out=dstT[:, :, :T], in_=pt[:, :].rearrange("p (c t) -> p c t", c=8)[:, :, :T]),
                  reads=[r_pb[bank]], writes=[r_dstT])

        def front_end(src_ap, T, tbank=0):
            i = cnt["fe"] % NFE
            cnt["fe"] += 1
            ni = cnt["nt"] % NNT
            cnt["nt"] += 1
            p.dma(lambda e: e.dma_start(out=xt[i][:T, :], in_=src_ap), writes=[r_xt[i]], q="sp")
            rms_rows(xt[i], r_xt[i], T, xs[i], r_xs[i], st2[i], r_st2[i], gmix_b)
            transpose_rows(xs[i], r_xs[i], T, nT[ni], r_nT[ni], bank=tbank)
            return ni

        with contextlib.ExitStack() as ab:
            wqkv = SB(ab, "wqkv", [128, 8, 1536], BF16)
            r_wqkv_l = [R(f"wqkv{c}") for c in range(8)]
            w3 = w_in.rearrange("(c p) n -> p c n", p=128)
            for c in range(8):
                p.dma(lambda e, c=c: e.dma_start(out=wqkv[:, c, :], in_=w3[:, c, 512:2048]), writes=[r_wqkv_l[c]], q="pool")

            NKV = 1
            KT = [SB(ab, f"KT{i}", [128, 4, NQ + 16], BF16) for i in range(NKV)]
            VX = [SB(ab, f"VX{i}", [128, 10, 4, 130], BF16) for i in range(NKV)]
            r_KT = [R(f"KT{i}") for i in range(NKV)]
            r_VX = [R(f"VX{i}") for i in range(NKV)]
            QT = [SB(ab, f"QT{X}", [128, 4, NQ], BF16) for X in range(2)]
            r_QT = [R(f"QT{X}") for X in range(2)]
            Qs = SB(ab, "Qs", [128, 4, NQ], BF16)
            r_Qs = R("Qs")
            OA = [SB(ab, f"O{X}", [128, 9, 4, 2, 129], F32) for X in range(2)]
            r_O = [[[R(f"O{X}_{t}_{h}") for h in range(4)] for t in range(9)] for X in range(2)]
            sqb = [SB(ab, f"sqb{i}", [128, 4, 128], BF16) for i in range(2)]
            r_sqb = [R(f"sqb{i}") for i in range(2)]
            lnb = [SB(ab, "lnb0", [128, 4, 128], F32)] * 2
            r_lnb = [R("lnb0")] * 2
            cnt["sq"] = 0
            cnt["kvt"] = 0
            valt = [SB(ab, f"valt{i}", [128, 1], F32) for i in range(4)]
            r_valt = [R(f"valt{i}") for i in range(4)]
            NPT = 3
            Pt = [SB(ab, f"Pt{i}", [128, 2, 384], BF16) for i in range(NPT)]
            r_Pt = [R(f"Pt{i}") for i in range(NPT)]
            cnt["val"] = 0
            cnt["pt"] = 0
            cnt["grp"] = 0
            cnt["sc"] = 0

            rsq = [[SB(ab, f"rsq{a}_{i}", [128, 4, 128], F32) for i in range(2)] for a in range(2)]
            r_rsq = [[R(f"rsq{a}_{i}") for i in range(2)] for a in range(2)]
            KB = [[1, 2], [4, 7]]
            TB2 = [0, 3]
            SBK, VBK, TBK = 5, 6, 0

            def kv_f1(c):
                i = cnt["fe"] % NFE
                cnt["fe"] += 1
                c["xi"] = i
                T = c["T"]
                p.dma(lambda e: e.dma_start(out=xt[i][:T, :], in_=c["src"]), writes=[r_xt[i]], q="sp")
                rms_rows(xt[i], r_xt[i], T, xs[i], r_xs[i], st2[i], r_st2[i], gmix_b)
                vi = cnt["val"] % 4
                cnt["val"] += 1
                c["vi"] = vi
                p.dma(lambda e: e.dma_start(out=valt[vi][:T, :], in_=c["val"]), writes=[r_valt[vi]], q="sp")

            def kv_f2(c):
                ni = cnt["nt"] % NNT
                cnt["nt"] += 1
                c["ni"] = ni
                i = c["xi"]
                transpose_rows(xs[i], r_xs[i], c["T"], nT[ni], r_nT[ni], bank=TB2[c["par"]])

            def kv_b1(c):
                T, ni, par = c["T"], c["ni"], c["par"]
                for a, col0 in ((0, 512), (1, 0)):
                    if a == 1 and c["own"] is None:
                        continue
                    bank = KB[a][par]
                    pk = pbank[bank]
                    for h in range(4):
                        for cc in range(8):
                            p.pe(lambda e, h=h, cc=cc, pk=pk, col0=col0: e.matmul(
                                pk[:, h * 128:h * 128 + T], lhsT=wqkv[:, cc, col0 + h * 128:col0 + (h + 1) * 128],
                                rhs=nT[ni][:, cc, :T], start=(cc == 0), stop=(cc == 7)),
                                reads=[r_wqkv_l[cc], r_nT[ni]], writes=[r_pb[bank]])
                    pk3 = pk[:, :].rearrange("p (h t) -> p h t", h=4)
                    j = cnt["sq"] % 2
                    cnt["sq"] += 1
                    p.act(lambda e, pk3=pk3, j=j: e.activation(out=sqb[j][:, :, :T], in_=pk3[:, :, :T], func=AF.Square),
                          reads=[r_pb[bank]], writes=[r_sqb[j]])
                    pss = pbank[SBK]
                    for h in range(4):
                        p.pe(lambda e, h=h, j=j, pss=pss: e.matmul(pss[:, h * 128:h * 128 + T], lhsT=blk1[:, :],
                                                                   rhs=sqb[j][:, h, :T], start=True, stop=True),
                             reads=[r_sqb[j], r_const], writes=[r_pb[SBK]])
                    pss3 = pss[:, :].rearrange("p (h t) -> p h t", h=4)
                    p.act(lambda e, pss3=pss3: e.activation(out=lnb[0][:, :, :T], in_=pss3[:, :, :T], func=AF.Ln,
                                                            scale=1.0 / 64, bias=epst[:, 0:1]),
                          reads=[r_pb[SBK], r_const], writes=[r_lnb[0]])
                    p.act(lambda e, a=a: e.activation(out=rsq[a][par][:, :, :T], in_=lnb[0][:, :, :T], func=AF.Exp, scale=-0.5),
                          reads=[r_lnb[0]], writes=[r_rsq[a][par]])

            def kv_b2(c):
                T, ni, par, kb, vidx, vi = c["T"], c["ni"], c["par"], c["kb"], c["vidx"], c["vi"]
                for a in range(2):
                    if a == 1 and c["own"] is None:
                        continue
                    bank = KB[a][par]
                    pk3 = pbank[bank][:, :].rearrange("p (h t) -> p h t", h=4)
                    if a == 0:
                        dst, r_dst, doff, gcol = KT[kb], r_KT[kb], c["koff"], 1
                    else:
                        X, doff = c["own"]
                        dst, r_dst, gcol = QT[X], r_QT[X], 0
                    p.dve(lambda e, pk3=pk3, dst=dst, doff=doff, gcol=gcol, a=a: e.scalar_tensor_tensor(
                        out=dst[:, :, doff:doff + T], in0=pk3[:, :, :T], scalar=qkg[:, gcol:gcol + 1],
                        in1=rsq[a][par][:, :, :T], op0=ALU.mult, op1=ALU.mult),
                        reads=[r_pb[bank], r_rsq[a][par], r_const], writes=[r_dst])
                vbk = (VBK, 7)[par] if c["own"] is None else VBK
                pv = pbank[vbk]
                for cc in range(8):
                    p.pe(lambda e, cc=cc: e.matmul(pv[:T, :], lhsT=nT[ni][:, cc, :T], rhs=wqkv[:, cc, 1024:1536],
                                                   start=(cc == 0), stop=(cc == 7)),
                         reads=[r_wqkv_l[cc], r_nT[ni]], writes=[r_pb[vbk]])
                p.act(lambda e: e.activation(out=VX[kb][:T, vidx, :, 0:128],
                                             in_=pv[:T, :].rearrange("p (h d) -> p h d", h=4), func=AF.Copy,
                                             scale=valt[vi][:T, 0:1]),
                      reads=[r_pb[vbk], r_valt[vi]], writes=[r_VX[kb]])
                p.dve(lambda e: e.tensor_scalar(out=VX[kb][:T, vidx, :, 128:129], in0=ones4[:T, :].rearrange("p (h o) -> p h o", o=1),
                                                scalar1=valt[vi][:T, 0:1], scalar2=None, op0=ALU.mult),
                      reads=[r_valt[vi], r_const], writes=[r_VX[kb]])

            def run_kv(tiles):
                cs = []
                for (src, T, kb, koff, vidx, val, own) in tiles:
                    cs.append(dict(src=src, T=T, kb=kb, koff=koff, vidx=vidx, val=val, own=own, par=cnt["kvt"] % 2))
                    cnt["kvt"] += 1
                n = len(cs)
                for it in range(n + 3):
                    if it < n:
                        kv_f1(cs[it])
                    if 0 <= it - 1 < n:
                        kv_f2(cs[it - 1])
                    if 0 <= it - 2 < n:
                        kv_b1(cs[it - 2])
                    if 0 <= it - 3 < n:
                        kv_b2(cs[it - 3])

            def attention(kb, Qsrc, r_Qsrc, ktiles, diag, finish):
                steps = []
                for h in range(4):
                    for G in GROUPS:
                        g0 = QT_OFFS[G[0]]
                        g1 = QT_OFFS[G[-1]] + QT_SIZES[G[-1]]
                        kl = [k for k in ktiles if (not diag) or k[3] is None or k[3] <= G[-1]]
                        last_for = {}
                        for ki, (koff, nk, vidx, lt) in enumerate(kl):
                            for t in G:
                                if diag and lt is not None and lt > t:
                                    continue
                                last_for[t] = ki
                        for ki, k in enumerate(kl):
                            steps.append(dict(h=h, G=G, g0=g0, g1=g1, ki=ki, k=k, first=(ki == 0), last=(ki == len(kl) - 1),
                                              last_for=last_for, gidx=cnt["grp"]))
                        cnt["grp"] += 1

                def emit_score(st):
                    koff, nk, vidx, lt = st["k"]
                    G, h = st["G"], st["h"]
                    qs = max(st["g0"], QT_OFFS[lt]) if (diag and lt is not None) else st["g0"]
                    n = st["g1"] - qs
                    si = cnt["sc"] % 2
                    cnt["sc"] += 1
                    st.update(qs=qs, n=n, si=si)
                    for c in range(2):
                        sb = 4 + 2 * si + c
                        has_mask = diag and lt is not None and lt >= G[0]
                        p.pe(lambda e, c=c, sb=sb: e.matmul(
                            pbank[sb][:nk, 0:n], lhsT=KT[kb][64 * c:64 * c + 64, h, koff:koff + nk],
                            rhs=Qsrc[64 * c:64 * c + 64, h, qs:qs + n], start=True, stop=True),
                            reads=[r_KT[kb], r_Qsrc], writes=[r_pb[sb]])
                        if has_mask:
                            p.pe(lambda e, c=c, sb=sb: e.matmul(
                                pbank[sb][:nk, 0:nk], lhsT=ident[:nk, :nk], rhs=maskneg[:nk, :nk],
                                start=False, stop=True, skip_group_check=True),
                                reads=[r_const], writes=[r_pb[sb]])

                def emit_rest(st):
                    koff, nk, vidx, lt = st["k"]
                    G, h, qs, n, si, ki = st["G"], st["h"], st["qs"], st["n"], st["si"], st["ki"]
                    if st["first"]:
                        for t in G:
                            ab_ = (3 * st["gidx"] + (t - G[0])) % 4
                            p.pe(lambda e, ab_=ab_, nt=QT_SIZES[t]: e.matmul(pbank[ab_][:nt, 0:258], lhsT=zer[0:1, 0:nt],
                                                                            rhs=zer[0:1, 0:258], start=True, stop=True),
                                 reads=[r_const], writes=[r_pb[ab_]])
                    pi = cnt["pt"] % NPT
                    cnt["pt"] += 1
                    scv = sc_h[si][:nk, :].rearrange("p (c m) -> p c m", c=2)
                    p.act(lambda e, scv=scv: e.activation(out=Pt[pi][:nk, :, 0:n], in_=scv[:, :, 0:n], func=AF.Exp),
                          reads=[r_pb[4 + 2 * si], r_pb[5 + 2 * si]], writes=[r_Pt[pi]])
                    for t in G:
                        if diag and lt is not None and lt > t:
                            continue
                        nt = QT_SIZES[t]
                        po = QT_OFFS[t] - qs
                        ab_ = (3 * st["gidx"] + (t - G[0])) % 4
                        acc = pbank[ab_][:, 0:258].rearrange("p (c d) -> p c d", c=2)
                        for c in range(2):
                            p.pe(lambda e, c=c, nt=nt, po=po, acc=acc, f=(st["last_for"][t] == ki): e.matmul(
                                acc[:nt, c, :], lhsT=Pt[pi][:nk, c, po:po + nt], rhs=VX[kb][:nk, vidx, h, 0:129],
                                start=False, stop=f, skip_group_check=True),
                                reads=[r_Pt[pi], r_VX[kb]], writes=[r_pb[ab_]])
                    if st["last"]:
                        for t in G:
                            ab_ = (3 * st["gidx"] + (t - G[0])) % 4
                            acc = pbank[ab_][:, 0:258].rearrange("p (c d) -> p c d", c=2)
                            finish(t, h, acc, r_pb[ab_], QT_SIZES[t])

                emit_score(steps[0])
                for si_, st in enumerate(steps):
                    if si_ + 1 < len(steps):
                        emit_score(steps[si_ + 1])
                    emit_rest(st)

            for X in range(2):
                kb = X % NKV
                tl = []
                for t in range(9):
                    T = QT_SIZES[t]
                    o = QT_OFFS[t]
                    tl.append((xq[X, o:o + T, :], T, kb, o, t, qval[X, o:o + T].rearrange("(p o) -> p o", o=1), (X, o)))
                tl.append((xm[:, :], 16, kb, NQ, 9, mval[X, :].rearrange("(p o) -> p o", o=1), None))
                run_kv(tl)
                ktiles = [(NQ, 16, 9, None)] + [(QT_OFFS[t], QT_SIZES[t], t, t) for t in range(9)]

                def fin_diag(t, h, acc, r_acc, nt, X=X):
                    p.dve(lambda e: e.tensor_copy(out=OA[X][:nt, t, h, :, :], in_=acc[:nt, :, :]),
                          reads=[r_acc], writes=[r_O[X][t][h]])
                attention(kb, QT[X], r_QT[X], ktiles, True, fin_diag)

            for i in range(7):
                kb = i % NKV
                run_kv([(xk[i, kt * 128:(kt + 1) * 128, :], 128, kb, kt * 128, kt,
                         kval[i, kt * 128:(kt + 1) * 128].rearrange("(p o) -> p o", o=1), None) for kt in range(8)])
                p.dve(lambda e, i=i: e.tensor_scalar(out=Qs[:, :, :], in0=QT[0][:, :, :], scalar1=selt[:, i:i + 1],
                                                     scalar2=None, op0=ALU.mult),
                      reads=[r_QT[0], r_const], writes=[r_Qs])
                p.dve(lambda e, i=i: e.scalar_tensor_tensor(out=Qs[:, :, :], in0=QT[1][:, :, :], scalar=selt[:, 7 + i:8 + i],
                                                            in1=Qs[:, :, :], op0=ALU.mult, op1=ALU.add),
                      reads=[r_QT[1], r_Qs, r_const], writes=[r_Qs])
                ktiles = [(kt * 128, 128, kt, None) for kt in range(8)]

                def fin_full(t, h, acc, r_acc, nt, i=i):
                    for X in range(2):
                        p.dve(lambda e, X=X: e.scalar_tensor_tensor(
                            out=OA[X][:nt, t, h, :, :], in0=acc[:nt, :, :], scalar=selt[:nt, 7 * X + i:7 * X + i + 1],
                            in1=OA[X][:nt, t, h, :, :], op0=ALU.mult, op1=ALU.add),
                            reads=[r_acc, r_O[X][t][h], r_const], writes=[r_O[X][t][h]])
                attention(kb, Qs, r_Qs, ktiles, False, fin_full)

            ob = [SB(ab, f"ob{i}", [128, 4, 128], F32) for i in range(2)]
            r_ob = [R(f"ob{i}") for i in range(2)]
            obf = [SB(ab, f"obf{i}", [128, 4, 128], BF16) for i in range(2)]
            r_obf = [R(f"obf{i}") for i in range(2)]
            rl = [SB(ab, f"rl{i}", [128, 4, 2], F32) for i in range(2)]
            r_rl = [R(f"rl{i}") for i in range(2)]
            s4 = [SB(ab, f"s4{i}", [128, 12], F32) for i in range(2)]
            r_s4 = [R(f"s4{i}") for i in range(2)]
            k = 0
            for X in range(2):
                for t in range(9):
                    nt = QT_SIZES[t]
                    o = QT_OFFS[t]
                    i = k % 2
                    k += 1
                    rO = [r_O[X][t][h] for h in range(4)]
                    p.dve(lambda e, X=X, t=t, nt=nt, i=i: e.tensor_scalar(out=rl[i][:nt, :, :], in0=OA[X][:nt, t, :, :, 128],
                                                                          scalar1=1e-30, scalar2=None, op0=ALU.max),
                          reads=rO, writes=[r_rl[i]])
                    p.dve(lambda e, nt=nt, i=i: e.reciprocal(out=rl[i][:nt, :, :], in_=rl[i][:nt, :, :]),
                          reads=[r_rl[i]], writes=[r_rl[i]])
                    p.dve(lambda e, nt=nt, i=i: e.tensor_scalar(out=rl[i][:nt, :, 1:2], in0=rl[i][:nt, :, 1:2],
                                                                scalar1=lamw[:nt, 6:7], scalar2=None, op0=ALU.mult),
                          reads=[r_rl[i], r_const], writes=[r_rl[i]])
                    for h in range(4):
                        p.dve(lambda e, X=X, t=t, nt=nt, i=i, h=h: e.tensor_scalar(
                            out=ob[i][:nt, h, :], in0=OA[X][:nt, t, h, 0, 0:128], scalar1=rl[i][:nt, h, 0:1],
                            scalar2=None, op0=ALU.mult), reads=rO + [r_rl[i]], writes=[r_ob[i]])
                        p.dve(lambda e, X=X, t=t, nt=nt, i=i, h=h: e.scalar_tensor_tensor(
                            out=ob[i][:nt, h, :], in0=OA[X][:nt, t, h, 1, 0:128], scalar=rl[i][:nt, h, 1:2],
                            in1=ob[i][:nt, h, :], op0=ALU.mult, op1=ALU.add), reads=rO + [r_rl[i], r_ob[i]], writes=[r_ob[i]])
                        p.act(lambda e, nt=nt, i=i, h=h: e.activation(out=obf[i][:nt, h, :], in_=ob[i][:nt, h, :], func=AF.Square,
                                                                      accum_out=s4[i][:nt, h:h + 1]),
                              reads=[r_ob[i]], writes=[r_obf[i], r_s4[i]])
                    p.act(lambda e, nt=nt, i=i: e.activation(out=s4[i][:nt, 8:12], in_=s4[i][:nt, 0:4], func=AF.Ln,
                                                             scale=1.0 / 128, bias=epst[:nt, 0:1]),
                          reads=[r_s4[i], r_const], writes=[r_s4[i]])
                    p.act(lambda e, nt=nt, i=i: e.activation(out=s4[i][:nt, 4:8], in_=s4[i][:nt, 8:12], func=AF.Exp, scale=-0.5),
                          reads=[r_s4[i]], writes=[r_s4[i]])
                    for h in range(4):
                        p.dve(lambda e, nt=nt, i=i, h=h: e.tensor_scalar(out=obf[i][:nt, h, :], in0=ob[i][:nt, h, :],
                                                                         scalar1=s4[i][:nt, 4 + h:5 + h], scalar2=None,
                                                                         op0=ALU.mult),
                              reads=[r_ob[i], r_s4[i]], writes=[r_obf[i]])
                    pt = pbank_bf[0]
                    for h in range(4):
                        p.pe(lambda e, nt=nt, i=i, h=h: e.transpose(out=pt[:, h * 128:h * 128 + nt], in_=obf[i][:nt, h, :],
                                                                    identity=ident[:nt, :nt]),
                             reads=[r_obf[i], r_const], writes=[r_pb[0]])
                    p.dve(lambda e, X=X, nt=nt, o=o: e.tensor_scalar(
                        out=mixYb[:, :, X, o:o + nt], in0=pt[:, 0:512].rearrange("p (h t) -> p h t", h=4)[:, :, :nt],
                        scalar1=subgt[:, 0:1], scalar2=None, op0=ALU.mult),
                        reads=[r_pb[0], r_const], writes=[r_mixT[X][t]])

        p.barrier()
        n2T = SB(outer, "n2T", [128, 8, 2, NQ], BF16)
        r_n2T = [[R(f"n2T{X}_{t}") for t in range(9)] for X in range(2)]
        r_hmid = [[R(f"hmid{X}_{t}") for t in range(9)] for X in range(2)]
        with contextlib.ExitStack() as c1:
            wu = SB(c1, "wu", [128, 8, 512], BF16)
            r_wu_l = [R(f"wu{c}") for c in range(8)]
            wo = SB(c1, "wo", [128, 8, D], BF16)
            r_wo_l = [R(f"wo{c}") for c in range(8)]
            mixYa = SB(c1, "mixYa", [128, 4, 2, NQ], BF16)
            gffn_b = SB(c1, "gffn_b", [128, D], F32)
            r_gffn = R("gffn_b")
            p.dma(lambda e: e.dma_start(out=gffn_b[:], in_=gffn_d.partition_broadcast(128), allow_slow_non_contiguous=True),
                  writes=[r_gffn], q="sp")
            wpl = SB(c1, "wpl", [128, 4, 128], BF16)
            r_wpl = R("wpl")
            w3 = w_in.rearrange("(c p) n -> p c n", p=128)
            wo3 = w_out.rearrange("(c p) n -> p c n", p=128)
            for c in range(8):
                p.dma(lambda e, c=c: e.dma_start(out=wu[:, c, :], in_=w3[:, c, 0:512]), writes=[r_wu_l[c]], q="pool")
            for c in range(8):
                p.dma(lambda e, c=c: e.dma_start(out=wo[:, c, :], in_=wo3[:, c, :]), writes=[r_wo_l[c]], q="pool")
            p.dma(lambda e: e.dma_start(out=wpl[:, :, :], in_=w_pool.rearrange("g c d -> c g d")), writes=[r_wpl], q="pool")

            PADW = 16
            uT = SB(c1, "uT", [128, 4, PADW + NQ], F32)
            r_uT = R("uT")
            sA = SB(c1, "sA", [128, PADW + NQ], F32)
            sB = SB(c1, "sB", [128, PADW + NQ], F32)
            r_sA = R("sA")
            r_sB = R("sB")
            plb = SB(c1, "plb", [128, 4, NQ], BF16)
            r_plb = R("plb")
            hm = [SB(c1, f"hm{i}", [128, D], F32) for i in range(3)]
            r_hm = [R(f"hm{i}") for i in range(3)]
            cnt["hm"] = 0
            cnt["wb"] = 0
            cnt["x2"] = 0
            xs2 = [SB(c1, f"xs2_{i}", [128, D], BF16) for i in range(2)]
            r_xs2 = [R(f"xs2_{i}") for i in range(2)]
            st3 = [SB(c1, f"st3_{i}", [128, 4], F32) for i in range(2)]
            r_st3 = [R(f"st3_{i}") for i in range(2)]
            pcb = SB(c1, "pcb", [128, NQ], F32)
            r_pcb = R("pcb")
            p.dve(lambda e: e.memset(uT[:], 0.0), writes=[r_uT])
            p.dve(lambda e: e.memset(sA[:], 0.0), writes=[r_sA])
            p.dve(lambda e: e.memset(sB[:], 0.0), writes=[r_sB])
            def build_chunk(X):
                ucs = [dict(T=QT_SIZES[t], o=QT_OFFS[t]) for t in range(9)]

                def u_f1(c, X=X):
                    i = cnt["fe"] % NFE
                    cnt["fe"] += 1
                    c["xi"] = i
                    T, o = c["T"], c["o"]
                    p.dma(lambda e: e.dma_start(out=xt[i][:T, :], in_=xq[X, o:o + T, :]), writes=[r_xt[i]], q="sp")
                    rms_rows(xt[i], r_xt[i], T, xs[i], r_xs[i], st2[i], r_st2[i], gmix_b)

                def u_f2(c, X=X):
                    ni = cnt["nt"] % NNT
                    cnt["nt"] += 1
                    c["ni"] = ni
                    transpose_rows(xs[c["xi"]], r_xs[c["xi"]], c["T"], nT[ni], r_nT[ni], bank=0)

                def u_f3(c, X=X):
                    T, o, ni = c["T"], c["o"], c["ni"]
                    ub = 1 + (cnt["ub"] % 2)
                    cnt["ub"] += 1
                    pu = pbank[ub]
                    for g in range(4):
                        for cc in range(8):
                            p.pe(lambda e, g=g, cc=cc: e.matmul(pu[:, g * 128:g * 128 + T], lhsT=wu[:, cc, g * 128:(g + 1) * 128],
                                                                rhs=nT[ni][:, cc, :T], start=(cc == 0), stop=(cc == 7)),
                                 reads=[r_wu_l[cc], r_nT[ni]], writes=[r_pb[ub]])
                    p.act(lambda e: e.activation(out=uT[:, :, PADW + o:PADW + o + T],
                                                 in_=pu[:, :].rearrange("p (g t) -> p g t", g=4)[:, :, :T],
                                                 func=AF.Copy), reads=[r_pb[ub]], writes=[r_uT])

                def p1_it(it):
                    if it < 9:
                        u_f1(ucs[it])
                    if 0 <= it - 1 < 9:
                        u_f2(ucs[it - 1])
                    if 0 <= it - 2 < 9:
                        u_f3(ucs[it - 2])
                def mid():
                    for g, w in enumerate((2, 4, 8, 16)):
                        src = uT[:, g, :]
                        r_src = r_uT
                        bufs = [(sA, r_sA), (sB, r_sB)]
                        sh = 1
                        bi = 0
                        while sh < w:
                            dst, r_dst = bufs[bi]
                            p.dve(lambda e, src=src, dst=dst, sh=sh: e.tensor_tensor(
                                out=dst[:, PADW:PADW + NQ], in0=src[:, PADW:PADW + NQ], in1=src[:, PADW - sh:PADW + NQ - sh],
                                op=ALU.add), reads=[r_src], writes=[r_dst])
                            src, r_src = dst, r_dst
                            sh *= 2
                            bi ^= 1
                        if w < 16:
                            p.dve(lambda e, src=src, g=g, w=w: e.scalar_tensor_tensor(
                                out=plb[:, g, :], in0=src[:, PADW:PADW + NQ], scalar=1.0 / w, in1=uT[:, g, PADW:PADW + NQ],
                                op0=ALU.mult, op1=ALU.subtract), reads=[r_src, r_uT], writes=[r_plb])
                        else:
                            p.dma(lambda e, X=X: e.dma_start(out=pcb[:, :], in_=pc16[X, :].partition_broadcast(128),
                                                             allow_slow_non_contiguous=True), writes=[r_pcb], q="sp")
                            p.dve(lambda e, src=src: e.tensor_tensor(out=src[:, PADW:PADW + NQ], in0=src[:, PADW:PADW + NQ],
                                                                     in1=pcb[:, :], op=ALU.mult),
                                  reads=[r_src, r_pcb], writes=[r_src])
                            p.dve(lambda e, src=src, g=g: e.tensor_tensor(out=plb[:, g, :], in0=src[:, PADW:PADW + NQ],
                                                                          in1=uT[:, g, PADW:PADW + NQ], op=ALU.subtract),
                                  reads=[r_src, r_uT], writes=[r_plb])
                    for g in range(4):
                        for (c0, n) in ((0, 352), (352, 352), (704, 352)):
                            pq = pbank[3]
                            p.pe(lambda e, g=g, c0=c0, n=n: e.matmul(pq[:, 0:n], lhsT=wpl[:, g, :], rhs=plb[:, g, c0:c0 + n],
                                                                     start=True, stop=True),
                                 reads=[r_wpl, r_plb], writes=[r_pb[3]])
                            p.act(lambda e, g=g, c0=c0, n=n, X=X: e.activation(out=mixYa[:, g, X, c0:c0 + n], in_=pq[:, 0:n],
                                                                              func=AF.Identity, scale=pst[:, g:g + 1],
                                                                              bias=bps[:, g:g + 1]),
                                  reads=[r_pb[3], r_const], writes=[r_mixYa[X]])
                wcs = [dict(t=t, T=QT_SIZES[t], o=QT_OFFS[t]) for t in range(9)]

                def w_g1(c, X=X):
                    t, T, o = c["t"], c["T"], c["o"]
                    i = cnt["hm"] % 3
                    cnt["hm"] += 1
                    c["hi"] = i
                    par = cnt["wb"] % 2
                    cnt["wb"] += 1
                    p.dma(lambda e: e.dma_start(out=hm[i][:T, :], in_=xq[X, o:o + T, :]), writes=[r_hm[i]], q="sp")
                    for half in range(2):
                        bk = 4 + 2 * par + half
                        ph = pbank[bk]
                        for fc in range(8):
                            p.pe(lambda e, fc=fc, half=half, ph=ph: e.matmul(
                                ph[:T, :], lhsT=(mixYa[:, fc, X, o:o + T] if fc < 4 else mixYb[:, fc - 4, X, o:o + T]),
                                rhs=wo[:, fc, half * 512:(half + 1) * 512], start=(fc == 0), stop=(fc == 7)),
                                reads=[r_wo_l[fc], r_mixT[X][t], r_mixYa[X]], writes=[r_pb[bk]])
                        p.dve(lambda e, half=half, ph=ph: e.tensor_tensor(
                            out=hm[i][:T, half * 512:(half + 1) * 512], in0=ph[:T, :], in1=hm[i][:T, half * 512:(half + 1) * 512],
                            op=ALU.add), reads=[r_pb[bk], r_hm[i]], writes=[r_hm[i]])
                    p.dma(lambda e: e.dma_start(out=hmid[X, o:o + T, :], in_=hm[i][:T, :]),
                          reads=[r_hm[i]], writes=[r_hmid[X][t]], q="pool")

                def w_g2(c, X=X):
                    i = c["hi"]
                    j = cnt["x2"] % 2
                    cnt["x2"] += 1
                    c["xj"] = j
                    rms_rows(hm[i], r_hm[i], c["T"], xs2[j], r_xs2[j], st3[j], r_st3[j], gffn_b, r_gffn)

                def w_g3(c, X=X):
                    t, T, o, j = c["t"], c["T"], c["o"], c["xj"]
                    pt = pbank_bf[3]
                    for cc in range(8):
                        p.pe(lambda e, cc=cc: e.transpose(out=pt[:, cc * 128:cc * 128 + T],
                                                          in_=xs2[j][:T, cc * 128:(cc + 1) * 128], identity=ident[:T, :T]),
                             reads=[r_xs2[j], r_const], writes=[r_pb[3]])
                    p.dve(lambda e: e.tensor_copy(out=n2T[:, :, X, o:o + T],
                                                  in_=pt[:, :].rearrange("p (c t) -> p c t", c=8)[:, :, :T]),
                          reads=[r_pb[3]], writes=[r_n2T[X][t]])

                def p3_it(it):
                    if it < 9:
                        w_g1(wcs[it])
                    if 0 <= it - 1 < 9:
                        w_g2(wcs[it - 1])
                    if 0 <= it - 2 < 9:
                        w_g3(wcs[it - 2])
                return p1_it, mid, p3_it

            chA = build_chunk(0)
            chB = build_chunk(1)
            for it in range(11):
                chA[0](it)
            chA[1]()
            for it in range(11):
                chA[2](it)
                chB[0](it)
            chB[1]()
            for it in range(11):
                chB[2](it)

        p.barrier()
        outs = []
        with contextlib.ExitStack() as c2:
            NFB = FB_PER_PASS
            wup = SB(c2, "wup", [128, 8, 2, NFB * 128], BF16)
            wdn = SB(c2, "wdn", [128, NFB, D], BF16)
            FGRP = [(0, 4), (4, 8), (8, NFB)]
            r_wup_l = [[R(f"wup{s_}_{k}") for k in range(len(FGRP))] for s_ in range(2)]
            r_wdn_l = [R(f"wdn{f}") for f in range(NFB)]
            Gt = [SB(c2, f"Gt{i}", [128, NFB, FG], BF16) for i in range(2)]
            r_Gt = [R(f"Gt{i}") for i in range(2)]
            cbuf = [[SB(c2, f"cbuf{i}_{s}", [128, FG], F32) for s in range(2)] for i in range(2)]
            r_cbuf = [[R(f"cbuf{i}_{s}") for s in range(2)] for i in range(2)]
            sg = [SB(c2, f"sg{i}", [128, FG], F32) for i in range(2)]
            r_sg = [R(f"sg{i}") for i in range(2)]
            yt = [SB(c2, f"yt{i}", [128, D], F32) for i in range(2)]
            r_yt = [R(f"yt{i}") for i in range(2)]
            r_yst = [R(f"yst{i}") for i in range(2)]
            r_y = [[R(f"y{X}_{t}") for t in range(8)] for X in range(2)]
            wu3 = w_up.rearrange("(c p) n -> p c n", p=128)
            wd3 = w_down.rearrange("(f p) n -> p f n", p=128)
            wk = 0
            gk = 0
            for ps_ in range(NPASS):
                fb0 = ps_ * NFB
                for k, (fa, fz) in enumerate(FGRP):
                    for s in range(2):
                        col0 = s * DFF + (fb0 + fa) * 128
                        p.dma(lambda e, s=s, col0=col0, fa=fa, fz=fz: e.dma_start(
                            out=wup[:, :, s, fa * 128:fz * 128], in_=wu3[:, :, col0:col0 + (fz - fa) * 128]),
                            writes=[r_wup_l[s][k]], q="pool")
                for f in range(NFB):
                    p.dma(lambda e, f=f, fb0=fb0: e.dma_start(out=wdn[:, f, :], in_=wd3[:, fb0 + f, :]), writes=[r_wdn_l[f]], q="pool")
                def up_fb(g, f):
                    X, t0, gb, rn, W = g["X"], g["t0"], g["gb"], g["rn"], g["W"]
                    fb = fb0 + f
                    ci = f % 2
                    for s_ in range(2):
                        pu = pbank[ci * 2 + s_]
                        for c in range(8):
                            p.pe(lambda e, c=c, s_=s_, pu=pu: e.matmul(
                                pu[:, 0:W + 2], lhsT=wup[:, c, s_, f * 128:(f + 1) * 128],
                                rhs=n2T[:, c, X, t0 - 2:t0 + W], start=(c == 0), stop=(c == 7)),
                                reads=[r_wup_l[s_][[k for k, (fa, fz) in enumerate(FGRP) if fa <= f < fz][0]]] + rn,
                                writes=[r_pb[ci * 2 + s_]])
                        col = s_ * 22 + fb
                        cbt = cbuf[ci][s_]
                        rcb = r_cbuf[ci][s_]
                        p.act(lambda e, pu=pu, cbt=cbt, col=col: e.activation(
                            out=cbt[:, 0:W], in_=pu[:, 2:W + 2], func=AF.Identity, scale=cw[:, 2, col:col + 1],
                            bias=cb[:, col:col + 1]), reads=[r_pb[ci * 2 + s_], r_const], writes=[rcb])
                        p.dve(lambda e, pu=pu, cbt=cbt, col=col: e.scalar_tensor_tensor(
                            out=cbt[:, 0:W], in0=pu[:, 1:W + 1], scalar=cw[:, 1, col:col + 1], in1=cbt[:, 0:W],
                            op0=ALU.mult, op1=ALU.add), reads=[r_pb[ci * 2 + s_], rcb, r_const], writes=[rcb])
                        p.dve(lambda e, pu=pu, cbt=cbt, col=col: e.scalar_tensor_tensor(
                            out=cbt[:, 0:W], in0=pu[:, 0:W], scalar=cw[:, 0, col:col + 1], in1=cbt[:, 0:W],
                            op0=ALU.mult, op1=ALU.add), reads=[r_pb[ci * 2 + s_], rcb, r_const], writes=[rcb])
                    p.act(lambda e: e.activation(out=sg[ci][:, 0:W], in_=cbuf[ci][0][:, 0:W], func=AF.Silu),
                          reads=[r_cbuf[ci][0]], writes=[r_sg[ci]])
                    p.dve(lambda e: e.tensor_tensor(out=Gt[gb][:, f, 0:W], in0=sg[ci][:, 0:W], in1=cbuf[ci][1][:, 0:W], op=ALU.mult),
                          reads=[r_sg[ci], r_cbuf[ci][1]], writes=[r_Gt[gb]])

                def down_unit(g, q):
                    X, gb = g["X"], g["gb"]
                    tt = g["tiles"][q]
                    yrow = (tt - 1) * 128
                    yi = q % 2
                    if ps_ == 0:
                        p.dma(lambda e: e.dma_start(out=yt[yi][:, :], in_=hmid[X, QT_OFFS[tt]:QT_OFFS[tt] + 128, :]),
                              reads=[r_hmid[X][tt]], writes=[r_yt[yi]], q="sp")
                    else:
                        p.dma(lambda e: e.dma_start(out=yt[yi][:, :], in_=y[X, yrow:yrow + 128, :]),
                              reads=[r_y[X][tt - 1]], writes=[r_yt[yi]], q="sp")
                    for half in range(2):
                        bk = 4 + 2 * (q % 2) + half
                        pd = pbank[bk]
                        for f in range(NFB):
                            p.pe(lambda e, f=f, half=half, pd=pd: e.matmul(
                                pd[:, :], lhsT=Gt[gb][:, f, q * 128:(q + 1) * 128],
                                rhs=wdn[:, f, half * 512:(half + 1) * 512], start=(f == 0), stop=(f == NFB - 1)),
                                reads=[r_Gt[gb], r_wdn_l[f]], writes=[r_pb[bk]])
                        p.dve(lambda e, half=half, pd=pd: e.tensor_tensor(
                            out=yt[yi][:, half * 512:(half + 1) * 512], in0=pd[:, :],
                            in1=yt[yi][:, half * 512:(half + 1) * 512], op=ALU.add),
                            reads=[r_pb[bk], r_yt[yi]], writes=[r_yt[yi]])
                    od = p.dma(lambda e: e.dma_start(out=y[X, yrow:yrow + 128, :], in_=yt[yi][:, :]),
                               reads=[r_yt[yi]], writes=[r_y[X][tt - 1]], q="pool", sem_res=r_yst[yi])
                    if ps_ == NPASS - 1:
                        outs.append(od)

                groups = []
                for X in range(2):
                    for tiles in ([1, 2, 3], [4, 5, 6], [7, 8]):
                        groups.append(dict(X=X, t0=QT_OFFS[tiles[0]], gb=gk % 2, tiles=tiles, W=128 * len(tiles),
                                           rn=[r_n2T[X][t] for t in tiles] + [r_n2T[X][tiles[0] - 1]]))
                        gk += 1
                slots = {2: 0, 5: 1, 8: 2}
                prev = None
                for g in groups:
                    for f in range(NFB):
                        up_fb(g, f)
                        if prev is not None and f in slots and slots[f] < len(prev["tiles"]):
                            down_unit(prev, slots[f])
                    prev = g
                for q in range(len(prev["tiles"])):
                    down_unit(prev, q)
            p.emit(final_waits=outs)
    return nc


_NC_CACHE = {}


def _host_layout(x, meta_tokens):
    per_core = []
    for core in range(8):
        b, j = core // 4, core % 4
        xb = x[b]
        chunks = (j, 7 - j)
        xq = np.zeros((2, NQ, D), np.float32)
        qval = np.ones((2, NQ), np.float32)
        mval = np.ones((2, 16), np.float32)
        for X, c in enumerate(chunks):
            s = 1024 * c
            xq[X, 32:] = xb[s:s + 1024]
            if c == 0:
                xq[X, 16:32] = meta_tokens
                qval[X, 0:16] = 0.0
                mval[X, :] = 0.0
            else:
                xq[X, 0:32] = xb[s - 32:s]
        xk = np.zeros((7, 1024, D), np.float32)
        kval = np.zeros((7, 1024), np.float32)
        sel = np.zeros((14,), np.float32)
        pc16 = np.full((2, NQ), 1.0 / 16.0, np.float32)
        if chunks[0] == 0:
            for pp in range(15):
                pc16[0, 16 + pp] = 1.0 / float(pp + 1)
        for i in range(7):
            if i < j:
                X, c, t = 0, chunks[0], i
            else:
                X, c, t = 1, chunks[1], i - j
            lim = 1024 * c - 32
            lo = 1024 * t
            hi = min(lo + 1024, lim)
            xk[i, :hi - lo] = xb[lo:hi]
            kval[i, :hi - lo] = 1.0
            sel[7 * X + i] = 1.0
        per_core.append(dict(xq=xq, xk=xk, xm=np.ascontiguousarray(meta_tokens, np.float32), kval=kval, qval=qval,
                             mval=mval, sel=sel, pc16=pc16))
    return per_core


def kernel(x, meta_tokens, norm_mix_g, w_in, w_pool, b_pool, pool_scale, q_norm_g, k_norm_g,
           lambda_q1, lambda_k1, lambda_q2, lambda_k2, subln_g, w_out, norm_ffn_g,
           w_up, conv_w, conv_b, w_down):
    f = lambda a: np.ascontiguousarray(np.asarray(a, np.float32))
    x = f(x)
    meta_tokens = f(meta_tokens)
    shared = {
        "w_in": f(w_in)[0], "w_pool": f(w_pool)[0], "b_pool": f(b_pool)[0], "pool_scale": f(pool_scale)[0],
        "q_norm_g": f(q_norm_g)[0], "k_norm_g": f(k_norm_g)[0], "lambda_q1": f(lambda_q1)[0],
        "lambda_k1": f(lambda_k1)[0], "lambda_q2": f(lambda_q2)[0], "lambda_k2": f(lambda_k2)[0],
        "subln_g": f(subln_g)[0], "w_out": f(w_out)[0], "norm_mix_g": f(norm_mix_g)[0],
        "norm_ffn_g": f(norm_ffn_g)[0], "w_up": f(w_up)[0], "conv_w": f(conv_w)[0], "conv_b": f(conv_b)[0],
        "w_down": f(w_down)[0],
    }
    if "nc" not in _NC_CACHE:
        _NC_CACHE["nc"] = build_program()
    nc = _NC_CACHE["nc"]
    per_core = _host_layout(x, meta_tokens)
    in_maps = [dict(shared, **pc) for pc in per_core]
    res = run_bass_kernel_spmd(nc, in_maps, core_ids=list(range(8)))
    out = np.empty((2, 8192, D), np.float32)
    for core in range(8):
        b, j = core // 4, core % 4
        yc = res.results[core]["y"]
        out[b, 1024 * j:1024 * (j + 1)] = yc[0]
        out[b, 1024 * (7 - j):1024 * (8 - j)] = yc[1]
    return out
```

```python
import contextlib
import numpy as np
import concourse.bass as bass
import concourse.mybir as mybir
from concourse.bass_utils import run_bass_kernel_spmd

F32 = mybir.dt.float32
BF16 = mybir.dt.bfloat16
AF = mybir.ActivationFunctionType
ALU = mybir.AluOpType
AX = mybir.AxisListType

ENGS = ("pe", "act", "dve", "pool", "sp")


class Res:
    __slots__ = ("name", "last_w", "readers", "dma_sem", "dma_cnt", "last_dma")

    def __init__(self, name):
        self.name = name
        self.last_dma = None
        self.last_w = None
        self.readers = []
        self.dma_sem = None
        self.dma_cnt = 0


class Ins:
    __slots__ = ("eng", "fn", "deps", "inc_val", "is_dma", "dma_res", "dma_val", "needed")

    def __init__(self, eng, fn):
        self.eng = eng
        self.fn = fn
        self.deps = []
        self.inc_val = None
        self.is_dma = False
        self.dma_res = None
        self.dma_val = 0
        self.needed = False


class Prog:
    def __init__(self, nc):
        self.nc = nc
        self.streams = {e: [] for e in ENGS}
        self.all_res = []
        self.barrier_deps = []
        self.barrier_seen = {e: True for e in ENGS}

    def res(self, name):
        r = Res(name)
        self.all_res.append(r)
        return r

    def barrier(self):
        deps = []
        for e in ENGS:
            if self.streams[e]:
                deps.append(self.streams[e][-1])
        for r in self.all_res:
            if r.last_dma is not None:
                deps.append(r.last_dma)
        self.barrier_deps = deps
        self.barrier_seen = {e: False for e in ENGS}

    def _add(self, eng, fn, reads, writes, is_dma=False, sem_res=None):
        ins = Ins(eng, fn)
        ins.is_dma = is_dma
        deps = []
        if not self.barrier_seen[eng]:
            self.barrier_seen[eng] = True
            deps.extend(self.barrier_deps)
        raw = set()
        for r in reads:
            if r.last_w is not None:
                deps.append(r.last_w)
                raw.add(id(r.last_w))
        for r in writes:
            if r.last_w is not None:
                deps.append(r.last_w)
            deps.extend(r.readers)
        seen = set()
        for d in deps:
            if d is ins or id(d) in seen:
                continue
            seen.add(id(d))
            if d.eng == eng and not d.is_dma and eng in ("pe", "sp"):
                continue
            if d.eng == eng and not d.is_dma and id(d) not in raw:
                continue
            ins.deps.append(d)
            d.needed = True
        for r in reads:
            r.readers.append(ins)
        for r in writes:
            r.last_w = ins
            r.readers = []
        if is_dma:
            tgt = sem_res if sem_res is not None else writes[0]
            ins.dma_res = tgt
            tgt.dma_cnt += 16
            ins.dma_val = tgt.dma_cnt
            tgt.last_dma = ins
        self.streams[eng].append(ins)
        return ins

    def pe(self, fn, reads=(), writes=()):
        return self._add("pe", fn, list(reads), list(writes))

    def act(self, fn, reads=(), writes=()):
        return self._add("act", fn, list(reads), list(writes))

    def dve(self, fn, reads=(), writes=()):
        return self._add("dve", fn, list(reads), list(writes))

    def pool(self, fn, reads=(), writes=()):
        return self._add("pool", fn, list(reads), list(writes))

    def dma(self, fn, reads=(), writes=(), q="sp", sem_res=None):
        return self._add(q, fn, list(reads), list(writes), is_dma=True, sem_res=sem_res)

    def emit(self, final_waits=()):
        nc = self.nc
        with contextlib.ExitStack() as st:
            sems = {}
            for e in ENGS:
                sems[e] = st.enter_context(nc.semaphore("s_" + e))
            for r in self.all_res:
                if r.dma_cnt > 0:
                    r.dma_sem = st.enter_context(nc.semaphore("d_" + r.name))
            for e in ENGS:
                c = 0
                for ins in self.streams[e]:
                    if ins.is_dma:
                        continue
                    if ins.needed:
                        c += 1
                        ins.inc_val = c
            block = st.enter_context(nc.Block())
            engmap = {"pe": "tensor", "act": "scalar", "dve": "vector", "pool": "gpsimd", "sp": "sync"}

            def make(e):
                def body(eng):
                    waited = {}
                    for ins in self.streams[e]:
                        for d in ins.deps:
                            if d.is_dma:
                                key = ("d", id(d.dma_res))
                                val = d.dma_val
                                sem = d.dma_res.dma_sem
                            else:
                                key = ("e", d.eng)
                                val = d.inc_val
                                sem = sems[d.eng]
                            if waited.get(key, 0) >= val:
                                continue
                            waited[key] = val
                            eng.wait_ge(sem, val)
                        r = ins.fn(eng)
                        if ins.is_dma:
                            r.then_inc(ins.dma_res.dma_sem, 16)
                        elif ins.needed:
                            r.then_inc(sems[e], 1)
                    if e == "sp":
                        for d in final_waits:
                            eng.wait_ge(d.dma_res.dma_sem, d.dma_val)
                return body

            for e in ENGS:
                getattr(block, engmap[e])(make(e))


D = 1024
NQ = 1056
QT_SIZES = [32] + [128] * 8
QT_OFFS = [0] + [32 + 128 * i for i in range(8)]
GROUPS = [(0, 1, 2), (3, 4, 5), (6, 7, 8)]
DFF = 2816
EPS = 1e-6
LAM_INIT = 0.2
NEG = -30000.0
FG = 384
NPASS = 2
FB_PER_PASS = 22 // NPASS


def build_program():
    nc = bass.Bass("TRN2", target_bir_lowering=False)
    dr = lambda name, shape, kind="ExternalInput": nc.dram_tensor(name, shape, F32, kind=kind).ap()
    xq = dr("xq", [2, NQ, D])
    xk = dr("xk", [7, 1024, D])
    xm = dr("xm", [16, D])
    kval = dr("kval", [7, 1024])
    qval = dr("qval", [2, NQ])
    mval = dr("mval", [2, 16])
    sel = dr("sel", [14])
    pc16 = dr("pc16", [2, NQ])
    w_in = dr("w_in", [D, 2048])
    w_pool = dr("w_pool", [4, 128, 128])
    b_pool = dr("b_pool", [4, 128])
    pool_scale = dr("pool_scale", [512])
    qg = dr("q_norm_g", [64])
    kg = dr("k_norm_g", [64])
    lq1 = dr("lambda_q1", [64])
    lk1 = dr("lambda_k1", [64])
    lq2 = dr("lambda_q2", [64])
    lk2 = dr("lambda_k2", [64])
    subg = dr("subln_g", [128])
    w_out = dr("w_out", [D, D])
    gmix_d = dr("norm_mix_g", [D])
    gffn_d = dr("norm_ffn_g", [D])
    w_up = dr("w_up", [D, 2 * DFF])
    conv_w = dr("conv_w", [3, 2 * DFF])
    conv_b = dr("conv_b", [2 * DFF])
    w_down = dr("w_down", [DFF, D])
    y = dr("y", [2, 1024, D], kind="ExternalOutput")
    hmid = dr("hmid", [2, NQ, D], kind="Internal")

    p = Prog(nc)
    R = p.res

    with contextlib.ExitStack() as outer:
        def SB(st, name, shape, dt):
            return st.enter_context(nc.sbuf_tensor(name, shape, dt))

        def PS(st, name, shape, dt):
            return st.enter_context(nc.psum_tensor(name, shape, dt))

        ident = SB(outer, "ident", [128, 128], BF16)
        identf = SB(outer, "identf", [128, 128], F32)
        maskneg = SB(outer, "maskneg", [128, 128], BF16)
        blk1 = SB(outer, "blk1", [128, 128], BF16)
        gmix_b = SB(outer, "gmix_b", [128, D], F32)
        qkg = SB(outer, "qkg", [128, 2], F32)
        selt = SB(outer, "selt", [128, 14], F32)
        lamt = SB(outer, "lamt", [128, 4, 64], F32)
        lamw = SB(outer, "lamw", [128, 8], F32)
        subgt = SB(outer, "subgt", [128, 1], F32)
        bpt = SB(outer, "bpt", [128, 4], F32)
        pst = SB(outer, "pst", [128, 4], F32)
        bps = SB(outer, "bps", [128, 4], F32)
        ones4 = SB(outer, "ones4", [128, 4], F32)
        zer = SB(outer, "zer", [128, 512], BF16)
        cw = SB(outer, "cw", [128, 3, 44], F32)
        cb = SB(outer, "cb", [128, 44], F32)
        mixYb = SB(outer, "mixYb", [128, 4, 2, NQ], BF16)
        r_const = R("const")
        r_mixT = [[R(f"mixT{X}_{t}") for t in range(9)] for X in range(2)]
        r_mixYa = [R(f"mixYa{X}") for X in range(2)]

        pb_h = [PS(outer, f"pb{i}", [128, 512], F32) for i in range(4)]
        sc_h = [PS(outer, f"sc{i}", [128, 1024], F32) for i in range(2)]
        pbank = list(pb_h) + [sc_h[i // 2][:, (i % 2) * 512:(i % 2 + 1) * 512] for i in range(4)]
        r_pb = [R(f"pb{i}") for i in range(8)]
        pbank_bf = [b.bitcast(BF16) for b in pb_h]

        p.pool(lambda e: e.memset(identf[:], 0.0), writes=[r_const])
        p.pool(lambda e: e.affine_select(out=identf[:], in_=identf[:], compare_op=ALU.not_equal, fill=1.0,
                                         base=0, pattern=[[-1, 128]], channel_multiplier=1),
               reads=[r_const], writes=[r_const])
        p.dve(lambda e: e.tensor_copy(out=ident[:], in_=identf[:]), reads=[r_const], writes=[r_const])
        p.pool(lambda e: e.memset(identf[:], 0.0), reads=[r_const], writes=[r_const])
        p.pool(lambda e: e.affine_select(out=identf[:], in_=identf[:], compare_op=ALU.is_ge, fill=NEG,
                                         base=0, pattern=[[1, 128]], channel_multiplier=-1),
               reads=[r_const], writes=[r_const])
        p.dve(lambda e: e.tensor_copy(out=maskneg[:], in_=identf[:]), reads=[r_const], writes=[r_const])
        p.dve(lambda e: e.memset(blk1[:], 0.0), reads=[r_const], writes=[r_const])
        p.dve(lambda e: e.memset(blk1[0:64, 0:64], 1.0), reads=[r_const], writes=[r_const])
        p.dve(lambda e: e.memset(blk1[64:128, 64:128], 1.0), reads=[r_const], writes=[r_const])
        p.dve(lambda e: e.memset(ones4[:], 1.0), reads=[r_const], writes=[r_const])
        p.dve(lambda e: e.memset(zer[:], 0.0), reads=[r_const], writes=[r_const])

        def small_dma(out_ap, in_ap):
            p.dma(lambda e: e.dma_start(out=out_ap, in_=in_ap, allow_slow_non_contiguous=True),
                  reads=[], writes=[r_const], q="sp")

        small_dma(gmix_b[:], gmix_d.partition_broadcast(128))
        small_dma(qkg[0:64, 0:1], qg.rearrange("(p o) -> p o", o=1))
        small_dma(qkg[64:128, 0:1], qg.rearrange("(p o) -> p o", o=1))
        small_dma(qkg[0:64, 1:2], kg.rearrange("(p o) -> p o", o=1))
        small_dma(qkg[64:128, 1:2], kg.rearrange("(p o) -> p o", o=1))
        small_dma(selt[:], sel.partition_broadcast(128))
        for i, l in enumerate((lq1, lk1, lq2, lk2)):
            small_dma(lamt[:, i, :], l.partition_broadcast(128))
        small_dma(subgt[:], subg.rearrange("(p o) -> p o", o=1))
        small_dma(bpt[:], b_pool.rearrange("g p -> p g"))
        small_dma(pst[:], pool_scale.rearrange("(g p) -> p g", p=128))
        small_dma(cw[:], conv_w.rearrange("k (f p) -> p k f", p=128))
        small_dma(cb[:], conv_b.rearrange("(f p) -> p f", p=128))
        p.dve(lambda e: e.tensor_scalar(out=qkg[:, 0:1], in0=qkg[:, 0:1], scalar1=0.125, scalar2=None, op0=ALU.mult),
              reads=[r_const], writes=[r_const])
        p.dve(lambda e: e.tensor_tensor(out=lamt[:, 0, :], in0=lamt[:, 0, :], in1=lamt[:, 1, :], op=ALU.mult),
              reads=[r_const], writes=[r_const])
        p.dve(lambda e: e.tensor_tensor(out=lamt[:, 2, :], in0=lamt[:, 2, :], in1=lamt[:, 3, :], op=ALU.mult),
              reads=[r_const], writes=[r_const])
        p.dve(lambda e: e.reduce_sum(out=lamw[:, 0:1], in_=lamt[:, 0, :], axis=AX.X), reads=[r_const], writes=[r_const])
        p.dve(lambda e: e.reduce_sum(out=lamw[:, 1:2], in_=lamt[:, 2, :], axis=AX.X), reads=[r_const], writes=[r_const])
        p.act(lambda e: e.activation(out=lamw[:, 2:4], in_=lamw[:, 0:2], func=AF.Exp), reads=[r_const], writes=[r_const])
        p.dve(lambda e: e.tensor_tensor(out=lamw[:, 4:5], in0=lamw[:, 3:4], in1=lamw[:, 2:3], op=ALU.subtract),
              reads=[r_const], writes=[r_const])
        p.dve(lambda e: e.tensor_scalar(out=lamw[:, 6:7], in0=lamw[:, 4:5], scalar1=-LAM_INIT, scalar2=None, op0=ALU.add),
              reads=[r_const], writes=[r_const])
        p.dve(lambda e: e.tensor_tensor(out=bps[:], in0=bpt[:], in1=pst[:], op=ALU.mult), reads=[r_const], writes=[r_const])
        p.dve(lambda e: e.tensor_scalar(out=subgt[:], in0=subgt[:], scalar1=1.0 - LAM_INIT, scalar2=None, op0=ALU.mult),
              reads=[r_const], writes=[r_const])

        NFE = 2
        xt = [SB(outer, f"xt{i}", [128, D], F32) for i in range(NFE)]
        r_xt = [R(f"xt{i}") for i in range(NFE)]
        xs = [SB(outer, f"xs{i}", [128, D], BF16) for i in range(NFE)]
        r_xs = [R(f"xs{i}") for i in range(NFE)]
        NNT = 4
        nT = [SB(outer, f"nT{i}", [128, 8, 128], BF16) for i in range(NNT)]
        r_nT = [R(f"nT{i}") for i in range(NNT)]
        st2 = [SB(outer, f"st2_{i}", [128, 4], F32) for i in range(NFE)]
        r_st2 = [R(f"st2_{i}") for i in range(NFE)]
        epst = SB(outer, "epst", [128, 1], F32)
        p.dve(lambda e: e.memset(epst[:], EPS), reads=[r_const], writes=[r_const])
        cnt = {"fe": 0, "nt": 0, "ub": 0}

        def rms_rows(src_t, r_src, T, dst_bf, r_dst, stt, r_stt, gb, r_gb=None):
            p.act(lambda e: e.activation(out=dst_bf[:T, :], in_=src_t[:T, :], func=AF.Square, accum_out=stt[:T, 0:1]),
                  reads=[r_src], writes=[r_dst, r_stt])
            p.act(lambda e: e.activation(out=stt[:T, 1:2], in_=stt[:T, 0:1], func=AF.Ln, scale=1.0 / D, bias=epst[:T, 0:1]),
                  reads=[r_stt, r_const], writes=[r_stt])
            p.act(lambda e: e.activation(out=stt[:T, 2:3], in_=stt[:T, 1:2], func=AF.Exp, scale=-0.5),
                  reads=[r_stt], writes=[r_stt])
            p.dve(lambda e: e.scalar_tensor_tensor(out=dst_bf[:T, :], in0=src_t[:T, :], scalar=stt[:T, 2:3], in1=gb[:T, :],
                                                   op0=ALU.mult, op1=ALU.mult),
                  reads=[r_src, r_stt, r_const] + ([r_gb] if r_gb else []), writes=[r_dst])

        def transpose_rows(src_bf, r_src, T, dstT, r_dstT, bank=0):
            pt = pbank_bf[bank]
            for c in range(8):
                p.pe(lambda e, c=c: e.transpose(out=pt[:, c * 128:c * 128 + T], in_=src_bf[:T, c * 128:(c + 1) * 128],
                                                identity=ident[:T, :T]),
                     reads=[r_src, r_const], writes=[r_pb[bank]])
            p.dve(lambda e: e.tensor_copy(out=dstT[:, :, :T], in_=pt[:, :].rearrange("p (c t) -> p c t", c=8)[:, :, :T]),
                  reads=[r_pb[bank]], writes=[r_dstT])

        def front_end(src_ap, T, tbank=0):
            i = cnt["fe"] % NFE
            cnt["fe"] += 1
            ni = cnt["nt"] % NNT
            cnt["nt"] += 1
            p.dma(lambda e: e.dma_start(out=xt[i][:T, :], in_=src_ap), writes=[r_xt[i]], q="sp")
            rms_rows(xt[i], r_xt[i], T, xs[i], r_xs[i], st2[i], r_st2[i], gmix_b)
            transpose_rows(xs[i], r_xs[i], T, nT[ni], r_nT[ni], bank=tbank)
            return ni

        with contextlib.ExitStack() as ab:
            wqkv = SB(ab, "wqkv", [128, 8, 1536], BF16)
            r_wqkv_l = [R(f"wqkv{c}") for c in range(8)]
            w3 = w_in.rearrange("(c p) n -> p c n", p=128)
            for c in range(8):
                p.dma(lambda e, c=c: e.dma_start(out=wqkv[:, c, :], in_=w3[:, c, 512:2048]), writes=[r_wqkv_l[c]], q="pool")

            NKV = 1
            KT = [SB(ab, f"KT{i}", [128, 4, NQ + 16], BF16) for i in range(NKV)]
            VX = [SB(ab, f"VX{i}", [128, 10, 4, 130], BF16) for i in range(NKV)]
            r_KT = [R(f"KT{i}") for i in range(NKV)]
            r_VX = [R(f"VX{i}") for i in range(NKV)]
            QT = [SB(ab, f"QT{X}", [128, 4, NQ], BF16) for X in range(2)]
            r_QT = [R(f"QT{X}") for X in range(2)]
            Qs = SB(ab, "Qs", [128, 4, NQ], BF16)
            r_Qs = R("Qs")
            OA = [SB(ab, f"O{X}", [128, 9, 4, 2, 129], F32) for X in range(2)]
            r_O = [[[R(f"O{X}_{t}_{h}") for h in range(4)] for t in range(9)] for X in range(2)]
            sqb = [SB(ab, f"sqb{i}", [128, 4, 128], BF16) for i in range(2)]
            r_sqb = [R(f"sqb{i}") for i in range(2)]
            lnb = [SB(ab, "lnb0", [128, 4, 128], F32)] * 2
            r_lnb = [R("lnb0")] * 2
            cnt["sq"] = 0
            cnt["kvt"] = 0
            valt = [SB(ab, f"valt{i}", [128, 1], F32) for i in range(4)]
            r_valt = [R(f"valt{i}") for i in range(4)]
            NPT = 3
            Pt = [SB(ab, f"Pt{i}", [128, 2, 384], BF16) for i in range(NPT)]
            r_Pt = [R(f"Pt{i}") for i in range(NPT)]
            cnt["val"] = 0
            cnt["pt"] = 0
            cnt["grp"] = 0
            cnt["sc"] = 0

            rsq = [[SB(ab, f"rsq{a}_{i}", [128, 4, 128], F32) for i in range(2)] for a in range(2)]
            r_rsq = [[R(f"rsq{a}_{i}") for i in range(2)] for a in range(2)]
            KB = [[1, 2], [4, 7]]
            TB2 = [0, 3]
            SBK, VBK, TBK = 5, 6, 0

            def kv_f1(c):
                i = cnt["fe"] % NFE
                cnt["fe"] += 1
                c["xi"] = i
                T = c["T"]
                p.dma(lambda e: e.dma_start(out=xt[i][:T, :], in_=c["src"]), writes=[r_xt[i]], q="sp")
                rms_rows(xt[i], r_xt[i], T, xs[i], r_xs[i], st2[i], r_st2[i], gmix_b)
                vi = cnt["val"] % 4
                cnt["val"] += 1
                c["vi"] = vi
                p.dma(lambda e: e.dma_start(out=valt[vi][:T, :], in_=c["val"]), writes=[r_valt[vi]], q="sp")

            def kv_f2(c):
                ni = cnt["nt"] % NNT
                cnt["nt"] += 1
                c["ni"] = ni
                i = c["xi"]
                transpose_rows(xs[i], r_xs[i], c["T"], nT[ni], r_nT[ni], bank=TB2[c["par"]])

            def kv_b1(c):
                T, ni, par = c["T"], c["ni"], c["par"]
                for a, col0 in ((0, 512), (1, 0)):
                    if a == 1 and c["own"] is None:
                        continue
                    bank = KB[a][par]
                    pk = pbank[bank]
                    for h in range(4):
                        for cc in range(8):
                            p.pe(lambda e, h=h, cc=cc, pk=pk, col0=col0: e.matmul(
                                pk[:, h * 128:h * 128 + T], lhsT=wqkv[:, cc, col0 + h * 128:col0 + (h + 1) * 128],
                                rhs=nT[ni][:, cc, :T], start=(cc == 0), stop=(cc == 7)),
                                reads=[r_wqkv_l[cc], r_nT[ni]], writes=[r_pb[bank]])
                    pk3 = pk[:, :].rearrange("p (h t) -> p h t", h=4)
                    j = cnt["sq"] % 2
                    cnt["sq"] += 1
                    p.act(lambda e, pk3=pk3, j=j: e.activation(out=sqb[j][:, :, :T], in_=pk3[:, :, :T], func=AF.Square),
                          reads=[r_pb[bank]], writes=[r_sqb[j]])
                    pss = pbank[SBK]
                    for h in range(4):
                        p.pe(lambda e, h=h, j=j, pss=pss: e.matmul(pss[:, h * 128:h * 128 + T], lhsT=blk1[:, :],
                                                                   rhs=sqb[j][:, h, :T], start=True, stop=True),
                             reads=[r_sqb[j], r_const], writes=[r_pb[SBK]])
                    pss3 = pss[:, :].rearrange("p (h t) -> p h t", h=4)
                    p.act(lambda e, pss3=pss3: e.activation(out=lnb[0][:, :, :T], in_=pss3[:, :, :T], func=AF.Ln,
                                                            scale=1.0 / 64, bias=epst[:, 0:1]),
                          reads=[r_pb[SBK], r_const], writes=[r_lnb[0]])
                    p.act(lambda e, a=a: e.activation(out=rsq[a][par][:, :, :T], in_=lnb[0][:, :, :T], func=AF.Exp, scale=-0.5),
                          reads=[r_lnb[0]], writes=[r_rsq[a][par]])

            def kv_b2(c):
                T, ni, par, kb, vidx, vi = c["T"], c["ni"], c["par"], c["kb"], c["vidx"], c["vi"]
                for a in range(2):
                    if a == 1 and c["own"] is None:
                        continue
                    bank = KB[a][par]
                    pk3 = pbank[bank][:, :].rearrange("p (h t) -> p h t", h=4)
                    if a == 0:
                        dst, r_dst, doff, gcol = KT[kb], r_KT[kb], c["koff"], 1
                    else:
                        X, doff = c["own"]
                        dst, r_dst, gcol = QT[X], r_QT[X], 0
                    p.dve(lambda e, pk3=pk3, dst=dst, doff=doff, gcol=gcol, a=a: e.scalar_tensor_tensor(
                        out=dst[:, :, doff:doff + T], in0=pk3[:, :, :T], scalar=qkg[:, gcol:gcol + 1],
                        in1=rsq[a][par][:, :, :T], op0=ALU.mult, op1=ALU.mult),
                        reads=[r_pb[bank], r_rsq[a][par], r_const], writes=[r_dst])
                pv = pbank[VBK]
                for cc in range(8):
                    p.pe(lambda e, cc=cc: e.matmul(pv[:T, :], lhsT=nT[ni][:, cc, :T], rhs=wqkv[:, cc, 1024:1536],
                                                   start=(cc == 0), stop=(cc == 7)),
                         reads=[r_wqkv_l[cc], r_nT[ni]], writes=[r_pb[VBK]])
                p.act(lambda e: e.activation(out=VX[kb][:T, vidx, :, 0:128],
                                             in_=pv[:T, :].rearrange("p (h d) -> p h d", h=4), func=AF.Copy,
                                             scale=valt[vi][:T, 0:1]),
                      reads=[r_pb[VBK], r_valt[vi]], writes=[r_VX[kb]])
                p.dve(lambda e: e.tensor_scalar(out=VX[kb][:T, vidx, :, 128:129], in0=ones4[:T, :].rearrange("p (h o) -> p h o", o=1),
                                                scalar1=valt[vi][:T, 0:1], scalar2=None, op0=ALU.mult),
                      reads=[r_valt[vi], r_const], writes=[r_VX[kb]])

            def run_kv(tiles):
                cs = []
                for (src, T, kb, koff, vidx, val, own) in tiles:
                    cs.append(dict(src=src, T=T, kb=kb, koff=koff, vidx=vidx, val=val, own=own, par=cnt["kvt"] % 2))
                    cnt["kvt"] += 1
                n = len(cs)
                for it in range(n + 3):
                    if it < n:
                        kv_f1(cs[it])
                    if 0 <= it - 1 < n:
                        kv_f2(cs[it - 1])
                    if 0 <= it - 2 < n:
                        kv_b1(cs[it - 2])
                    if 0 <= it - 3 < n:
                        kv_b2(cs[it - 3])

            def attention(kb, Qsrc, r_Qsrc, ktiles, diag, finish):
                steps = []
                for h in range(4):
                    for G in GROUPS:
                        g0 = QT_OFFS[G[0]]
                        g1 = QT_OFFS[G[-1]] + QT_SIZES[G[-1]]
                        kl = [k for k in ktiles if (not diag) or k[3] is None or k[3] <= G[-1]]
                        last_for = {}
                        for ki, (koff, nk, vidx, lt) in enumerate(kl):
                            for t in G:
                                if diag and lt is not None and lt > t:
                                    continue
                                last_for[t] = ki
                        for ki, k in enumerate(kl):
                            steps.append(dict(h=h, G=G, g0=g0, g1=g1, ki=ki, k=k, first=(ki == 0), last=(ki == len(kl) - 1),
                                              last_for=last_for, gidx=cnt["grp"]))
                        cnt["grp"] += 1

                def emit_score(st):
                    koff, nk, vidx, lt = st["k"]
                    G, h = st["G"], st["h"]
                    qs = max(st["g0"], QT_OFFS[lt]) if (diag and lt is not None) else st["g0"]
                    n = st["g1"] - qs
                    si = cnt["sc"] % 2
                    cnt["sc"] += 1
                    st.update(qs=qs, n=n, si=si)
                    for c in range(2):
                        sb = 4 + 2 * si + c
                        has_mask = diag and lt is not None and lt >= G[0]
                        p.pe(lambda e, c=c, sb=sb: e.matmul(
                            pbank[sb][:nk, 0:n], lhsT=KT[kb][64 * c:64 * c + 64, h, koff:koff + nk],
                            rhs=Qsrc[64 * c:64 * c + 64, h, qs:qs + n], start=True, stop=True),
                            reads=[r_KT[kb], r_Qsrc], writes=[r_pb[sb]])
                        if has_mask:
                            p.pe(lambda e, c=c, sb=sb: e.matmul(
                                pbank[sb][:nk, 0:nk], lhsT=ident[:nk, :nk], rhs=maskneg[:nk, :nk],
                                start=False, stop=True, skip_group_check=True),
                                reads=[r_const], writes=[r_pb[sb]])

                def emit_rest(st):
                    koff, nk, vidx, lt = st["k"]
                    G, h, qs, n, si, ki = st["G"], st["h"], st["qs"], st["n"], st["si"], st["ki"]
                    if st["first"]:
                        for t in G:
                            ab_ = (3 * st["gidx"] + (t - G[0])) % 4
                            p.pe(lambda e, ab_=ab_, nt=QT_SIZES[t]: e.matmul(pbank[ab_][:nt, 0:258], lhsT=zer[0:1, 0:nt],
                                                                            rhs=zer[0:1, 0:258], start=True, stop=True),
                                 reads=[r_const], writes=[r_pb[ab_]])
                    pi = cnt["pt"] % NPT
                    cnt["pt"] += 1
                    scv = sc_h[si][:nk, :].rearrange("p (c m) -> p c m", c=2)
                    p.act(lambda e, scv=scv: e.activation(out=Pt[pi][:nk, :, 0:n], in_=scv[:, :, 0:n], func=AF.Exp),
                          reads=[r_pb[4 + 2 * si], r_pb[5 + 2 * si]], writes=[r_Pt[pi]])
                    for t in G:
                        if diag and lt is not None and lt > t:
                            continue
                        nt = QT_SIZES[t]
                        po = QT_OFFS[t] - qs
                        ab_ = (3 * st["gidx"] + (t - G[0])) % 4
                        acc = pbank[ab_][:, 0:258].rearrange("p (c d) -> p c d", c=2)
                        for c in range(2):
                            p.pe(lambda e, c=c, nt=nt, po=po, acc=acc, f=(st["last_for"][t] == ki): e.matmul(
                                acc[:nt, c, :], lhsT=Pt[pi][:nk, c, po:po + nt], rhs=VX[kb][:nk, vidx, h, 0:129],
                                start=False, stop=f, skip_group_check=True),
                                reads=[r_Pt[pi], r_VX[kb]], writes=[r_pb[ab_]])
                    if st["last"]:
                        for t in G:
                            ab_ = (3 * st["gidx"] + (t - G[0])) % 4
                            acc = pbank[ab_][:, 0:258].rearrange("p (c d) -> p c d", c=2)
                            finish(t, h, acc, r_pb[ab_], QT_SIZES[t])

                emit_score(steps[0])
                for si_, st in enumerate(steps):
                    if si_ + 1 < len(steps):
                        emit_score(steps[si_ + 1])
                    emit_rest(st)

            for X in range(2):
                kb = X % NKV
                tl = []
                for t in range(9):
                    T = QT_SIZES[t]
                    o = QT_OFFS[t]
                    tl.append((xq[X, o:o + T, :], T, kb, o, t, qval[X, o:o + T].rearrange("(p o) -> p o", o=1), (X, o)))
                tl.append((xm[:, :], 16, kb, NQ, 9, mval[X, :].rearrange("(p o) -> p o", o=1), None))
                run_kv(tl)
                ktiles = [(NQ, 16, 9, None)] + [(QT_OFFS[t], QT_SIZES[t], t, t) for t in range(9)]

                def fin_diag(t, h, acc, r_acc, nt, X=X):
                    p.dve(lambda e: e.tensor_copy(out=OA[X][:nt, t, h, :, :], in_=acc[:nt, :, :]),
                          reads=[r_acc], writes=[r_O[X][t][h]])
                attention(kb, QT[X], r_QT[X], ktiles, True, fin_diag)

            for i in range(7):
                kb = i % NKV
                run_kv([(xk[i, kt * 128:(kt + 1) * 128, :], 128, kb, kt * 128, kt,
                         kval[i, kt * 128:(kt + 1) * 128].rearrange("(p o) -> p o", o=1), None) for kt in range(8)])
                p.dve(lambda e, i=i: e.tensor_scalar(out=Qs[:, :, :], in0=QT[0][:, :, :], scalar1=selt[:, i:i + 1],
                                                     scalar2=None, op0=ALU.mult),
                      reads=[r_QT[0], r_const], writes=[r_Qs])
                p.dve(lambda e, i=i: e.scalar_tensor_tensor(out=Qs[:, :, :], in0=QT[1][:, :, :], scalar=selt[:, 7 + i:8 + i],
                                                            in1=Qs[:, :, :], op0=ALU.mult, op1=ALU.add),
                      reads=[r_QT[1], r_Qs, r_const], writes=[r_Qs])
                ktiles = [(kt * 128, 128, kt, None) for kt in range(8)]

                def fin_full(t, h, acc, r_acc, nt, i=i):
                    for X in range(2):
                        p.dve(lambda e, X=X: e.scalar_tensor_tensor(
                            out=OA[X][:nt, t, h, :, :], in0=acc[:nt, :, :], scalar=selt[:nt, 7 * X + i:7 * X + i + 1],
                            in1=OA[X][:nt, t, h, :, :], op0=ALU.mult, op1=ALU.add),
                            reads=[r_acc, r_O[X][t][h], r_const], writes=[r_O[X][t][h]])
                attention(kb, Qs, r_Qs, ktiles, False, fin_full)

            ob = [SB(ab, f"ob{i}", [128, 4, 128], F32) for i in range(2)]
            r_ob = [R(f"ob{i}") for i in range(2)]
            obf = [SB(ab, f"obf{i}", [128, 4, 128], BF16) for i in range(2)]
            r_obf = [R(f"obf{i}") for i in range(2)]
            rl = [SB(ab, f"rl{i}", [128, 4, 2], F32) for i in range(2)]
            r_rl = [R(f"rl{i}") for i in range(2)]
            s4 = [SB(ab, f"s4{i}", [128, 12], F32) for i in range(2)]
            r_s4 = [R(f"s4{i}") for i in range(2)]
            k = 0
            for X in range(2):
                for t in range(9):
                    nt = QT_SIZES[t]
                    o = QT_OFFS[t]
                    i = k % 2
                    k += 1
                    rO = [r_O[X][t][h] for h in range(4)]
                    p.dve(lambda e, X=X, t=t, nt=nt, i=i: e.tensor_scalar(out=rl[i][:nt, :, :], in0=OA[X][:nt, t, :, :, 128],
                                                                          scalar1=1e-30, scalar2=None, op0=ALU.max),
                          reads=rO, writes=[r_rl[i]])
                    p.dve(lambda e, nt=nt, i=i: e.reciprocal(out=rl[i][:nt, :, :], in_=rl[i][:nt, :, :]),
                          reads=[r_rl[i]], writes=[r_rl[i]])
                    p.dve(lambda e, nt=nt, i=i: e.tensor_scalar(out=rl[i][:nt, :, 1:2], in0=rl[i][:nt, :, 1:2],
                                                                scalar1=lamw[:nt, 6:7], scalar2=None, op0=ALU.mult),
                          reads=[r_rl[i], r_const], writes=[r_rl[i]])
                    for h in range(4):
                        p.dve(lambda e, X=X, t=t, nt=nt, i=i, h=h: e.tensor_scalar(
                            out=ob[i][:nt, h, :], in0=OA[X][:nt, t, h, 0, 0:128], scalar1=rl[i][:nt, h, 0:1],
                            scalar2=None, op0=ALU.mult), reads=rO + [r_rl[i]], writes=[r_ob[i]])
                        p.dve(lambda e, X=X, t=t, nt=nt, i=i, h=h: e.scalar_tensor_tensor(
                            out=ob[i][:nt, h, :], in0=OA[X][:nt, t, h, 1, 0:128], scalar=rl[i][:nt, h, 1:2],
                            in1=ob[i][:nt, h, :], op0=ALU.mult, op1=ALU.add), reads=rO + [r_rl[i], r_ob[i]], writes=[r_ob[i]])
                        p.act(lambda e, nt=nt, i=i, h=h: e.activation(out=obf[i][:nt, h, :], in_=ob[i][:nt, h, :], func=AF.Square,
                                                                      accum_out=s4[i][:nt, h:h + 1]),
                              reads=[r_ob[i]], writes=[r_obf[i], r_s4[i]])
                    p.act(lambda e, nt=nt, i=i: e.activation(out=s4[i][:nt, 8:12], in_=s4[i][:nt, 0:4], func=AF.Ln,
                                                             scale=1.0 / 128, bias=epst[:nt, 0:1]),
                          reads=[r_s4[i], r_const], writes=[r_s4[i]])
                    p.act(lambda e, nt=nt, i=i: e.activation(out=s4[i][:nt, 4:8], in_=s4[i][:nt, 8:12], func=AF.Exp, scale=-0.5),
                          reads=[r_s4[i]], writes=[r_s4[i]])
                    for h in range(4):
                        p.dve(lambda e, nt=nt, i=i, h=h: e.tensor_scalar(out=obf[i][:nt, h, :], in0=ob[i][:nt, h, :],
                                                                         scalar1=s4[i][:nt, 4 + h:5 + h], scalar2=None,
                                                                         op0=ALU.mult),
                              reads=[r_ob[i], r_s4[i]], writes=[r_obf[i]])
                    pt = pbank_bf[0]
                    for h in range(4):
                        p.pe(lambda e, nt=nt, i=i, h=h: e.transpose(out=pt[:, h * 128:h * 128 + nt], in_=obf[i][:nt, h, :],
                                                                    identity=ident[:nt, :nt]),
                             reads=[r_obf[i], r_const], writes=[r_pb[0]])
                    p.dve(lambda e, X=X, nt=nt, o=o: e.tensor_scalar(
                        out=mixYb[:, :, X, o:o + nt], in0=pt[:, 0:512].rearrange("p (h t) -> p h t", h=4)[:, :, :nt],
                        scalar1=subgt[:, 0:1], scalar2=None, op0=ALU.mult),
                        reads=[r_pb[0], r_const], writes=[r_mixT[X][t]])

        p.barrier()
        n2T = SB(outer, "n2T", [128, 8, 2, NQ], BF16)
        r_n2T = [[R(f"n2T{X}_{t}") for t in range(9)] for X in range(2)]
        r_hmid = [[R(f"hmid{X}_{t}") for t in range(9)] for X in range(2)]
        with contextlib.ExitStack() as c1:
            wu = SB(c1, "wu", [128, 8, 512], BF16)
            r_wu_l = [R(f"wu{c}") for c in range(8)]
            wo = SB(c1, "wo", [128, 8, D], BF16)
            r_wo_l = [R(f"wo{c}") for c in range(8)]
            mixYa = SB(c1, "mixYa", [128, 4, 2, NQ], BF16)
            gffn_b = SB(c1, "gffn_b", [128, D], F32)
            r_gffn = R("gffn_b")
            p.dma(lambda e: e.dma_start(out=gffn_b[:], in_=gffn_d.partition_broadcast(128), allow_slow_non_contiguous=True),
                  writes=[r_gffn], q="sp")
            wpl = SB(c1, "wpl", [128, 4, 128], BF16)
            r_wpl = R("wpl")
            w3 = w_in.rearrange("(c p) n -> p c n", p=128)
            wo3 = w_out.rearrange("(c p) n -> p c n", p=128)
            for c in range(8):
                p.dma(lambda e, c=c: e.dma_start(out=wu[:, c, :], in_=w3[:, c, 0:512]), writes=[r_wu_l[c]], q="pool")
            for c in range(8):
                p.dma(lambda e, c=c: e.dma_start(out=wo[:, c, :], in_=wo3[:, c, :]), writes=[r_wo_l[c]], q="pool")
            p.dma(lambda e: e.dma_start(out=wpl[:, :, :], in_=w_pool.rearrange("g c d -> c g d")), writes=[r_wpl], q="pool")

            PADW = 16
            uT = SB(c1, "uT", [128, 4, PADW + NQ], F32)
            r_uT = R("uT")
            sA = SB(c1, "sA", [128, PADW + NQ], F32)
            sB = SB(c1, "sB", [128, PADW + NQ], F32)
            r_sA = R("sA")
            r_sB = R("sB")
            plb = SB(c1, "plb", [128, 4, NQ], BF16)
            r_plb = R("plb")
            hm = [SB(c1, f"hm{i}", [128, D], F32) for i in range(3)]
            r_hm = [R(f"hm{i}") for i in range(3)]
            cnt["hm"] = 0
            cnt["wb"] = 0
            cnt["x2"] = 0
            xs2 = [SB(c1, f"xs2_{i}", [128, D], BF16) for i in range(2)]
            r_xs2 = [R(f"xs2_{i}") for i in range(2)]
            st3 = [SB(c1, f"st3_{i}", [128, 4], F32) for i in range(2)]
            r_st3 = [R(f"st3_{i}") for i in range(2)]
            pcb = SB(c1, "pcb", [128, NQ], F32)
            r_pcb = R("pcb")
            p.dve(lambda e: e.memset(uT[:], 0.0), writes=[r_uT])
            p.dve(lambda e: e.memset(sA[:], 0.0), writes=[r_sA])
            p.dve(lambda e: e.memset(sB[:], 0.0), writes=[r_sB])
            def build_chunk(X):
                ucs = [dict(T=QT_SIZES[t], o=QT_OFFS[t]) for t in range(9)]

                def u_f1(c, X=X):
                    i = cnt["fe"] % NFE
                    cnt["fe"] += 1
                    c["xi"] = i
                    T, o = c["T"], c["o"]
                    p.dma(lambda e: e.dma_start(out=xt[i][:T, :], in_=xq[X, o:o + T, :]), writes=[r_xt[i]], q="sp")
                    rms_rows(xt[i], r_xt[i], T, xs[i], r_xs[i], st2[i], r_st2[i], gmix_b)

                def u_f2(c, X=X):
                    ni = cnt["nt"] % NNT
                    cnt["nt"] += 1
                    c["ni"] = ni
                    transpose_rows(xs[c["xi"]], r_xs[c["xi"]], c["T"], nT[ni], r_nT[ni], bank=0)

                def u_f3(c, X=X):
                    T, o, ni = c["T"], c["o"], c["ni"]
                    ub = 1 + (cnt["ub"] % 2)
                    cnt["ub"] += 1
                    pu = pbank[ub]
                    for g in range(4):
                        for cc in range(8):
                            p.pe(lambda e, g=g, cc=cc: e.matmul(pu[:, g * 128:g * 128 + T], lhsT=wu[:, cc, g * 128:(g + 1) * 128],
                                                                rhs=nT[ni][:, cc, :T], start=(cc == 0), stop=(cc == 7)),
                                 reads=[r_wu_l[cc], r_nT[ni]], writes=[r_pb[ub]])
                    p.act(lambda e: e.activation(out=uT[:, :, PADW + o:PADW + o + T],
                                                 in_=pu[:, :].rearrange("p (g t) -> p g t", g=4)[:, :, :T],
                                                 func=AF.Copy), reads=[r_pb[ub]], writes=[r_uT])

                def p1_it(it):
                    if it < 9:
                        u_f1(ucs[it])
                    if 0 <= it - 1 < 9:
                        u_f2(ucs[it - 1])
                    if 0 <= it - 2 < 9:
                        u_f3(ucs[it - 2])
                def mid():
                    for g, w in enumerate((2, 4, 8, 16)):
                        src = uT[:, g, :]
                        r_src = r_uT
                        bufs = [(sA, r_sA), (sB, r_sB)]
                        sh = 1
                        bi = 0
                        while sh < w:
                            dst, r_dst = bufs[bi]
                            p.dve(lambda e, src=src, dst=dst, sh=sh: e.tensor_tensor(
                                out=dst[:, PADW:PADW + NQ], in0=src[:, PADW:PADW + NQ], in1=src[:, PADW - sh:PADW + NQ - sh],
                                op=ALU.add), reads=[r_src], writes=[r_dst])
                            src, r_src = dst, r_dst
                            sh *= 2
                            bi ^= 1
                        if w < 16:
                            p.dve(lambda e, src=src, g=g, w=w: e.scalar_tensor_tensor(
                                out=plb[:, g, :], in0=src[:, PADW:PADW + NQ], scalar=1.0 / w, in1=uT[:, g, PADW:PADW + NQ],
                                op0=ALU.mult, op1=ALU.subtract), reads=[r_src, r_uT], writes=[r_plb])
                        else:
                            p.dma(lambda e, X=X: e.dma_start(out=pcb[:, :], in_=pc16[X, :].partition_broadcast(128),
                                                             allow_slow_non_contiguous=True), writes=[r_pcb], q="sp")
                            p.dve(lambda e, src=src: e.tensor_tensor(out=src[:, PADW:PADW + NQ], in0=src[:, PADW:PADW + NQ],
                                                                     in1=pcb[:, :], op=ALU.mult),
                                  reads=[r_src, r_pcb], writes=[r_src])
                            p.dve(lambda e, src=src, g=g: e.tensor_tensor(out=plb[:, g, :], in0=src[:, PADW:PADW + NQ],
                                                                          in1=uT[:, g, PADW:PADW + NQ], op=ALU.subtract),
                                  reads=[r_src, r_uT], writes=[r_plb])
                    for g in range(4):
                        for (c0, n) in ((0, 352), (352, 352), (704, 352)):
                            pq = pbank[3]
                            p.pe(lambda e, g=g, c0=c0, n=n: e.matmul(pq[:, 0:n], lhsT=wpl[:, g, :], rhs=plb[:, g, c0:c0 + n],
                                                                     start=True, stop=True),
                                 reads=[r_wpl, r_plb], writes=[r_pb[3]])
                            p.act(lambda e, g=g, c0=c0, n=n, X=X: e.activation(out=mixYa[:, g, X, c0:c0 + n], in_=pq[:, 0:n],
                                                                              func=AF.Identity, scale=pst[:, g:g + 1],
                                                                              bias=bps[:, g:g + 1]),
                                  reads=[r_pb[3], r_const], writes=[r_mixYa[X]])
                wcs = [dict(t=t, T=QT_SIZES[t], o=QT_OFFS[t]) for t in range(9)]

                def w_g1(c, X=X):
                    t, T, o = c["t"], c["T"], c["o"]
                    i = cnt["hm"] % 3
                    cnt["hm"] += 1
                    c["hi"] = i
                    par = cnt["wb"] % 2
                    cnt["wb"] += 1
                    p.dma(lambda e: e.dma_start(out=hm[i][:T, :], in_=xq[X, o:o + T, :]), writes=[r_hm[i]], q="sp")
                    for half in range(2):
                        bk = 4 + 2 * par + half
                        ph = pbank[bk]
                        for fc in range(8):
                            p.pe(lambda e, fc=fc, half=half, ph=ph: e.matmul(
                                ph[:T, :], lhsT=(mixYa[:, fc, X, o:o + T] if fc < 4 else mixYb[:, fc - 4, X, o:o + T]),
                                rhs=wo[:, fc, half * 512:(half + 1) * 512], start=(fc == 0), stop=(fc == 7)),
                                reads=[r_wo_l[fc], r_mixT[X][t], r_mixYa[X]], writes=[r_pb[bk]])
                        p.dve(lambda e, half=half, ph=ph: e.tensor_tensor(
                            out=hm[i][:T, half * 512:(half + 1) * 512], in0=ph[:T, :], in1=hm[i][:T, half * 512:(half + 1) * 512],
                            op=ALU.add), reads=[r_pb[bk], r_hm[i]], writes=[r_hm[i]])
                    p.dma(lambda e: e.dma_start(out=hmid[X, o:o + T, :], in_=hm[i][:T, :]),
                          reads=[r_hm[i]], writes=[r_hmid[X][t]], q="pool")

                def w_g2(c, X=X):
                    i = c["hi"]
                    j = cnt["x2"] % 2
                    cnt["x2"] += 1
                    c["xj"] = j
                    rms_rows(hm[i], r_hm[i], c["T"], xs2[j], r_xs2[j], st3[j], r_st3[j], gffn_b, r_gffn)

                def w_g3(c, X=X):
                    t, T, o, j = c["t"], c["T"], c["o"], c["xj"]
                    pt = pbank_bf[3]
                    for cc in range(8):
                        p.pe(lambda e, cc=cc: e.transpose(out=pt[:, cc * 128:cc * 128 + T],
                                                          in_=xs2[j][:T, cc * 128:(cc + 1) * 128], identity=ident[:T, :T]),
                             reads=[r_xs2[j], r_const], writes=[r_pb[3]])
                    p.dve(lambda e: e.tensor_copy(out=n2T[:, :, X, o:o + T],
                                                  in_=pt[:, :].rearrange("p (c t) -> p c t", c=8)[:, :, :T]),
                          reads=[r_pb[3]], writes=[r_n2T[X][t]])

                def p3_it(it):
                    if it < 9:
                        w_g1(wcs[it])
                    if 0 <= it - 1 < 9:
                        w_g2(wcs[it - 1])
                    if 0 <= it - 2 < 9:
                        w_g3(wcs[it - 2])
                return p1_it, mid, p3_it

            chA = build_chunk(0)
            chB = build_chunk(1)
            for it in range(11):
                chA[0](it)
            chA[1]()
            for it in range(11):
                chA[2](it)
                chB[0](it)
            chB[1]()
            for it in range(11):
                chB[2](it)

        p.barrier()
        outs = []
        with contextlib.ExitStack() as c2:
            NFB = FB_PER_PASS
            wup = SB(c2, "wup", [128, 8, 2, NFB * 128], BF16)
            wdn = SB(c2, "wdn", [128, NFB, D], BF16)
            FGRP = [(0, 4), (4, 8), (8, NFB)]
            r_wup_l = [[R(f"wup{s_}_{k}") for k in range(len(FGRP))] for s_ in range(2)]
            r_wdn_l = [R(f"wdn{f}") for f in range(NFB)]
            Gt = [SB(c2, f"Gt{i}", [128, NFB, FG], BF16) for i in range(2)]
            r_Gt = [R(f"Gt{i}") for i in range(2)]
            cbuf = [[SB(c2, f"cbuf{i}_{s}", [128, FG], F32) for s in range(2)] for i in range(2)]
            r_cbuf = [[R(f"cbuf{i}_{s}") for s in range(2)] for i in range(2)]
            sg = [SB(c2, f"sg{i}", [128, FG], F32) for i in range(2)]
            r_sg = [R(f"sg{i}") for i in range(2)]
            yt = [SB(c2, f"yt{i}", [128, D], F32) for i in range(2)]
            r_yt = [R(f"yt{i}") for i in range(2)]
            r_yst = [R(f"yst{i}") for i in range(2)]
            r_y = [[R(f"y{X}_{t}") for t in range(8)] for X in range(2)]
            wu3 = w_up.rearrange("(c p) n -> p c n", p=128)
            wd3 = w_down.rearrange("(f p) n -> p f n", p=128)
            wk = 0
            gk = 0
            for ps_ in range(NPASS):
                fb0 = ps_ * NFB
                for k, (fa, fz) in enumerate(FGRP):
                    for s in range(2):
                        col0 = s * DFF + (fb0 + fa) * 128
                        p.dma(lambda e, s=s, col0=col0, fa=fa, fz=fz: e.dma_start(
                            out=wup[:, :, s, fa * 128:fz * 128], in_=wu3[:, :, col0:col0 + (fz - fa) * 128]),
                            writes=[r_wup_l[s][k]], q="pool")
                for f in range(NFB):
                    p.dma(lambda e, f=f, fb0=fb0: e.dma_start(out=wdn[:, f, :], in_=wd3[:, fb0 + f, :]), writes=[r_wdn_l[f]], q="pool")
                def up_fb(g, f):
                    X, t0, gb, rn, W = g["X"], g["t0"], g["gb"], g["rn"], g["W"]
                    fb = fb0 + f
                    ci = f % 2
                    for s_ in range(2):
                        pu = pbank[ci * 2 + s_]
                        for c in range(8):
                            p.pe(lambda e, c=c, s_=s_, pu=pu: e.matmul(
                                pu[:, 0:W + 2], lhsT=wup[:, c, s_, f * 128:(f + 1) * 128],
                                rhs=n2T[:, c, X, t0 - 2:t0 + W], start=(c == 0), stop=(c == 7)),
                                reads=[r_wup_l[s_][[k for k, (fa, fz) in enumerate(FGRP) if fa <= f < fz][0]]] + rn,
                                writes=[r_pb[ci * 2 + s_]])
                        col = s_ * 22 + fb
                        cbt = cbuf[ci][s_]
                        rcb = r_cbuf[ci][s_]
                        p.act(lambda e, pu=pu, cbt=cbt, col=col: e.activation(
                            out=cbt[:, 0:W], in_=pu[:, 2:W + 2], func=AF.Identity, scale=cw[:, 2, col:col + 1],
                            bias=cb[:, col:col + 1]), reads=[r_pb[ci * 2 + s_], r_const], writes=[rcb])
                        p.dve(lambda e, pu=pu, cbt=cbt, col=col: e.scalar_tensor_tensor(
                            out=cbt[:, 0:W], in0=pu[:, 1:W + 1], scalar=cw[:, 1, col:col + 1], in1=cbt[:, 0:W],
                            op0=ALU.mult, op1=ALU.add), reads=[r_pb[ci * 2 + s_], rcb, r_const], writes=[rcb])
                        p.dve(lambda e, pu=pu, cbt=cbt, col=col: e.scalar_tensor_tensor(
                            out=cbt[:, 0:W], in0=pu[:, 0:W], scalar=cw[:, 0, col:col + 1], in1=cbt[:, 0:W],
                            op0=ALU.mult, op1=ALU.add), reads=[r_pb[ci * 2 + s_], rcb, r_const], writes=[rcb])
                    p.act(lambda e: e.activation(out=sg[ci][:, 0:W], in_=cbuf[ci][0][:, 0:W], func=AF.Silu),
                          reads=[r_cbuf[ci][0]], writes=[r_sg[ci]])
                    p.dve(lambda e: e.tensor_tensor(out=Gt[gb][:, f, 0:W], in0=sg[ci][:, 0:W], in1=cbuf[ci][1][:, 0:W], op=ALU.mult),
                          reads=[r_sg[ci], r_cbuf[ci][1]], writes=[r_Gt[gb]])

                def down_unit(g, q):
                    X, gb = g["X"], g["gb"]
                    tt = g["tiles"][q]
                    yrow = (tt - 1) * 128
                    yi = q % 2
                    if ps_ == 0:
                        p.dma(lambda e: e.dma_start(out=yt[yi][:, :], in_=hmid[X, QT_OFFS[tt]:QT_OFFS[tt] + 128, :]),
                              reads=[r_hmid[X][tt]], writes=[r_yt[yi]], q="sp")
                    else:
                        p.dma(lambda e: e.dma_start(out=yt[yi][:, :], in_=y[X, yrow:yrow + 128, :]),
                              reads=[r_y[X][tt - 1]], writes=[r_yt[yi]], q="sp")
                    for half in range(2):
                        bk = 4 + 2 * (q % 2) + half
                        pd = pbank[bk]
                        for f in range(NFB):
                            p.pe(lambda e, f=f, half=half, pd=pd: e.matmul(
                                pd[:, :], lhsT=Gt[gb][:, f, q * 128:(q + 1) * 128],
                                rhs=wdn[:, f, half * 512:(half + 1) * 512], start=(f == 0), stop=(f == NFB - 1)),
                                reads=[r_Gt[gb], r_wdn_l[f]], writes=[r_pb[bk]])
                        p.dve(lambda e, half=half, pd=pd: e.tensor_tensor(
                            out=yt[yi][:, half * 512:(half + 1) * 512], in0=pd[:, :],
                            in1=yt[yi][:, half * 512:(half + 1) * 512], op=ALU.add),
                            reads=[r_pb[bk], r_yt[yi]], writes=[r_yt[yi]])
                    od = p.dma(lambda e: e.dma_start(out=y[X, yrow:yrow + 128, :], in_=yt[yi][:, :]),
                               reads=[r_yt[yi]], writes=[r_y[X][tt - 1]], q="pool", sem_res=r_yst[yi])
                    if ps_ == NPASS - 1:
                        outs.append(od)

                groups = []
                for X in range(2):
                    for tiles in ([1, 2, 3], [4, 5, 6], [7, 8]):
                        groups.append(dict(X=X, t0=QT_OFFS[tiles[0]], gb=gk % 2, tiles=tiles, W=128 * len(tiles),
                                           rn=[r_n2T[X][t] for t in tiles] + [r_n2T[X][tiles[0] - 1]]))
                        gk += 1
                slots = {2: 0, 5: 1, 8: 2}
                prev = None
                for g in groups:
                    for f in range(NFB):
                        up_fb(g, f)
                        if prev is not None and f in slots and slots[f] < len(prev["tiles"]):
                            down_unit(prev, slots[f])
                    prev = g
                for q in range(len(prev["tiles"])):
                    down_unit(prev, q)
            p.emit(final_waits=outs)
    return nc


_NC_CACHE = {}


def _host_layout(x, meta_tokens):
    per_core = []
    for core in range(8):
        b, j = core // 4, core % 4
        xb = x[b]
        chunks = (j, 7 - j)
        xq = np.zeros((2, NQ, D), np.float32)
        qval = np.ones((2, NQ), np.float32)
        mval = np.ones((2, 16), np.float32)
        for X, c in enumerate(chunks):
            s = 1024 * c
            xq[X, 32:] = xb[s:s + 1024]
            if c == 0:
                xq[X, 16:32] = meta_tokens
                qval[X, 0:16] = 0.0
                mval[X, :] = 0.0
            else:
                xq[X, 0:32] = xb[s - 32:s]
        xk = np.zeros((7, 1024, D), np.float32)
        kval = np.zeros((7, 1024), np.float32)
        sel = np.zeros((14,), np.float32)
        pc16 = np.full((2, NQ), 1.0 / 16.0, np.float32)
        if chunks[0] == 0:
            for pp in range(15):
                pc16[0, 16 + pp] = 1.0 / float(pp + 1)
        for i in range(7):
            if i < j:
                X, c, t = 0, chunks[0], i
            else:
                X, c, t = 1, chunks[1], i - j
            lim = 1024 * c - 32
            lo = 1024 * t
            hi = min(lo + 1024, lim)
            xk[i, :hi - lo] = xb[lo:hi]
            kval[i, :hi - lo] = 1.0
            sel[7 * X + i] = 1.0
        per_core.append(dict(xq=xq, xk=xk, xm=np.ascontiguousarray(meta_tokens, np.float32), kval=kval, qval=qval,
                             mval=mval, sel=sel, pc16=pc16))
    return per_core


def kernel(x, meta_tokens, norm_mix_g, w_in, w_pool, b_pool, pool_scale, q_norm_g, k_norm_g,
           lambda_q1, lambda_k1, lambda_q2, lambda_k2, subln_g, w_out, norm_ffn_g,
           w_up, conv_w, conv_b, w_down):
    f = lambda a: np.ascontiguousarray(np.asarray(a, np.float32))
    x = f(x)
    meta_tokens = f(meta_tokens)
    shared = {
        "w_in": f(w_in)[0], "w_pool": f(w_pool)[0], "b_pool": f(b_pool)[0], "pool_scale": f(pool_scale)[0],
        "q_norm_g": f(q_norm_g)[0], "k_norm_g": f(k_norm_g)[0], "lambda_q1": f(lambda_q1)[0],
        "lambda_k1": f(lambda_k1)[0], "lambda_q2": f(lambda_q2)[0], "lambda_k2": f(lambda_k2)[0],
        "subln_g": f(subln_g)[0], "w_out": f(w_out)[0], "norm_mix_g": f(norm_mix_g)[0],
        "norm_ffn_g": f(norm_ffn_g)[0], "w_up": f(w_up)[0], "conv_w": f(conv_w)[0], "conv_b": f(conv_b)[0],
        "w_down": f(w_down)[0],
    }
    if "nc" not in _NC_CACHE:
        _NC_CACHE["nc"] = build_program()
    nc = _NC_CACHE["nc"]
    per_core = _host_layout(x, meta_tokens)
    in_maps = [dict(shared, **pc) for pc in per_core]
    res = run_bass_kernel_spmd(nc, in_maps, core_ids=list(range(8)))
    out = np.empty((2, 8192, D), np.float32)
    for core in range(8):
        b, j = core // 4, core % 4
        yc = res.results[core]["y"]
        out[b, 1024 * j:1024 * (j + 1)] = yc[0]
        out[b, 1024 * (7 - j):1024 * (8 - j)] = yc[1]
    return out
```
